# Optimizing a Trainium2 kernel written in Bass

```python
import jax, jax.numpy as jnp
from jax import lax
import numpy as np

D_MODEL = 4096
BATCH = 2
SEQ = 4096
DEPTH = 2

CHUNK = 64
CONV_K = 4
ML_WIDTH = D_MODEL // 2
ML_HEADS = 4
ML_DV = ML_WIDTH // ML_HEADS
ML_DK = ML_DV // 2
GDN_WIDTH = D_MODEL // 2
GDN_HEAD_DIM = 128
GDN_HEADS = GDN_WIDTH // GDN_HEAD_DIM
SSM_WIDTH = D_MODEL
SSM_HEAD_DIM = 64
SSM_HEADS = SSM_WIDTH // SSM_HEAD_DIM
SSM_GROUPS = 8
SSM_STATE = 128
MIX_WIDTH = ML_WIDTH + GDN_WIDTH + SSM_WIDTH

DEEPNORM_ALPHA = (2 * DEPTH) ** 0.25
DEEPNORM_BETA = (8 * DEPTH) ** -0.25
RMS_EPS = 1e-6
LN_EPS = 1e-5

ML_SPLITS = [ML_HEADS * ML_DK, ML_HEADS * ML_DK, ML_WIDTH, ML_WIDTH, ML_WIDTH, ML_HEADS, ML_HEADS]
GDN_SPLITS = [GDN_WIDTH, GDN_WIDTH, GDN_WIDTH, GDN_WIDTH, GDN_HEADS, GDN_HEADS]
SSM_SPLITS = [SSM_WIDTH, SSM_WIDTH, SSM_GROUPS * SSM_STATE, SSM_GROUPS * SSM_STATE, SSM_HEADS]
ML_VALUE = [False, False, True, False, False, False, False]
GDN_VALUE = [False, False, True, False, False, False]
SSM_VALUE = [False, True, False, False, False]
ML_COLS = sum(ML_SPLITS)
GDN_COLS = sum(GDN_SPLITS)
SSM_COLS = sum(SSM_SPLITS)
IN_COLS = ML_COLS + GDN_COLS + SSM_COLS

kernel_name = "hybrid_mlstm_gdn_mamba2_deepnorm"


def _split(u, sizes):
    idx = np.cumsum(sizes)[:-1].tolist()
    return jnp.split(u, idx, axis=-1)


def _to_chunks(a):
    b, s = a.shape[:2]
    return jnp.moveaxis(a.reshape(b, s // CHUNK, CHUNK, *a.shape[2:]), 1, 0)


def _from_chunks(a):
    nc, b, l = a.shape[:3]
    return jnp.moveaxis(a, 0, 1).reshape(b, nc * l, *a.shape[3:])


def _rms(u, w):
    return u * lax.rsqrt(jnp.mean(u * u, -1, keepdims=True) + RMS_EPS) * w


def _l2n(u):
    return u * lax.rsqrt(jnp.sum(u * u, -1, keepdims=True) + RMS_EPS)


def _layernorm(u, g, b):
    u = u.astype(jnp.float32)
    mu = jnp.mean(u, -1, keepdims=True)
    var = jnp.mean(jnp.square(u - mu), -1, keepdims=True)
    return (u - mu) * lax.rsqrt(var + LN_EPS) * g.astype(jnp.float32) + b.astype(jnp.float32)


def _causal_conv(u, w):
    return lax.conv_general_dilated(u, w[:, None, :], (1,), [(w.shape[0] - 1, 0)],
                                    dimension_numbers=("NWC", "WIO", "NWC"),
                                    feature_group_count=u.shape[-1])


def _masks():
    causal = jnp.tril(jnp.ones((CHUNK, CHUNK), bool))
    strict = jnp.tril(jnp.ones((CHUNK, CHUNK), bool), k=-1)
    return causal, strict


def _mlstm_group(q, k, v, o, z, i_raw, f_raw, i_bias, f_bias, norm_w):
    f32 = jnp.float32
    b, s, _ = q.shape
    causal, _ = _masks()
    q = q.astype(f32).reshape(b, s, ML_HEADS, ML_DK) * ML_DK ** -0.5
    k = k.astype(f32).reshape(b, s, ML_HEADS, ML_DK)
    v = v.astype(f32).reshape(b, s, ML_HEADS, ML_DV)
    li = i_raw.astype(f32) + i_bias.astype(f32)
    lf = jax.nn.log_sigmoid(f_raw.astype(f32) + f_bias.astype(f32))

    def step(carry, inp):
        C, n, m = carry
        q_, k_, v_, li_, lf_ = inp
        bcum = jnp.cumsum(lf_, axis=1).transpose(0, 2, 1)
        li_h = li_.transpose(0, 2, 1)
        Dlog = jnp.where(causal, bcum[..., :, None] - bcum[..., None, :] + li_h[..., None, :], -jnp.inf)
        inter = bcum + m[..., None]
        m_t = jnp.maximum(inter, Dlog.max(-1))
        w_intra = jnp.exp(Dlog - m_t[..., None])
        w_inter = jnp.exp(inter - m_t)
        sc = jnp.einsum("blhd,bshd->bhls", q_, k_) * w_intra
        num = (jnp.einsum("bhls,bshv->blhv", sc, v_)
               + w_inter.transpose(0, 2, 1)[..., None] * jnp.einsum("blhd,bhdv->blhv", q_, C))
        den = sc.sum(-1) + w_inter * jnp.einsum("blhd,bhd->bhl", q_, n)
        h = num / jnp.maximum(jnp.abs(den), jnp.exp(-m_t)).transpose(0, 2, 1)[..., None]
        bL = bcum[..., -1]
        g_s = bL[..., None] - bcum + li_h
        m_new = jnp.maximum(bL + m, g_s.max(-1))
        ws = jnp.exp(g_s - m_new[..., None])
        dec = jnp.exp(bL + m - m_new)
        C = dec[..., None, None] * C + jnp.einsum("bhs,bshd,bshv->bhdv", ws, k_, v_)
        n = dec[..., None] * n + jnp.einsum("bhs,bshd->bhd", ws, k_)
        return (C, n, m_new), h

    init = (jnp.zeros((b, ML_HEADS, ML_DK, ML_DV), f32), jnp.zeros((b, ML_HEADS, ML_DK), f32),
            jnp.zeros((b, ML_HEADS), f32))
    _, h = lax.scan(step, init, (_to_chunks(q), _to_chunks(k), _to_chunks(v), _to_chunks(li), _to_chunks(lf)))
    h = _from_chunks(h)
    h = _rms(h, norm_w.astype(f32).reshape(ML_HEADS, ML_DV)).reshape(b, s, ML_WIDTH)
    return h * jax.nn.sigmoid(o.astype(f32)) * jax.nn.silu(z.astype(f32))


def _gdn_group(q, k, v, z, beta_raw, a_raw, conv_w, A_log, dt_bias, norm_w):
    f32 = jnp.float32
    b, s, _ = q.shape
    causal, strict = _masks()
    qkv = jax.nn.silu(_causal_conv(jnp.concatenate([q, k, v], -1).astype(f32), conv_w.astype(f32)))
    q, k, v = jnp.split(qkv, 3, axis=-1)
    q = _l2n(q.reshape(b, s, GDN_HEADS, GDN_HEAD_DIM)) * GDN_HEAD_DIM ** -0.5
    k = _l2n(k.reshape(b, s, GDN_HEADS, GDN_HEAD_DIM))
    v = v.reshape(b, s, GDN_HEADS, GDN_HEAD_DIM)
    beta = jax.nn.sigmoid(beta_raw.astype(f32))
    g = -jnp.exp(A_log.astype(f32)) * jax.nn.softplus(a_raw.astype(f32) + dt_bias.astype(f32))

    def step(S, inp):
        q_, k_, v_, b_, g_ = inp
        q_, k_, v_ = (t.transpose(0, 2, 1, 3) for t in (q_, k_, v_))
        b_, g_ = b_.transpose(0, 2, 1), g_.transpose(0, 2, 1)
        gc = jnp.cumsum(g_, -1)
        decay = jnp.exp(jnp.where(causal, gc[..., :, None] - gc[..., None, :], -jnp.inf))
        A = jnp.where(strict, b_[..., :, None] * jnp.einsum("bhld,bhsd->bhls", k_, k_) * decay, 0.0)
        rhs = jnp.concatenate([v_ * b_[..., None], k_ * (b_ * jnp.exp(gc))[..., None]], -1)
        X = lax.linalg.triangular_solve(A, rhs, left_side=True, lower=True, unit_diagonal=True)
        u, w = X[..., :GDN_HEAD_DIM], X[..., GDN_HEAD_DIM:]
        v_new = u - jnp.einsum("bhlk,bhkv->bhlv", w, S)
        attn = jnp.einsum("bhlk,bhsk->bhls", q_, k_) * decay
        o = (jnp.einsum("bhlk,bhkv->bhlv", q_ * jnp.exp(gc)[..., None], S)
             + jnp.einsum("bhls,bhsv->bhlv", attn, v_new))
        gL = gc[..., -1:]
        S = S * jnp.exp(gL)[..., None] + jnp.einsum("bhsk,bhsv->bhkv", k_ * jnp.exp(gL - gc)[..., None], v_new)
        return S, o.transpose(0, 2, 1, 3)

    init = jnp.zeros((b, GDN_HEADS, GDN_HEAD_DIM, GDN_HEAD_DIM), f32)
    _, o = lax.scan(step, init, (_to_chunks(q), _to_chunks(k), _to_chunks(v), _to_chunks(beta), _to_chunks(g)))
    o = _rms(_from_chunks(o), norm_w.astype(f32))
    return o.reshape(b, s, GDN_WIDTH) * jax.nn.silu(z.astype(f32))


def _ssm_group(z, xs, Bm, Cm, dt_raw, conv_w, conv_b, A_log, dt_bias, Dskip, norm_w):
    f32 = jnp.float32
    b, s, _ = z.shape
    causal, _ = _masks()
    R = SSM_HEADS // SSM_GROUPS
    xbc = jax.nn.silu(_causal_conv(jnp.concatenate([xs, Bm, Cm], -1).astype(f32), conv_w.astype(f32))
                      + conv_b.astype(f32))
    x, Bm, Cm = _split(xbc, [SSM_WIDTH, SSM_GROUPS * SSM_STATE, SSM_GROUPS * SSM_STATE])
    x = x.reshape(b, s, SSM_HEADS, SSM_HEAD_DIM)
    Bm = Bm.reshape(b, s, SSM_GROUPS, SSM_STATE)
    Cm = Cm.reshape(b, s, SSM_GROUPS, SSM_STATE)
    dt = jax.nn.softplus(dt_raw.astype(f32) + dt_bias.astype(f32))
    a = dt * (-jnp.exp(A_log.astype(f32)))
    xdt = x * dt[..., None]

    def step(st, inp):
        x_, a_, B_, C_ = inp
        acs = jnp.cumsum(a_, axis=1).transpose(0, 2, 1)
        Lm = jnp.exp(jnp.where(causal, acs[..., :, None] - acs[..., None, :], -jnp.inf))
        Lg = Lm.reshape(b, SSM_GROUPS, R, CHUNK, CHUNK)
        xg = x_.reshape(b, CHUNK, SSM_GROUPS, R, SSM_HEAD_DIM)
        CB = jnp.einsum("blgn,bsgn->bgls", C_, B_)
        y_diag = jnp.einsum("bgls,bgrls,bsgrp->blgrp", CB, Lg, xg)
        sg = st.reshape(b, SSM_GROUPS, R, SSM_HEAD_DIM, SSM_STATE)
        y_off = jnp.einsum("blgn,bgrpn,bgrl->blgrp", C_, sg, jnp.exp(acs).reshape(b, SSM_GROUPS, R, CHUNK))
        aL = acs[..., -1]
        wdec = jnp.exp(aL[..., None] - acs).reshape(b, SSM_GROUPS, R, CHUNK)
        new = jnp.einsum("bsgn,bgrs,bsgrp->bgrpn", B_, wdec, xg).reshape(st.shape)
        st = st * jnp.exp(aL)[..., None, None] + new
        return st, (y_diag + y_off).reshape(b, CHUNK, SSM_HEADS, SSM_HEAD_DIM)

    init = jnp.zeros((b, SSM_HEADS, SSM_HEAD_DIM, SSM_STATE), f32)
    _, y = lax.scan(step, init, (_to_chunks(xdt), _to_chunks(a), _to_chunks(Bm), _to_chunks(Cm)))
    y = _from_chunks(y) + Dskip.astype(f32)[:, None] * x
    y = y.reshape(b, s, SSM_WIDTH) * jax.nn.silu(z.astype(f32))
    y = _rms(y.reshape(b, s, SSM_GROUPS, SSM_WIDTH // SSM_GROUPS),
             norm_w.astype(f32).reshape(SSM_GROUPS, SSM_WIDTH // SSM_GROUPS))
    return y.reshape(b, s, SSM_WIDTH)


def setup_inputs(seed: int = 0) -> dict:
    key = jax.random.key(seed)
    ks = jax.random.split(key, 20)
    f32 = jnp.float32

    def inv_softplus_dt(k, n):
        dt = jnp.exp(jax.random.uniform(k, (DEPTH, n), f32, np.log(1e-3), np.log(1e-1)))
        return dt + jnp.log(-jnp.expm1(-dt))

    segs = list(zip(ML_SPLITS + GDN_SPLITS + SSM_SPLITS, ML_VALUE + GDN_VALUE + SSM_VALUE))
    col_scale = jnp.concatenate([jnp.full((n,), DEEPNORM_BETA if isv else 1.0, f32) for n, isv in segs])
    x = jax.random.normal(ks[0], (BATCH, SEQ, D_MODEL), f32)
    w_in = jax.random.normal(ks[1], (DEPTH, D_MODEL, IN_COLS), f32) * (D_MODEL ** -0.5) * col_scale
    w_out = jax.random.normal(ks[2], (DEPTH, MIX_WIDTH, D_MODEL), f32) * (MIX_WIDTH ** -0.5) * DEEPNORM_BETA
    ml_i_bias = -2.0 + 0.1 * jax.random.normal(ks[3], (DEPTH, ML_HEADS), f32)
    ml_f_bias = jnp.linspace(3.0, 6.0, ML_HEADS, dtype=f32) + 0.1 * jax.random.normal(ks[4], (DEPTH, ML_HEADS), f32)
    ml_norm_w = 1.0 + 0.02 * jax.random.normal(ks[5], (DEPTH, ML_WIDTH), f32)
    gdn_conv_w = jax.random.normal(ks[6], (DEPTH, CONV_K, 3 * GDN_WIDTH), f32) * CONV_K ** -0.5
    gdn_A_log = jnp.log(jax.random.uniform(ks[7], (DEPTH, GDN_HEADS), f32, 1.0, 16.0))
    gdn_dt_bias = inv_softplus_dt(ks[8], GDN_HEADS)
    gdn_norm_w = 1.0 + 0.02 * jax.random.normal(ks[9], (DEPTH, GDN_HEAD_DIM), f32)
    conv_ch = SSM_WIDTH + 2 * SSM_GROUPS * SSM_STATE
    ssm_conv_w = jax.random.normal(ks[10], (DEPTH, CONV_K, conv_ch), f32) * CONV_K ** -0.5
    ssm_conv_b = 0.02 * jax.random.normal(ks[11], (DEPTH, conv_ch), f32)
    ssm_A_log = jnp.log(jax.random.uniform(ks[12], (DEPTH, SSM_HEADS), f32, 1.0, 16.0))
    ssm_dt_bias = inv_softplus_dt(ks[13], SSM_HEADS)
    ssm_D = 1.0 + 0.1 * jax.random.normal(ks[14], (DEPTH, SSM_HEADS), f32)
    ssm_norm_w = 1.0 + 0.02 * jax.random.normal(ks[15], (DEPTH, SSM_WIDTH), f32)
    ln_g = 1.0 + 0.02 * jax.random.normal(ks[16], (DEPTH, D_MODEL), f32)
    ln_b = 0.02 * jax.random.normal(ks[17], (DEPTH, D_MODEL), f32)
    return {"x": x, "w_in": w_in, "w_out": w_out, "ml_i_bias": ml_i_bias, "ml_f_bias": ml_f_bias,
            "ml_norm_w": ml_norm_w, "gdn_conv_w": gdn_conv_w, "gdn_A_log": gdn_A_log,
            "gdn_dt_bias": gdn_dt_bias, "gdn_norm_w": gdn_norm_w, "ssm_conv_w": ssm_conv_w,
            "ssm_conv_b": ssm_conv_b, "ssm_A_log": ssm_A_log, "ssm_dt_bias": ssm_dt_bias,
            "ssm_D": ssm_D, "ssm_norm_w": ssm_norm_w, "ln_g": ln_g, "ln_b": ln_b}


def reference(x, w_in, w_out, ml_i_bias, ml_f_bias, ml_norm_w, gdn_conv_w, gdn_A_log, gdn_dt_bias,
              gdn_norm_w, ssm_conv_w, ssm_conv_b, ssm_A_log, ssm_dt_bias, ssm_D, ssm_norm_w, ln_g, ln_b):
    for l in range(DEPTH):
        proj = jnp.einsum("bsd,dn->bsn", x, w_in[l])
        ml_p, gdn_p, ssm_p = _split(proj, [ML_COLS, GDN_COLS, SSM_COLS])
        y_ml = _mlstm_group(*_split(ml_p, ML_SPLITS), ml_i_bias[l], ml_f_bias[l], ml_norm_w[l])
        y_gdn = _gdn_group(*_split(gdn_p, GDN_SPLITS), gdn_conv_w[l], gdn_A_log[l], gdn_dt_bias[l], gdn_norm_w[l])
        y_ssm = _ssm_group(*_split(ssm_p, SSM_SPLITS), ssm_conv_w[l], ssm_conv_b[l], ssm_A_log[l],
                           ssm_dt_bias[l], ssm_D[l], ssm_norm_w[l])
        y_mix = jnp.concatenate([y_ml, y_gdn, y_ssm], -1).astype(x.dtype)
        y = jnp.einsum("bsm,md->bsd", y_mix, w_out[l])
        x = _layernorm(DEEPNORM_ALPHA * x.astype(jnp.float32) + y.astype(jnp.float32), ln_g[l], ln_b[l]).astype(x.dtype)
    return x
```

```python
from contextlib import ExitStack
import os
GSTOP = int(os.environ.get('GSTOP', '99'))
import numpy as np
import concourse.bass as bass
import concourse.mybir as mybir
from concourse.bass_utils import run_bass_kernel_spmd

F32 = mybir.dt.float32
BF16 = mybir.dt.bfloat16
ALU = mybir.AluOpType
AF = mybir.ActivationFunctionType

SIG_ROT = 30000


class Prog:
    ENG = ("pe", "act", "dve", "pool", "sp")

    def __init__(self, nc, stack):
        self._stack = stack
        self.nc = nc
        self.ops = []
        self.lw = {}
        self.rs = {}
        self.dma_cnt = {}

    def op(self, eng, fn, r=(), w=(), dma=None):
        r = [self.canon(k) for k in r]
        w = [self.canon(k) for k in w]
        i = len(self.ops)
        deps = set()
        raw = set()
        for k in r:
            j = self.lw.get(k)
            if j is not None:
                deps.add(j)
                raw.add(j)
        for k in w:
            j = self.lw.get(k)
            if j is not None:
                deps.add(j)
            for j in self.rs.get(k, ()):
                deps.add(j)
        keep = set()
        for j in deps:
            pj = self.ops[j]
            if pj["dma"] is None and dma is None and pj["eng"] == eng:
                if eng == "pe":
                    continue
            keep.add(j)
        o = dict(eng=eng, fn=fn, deps=keep, dma=dma, sig=False, sidx=None)
        if dma is not None:
            c = self.dma_cnt.get(dma, 0) + 1
            self.dma_cnt[dma] = c
            o["sidx"] = c
            o["sig"] = True
        self.ops.append(o)
        for j in keep:
            self.ops[j]["sig"] = True
        for k in r:
            self.rs.setdefault(k, []).append(i)
        for k in w:
            self.lw[k] = i
            self.rs[k] = []
        return i

    ALIAS = {"g_rn0": "g_rn", "s_rstd0": "s_rstd", "g_rstd0": "g_rstd", "m_rstd0": "m_rstd", "m_rden0": "m_rden",
             "m_W0": "m_W", "m_d1b": "m_d1", "m_a2": "m_a", "s_dt_b": "s_dt_a", "g_sp_b": "g_sp_a", "sUh": "Uh", "gUh": "Uh",
             "m_wi": "m_a", "m_emm": "m_l1", "m_nG": "m_t", "g_ktm": ("s_xtm", 0), "g_vtm": ("s_xtm", 1),
             "g_Rv": "s_xdt", "g_Rk": "s_xdt"}

    def canon(self, k):
        if isinstance(k, tuple):
            if k[0] == "cvp":
                return ("cv",) + k[1:]
            if k[0] in ("sU", "gU"):
                return ("U",) + k[1:]
            if k[0] in ("g_sz", "m_g"):
                return ("s_sz",) + k[1:]
            if k[0] == "m_zt":
                return ("s_sz", k[1], 2 + k[2])
            return k
        return self.ALIAS.get(k, k)

    def emit(self, final_wait_eng="sp"):
        nc = self.nc
        cnt = {e: 0 for e in self.ENG}
        for o in self.ops:
            if o["dma"] is None and o["sig"]:
                cnt[o["eng"]] += 1
                o["sidx"] = cnt[o["eng"]]
        sems = {}
        stack = self._stack
        for e in self.ENG:
            n = (cnt[e] + SIG_ROT - 1) // SIG_ROT
            for q in range(max(n, 1)):
                sems[("E", e, q)] = stack.enter_context(nc.semaphore(f"s_{e}_{q}"))
        for k in self.dma_cnt:
            nm = "d_" + "_".join(str(x) for x in (k if isinstance(k, tuple) else (k,)))
            sems[("D", k)] = stack.enter_context(nc.semaphore(nm))
        self.nsem = len(sems)

        def chan(o):
            if o["dma"] is not None:
                return ("D", o["dma"]), 16 * o["sidx"]
            q, rr = divmod(o["sidx"] - 1, SIG_ROT)
            return ("E", o["eng"], q), rr + 1

        per = {e: [] for e in self.ENG}
        for o in self.ops:
            per[o["eng"]].append(o)
        final = {}
        for o in self.ops:
            if o["dma"] is not None:
                ch, v = chan(o)
                final[ch] = max(final.get(ch, 0), v)
        block = stack.enter_context(nc.Block())
        deco = {"pe": block.tensor, "act": block.scalar, "dve": block.vector,
                "pool": block.gpsimd, "sp": block.sync}
        ops = self.ops

        def make(e):
            def body(engobj):
                seen = {}
                for o in per[e]:
                    need = {}
                    for j in o["deps"]:
                        ch, v = chan(ops[j])
                        if v > need.get(ch, 0):
                            need[ch] = v
                    for ch, v in need.items():
                        if seen.get(ch, 0) >= v:
                            continue
                        engobj.wait_ge(sems[ch], v)
                        seen[ch] = v
                    ins = o["fn"](engobj)
                    if o["sig"]:
                        ch, v = chan(o)
                        ins.then_inc(sems[ch], 16 if o["dma"] is not None else 1)
                if e == final_wait_eng:
                    for ch, v in final.items():
                        if seen.get(ch, 0) < v:
                            engobj.wait_ge(sems[ch], v)
            return body

        for e in self.ENG:
            if per[e] or e == final_wait_eng:
                deco[e](make(e))


D_MODEL = 4096
SEQ = 4096
BATCH = 2
DEPTH = 2
NG = 4
WT = 256
NT = 28
NCOL = NT * WT
MIXW = 2048
RMS_EPS = 1e-6
LN_EPS = 1e-5
ALPHA = (2 * DEPTH) ** 0.25

PR_SDTB, PR_SALOG, PR_SD, PR_GDTB, PR_GALOG, PR_MIB, PR_MFB = 0, 16, 32, 48, 52, 56, 57
PR_MLNW, PR_GNW, PR_SNW = 64, 64 + 512, 64 + 512 + 128
PR_N = 64 + 512 + 128 + 1024
C_ID, C_TRI, C_U, C_ONES, C_BDS, C_BD, C_OFFT, C_NEG = range(8)
NCONST = 8


def make_consts():
    i = np.arange(128)
    k, l = i[:, None], i[None, :]
    c = np.zeros((NCONST, 128, 128), np.float32)
    c[C_ID] = np.eye(128)
    c[C_TRI] = (k <= l)
    c[C_U] = (k > l)
    c[C_ONES] = 1.0
    same = (k // 64) == (l // 64)
    c[C_BDS] = (k > l) & same
    c[C_BD] = same
    c[C_OFFT] = (k < 64) & (l >= 64)
    c[C_NEG] = np.where(k > l, -30000.0, 0.0)
    return np.ascontiguousarray(c.transpose(1, 0, 2).reshape(128, NCONST * 128))


def build_layer(D=D_MODEL, S=SEQ, TB=256, debug=False, parts=("ssm", "gdn", "ml")):
    KC = D // 128
    NB = S // TB
    NTT = TB // 128
    nc = bass.Bass("TRN2", target_bir_lowering=False)
    x_d = nc.dram_tensor("x", [S, D], F32, kind="ExternalInput").ap()
    wcat_d = nc.dram_tensor("wcat", [D, NCOL], F32, kind="ExternalInput").ap()
    wout_d = nc.dram_tensor("wout", [MIXW, D], F32, kind="ExternalInput").ap()
    convp_d = nc.dram_tensor("convp", [128, 24 * 5], F32, kind="ExternalInput").ap()
    prow_d = nc.dram_tensor("prow", [1, PR_N], F32, kind="ExternalInput").ap()
    const_d = nc.dram_tensor("consts", [128, NCONST * 128], F32, kind="ExternalInput").ap()
    ypart_d = nc.dram_tensor("ypart", [S, D], F32, kind="ExternalOutput").ap()
    wscr_d = nc.dram_tensor("wscr", [NT, 128, KC, WT], BF16, kind="Internal").ap()
    NOC = D // WT
    woscr_d = nc.dram_tensor("woscr", [NOC, 128, 16, WT], BF16, kind="Internal").ap()
    if debug:
        ymix_d = nc.dram_tensor("ymix", [16, 128, S], BF16, kind="ExternalOutput").ap()

    with ExitStack() as st:
        def T(name, shape, dt=F32):
            return st.enter_context(nc.sbuf_tensor("sb_" + name, shape, dt))
        P = Prog(nc, st)
        psb = [st.enter_context(nc.psum_tensor(f"psb{i}", [128, 512], F32)) for i in range(8)]

        class PSA:
            def __init__(self, banks):
                self.banks = banks
                self.pos = 0

            def get(self, nq):
                bank = self.banks[self.pos % len(self.banks)]
                self.pos += 1
                return psb[bank][:, 0:nq * 128], [("ps", bank)]
        ps_proj = PSA([0, 1])
        ps_mix = PSA([2, 3, 4, 5, 6, 7])

        consts = T("consts", [128, NCONST, 128])
        cid = consts[:, C_ID, :]
        ctri = consts[:, C_TRI, :]
        cU = consts[:, C_U, :]
        cones = consts[:, C_ONES, :]
        idb = T("idb", [128, 128], BF16)
        prow = T("prow", [128, PR_N])
        convp = T("convp", [128, 24, 5])
        nA_s = T("nA_s", [128, 16])
        nA_g = T("nA_g", [128, 4])
        XLW = 512 if D >= 512 else D
        xld = [T(f"xld{i}", [128, XLW]) for i in range(2)]
        xT = T("xT", [128, KC, TB], BF16)
        wt = [T(f"wt{i}", [128, max(KC, 16), WT], BF16) for i in range(2)]
        CK = 2 if KC >= 2 else 1
        cin = [T(f"cin{i}", [128, CK, WT]) for i in range(2)]
        cout = [T(f"cout{i}", [128, CK, WT], BF16) for i in range(2)]
        U = T("U", [128, 12, 3 + TB])
        hist_s = T("hist_s", [128, 12, 3])
        hist_g = T("hist_g", [128, 12, 3])
        cv = T("cv", [128, 12, TB])
        ymixT = T("ymixT", [128, 16, TB], BF16)
        stage = [T(f"stage{i}", [128, WT]) for i in range(2)]
        junk = T("junk", [128, 512])
        epsc = T("epsc", [128, 1])

        P.op("sp", lambda e: e.dma_start(out=consts[:], in_=const_d.rearrange("p (c k) -> p c k", c=NCONST)), w=["consts"], dma="consts")
        P.op("sp", lambda e: e.dma_start(out=prow[:], in_=prow_d.partition_broadcast(128)[:, 0, :]), w=["prow"], dma="prow")
        P.op("sp", lambda e: e.dma_start(out=convp[:], in_=convp_d.rearrange("p (c k) -> p c k", c=24)), w=["convp"], dma="convp")
        P.op("dve", lambda e: e.tensor_copy(out=idb[:], in_=cid), r=["consts"], w=["idb"])
        P.op("pool", lambda e: e.memset(epsc[:], RMS_EPS), w=["epsc"])
        P.op("act", lambda e: e.activation(out=nA_s[:], in_=prow[:, PR_SALOG:PR_SALOG + 16], func=AF.Exp), r=["prow"], w=["nA_s"])
        P.op("dve", lambda e: e.tensor_scalar(out=nA_s[:], in0=nA_s[:], scalar1=-1.0, scalar2=None, op0=ALU.mult), r=["nA_s"], w=["nA_s"])
        P.op("act", lambda e: e.activation(out=nA_g[:], in_=prow[:, PR_GALOG:PR_GALOG + 4], func=AF.Exp), r=["prow"], w=["nA_g"])
        P.op("dve", lambda e: e.tensor_scalar(out=nA_g[:], in0=nA_g[:], scalar1=-1.0, scalar2=None, op0=ALU.mult), r=["nA_g"], w=["nA_g"])

        wv = wcat_d.rearrange("(kc p) c -> p kc c", p=128)
        n = 0
        cast_engs = ["dve", "pool", "act"]
        for t in range(NT):
            for k0 in range(0, KC, CK):
                s = n % 2
                P.op("sp", lambda e, s=s, t=t, k0=k0: e.dma_start(out=cin[s][:], in_=wv[:, k0:k0 + CK, t * WT:(t + 1) * WT]),
                     w=[("cin", s)], dma=("cin", s))
                ce = cast_engs[n % 3]
                if ce == "act":
                    P.op("act", lambda e, s=s: e.activation(out=cout[s][:], in_=cin[s][:], func=AF.Copy), r=[("cin", s)], w=[("cout", s)])
                else:
                    P.op(ce, lambda e, s=s: e.tensor_copy(out=cout[s][:], in_=cin[s][:]), r=[("cin", s)], w=[("cout", s)])
                P.op("pool", lambda e, s=s, t=t, k0=k0: e.dma_start(out=wscr_d[t][:, k0:k0 + CK, :], in_=cout[s][:]),
                     r=[("cout", s)], w=[("wscr", t)], dma=("cout", s))
                n += 1
        wov = wout_d.rearrange("(kc p) c -> p kc c", p=128)
        for t in range(NOC):
            for k0 in range(0, 16, CK):
                s = n % 2
                P.op("sp", lambda e, s=s, t=t, k0=k0: e.dma_start(out=cin[s][:], in_=wov[:, k0:k0 + CK, t * WT:(t + 1) * WT]),
                     w=[("cin", s)], dma=("cin", s))
                ce = cast_engs[n % 3]
                if ce == "act":
                    P.op("act", lambda e, s=s: e.activation(out=cout[s][:], in_=cin[s][:], func=AF.Copy), r=[("cin", s)], w=[("cout", s)])
                else:
                    P.op(ce, lambda e, s=s: e.tensor_copy(out=cout[s][:], in_=cin[s][:]), r=[("cin", s)], w=[("cout", s)])
                P.op("pool", lambda e, s=s, t=t, k0=k0: e.dma_start(out=woscr_d[t][:, k0:k0 + CK, :], in_=cout[s][:]),
                     r=[("cout", s)], w=[("woscr", t)], dma=("cout", s))
                n += 1

        wslot = [0]

        def load_w(t):
            s = wslot[0] % 2
            wslot[0] += 1
            P.op("sp", lambda e: e.dma_start(out=wt[s][:, 0:KC, :], in_=wscr_d[t]), r=[("wscr", t)], w=[("wt", s)], dma=("wt", s))
            return s

        def load_wo(t):
            s = wslot[0] % 2
            wslot[0] += 1
            P.op("sp", lambda e: e.dma_start(out=wt[s][:, 0:16, :], in_=woscr_d[t]), r=[("woscr", t)], w=[("wt", s)], dma=("wt", s))
            return s

        def proj_fm(s, j, M=128, c0=None):
            ps, keys = ps_proj.get(2)
            cc = j * 128 if c0 is None else c0

            def fn(e):
                for kc in range(KC):
                    ins = e.matmul(ps[0:M, 0:TB], lhsT=wt[s][:, kc, cc:cc + M], rhs=xT[:, kc, :], start=(kc == 0), stop=(kc == KC - 1))
                return ins
            P.op("pe", fn, r=[("wt", s), "xT"], w=keys)
            return ps, keys

        def proj_tm(s, tt):
            ps, keys = ps_proj.get(2)

            def fn(e):
                for kc in range(KC):
                    ins = e.matmul(ps[:, 0:WT], lhsT=xT[:, kc, tt * 128:(tt + 1) * 128], rhs=wt[s][:, kc, :], start=(kc == 0), stop=(kc == KC - 1))
                return ins
            P.op("pe", fn, r=[("wt", s), "xT"], w=keys)
            return ps, keys

        def load_xT(tb):
            n = 0
            for tt in range(NTT):
                r0 = tb * TB + tt * 128
                for c0 in range(0, D, XLW):
                    s = n % 2
                    n += 1
                    P.op("sp", lambda e, s=s, r0=r0, c0=c0: e.dma_start(out=xld[s][:], in_=x_d[r0:r0 + 128, c0:c0 + XLW]),
                         w=[("xld", s)], dma=("xld", s))
                    for q0 in range(0, XLW, 512):
                        nq = min(4, (XLW - q0) // 128)
                        ps, keys = ps_mix.get(nq)

                        def fn(e, s=s, q0=q0, nq=nq, ps=ps):
                            for q in range(nq):
                                ins = e.transpose(out=ps[:, q * 128:(q + 1) * 128], in_=xld[s][:, q0 + q * 128:q0 + (q + 1) * 128], identity=cid)
                            return ins
                        P.op("pe", fn, r=[("xld", s), "consts"], w=keys)
                        kc0 = (c0 + q0) // 128
                        P.op("act", lambda e, ps=ps, kc0=kc0, nq=nq, tt=tt: e.activation(
                            out=xT[:, kc0:kc0 + nq, tt * 128:(tt + 1) * 128],
                            in_=ps[:, 0:nq * 128].rearrange("p (q t) -> p q t", q=nq), func=AF.Copy),
                            r=keys, w=["xT"])

        def conv_blocks(Ubuf, hist, cbase, nblk, key):
            P.op("pool", lambda e: e.tensor_copy(out=Ubuf[:, 0:nblk, 0:3], in_=hist[:, 0:nblk, :]), r=[key + "hist"], w=[key + "Uh"])
            for b in range(nblk):
                eng = "dve"
                rk = [(key + "U", b), key + "Uh", "convp"]

                P.op(eng, lambda e, b=b: e.tensor_scalar(out=cv[:, b, :], in0=Ubuf[:, b, 3:3 + TB], scalar1=convp[:, cbase + b, 3:4], scalar2=convp[:, cbase + b, 4:5],
                                                         op0=ALU.mult, op1=ALU.add), r=rk, w=[("cvp", b)])
                for k in range(3):
                    P.op(eng, lambda e, b=b, k=k: e.scalar_tensor_tensor(out=cv[:, b, :], in0=Ubuf[:, b, k:k + TB], scalar=convp[:, cbase + b, k:k + 1], in1=cv[:, b, :],
                                                                         op0=ALU.mult, op1=ALU.add), r=rk + [("cvp", b)], w=[("cvp", b)])
                P.op("act", lambda e, b=b: e.activation(out=cv[:, b, :], in_=cv[:, b, :], func=AF.Silu), r=[("cvp", b)], w=[("cv", b)])
            P.op("pool", lambda e: e.tensor_copy(out=hist[:, 0:nblk, :], in_=Ubuf[:, 0:nblk, TB:TB + 3]),
                 r=[(key + "U", b) for b in range(nblk)] + [key + "Uh"], w=[key + "hist"])

        def softplus_(out, xin, nparts, shape_keys_r, wkey, tmpa, tmpb):
            P.op("act", lambda e: e.activation(out=tmpa, in_=xin, func=AF.Abs), r=shape_keys_r, w=[wkey + "_a"])
            P.op("act", lambda e: e.activation(out=tmpa, in_=tmpa, func=AF.Exp, scale=-1.0), r=[wkey + "_a"], w=[wkey + "_b"])
            P.op("act", lambda e: e.activation(out=tmpb, in_=tmpa, func=AF.Ln, bias=1.0), r=[wkey + "_b"], w=[wkey + "_c"])
            P.op("dve", lambda e: e.scalar_tensor_tensor(out=out, in0=xin, scalar=0.0, in1=tmpb, op0=ALU.max, op1=ALU.add),
                 r=list(shape_keys_r) + [wkey + "_c"], w=[wkey])

        def transpose_out_bf(src_bf, ncb, blk0, c, rkeys):
            ps, keys = ps_mix.get((ncb + 1) // 2)
            psv = ps.bitcast(BF16)

            def fn(e):
                for i in range(ncb):
                    ins = e.transpose(out=psv[:, i * 128:(i + 1) * 128], in_=src_bf[:, i * 128:(i + 1) * 128], identity=idb[:])
                return ins
            P.op("pe", fn, r=list(rkeys) + ["idb"], w=keys)
            P.op("act", lambda e: e.activation(out=ymixT[:, blk0:blk0 + ncb, c * 128:(c + 1) * 128],
                                                in_=psv[:, 0:ncb * 128].rearrange("p (b t) -> p b t", b=ncb), func=AF.Copy),
                 r=keys, w=[("ymixT", blk0)])

        s_small = T("s_small", [128, NTT, 32])
        big_tm = T("big_tm", [128, NTT, 1024])
        s_xtm = T("s_xtm", [128, 1024])
        s_xdt = T("s_xdt", [128, 1024])
        if "ssm" in parts:
            s_sz = big_tm
            s_dt = T("s_dt", [128, NTT, 16])
            s_a = T("s_a", [128, NTT, 16])
            s_t1 = T("s_t1", [128, NTT, 16])
            s_t2 = T("s_t2", [128, NTT, 16])
            s_xx = T("s_xx", [128, NTT, 16])
            s_state = T("s_state", [128, 2, 512])
            s_pre = T("s_pre", [128, 48])
            s_ex = T("s_ex", [128, 48])
            s_xw = T("s_xw", [128, 1024])
            s_xd = T("s_xd", [128, 1024])
            s_btm = T("s_btm", [128, 2, 128])
            s_gu = T("s_gu", [128, 8, 128])
            s_mt = T("s_mt", [128, 8, 128])
            s_cbm = T("s_cbm", [128, 128])
            s_y = T("s_y", [128, 512])
            s_ybf = T("s_ybf", [128, 512], BF16)
            s_ssq = T("s_ssq", [128, 1])
            s_rstd = T("s_rstd", [128, 1])
            P.op("pool", lambda e: e.memset(s_state[:], 0.0), w=["s_state0", "s_state1"])
            P.op("pool", lambda e: e.memset(hist_s[:], 0.0), w=["shist"])

        def ssm_proj(tb):
            for ti, t in enumerate(range(8, 14)):
                s = load_w(t)
                for j in range(2):
                    ps, keys = proj_fm(s, j)
                    b = ti * 2 + j
                    P.op("act", lambda e, ps=ps, b=b: e.activation(out=U[:, b, 3:3 + TB], in_=ps[:, 0:TB], func=AF.Copy), r=keys, w=[("sU", b)])
            for ti, t in enumerate(range(22, 26)):
                s = load_w(t)
                for tt in range(NTT):
                    ps, keys = proj_tm(s, tt)
                    P.op("act", lambda e, ps=ps, ti=ti, tt=tt: e.activation(out=s_sz[:, tt, ti * WT:(ti + 1) * WT], in_=ps[:, 0:WT], func=AF.Silu),
                         r=keys, w=[("s_sz", tt, ti)])

        def small_proj(tb):
            s = load_w(27)
            for tt in range(NTT):
                ps, keys = proj_tm(s, tt)
                P.op("dve", lambda e, ps=ps, tt=tt: e.tensor_copy(out=s_small[:, tt, :], in_=ps[:, 0:32]), r=keys, w=[("small", tt)])
            return s

        def ssm_mixer(tb):
            conv_blocks(U, hist_s, 12, 12, "s")
            smk = [("small", tt) for tt in range(NTT)]
            P.op("dve", lambda e: e.tensor_tensor(out=s_xx[:], in0=s_small[:, :, 8:24],
                                                  in1=prow[:, PR_SDTB:PR_SDTB + 16].unsqueeze(1).broadcast_to([128, NTT, 16]), op=ALU.add),
                 r=smk + ["prow"], w=["s_xx"])
            softplus_(s_dt[:], s_xx[:], 128, ["s_xx"], "s_dt", s_t1[:], s_t2[:])
            P.op("dve", lambda e: e.tensor_tensor(out=s_a[:], in0=s_dt[:], in1=nA_s[:].unsqueeze(1).broadcast_to([128, NTT, 16]), op=ALU.mult),
                 r=["s_dt", "nA_s"], w=["s_a"])
            for c in range(NTT):
                csl = slice(c * 128, (c + 1) * 128)
                ps, keys = ps_mix.get(1)

                def fn(e, ps=ps, c=c):
                    e.matmul(ps[:, 0:16], lhsT=ctri, rhs=s_a[:, c, :], start=True, stop=True)
                    return e.matmul(ps[:, 16:32], lhsT=cones, rhs=s_a[:, c, :], start=True, stop=True)
                P.op("pe", fn, r=["consts", "s_a"], w=keys)
                P.op("act", lambda e, ps=ps: e.activation(out=s_pre[:, 0:32], in_=ps[:, 0:32], func=AF.Copy), r=keys, w=["s_pre"])
                P.op("dve", lambda e: e.tensor_tensor(out=s_pre[:, 32:48], in0=s_pre[:, 16:32], in1=s_pre[:, 0:16], op=ALU.subtract), r=["s_pre"], w=["s_pre2"])
                P.op("act", lambda e: e.activation(out=s_ex[:], in_=s_pre[:], func=AF.Exp), r=["s_pre", "s_pre2"], w=["s_ex"])
                for half in range(2):
                    ps, keys = ps_mix.get(4)

                    def fn(e, ps=ps, half=half, csl=csl):
                        for q in range(4):
                            ins = e.transpose(out=ps[:, q * 128:(q + 1) * 128], in_=cv[:, half * 4 + q, csl], identity=cid)
                        return ins
                    P.op("pe", fn, r=[("cv", half * 4 + q) for q in range(4)] + ["consts"], w=keys)
                    P.op("act", lambda e, ps=ps, half=half: e.activation(out=s_xtm[:, half * 512:(half + 1) * 512], in_=ps[:, 0:512], func=AF.Copy),
                         r=keys, w=[("s_xtm", half)])
                ps, keys = ps_mix.get(2)

                def fn(e, ps=ps, csl=csl):
                    e.transpose(out=ps[:, 0:128], in_=cv[:, 8, csl], identity=cid)
                    return e.transpose(out=ps[:, 128:256], in_=cv[:, 9, csl], identity=cid)
                P.op("pe", fn, r=[("cv", 8), ("cv", 9), "consts"], w=keys)
                P.op("dve", lambda e, ps=ps: e.tensor_copy(out=s_btm[:], in_=ps[:, 0:256].rearrange("p (g n) -> p g n", g=2)), r=keys, w=["s_btm"])
                xk = [("s_xtm", 0), ("s_xtm", 1)]
                v3 = lambda t_: t_[:].rearrange("p (h q) -> p h q", h=16)
                P.op("pool", lambda e, c=c: e.tensor_tensor(out=v3(s_xdt), in0=v3(s_xtm), in1=s_dt[:, c, :].unsqueeze(2).broadcast_to([128, 16, 64]), op=ALU.mult),
                     r=xk + ["s_dt"], w=["s_xdt"])
                P.op("pool", lambda e: e.tensor_tensor(out=v3(s_xw), in0=v3(s_xdt), in1=s_ex[:, 32:48].unsqueeze(2).broadcast_to([128, 16, 64]), op=ALU.mult),
                     r=["s_xdt", "s_ex"], w=["s_xw"])
                P.op("pool", lambda e: e.tensor_tensor(out=v3(s_xd), in0=v3(s_xtm), in1=prow[:, PR_SD:PR_SD + 16].unsqueeze(2).broadcast_to([128, 16, 64]), op=ALU.mult),
                     r=xk + ["prow"], w=["s_xd"])
                for gi in range(2):
                    ps, keys = ps_mix.get(1)
                    P.op("pe", lambda e, ps=ps, gi=gi, csl=csl: e.matmul(ps[:, 0:128], lhsT=cv[:, 8 + gi, csl], rhs=cv[:, 10 + gi, csl], start=True, stop=True),
                         r=[("cv", 8 + gi), ("cv", 10 + gi)], w=keys)
                    P.op("dve", lambda e, ps=ps: e.tensor_tensor(out=s_cbm[:], in0=ps[:, 0:128], in1=ctri, op=ALU.mult), r=keys + ["consts"], w=["s_cbm"])
                    P.op("pool", lambda e, c=c, gi=gi: e.tensor_tensor(out=s_gu[:], in0=s_a[:, c, gi * 8:(gi + 1) * 8].unsqueeze(2).broadcast_to([128, 8, 128]),
                                                                       in1=cU.unsqueeze(1).broadcast_to([128, 8, 128]), op=ALU.mult),
                         r=["s_a", "consts"], w=["s_gu"])
                    for hh in range(2):
                        ps, keys = ps_mix.get(4)

                        def fn(e, ps=ps, hh=hh):
                            for q in range(4):
                                ins = e.matmul(ps[:, q * 128:(q + 1) * 128], lhsT=s_gu[:, hh * 4 + q, :], rhs=ctri, start=True, stop=True)
                            return ins
                        P.op("pe", fn, r=["s_gu", "consts"], w=keys)
                        P.op("act", lambda e, ps=ps, hh=hh: e.activation(out=s_mt[:, hh * 4:(hh + 1) * 4, :], in_=ps[:, 0:512].rearrange("p (h l) -> p h l", h=4), func=AF.Exp),
                             r=keys, w=[("s_mte", hh)])
                    P.op("dve", lambda e: e.tensor_tensor(out=s_mt[:], in0=s_mt[:], in1=s_cbm[:].unsqueeze(1).broadcast_to([128, 8, 128]), op=ALU.mult),
                         r=[("s_mte", 0), ("s_mte", 1), "s_cbm"], w=[("s_mte", 0), ("s_mte", 1)])
                    psd, kd = ps_mix.get(4)

                    def fn(e, psd=psd, gi=gi):
                        for h in range(8):
                            hg = gi * 8 + h
                            ins = e.matmul(psd[:, h * 64:(h + 1) * 64], lhsT=s_mt[:, h, :], rhs=s_xdt[:, hg * 64:(hg + 1) * 64], start=True, stop=True)
                        return ins
                    P.op("pe", fn, r=[("s_mte", 0), ("s_mte", 1), "s_xdt"], w=kd)
                    pso, ko = ps_mix.get(4)
                    P.op("pe", lambda e, pso=pso, gi=gi, csl=csl: e.matmul(pso[:, 0:512], lhsT=cv[:, 10 + gi, csl], rhs=s_state[:, gi, :], start=True, stop=True),
                         r=[("cv", 10 + gi), f"s_state{gi}"], w=ko)
                    y3 = s_y[:].rearrange("p (h q) -> p h q", h=8)
                    P.op("dve", lambda e, pso=pso, gi=gi: e.tensor_tensor(out=y3, in0=pso[:, 0:512].rearrange("p (h q) -> p h q", h=8),
                                                                          in1=s_ex[:, gi * 8:(gi + 1) * 8].unsqueeze(2).broadcast_to([128, 8, 64]), op=ALU.mult),
                         r=ko + ["s_ex"], w=["s_y"])
                    P.op("pool", lambda e, gi=gi: e.tensor_tensor(out=s_y[:], in0=s_y[:], in1=s_xd[:, gi * 512:(gi + 1) * 512], op=ALU.add), r=["s_y", "s_xd"], w=["s_y"])
                    P.op("dve", lambda e, psd=psd: e.tensor_tensor(out=s_y[:], in0=s_y[:], in1=psd[:, 0:512], op=ALU.add), r=kd + ["s_y"], w=["s_y"])
                    P.op("pool", lambda e, c=c, gi=gi: e.tensor_tensor(out=s_y[:], in0=s_y[:], in1=s_sz[:, c, gi * 512:(gi + 1) * 512], op=ALU.mult),
                         r=["s_y", ("s_sz", c, 2 * gi), ("s_sz", c, 2 * gi + 1)], w=["s_y"])
                    P.op("act", lambda e: e.activation(out=junk[:], in_=s_y[:], func=AF.Square, accum_out=s_ssq[:]), r=["s_y"], w=["junk", "s_ssq"])
                    P.op("act", lambda e: e.activation(out=s_rstd[:], in_=s_ssq[:], func=AF.Sqrt, scale=1.0 / 512, bias=epsc[:]), r=["s_ssq", "epsc"], w=["s_rstd0"])
                    P.op("dve", lambda e: e.reciprocal(out=s_rstd[:], in_=s_rstd[:]), r=["s_rstd0"], w=["s_rstd"])
                    P.op("dve", lambda e, gi=gi: e.scalar_tensor_tensor(out=s_ybf[:], in0=s_y[:], scalar=s_rstd[:], in1=prow[:, PR_SNW + gi * 512:PR_SNW + (gi + 1) * 512],
                                                                        op0=ALU.mult, op1=ALU.mult), r=["s_y", "s_rstd", "prow"], w=["s_ybf"])
                    transpose_out_bf(s_ybf, 4, 8 + gi * 4, c, ["s_ybf"])
                    pss, ks = ps_mix.get(4)
                    P.op("pe", lambda e, pss=pss, gi=gi: e.matmul(pss[:, 0:512], lhsT=s_btm[:, gi, :], rhs=s_xw[:, gi * 512:(gi + 1) * 512], start=True, stop=True),
                         r=["s_btm", "s_xw"], w=ks)
                    st3 = s_state[:, gi, :].rearrange("p (h q) -> p h q", h=8)
                    P.op("pool", lambda e, st3=st3, gi=gi: e.tensor_tensor(out=st3, in0=st3, in1=s_ex[:, 16 + gi * 8:16 + (gi + 1) * 8].unsqueeze(2).broadcast_to([128, 8, 64]), op=ALU.mult),
                         r=[f"s_state{gi}", "s_ex"], w=[f"s_state{gi}"])
                    P.op("dve", lambda e, pss=pss, gi=gi: e.tensor_tensor(out=s_state[:, gi, :], in0=s_state[:, gi, :], in1=pss[:, 0:512], op=ALU.add),
                         r=ks + [f"s_state{gi}"], w=[f"s_state{gi}"])

        if "gdn" in parts:
            g_sz = big_tm[:, :, 0:512]
            g_sq = T("g_sq", [128, TB])
            g_rn = T("g_rn", [128, TB])
            g_beta = T("g_beta", [128, NTT, 4])
            g_xx = T("g_xx", [128, NTT, 4])
            g_t1s = T("g_t1s", [128, NTT, 4])
            g_t2s = T("g_t2s", [128, NTT, 4])
            g_sp = T("g_sp", [128, NTT, 4])
            g_g = T("g_g", [128, NTT, 4])
            g_pre = T("g_pre", [128, 12])
            g_ex = T("g_ex", [128, 12])
            g_bg = T("g_bg", [128, 4])
            g_ktm = s_xtm[:, 0:512].rearrange("p (h d) -> p h d", h=4)
            g_vtm = s_xtm[:, 512:1024].rearrange("p (h d) -> p h d", h=4)
            g_R = s_xdt[:].rearrange("p (h c) -> p h c", h=4)
            g_kd = T("g_kd", [128, 4, 128])
            g_GU = T("g_GU", [128, 2, 128])
            g_E = T("g_E", [128, 2, 128])
            g_t1 = T("g_t1", [128, 128])
            g_Af = T("g_Af", [128, 128])
            g_A0 = T("g_A0", [128, 2, 128])
            g_AToff = T("g_AToff", [128, 128])
            g_attnT = T("g_attnT", [128, 128])
            g_PP = [T(f"g_PP{i}", [128, 2, 128]) for i in range(2)]
            g_TT = [T(f"g_TT{i}", [128, 128]) for i in range(2)]
            g_X1 = T("g_X1", [128, 256])
            g_X2 = T("g_X2", [128, 256])
            g_uw = T("g_uw", [128, 2, 128])
            g_vn = T("g_vn", [128, 128])
            g_tmp = T("g_tmp", [128, 128])
            g_o = T("g_o", [128, 4, 128])
            g_o2 = T("g_o2", [128, 4, 128])
            g_ssq = T("g_ssq", [128, 4])
            g_rstd = T("g_rstd", [128, 4])
            g_obf = T("g_obf", [128, 512], BF16)
            g_S = T("g_S", [128, 4, 128])
            P.op("pool", lambda e: e.memset(g_S[:], 0.0), w=[("g_S", h) for h in range(4)])
            P.op("pool", lambda e: e.memset(hist_g[:], 0.0), w=["ghist"])
            cBDm = consts[:, C_BD, :]
            cOFFT = consts[:, C_OFFT, :]
            cNEG = consts[:, C_NEG, :]

        def gdn_proj(tb):
            for ti, t in enumerate(range(2, 8)):
                s = load_w(t)
                for j in range(2):
                    ps, keys = proj_fm(s, j)
                    b = ti * 2 + j
                    P.op("act", lambda e, ps=ps, b=b: e.activation(out=U[:, b, 3:3 + TB], in_=ps[:, 0:TB], func=AF.Copy), r=keys, w=[("gU", b)])
            for ti, t in enumerate(range(20, 22)):
                s = load_w(t)
                for tt in range(NTT):
                    ps, keys = proj_tm(s, tt)
                    P.op("act", lambda e, ps=ps, ti=ti, tt=tt: e.activation(out=g_sz[:, tt, ti * WT:(ti + 1) * WT], in_=ps[:, 0:WT], func=AF.Silu),
                         r=keys, w=[("g_sz", tt, ti)])

        def gdn_mixer(tb):
            conv_blocks(U, hist_g, 0, 12, "g")
            for b in range(8):
                P.op("act", lambda e, b=b: e.activation(out=g_sq[:], in_=cv[:, b, :], func=AF.Square), r=[("cv", b)], w=["g_sq"])
                ps, keys = ps_mix.get(2)
                P.op("pe", lambda e, ps=ps: e.matmul(ps[:, 0:TB], lhsT=cones, rhs=g_sq[:], start=True, stop=True), r=["g_sq", "consts"], w=keys)
                P.op("act", lambda e, ps=ps: e.activation(out=g_rn[:], in_=ps[:, 0:TB], func=AF.Sqrt, bias=epsc[:]), r=keys + ["epsc"], w=["g_rn0"])
                P.op("dve", lambda e: e.reciprocal(out=g_rn[:], in_=g_rn[:]), r=["g_rn0"], w=["g_rn"])
                sc_ = 128 ** -0.5 if b < 4 else 1.0
                P.op("dve", lambda e, b=b, sc_=sc_: e.scalar_tensor_tensor(out=cv[:, b, :], in0=cv[:, b, :], scalar=sc_, in1=g_rn[:], op0=ALU.mult, op1=ALU.mult),
                     r=[("cv", b), "g_rn"], w=[("cv", b)])
            if GSTOP <= 1:
                return
            smk = [("small", tt) for tt in range(NTT)]
            P.op("act", lambda e: e.activation(out=g_beta[:], in_=s_small[:, :, 0:4], func=AF.Sigmoid), r=smk, w=["g_beta"])
            P.op("dve", lambda e: e.tensor_tensor(out=g_xx[:], in0=s_small[:, :, 4:8],
                                                  in1=prow[:, PR_GDTB:PR_GDTB + 4].unsqueeze(1).broadcast_to([128, NTT, 4]), op=ALU.add),
                 r=smk + ["prow"], w=["g_xx"])
            softplus_(g_sp[:], g_xx[:], 128, ["g_xx"], "g_sp", g_t1s[:], g_t2s[:])
            P.op("dve", lambda e: e.tensor_tensor(out=g_g[:], in0=g_sp[:], in1=nA_g[:].unsqueeze(1).broadcast_to([128, NTT, 4]), op=ALU.mult),
                 r=["g_sp", "nA_g"], w=["g_g"])
            if GSTOP <= 2:
                return
            for c in range(NTT):
                csl = slice(c * 128, (c + 1) * 128)
                ps, keys = ps_mix.get(1)

                def fn(e, ps=ps, c=c):
                    e.matmul(ps[:, 0:4], lhsT=ctri, rhs=g_g[:, c, :], start=True, stop=True)
                    return e.matmul(ps[:, 4:8], lhsT=cones, rhs=g_g[:, c, :], start=True, stop=True)
                P.op("pe", fn, r=["consts", "g_g"], w=keys)
                P.op("act", lambda e, ps=ps: e.activation(out=g_pre[:, 0:8], in_=ps[:, 0:8], func=AF.Copy), r=keys, w=["g_pre"])
                P.op("dve", lambda e: e.tensor_tensor(out=g_pre[:, 8:12], in0=g_pre[:, 4:8], in1=g_pre[:, 0:4], op=ALU.subtract), r=["g_pre"], w=["g_pre2"])
                P.op("act", lambda e: e.activation(out=g_ex[:], in_=g_pre[:], func=AF.Exp), r=["g_pre", "g_pre2"], w=["g_ex"])
                P.op("dve", lambda e, c=c: e.tensor_tensor(out=g_bg[:], in0=g_beta[:, c, :], in1=g_ex[:, 0:4], op=ALU.mult), r=["g_beta", "g_ex"], w=["g_bg"])
                for which, dst, key in ((4, g_ktm, "g_ktm"), (8, g_vtm, "g_vtm")):
                    ps, keys = ps_mix.get(4)

                    def fn(e, ps=ps, which=which, csl=csl):
                        for q in range(4):
                            ins = e.transpose(out=ps[:, q * 128:(q + 1) * 128], in_=cv[:, which + q, csl], identity=cid)
                        return ins
                    P.op("pe", fn, r=[("cv", which + q) for q in range(4)] + ["consts"], w=keys)
                    P.op("act", lambda e, ps=ps, dst=dst: e.activation(out=dst[:], in_=ps[:, 0:512].rearrange("p (h d) -> p h d", h=4), func=AF.Copy), r=keys, w=[key])
                P.op("dve", lambda e, c=c: e.tensor_tensor(out=g_R[:, :, 0:128], in0=g_vtm[:], in1=g_beta[:, c, :].unsqueeze(2).broadcast_to([128, 4, 128]), op=ALU.mult),
                     r=["g_vtm", "g_beta"], w=["g_Rv"])
                P.op("pool", lambda e: e.tensor_tensor(out=g_R[:, :, 128:256], in0=g_ktm[:], in1=g_bg[:].unsqueeze(2).broadcast_to([128, 4, 128]), op=ALU.mult),
                     r=["g_ktm", "g_bg"], w=["g_Rk"])
                P.op("pool", lambda e: e.tensor_tensor(out=g_kd[:], in0=g_ktm[:], in1=g_ex[:, 8:12].unsqueeze(2).broadcast_to([128, 4, 128]), op=ALU.mult),
                     r=["g_ktm", "g_ex"], w=["g_kd"])
                if GSTOP <= 3:
                    continue
                for h in range(4):
                    gcol = g_g[:, c, h:h + 1]
                    P.op("dve", lambda e, gcol=gcol: e.tensor_scalar(out=g_GU[:, 0, :], in0=cU, scalar1=gcol, scalar2=None, op0=ALU.mult), r=["consts", "g_g"], w=["g_GU0"])
                    P.op("dve", lambda e, gcol=gcol: e.tensor_scalar(out=g_GU[:, 1, :], in0=ctri, scalar1=gcol, scalar2=None, op0=ALU.mult), r=["consts", "g_g"], w=["g_GU1"])
                    ps, keys = ps_mix.get(4)
                    kT = cv[:, 4 + h, csl]
                    qT_ = cv[:, h, csl]

                    def fn(e, ps=ps, kT=kT, qT_=qT_):
                        e.matmul(ps[:, 0:128], lhsT=g_GU[:, 0, :], rhs=ctri, start=True, stop=False)
                        e.matmul(ps[:, 0:128], lhsT=cid, rhs=cNEG, start=False, stop=True)
                        e.matmul(ps[:, 128:256], lhsT=g_GU[:, 1, :], rhs=cU, start=True, stop=True)
                        e.matmul(ps[:, 256:384], lhsT=kT, rhs=kT, start=True, stop=True)
                        return e.matmul(ps[:, 384:512], lhsT=kT, rhs=qT_, start=True, stop=True)
                    P.op("pe", fn, r=["g_GU0", "g_GU1", "consts", ("cv", 4 + h), ("cv", h)], w=keys)
                    P.op("act", lambda e, ps=ps: e.activation(out=g_E[:], in_=ps[:, 0:256].rearrange("p (a l) -> p a l", a=2), func=AF.Exp), r=keys, w=["g_E"])
                    P.op("dve", lambda e, ps=ps: e.tensor_tensor(out=g_t1[:], in0=ps[:, 256:384], in1=g_E[:, 1, :], op=ALU.mult), r=keys + ["g_E"], w=["g_t1"])
                    P.op("dve", lambda e, ps=ps: e.tensor_tensor(out=g_attnT[:], in0=ps[:, 384:512], in1=g_E[:, 0, :], op=ALU.mult), r=keys + ["g_E"], w=["g_attnT"])
                    P.op("dve", lambda e, c=c, h=h: e.scalar_tensor_tensor(out=g_Af[:], in0=g_t1[:], scalar=g_beta[:, c, h:h + 1], in1=cU, op0=ALU.mult, op1=ALU.mult),
                         r=["g_t1", "g_beta", "consts"], w=["g_Af"])
                    if GSTOP <= 4:
                        continue
                    ps, keys = ps_mix.get(1)
                    P.op("pe", lambda e, ps=ps: e.transpose(out=ps[:, 0:128], in_=g_Af[:], identity=cid), r=["g_Af", "consts"], w=keys)
                    P.op("pool", lambda e: e.tensor_tensor(out=g_A0[:, 0, :], in0=g_Af[:], in1=cBDm, op=ALU.mult), r=["g_Af", "consts"], w=["g_A0a"])
                    P.op("dve", lambda e, ps=ps: e.tensor_tensor(out=g_A0[:, 1, :], in0=ps[:, 0:128], in1=cBDm, op=ALU.mult), r=keys + ["consts"], w=["g_A0b"])
                    P.op("dve", lambda e, ps=ps: e.tensor_tensor(out=g_AToff[:], in0=ps[:, 0:128], in1=cOFFT, op=ALU.mult), r=keys + ["consts"], w=["g_AToff"])
                    P.op("pool", lambda e: e.tensor_tensor(out=g_TT[0][:], in0=cid, in1=g_A0[:, 1, :], op=ALU.subtract), r=["g_A0b", "consts"], w=[("g_TT", 0)])
                    if GSTOP <= 5:
                        continue
                    Pc, Pk = g_A0, ["g_A0a", "g_A0b"]
                    tcur = 0
                    for k in range(1, 6):
                        ps, keys = ps_mix.get(2)

                        def fn(e, ps=ps, Pc=Pc, k=k):
                            ins = e.matmul(ps[:, 0:128], lhsT=Pc[:, 1, :], rhs=Pc[:, 0, :], start=True, stop=True)
                            if k < 5:
                                ins = e.matmul(ps[:, 128:256], lhsT=Pc[:, 0, :], rhs=Pc[:, 1, :], start=True, stop=True)
                            return ins
                        P.op("pe", fn, r=Pk, w=keys)
                        Pn = g_PP[k % 2]
                        nk = [("g_PP", k % 2)]
                        P.op("act", lambda e, ps=ps, Pn=Pn: e.activation(out=Pn[:], in_=ps[:, 0:256].rearrange("p (a l) -> p a l", a=2), func=AF.Copy), r=keys, w=nk)
                        ps, keys = ps_mix.get(1)
                        P.op("pe", lambda e, ps=ps, Pn=Pn, tcur=tcur: e.matmul(ps[:, 0:128], lhsT=Pn[:, 0, :], rhs=g_TT[tcur][:], start=True, stop=True),
                             r=nk + [("g_TT", tcur)], w=keys)
                        P.op("dve", lambda e, ps=ps, tcur=tcur: e.tensor_tensor(out=g_TT[1 - tcur][:], in0=g_TT[tcur][:], in1=ps[:, 0:128], op=ALU.add),
                             r=keys + [("g_TT", tcur)], w=[("g_TT", 1 - tcur)])
                        tcur = 1 - tcur
                        Pc, Pk = Pn, nk
                    if GSTOP <= 6:
                        continue
                    TTf = g_TT[tcur]
                    tk = [("g_TT", tcur)]
                    ps, keys = ps_mix.get(2)
                    P.op("pe", lambda e, ps=ps, TTf=TTf, h=h: e.matmul(ps[:, 0:256], lhsT=TTf[:], rhs=g_R[:, h, :], start=True, stop=True), r=tk + ["g_Rv", "g_Rk"], w=keys)
                    P.op("act", lambda e, ps=ps: e.activation(out=g_X1[:], in_=ps[:, 0:256], func=AF.Copy), r=keys, w=["g_X1"])
                    ps, keys = ps_mix.get(2)
                    P.op("pe", lambda e, ps=ps: e.matmul(ps[:, 0:256], lhsT=g_AToff[:], rhs=g_X1[:], start=True, stop=True), r=["g_AToff", "g_X1"], w=keys)
                    P.op("dve", lambda e, ps=ps, h=h: e.tensor_tensor(out=g_X2[:], in0=g_R[:, h, :], in1=ps[:, 0:256], op=ALU.subtract), r=keys + ["g_Rv", "g_Rk"], w=["g_X2"])
                    ps, keys = ps_mix.get(2)

                    def fn(e, ps=ps, TTf=TTf):
                        e.matmul(ps[:, 0:128], lhsT=TTf[:], rhs=g_X2[:, 0:128], start=True, stop=True)
                        return e.matmul(ps[:, 128:256], lhsT=g_X2[:, 128:256], rhs=TTf[:], start=True, stop=True)
                    P.op("pe", fn, r=tk + ["g_X2"], w=keys)
                    P.op("act", lambda e, ps=ps: e.activation(out=g_uw[:], in_=ps[:, 0:256].rearrange("p (a l) -> p a l", a=2), func=AF.Copy), r=keys, w=["g_uw"])
                    if GSTOP <= 7:
                        continue
                    ps, keys = ps_mix.get(2)

                    def fn(e, ps=ps, h=h, qT_=qT_):
                        e.matmul(ps[:, 0:128], lhsT=g_uw[:, 1, :], rhs=g_S[:, h, :], start=True, stop=True)
                        return e.matmul(ps[:, 128:256], lhsT=qT_, rhs=g_S[:, h, :], start=True, stop=True)
                    P.op("pe", fn, r=["g_uw", ("g_S", h), ("cv", h)], w=keys)
                    P.op("dve", lambda e, ps=ps: e.tensor_tensor(out=g_vn[:], in0=g_uw[:, 0, :], in1=ps[:, 0:128], op=ALU.subtract), r=keys + ["g_uw"], w=["g_vn"])
                    P.op("dve", lambda e, ps=ps, h=h: e.tensor_scalar(out=g_tmp[:], in0=ps[:, 128:256], scalar1=g_ex[:, h:h + 1], scalar2=None, op0=ALU.mult), r=keys + ["g_ex"], w=["g_tmp"])
                    ps, keys = ps_mix.get(2)

                    def fn(e, ps=ps, h=h):
                        e.matmul(ps[:, 0:128], lhsT=g_attnT[:], rhs=g_vn[:], start=True, stop=True)
                        return e.matmul(ps[:, 128:256], lhsT=g_kd[:, h, :], rhs=g_vn[:], start=True, stop=True)
                    P.op("pe", fn, r=["g_attnT", "g_vn", "g_kd"], w=keys)
                    P.op("dve", lambda e, ps=ps, h=h: e.tensor_tensor(out=g_o[:, h, :], in0=g_tmp[:], in1=ps[:, 0:128], op=ALU.add), r=keys + ["g_tmp"], w=[("g_o", h)])
                    P.op("dve", lambda e, ps=ps, h=h: e.scalar_tensor_tensor(out=g_S[:, h, :], in0=g_S[:, h, :], scalar=g_ex[:, 4 + h:5 + h], in1=ps[:, 128:256],
                                                                             op0=ALU.mult, op1=ALU.add), r=keys + [("g_S", h), "g_ex"], w=[("g_S", h)])
                if GSTOP <= 8:
                    continue
                ok = [("g_o", h) for h in range(4)]
                P.op("act", lambda e: e.activation(out=g_o2[:], in_=g_o[:], func=AF.Square), r=ok, w=["g_o2"])
                P.op("dve", lambda e: e.tensor_reduce(out=g_ssq[:], in_=g_o2[:], axis=mybir.AxisListType.X, op=ALU.add), r=["g_o2"], w=["g_ssq"])
                P.op("act", lambda e: e.activation(out=g_rstd[:], in_=g_ssq[:], func=AF.Sqrt, scale=1.0 / 128, bias=epsc[:]), r=["g_ssq", "epsc"], w=["g_rstd0"])
                P.op("dve", lambda e: e.reciprocal(out=g_rstd[:], in_=g_rstd[:]), r=["g_rstd0"], w=["g_rstd"])
                P.op("pool", lambda e: e.tensor_tensor(out=g_o2[:], in0=g_o[:], in1=g_rstd[:].unsqueeze(2).broadcast_to([128, 4, 128]), op=ALU.mult), r=ok + ["g_rstd", "g_o2"], w=["g_o2"])
                P.op("pool", lambda e: e.tensor_tensor(out=g_o2[:], in0=g_o2[:], in1=prow[:, PR_GNW:PR_GNW + 128].unsqueeze(1).broadcast_to([128, 4, 128]), op=ALU.mult),
                     r=["g_o2", "prow"], w=["g_o2"])
                P.op("dve", lambda e, c=c: e.tensor_tensor(out=g_obf[:], in0=g_o2[:].rearrange("p h d -> p (h d)"), in1=g_sz[:, c, :], op=ALU.mult),
                     r=["g_o2", ("g_sz", c, 0), ("g_sz", c, 1)], w=["g_obf"])
                transpose_out_bf(g_obf, 4, 4, c, ["g_obf"])

        if "ml" in parts:
            m_qT = T("m_qT", [128, 2, TB], BF16)
            m_kT = T("m_kT", [128, 2, TB], BF16)
            m_ktm = T("m_ktm", [128, NTT, 256])
            m_v = T("m_v", [128, NTT, 512], BF16)
            m_g = big_tm[:, :, 0:512]
            m_zt = big_tm[:, :, 512:1024]
            RN = ["li", "t", "a", "l1", "lf", "m", "F", "G", "H", "one"]
            m_rows = {nm: T("m_r_" + nm, [1, TB]) for nm in RN}
            m_rows["wi"] = m_rows["a"]
            m_rows["emm"] = m_rows["l1"]
            m_rows["nG"] = m_rows["t"]
            m_carry = T("m_carry", [1, 1])
            m_C = T("m_C", [128, 2, 512])
            m_Cb = T("m_Cb", [128, 2, 512], BF16)
            m_n = T("m_n", [128, 2])
            m_nb = T("m_nb", [128, 2], BF16)
            m_W = T("m_W", [128, 128])
            m_Wm = T("m_Wm", [128, 128])
            m_sc = T("m_sc", [128, 128], BF16)
            m_cols = T("m_cols", [128, 4])
            m_tmp = T("m_tmp", [128, 512])
            m_h = T("m_h", [128, 512])
            m_den = T("m_den", [128, 2])
            m_d1 = T("m_d1", [128, 1])
            m_rden = T("m_rden", [128, 1])
            m_ssq = T("m_ssq", [128, 1])
            m_rstd = T("m_rstd", [128, 1])
            m_comb = T("m_comb", [128, 1])
            m_g2 = T("m_g2", [128, 512])
            m_hbf = T("m_hbf", [128, 512], BF16)
            m_kws = T("m_kws", [128, 256], BF16)
            m_onesb = T("m_onesb", [128, 1], BF16)
            P.op("pool", lambda e: e.memset(m_C[:], 0.0), w=["m_C"])
            P.op("pool", lambda e: e.memset(m_Cb[:], 0.0), w=["m_Cb"])
            P.op("pool", lambda e: e.memset(m_n[:], 0.0), w=["m_n"])
            P.op("pool", lambda e: e.memset(m_nb[:], 0.0), w=["m_nb"])
            P.op("pool", lambda e: e.memset(m_carry[:], 0.0), w=["m_carry"])
            P.op("pool", lambda e: e.memset(m_rows["one"][:], 1.0), w=["m_one"])
            P.op("pool", lambda e: e.memset(m_onesb[:], 1.0), w=["m_onesb"])

        def ml_proj(tb):
            s = load_w(0)
            for j in range(2):
                ps, keys = proj_fm(s, j)
                P.op("act", lambda e, ps=ps, j=j: e.activation(out=m_qT[:, j, :], in_=ps[:, 0:TB], func=AF.Identity, scale=256 ** -0.5), r=keys, w=[("m_qT", j)])
            s = load_w(1)
            for j in range(2):
                ps, keys = proj_fm(s, j)
                P.op("act", lambda e, ps=ps, j=j: e.activation(out=m_kT[:, j, :], in_=ps[:, 0:TB], func=AF.Copy), r=keys, w=[("m_kT", j)])
            for ti, t in enumerate(range(14, 20)):
                s = load_w(t)
                for tt in range(NTT):
                    ps, keys = proj_tm(s, tt)
                    cs = slice((ti % 2) * WT, (ti % 2 + 1) * WT)
                    if ti < 2:
                        P.op("act", lambda e, ps=ps, tt=tt, cs=cs: e.activation(out=m_v[:, tt, cs], in_=ps[:, 0:WT], func=AF.Copy), r=keys, w=[("m_v", tt, ti)])
                    elif ti < 4:
                        P.op("act", lambda e, ps=ps, tt=tt, cs=cs: e.activation(out=m_g[:, tt, cs], in_=ps[:, 0:WT], func=AF.Sigmoid), r=keys, w=[("m_g", tt, ti % 2)])
                    else:
                        P.op("act", lambda e, ps=ps, tt=tt, cs=cs: e.activation(out=m_zt[:, tt, cs], in_=ps[:, 0:WT], func=AF.Silu), r=keys, w=[("m_zt", tt, ti % 2)])
            s = load_w(26)
            for tt in range(NTT):
                ps, keys = proj_tm(s, tt)
                P.op("act", lambda e, ps=ps, tt=tt: e.activation(out=m_ktm[:, tt, :], in_=ps[:, 0:WT], func=AF.Copy), r=keys, w=[("m_ktm", tt)])
            s = load_w(27)
            ps, keys = proj_fm(s, 0, M=1, c0=24)
            P.op("act", lambda e, ps=ps: e.activation(out=m_rows["li"][:], in_=ps[0:1, 0:TB], func=AF.Identity, bias=prow[0:1, PR_MIB:PR_MIB + 1]), r=keys + ["prow"], w=["m_li"])
            ps, keys = proj_fm(s, 0, M=1, c0=25)
            P.op("act", lambda e, ps=ps: e.activation(out=m_rows["t"][:], in_=ps[0:1, 0:TB], func=AF.Identity, bias=prow[0:1, PR_MFB:PR_MFB + 1]), r=keys + ["prow"], w=["m_t"])

        def ml_mixer(tb):
            R_ = m_rows
            for tt in range(NTT):
                P.op("pool", lambda e, tt=tt: e.tensor_tensor(out=m_g[:, tt, :], in0=m_g[:, tt, :], in1=m_zt[:, tt, :], op=ALU.mult),
                     r=[("m_g", tt, 0), ("m_g", tt, 1), ("m_zt", tt, 0), ("m_zt", tt, 1)], w=[("m_g", tt, 0), ("m_g", tt, 1)])
            P.op("act", lambda e: e.activation(out=R_["a"][:], in_=R_["t"][:], func=AF.Abs), r=["m_t"], w=["m_a"])
            P.op("act", lambda e: e.activation(out=R_["a"][:], in_=R_["a"][:], func=AF.Exp, scale=-1.0), r=["m_a"], w=["m_a2"])
            P.op("act", lambda e: e.activation(out=R_["l1"][:], in_=R_["a"][:], func=AF.Ln, bias=1.0), r=["m_a2"], w=["m_l1"])
            P.op("dve", lambda e: e.scalar_tensor_tensor(out=R_["lf"][:], in0=R_["t"][:], scalar=0.0, in1=R_["l1"][:], op0=ALU.min, op1=ALU.subtract), r=["m_t", "m_l1"], w=["m_lf"])
            P.op("dve", lambda e: e.tensor_tensor_scan(out=R_["m"][:], data0=R_["lf"][:], data1=R_["li"][:], initial=m_carry[:], op0=ALU.add, op1=ALU.max),
                 r=["m_lf", "m_li", "m_carry"], w=["m_m"])
            P.op("dve", lambda e: e.tensor_tensor_scan(out=R_["F"][:], data0=R_["one"][:], data1=R_["lf"][:], initial=0.0, op0=ALU.mult, op1=ALU.add),
                 r=["m_lf", "m_one"], w=["m_F"])
            P.op("dve", lambda e: e.tensor_tensor(out=R_["G"][:], in0=R_["F"][:], in1=R_["m"][:], op=ALU.subtract), r=["m_F", "m_m"], w=["m_G"])
            P.op("dve", lambda e: e.tensor_tensor(out=R_["H"][:], in0=R_["li"][:], in1=R_["F"][:], op=ALU.subtract), r=["m_F", "m_li"], w=["m_H"])
            P.op("dve", lambda e: e.tensor_scalar(out=R_["nG"][:], in0=R_["G"][:], scalar1=-1.0, scalar2=None, op0=ALU.mult), r=["m_G"], w=["m_nG"])
            P.op("act", lambda e: e.activation(out=R_["emm"][:], in_=R_["m"][:], func=AF.Exp, scale=-1.0), r=["m_m"], w=["m_emm"])
            one = R_["one"]
            for c in range(NTT):
                csl = slice(c * 128, (c + 1) * 128)
                bias_ap = m_carry[:] if c == 0 else R_["nG"][0:1, c * 128 - 1:c * 128]
                P.op("act", lambda e, csl=csl, bias_ap=bias_ap: e.activation(out=R_["wi"][0:1, csl], in_=R_["G"][0:1, csl], func=AF.Exp, bias=bias_ap),
                     r=["m_G", "m_nG", "m_carry"], w=["m_wi"])
                ps, keys = ps_mix.get(1)

                def fn(e, ps=ps, csl=csl):
                    e.matmul(ps[:, 0:128], lhsT=R_["H"][0:1, csl], rhs=one[0:1, 0:128], start=True, stop=False)
                    return e.matmul(ps[:, 0:128], lhsT=one[0:1, 0:128], rhs=R_["G"][0:1, csl], start=False, stop=True)
                P.op("pe", fn, r=["m_H", "m_G", "m_one"], w=keys)
                P.op("dve", lambda e, ps=ps: e.tensor_scalar(out=m_W[:], in0=ps[:, 0:128], scalar1=0.0, scalar2=None, op0=ALU.min), r=keys, w=["m_W0"])
                P.op("act", lambda e: e.activation(out=m_W[:], in_=m_W[:], func=AF.Exp), r=["m_W0"], w=["m_W"])
                P.op("pool", lambda e: e.tensor_tensor(out=m_Wm[:], in0=m_W[:], in1=ctri, op=ALU.mult), r=["m_W", "consts"], w=["m_Wm"])
                ps, keys = ps_mix.get(1)

                def fn(e, ps=ps, csl=csl, c=c):
                    e.matmul(ps[:, 0:1], lhsT=R_["wi"][0:1, csl], rhs=one[0:1, 0:1], start=True, stop=True)
                    e.matmul(ps[:, 1:2], lhsT=R_["emm"][0:1, csl], rhs=one[0:1, 0:1], start=True, stop=True)
                    return e.matmul(ps[:, 2:3], lhsT=one[0:1, 0:128], rhs=R_["wi"][0:1, c * 128 + 127:c * 128 + 128], start=True, stop=True)
                P.op("pe", fn, r=["m_wi", "m_emm", "m_one"], w=keys)
                P.op("act", lambda e, ps=ps: e.activation(out=m_cols[:, 0:3], in_=ps[:, 0:3], func=AF.Copy), r=keys, w=["m_cols"])
                ps, keys = ps_mix.get(1)

                def fn(e, ps=ps, csl=csl):
                    e.matmul(ps[:, 0:128], lhsT=m_kT[:, 0, csl], rhs=m_qT[:, 0, csl], start=True, stop=False)
                    return e.matmul(ps[:, 0:128], lhsT=m_kT[:, 1, csl], rhs=m_qT[:, 1, csl], start=False, stop=True)
                P.op("pe", fn, r=[("m_kT", 0), ("m_kT", 1), ("m_qT", 0), ("m_qT", 1)], w=keys)
                P.op("dve", lambda e, ps=ps: e.tensor_tensor(out=m_sc[:], in0=ps[:, 0:128], in1=m_Wm[:], op=ALU.mult), r=keys + ["m_Wm"], w=["m_sc"])
                psi, ki = ps_mix.get(4)

                def fn(e, psi=psi, csl=csl):
                    e.matmul(psi[:, 0:512], lhsT=m_qT[:, 0, csl], rhs=m_Cb[:, 0, :], start=True, stop=False)
                    return e.matmul(psi[:, 0:512], lhsT=m_qT[:, 1, csl], rhs=m_Cb[:, 1, :], start=False, stop=True)
                P.op("pe", fn, r=[("m_qT", 0), ("m_qT", 1), "m_Cb"], w=ki)
                P.op("dve", lambda e, psi=psi: e.tensor_scalar(out=m_tmp[:], in0=psi[:, 0:512], scalar1=m_cols[:, 0:1], scalar2=None, op0=ALU.mult), r=ki + ["m_cols"], w=["m_tmp"])
                psa, ka = ps_mix.get(4)
                P.op("pe", lambda e, psa=psa, c=c: e.matmul(psa[:, 0:512], lhsT=m_sc[:], rhs=m_v[:, c, :], start=True, stop=True), r=["m_sc", ("m_v", c, 0), ("m_v", c, 1)], w=ka)
                P.op("dve", lambda e, psa=psa: e.tensor_tensor(out=m_h[:], in0=m_tmp[:], in1=psa[:, 0:512], op=ALU.add), r=ka + ["m_tmp"], w=["m_h"])
                ps, keys = ps_mix.get(1)

                def fn(e, ps=ps, csl=csl):
                    e.matmul(ps[:, 0:1], lhsT=m_sc[:], rhs=m_onesb[:], start=True, stop=True)
                    e.matmul(ps[:, 1:2], lhsT=m_qT[:, 0, csl], rhs=m_nb[:, 0:1], start=True, stop=False)
                    return e.matmul(ps[:, 1:2], lhsT=m_qT[:, 1, csl], rhs=m_nb[:, 1:2], start=False, stop=True)
                P.op("pe", fn, r=["m_sc", "m_onesb", ("m_qT", 0), ("m_qT", 1), "m_nb"], w=keys)
                P.op("act", lambda e, ps=ps: e.activation(out=m_den[:], in_=ps[:, 0:2], func=AF.Copy), r=keys, w=["m_den"])
                P.op("dve", lambda e: e.scalar_tensor_tensor(out=m_d1[:], in0=m_den[:, 1:2], scalar=m_cols[:, 0:1], in1=m_den[:, 0:1], op0=ALU.mult, op1=ALU.add),
                     r=["m_den", "m_cols"], w=["m_d1"])
                P.op("act", lambda e: e.activation(out=m_d1[:], in_=m_d1[:], func=AF.Abs), r=["m_d1"], w=["m_d1b"])
                P.op("dve", lambda e: e.tensor_tensor(out=m_rden[:], in0=m_d1[:], in1=m_cols[:, 1:2], op=ALU.max), r=["m_d1b", "m_cols"], w=["m_rden0"])
                P.op("dve", lambda e: e.reciprocal(out=m_rden[:], in_=m_rden[:]), r=["m_rden0"], w=["m_rden"])
                P.op("dve", lambda e: e.tensor_scalar(out=m_h[:], in0=m_h[:], scalar1=m_rden[:], scalar2=None, op0=ALU.mult), r=["m_h", "m_rden"], w=["m_h"])
                P.op("act", lambda e: e.activation(out=junk[:], in_=m_h[:], func=AF.Square, accum_out=m_ssq[:]), r=["m_h"], w=["junk", "m_ssq"])
                P.op("act", lambda e: e.activation(out=m_rstd[:], in_=m_ssq[:], func=AF.Sqrt, scale=1.0 / 512, bias=epsc[:]), r=["m_ssq", "epsc"], w=["m_rstd0"])
                P.op("dve", lambda e: e.reciprocal(out=m_rstd[:], in_=m_rstd[:]), r=["m_rstd0"], w=["m_rstd"])
                P.op("dve", lambda e: e.tensor_copy(out=m_comb[:], in_=m_rstd[:]), r=["m_rstd"], w=["m_comb"])
                P.op("pool", lambda e, c=c: e.tensor_tensor(out=m_g2[:], in0=m_g[:, c, :], in1=prow[:, PR_MLNW:PR_MLNW + 512], op=ALU.mult), r=[("m_g", c, 0), ("m_g", c, 1), "prow"], w=["m_g2"])
                P.op("dve", lambda e: e.scalar_tensor_tensor(out=m_hbf[:], in0=m_h[:], scalar=m_comb[:], in1=m_g2[:], op0=ALU.mult, op1=ALU.mult),
                     r=["m_h", "m_comb", "m_g2"], w=["m_hbf"])
                transpose_out_bf(m_hbf, 4, 0, c, ["m_hbf"])
                P.op("dve", lambda e, c=c: e.tensor_scalar(out=m_kws[:], in0=m_ktm[:, c, :], scalar1=m_W[:, 127:128], scalar2=None, op0=ALU.mult), r=[("m_ktm", c), "m_W"], w=["m_kws"])
                for kc in range(2):
                    ps, keys = ps_mix.get(4)
                    P.op("pe", lambda e, ps=ps, kc=kc, c=c: e.matmul(ps[:, 0:512], lhsT=m_kws[:, kc * 128:(kc + 1) * 128], rhs=m_v[:, c, :], start=True, stop=True),
                         r=["m_kws", ("m_v", c, 0), ("m_v", c, 1)], w=keys)
                    P.op("dve", lambda e, ps=ps, kc=kc: e.scalar_tensor_tensor(out=m_C[:, kc, :], in0=m_C[:, kc, :], scalar=m_cols[:, 2:3], in1=ps[:, 0:512], op0=ALU.mult, op1=ALU.add),
                         r=keys + ["m_C", "m_cols", "m_Cb"], w=["m_C"])
                ps, keys = ps_mix.get(1)

                def fn(e, ps=ps):
                    e.matmul(ps[:, 0:1], lhsT=m_kws[:, 0:128], rhs=m_onesb[:], start=True, stop=True)
                    return e.matmul(ps[:, 1:2], lhsT=m_kws[:, 128:256], rhs=m_onesb[:], start=True, stop=True)
                P.op("pe", fn, r=["m_kws", "m_onesb"], w=keys)
                P.op("dve", lambda e, ps=ps: e.scalar_tensor_tensor(out=m_n[:], in0=m_n[:], scalar=m_cols[:, 2:3], in1=ps[:, 0:2], op0=ALU.mult, op1=ALU.add),
                     r=keys + ["m_n", "m_cols"], w=["m_n"])
                P.op("pool", lambda e: e.tensor_copy(out=m_Cb[:], in_=m_C[:]), r=["m_C"], w=["m_Cb"])
                P.op("pool", lambda e: e.tensor_copy(out=m_nb[:], in_=m_n[:]), r=["m_n"], w=["m_nb"])
            P.op("dve", lambda e: e.tensor_copy(out=m_carry[:], in_=R_["m"][0:1, TB - 1:TB]), r=["m_m", "m_wi"], w=["m_carry"])

        if "ssm" not in parts or "gdn" not in parts or "ml" not in parts:
            P.op("pool", lambda e: e.memset(ymixT[:], 0.0), w=[("ymixT", b) for b in range(16)])
        for tb in range(NB):
            load_xT(tb)
            if "ssm" in parts:
                ssm_proj(tb)
            if "ssm" in parts or "gdn" in parts:
                small_proj(tb)
            if "ssm" in parts:
                ssm_mixer(tb)
            if "gdn" in parts:
                gdn_proj(tb)
                gdn_mixer(tb)
            if "ml" in parts:
                ml_proj(tb)
                ml_mixer(tb)
            ymk = [("ymixT", b) for b in (0, 4, 8, 12)] + [("ymixT", b) for b in range(16)]
            if debug:
                P.op("pool", lambda e, tb=tb: e.dma_start(out=ymix_d[:, :, tb * TB:(tb + 1) * TB].rearrange("b p t -> p b t"), in_=ymixT[:]),
                     r=ymk, w=[("ymixd", tb)], dma="ymixd")
            n = 0
            for t in range(NOC):
                s = load_wo(t)
                for tt in range(NTT):
                    ps, keys = ps_proj.get(2)

                    def fn(e, ps=ps, s=s, tt=tt):
                        for kc in range(16):
                            ins = e.matmul(ps[:, 0:WT], lhsT=ymixT[:, kc, tt * 128:(tt + 1) * 128], rhs=wt[s][:, kc, :], start=(kc == 0), stop=(kc == 15))
                        return ins
                    P.op("pe", fn, r=ymk + [("wt", s)], w=keys)
                    ss = n % 2
                    n += 1
                    P.op("act", lambda e, ps=ps, ss=ss: e.activation(out=stage[ss][:], in_=ps[:, 0:WT], func=AF.Copy), r=keys, w=[("stage", ss)])
                    r0 = tb * TB + tt * 128
                    P.op("pool", lambda e, ss=ss, r0=r0, t=t: e.dma_start(out=ypart_d[r0:r0 + 128, t * WT:(t + 1) * WT], in_=stage[ss][:]),
                         r=[("stage", ss)], w=[("ypart", r0, t)], dma=("stage", ss))
        P.emit()
        build_layer.last_prog = P
    return nc


ML_OFF, GDN_OFF, SSM_OFF = 0, 8200, 16424


def core_columns(g):
    r = np.arange
    ml = ML_OFF
    gd = GDN_OFF
    ss = SSM_OFF
    cols = [
        ml + g * 256 + r(256),
        ml + 1024 + g * 256 + r(256),
        gd + g * 512 + r(512),
        gd + 2048 + g * 512 + r(512),
        gd + 4096 + g * 512 + r(512),
        ss + 4096 + g * 1024 + r(1024),
        ss + 8192 + g * 256 + r(256),
        ss + 9216 + g * 256 + r(256),
        ml + 2048 + g * 512 + r(512),
        ml + 4096 + g * 512 + r(512),
        ml + 6144 + g * 512 + r(512),
        gd + 6144 + g * 512 + r(512),
        ss + g * 1024 + r(1024),
        ml + 1024 + g * 256 + r(256),
        gd + 8192 + g * 4 + r(4),
        gd + 8208 + g * 4 + r(4),
        ss + 10240 + g * 16 + r(16),
        np.array([ml + 8192 + g]),
        np.array([ml + 8196 + g]),
    ]
    return np.concatenate(cols)


def pack_core(inp, l, g):
    cols = core_columns(g)
    D = inp["w_in"].shape[1]
    wcat = np.zeros((D, NCOL), np.float32)
    wcat[:, :cols.size] = inp["w_in"][l][:, cols]
    mixrows = np.concatenate([g * 512 + np.arange(512), 2048 + g * 512 + np.arange(512), 4096 + g * 1024 + np.arange(1024)])
    wout = np.ascontiguousarray(inp["w_out"][l][mixrows, :])
    gch = np.concatenate([g * 512 + np.arange(512), 2048 + g * 512 + np.arange(512), 4096 + g * 512 + np.arange(512)])
    sch = np.concatenate([g * 1024 + np.arange(1024), 4096 + g * 256 + np.arange(256), 5120 + g * 256 + np.arange(256)])
    cp = np.zeros((24 * 128, 5), np.float32)
    cp[:1536, 0:4] = inp["gdn_conv_w"][l][:, gch].T
    cp[1536:, 0:4] = inp["ssm_conv_w"][l][:, sch].T
    cp[1536:, 4] = inp["ssm_conv_b"][l][sch]
    convp = np.ascontiguousarray(cp.reshape(24, 128, 5).transpose(1, 0, 2).reshape(128, 120))
    pr = np.zeros((1, PR_N), np.float32)
    pr[0, PR_SDTB:PR_SDTB + 16] = inp["ssm_dt_bias"][l][16 * g:16 * g + 16]
    pr[0, PR_SALOG:PR_SALOG + 16] = inp["ssm_A_log"][l][16 * g:16 * g + 16]
    pr[0, PR_SD:PR_SD + 16] = inp["ssm_D"][l][16 * g:16 * g + 16]
    pr[0, PR_GDTB:PR_GDTB + 4] = inp["gdn_dt_bias"][l][4 * g:4 * g + 4]
    pr[0, PR_GALOG:PR_GALOG + 4] = inp["gdn_A_log"][l][4 * g:4 * g + 4]
    pr[0, PR_MIB] = inp["ml_i_bias"][l][g]
    pr[0, PR_MFB] = inp["ml_f_bias"][l][g]
    pr[0, PR_MLNW:PR_MLNW + 512] = inp["ml_norm_w"][l][g * 512:(g + 1) * 512]
    pr[0, PR_GNW:PR_GNW + 128] = inp["gdn_norm_w"][l]
    pr[0, PR_SNW:PR_SNW + 1024] = inp["ssm_norm_w"][l][g * 1024:(g + 1) * 1024]
    return {"wcat": wcat, "wout": wout, "convp": convp, "prow": pr, "consts": make_consts()}


def build_reduce_ln(ROWS=1024, D=D_MODEL):
    nc = bass.Bass("TRN2", target_bir_lowering=False)
    x_d = nc.dram_tensor("xin", [ROWS, D], F32, kind="ExternalInput").ap()
    p_d = [nc.dram_tensor(f"p{j}", [ROWS, D], F32, kind="ExternalInput").ap() for j in range(4)]
    g_d = nc.dram_tensor("lng", [1, D], F32, kind="ExternalInput").ap()
    b_d = nc.dram_tensor("lnb", [1, D], F32, kind="ExternalInput").ap()
    o_d = nc.dram_tensor("out", [ROWS, D], F32, kind="ExternalOutput").ap()
    CW = 1024 if D >= 1024 else D
    NCH = D // CW
    with ExitStack() as st:
        def T(name, shape, dt=F32):
            return st.enter_context(nc.sbuf_tensor("sb_" + name, shape, dt))
        P = Prog(nc, st)
        gb = T("gb", [128, D])
        bb = T("bb", [128, D])
        acc = [T(f"acc{i}", [128, D]) for i in range(2)]
        ld = [[T(f"ld{i}_{j}", [128, CW]) for j in range(5)] for i in range(2)]
        junk = T("junk", [128, D])
        stt = T("stt", [128, 8])
        epsl = T("epsl", [128, 1])
        P.op("sp", lambda e: e.dma_start(out=gb[:], in_=g_d.partition_broadcast(128)[:, 0, :]), w=["gb"], dma="gb")
        P.op("sp", lambda e: e.dma_start(out=bb[:], in_=b_d.partition_broadcast(128)[:, 0, :]), w=["bb"], dma="bb")
        P.op("pool", lambda e: e.memset(epsl[:], LN_EPS), w=["epsl"])
        n = 0
        for ti in range(ROWS // 128):
            r0 = ti * 128
            a = acc[ti % 2]
            ak = ("acc", ti % 2)
            for ch in range(NCH):
                s = n % 2
                n += 1
                cs = slice(ch * CW, (ch + 1) * CW)
                srcs = [x_d] + p_d
                for j in range(5):
                    q = "sp" if j % 2 == 0 else "act"
                    P.op(q, lambda e, s=s, j=j, cs=cs, r0=r0, srcs=srcs: e.dma_start(out=ld[s][j][:], in_=srcs[j][r0:r0 + 128, cs]),
                         w=[("ld", s, j)], dma=("ld", s, j))
                P.op("dve", lambda e, s=s, cs=cs, a=a: e.scalar_tensor_tensor(out=a[:, cs], in0=ld[s][0][:], scalar=float(ALPHA), in1=ld[s][1][:], op0=ALU.mult, op1=ALU.add),
                     r=[("ld", s, 0), ("ld", s, 1)], w=[ak])
                P.op("pool", lambda e, s=s: e.tensor_tensor(out=ld[s][2][:], in0=ld[s][2][:], in1=ld[s][3][:], op=ALU.add), r=[("ld", s, 2), ("ld", s, 3)], w=[("ld", s, 2)])
                P.op("pool", lambda e, s=s: e.tensor_tensor(out=ld[s][2][:], in0=ld[s][2][:], in1=ld[s][4][:], op=ALU.add), r=[("ld", s, 2), ("ld", s, 4)], w=[("ld", s, 2)])
                P.op("dve", lambda e, s=s, cs=cs, a=a: e.tensor_tensor(out=a[:, cs], in0=a[:, cs], in1=ld[s][2][:], op=ALU.add), r=[ak, ("ld", s, 2)], w=[ak])
            P.op("act", lambda e, a=a: e.activation(out=junk[:], in_=a[:], func=AF.Identity, accum_out=stt[:, 0:1]), r=[ak], w=["junk", "st0"])
            P.op("dve", lambda e: e.tensor_scalar(out=stt[:, 1:2], in0=stt[:, 0:1], scalar1=-1.0 / D, scalar2=None, op0=ALU.mult), r=["st0"], w=["st1"])
            P.op("dve", lambda e, a=a: e.tensor_scalar(out=a[:], in0=a[:], scalar1=stt[:, 1:2], scalar2=None, op0=ALU.add), r=[ak, "st1"], w=[ak])
            P.op("act", lambda e, a=a: e.activation(out=junk[:], in_=a[:], func=AF.Square, accum_out=stt[:, 2:3]), r=[ak], w=["junk", "st2"])
            P.op("act", lambda e: e.activation(out=stt[:, 3:4], in_=stt[:, 2:3], func=AF.Sqrt, scale=1.0 / D, bias=epsl[:]), r=["st2", "epsl"], w=["st3"])
            P.op("dve", lambda e: e.reciprocal(out=stt[:, 4:5], in_=stt[:, 3:4]), r=["st3"], w=["st4"])
            P.op("dve", lambda e, a=a: e.scalar_tensor_tensor(out=a[:], in0=a[:], scalar=stt[:, 4:5], in1=gb[:], op0=ALU.mult, op1=ALU.mult), r=[ak, "st4", "gb"], w=[ak])
            P.op("pool", lambda e, a=a: e.tensor_tensor(out=a[:], in0=a[:], in1=bb[:], op=ALU.add), r=[ak, "bb"], w=[ak])
            P.op("pool", lambda e, a=a, r0=r0: e.dma_start(out=o_d[r0:r0 + 128, :], in_=a[:]), r=[ak], w=[("out", ti)], dma=ak)
        P.emit()
    return nc


_PROGS = {}


def kernel(x, w_in, w_out, ml_i_bias, ml_f_bias, ml_norm_w, gdn_conv_w, gdn_A_log, gdn_dt_bias,
           gdn_norm_w, ssm_conv_w, ssm_conv_b, ssm_A_log, ssm_dt_bias, ssm_D, ssm_norm_w, ln_g, ln_b):
    inp = dict(w_in=np.asarray(w_in), w_out=np.asarray(w_out), ml_i_bias=np.asarray(ml_i_bias), ml_f_bias=np.asarray(ml_f_bias),
               ml_norm_w=np.asarray(ml_norm_w), gdn_conv_w=np.asarray(gdn_conv_w), gdn_A_log=np.asarray(gdn_A_log),
               gdn_dt_bias=np.asarray(gdn_dt_bias), gdn_norm_w=np.asarray(gdn_norm_w), ssm_conv_w=np.asarray(ssm_conv_w),
               ssm_conv_b=np.asarray(ssm_conv_b), ssm_A_log=np.asarray(ssm_A_log), ssm_dt_bias=np.asarray(ssm_dt_bias),
               ssm_D=np.asarray(ssm_D), ssm_norm_w=np.asarray(ssm_norm_w))
    ln_g = np.asarray(ln_g, np.float32)
    ln_b = np.asarray(ln_b, np.float32)
    xcur = np.ascontiguousarray(np.asarray(x, np.float32))
    B, S, D = xcur.shape
    if "layer" not in _PROGS:
        _PROGS["layer"] = build_layer(D=D, S=S)
        _PROGS["ln"] = build_reduce_ln(ROWS=B * S // 8, D=D)
    RPC = B * S // 8
    for l in range(DEPTH):
        packs = [pack_core(inp, l, g) for g in range(NG)]
        in_maps = [dict(packs[c % NG], x=xcur[c // NG]) for c in range(8)]
        res = run_bass_kernel_spmd(_PROGS["layer"], in_maps, core_ids=list(range(8)))
        yp = [r["ypart"] for r in res.results]
        del in_maps, packs
        xf = xcur.reshape(B * S, D)
        in_maps = []
        for c in range(8):
            b = (c * RPC) // S
            r0 = (c * RPC) % S
            m = {"xin": np.ascontiguousarray(xf[c * RPC:(c + 1) * RPC]), "lng": ln_g[l][None, :], "lnb": ln_b[l][None, :]}
            for j in range(NG):
                m[f"p{j}"] = np.ascontiguousarray(yp[b * NG + j][r0:r0 + RPC])
            in_maps.append(m)
        res = run_bass_kernel_spmd(_PROGS["ln"], in_maps, core_ids=list(range(8)))
        xcur = np.concatenate([r["out"] for r in res.results], axis=0).reshape(B, S, D)
    return xcur.astype(np.float32)
```

```python
from contextlib import ExitStack
import os
GSTOP = int(os.environ.get('GSTOP', '99'))
import numpy as np
import concourse.bass as bass
import concourse.mybir as mybir
from concourse.bass_utils import run_bass_kernel_spmd

F32 = mybir.dt.float32
BF16 = mybir.dt.bfloat16
ALU = mybir.AluOpType
AF = mybir.ActivationFunctionType

SIG_ROT = 30000


class Prog:
    ENG = ("pe", "act", "dve", "pool", "sp")

    def __init__(self, nc, stack):
        self._stack = stack
        self.nc = nc
        self.ops = []
        self.lw = {}
        self.rs = {}
        self.dma_cnt = {}

    def op(self, eng, fn, r=(), w=(), dma=None, inc=16):
        r = [self.canon(k) for k in r]
        w = [self.canon(k) for k in w]
        i = len(self.ops)
        deps = set()
        raw = set()
        for k in r:
            j = self.lw.get(k)
            if j is not None:
                deps.add(j)
                raw.add(j)
        for k in w:
            j = self.lw.get(k)
            if j is not None:
                deps.add(j)
            for j in self.rs.get(k, ()):
                deps.add(j)
        keep = set()
        for j in deps:
            pj = self.ops[j]
            if pj["dma"] is None and dma is None and pj["eng"] == eng:
                if eng == "pe":
                    continue
            keep.add(j)
        o = dict(eng=eng, fn=fn, deps=keep, dma=dma, sig=False, sidx=None, inc=inc)
        if dma is not None:
            c = self.dma_cnt.get(dma, 0) + 1
            self.dma_cnt[dma] = c
            o["sidx"] = c
            o["sig"] = True
        self.ops.append(o)
        for j in keep:
            self.ops[j]["sig"] = True
        for k in r:
            self.rs.setdefault(k, []).append(i)
        for k in w:
            self.lw[k] = i
            self.rs[k] = []
        return i

    ALIAS = {"g_rn0": "g_rn", "s_rstd0": "s_rstd", "g_rstd0": "g_rstd", "m_rstd0": "m_rstd", "m_rden0": "m_rden",
             "m_W0": "m_W", "m_d1b": "m_d1", "m_a2": "m_a", "s_dt_b": "s_dt_a", "g_sp_b": "g_sp_a", "sUh": "Uh", "gUh": "Uh",
             "m_wi": "m_a", "m_emm": "m_l1", "m_nG": "m_t", "g_ktm": ("s_xtm", 0), "g_vtm": ("s_xtm", 1),
             "g_Rv": "s_xdt", "g_Rk": "s_xdt"}

    def canon(self, k):
        if isinstance(k, tuple):
            if k[0] == "cvp":
                return ("cv",) + k[1:]
            if k[0] in ("sU", "gU"):
                return ("U",) + k[1:]
            if k[0] in ("g_sz", "m_g"):
                return ("s_sz",) + k[1:]
            if k[0] == "m_zt":
                return ("s_sz", k[1], 2 + k[2])
            return k
        return self.ALIAS.get(k, k)

    def emit(self, final_wait_eng="sp"):
        nc = self.nc
        cnt = {e: 0 for e in self.ENG}
        for o in self.ops:
            if o["dma"] is None and o["sig"]:
                cnt[o["eng"]] += 1
                o["sidx"] = cnt[o["eng"]]
        sems = {}
        stack = self._stack
        for e in self.ENG:
            n = (cnt[e] + SIG_ROT - 1) // SIG_ROT
            for q in range(max(n, 1)):
                sems[("E", e, q)] = stack.enter_context(nc.semaphore(f"s_{e}_{q}"))
        for k in self.dma_cnt:
            nm = "d_" + "_".join(str(x) for x in (k if isinstance(k, tuple) else (k,)))
            sems[("D", k)] = stack.enter_context(nc.semaphore(nm))
        self.nsem = len(sems)

        def chan(o):
            if o["dma"] is not None:
                return ("D", o["dma"]), o["inc"] * o["sidx"]
            q, rr = divmod(o["sidx"] - 1, SIG_ROT)
            return ("E", o["eng"], q), rr + 1

        per = {e: [] for e in self.ENG}
        for o in self.ops:
            per[o["eng"]].append(o)
        final = {}
        for o in self.ops:
            if o["dma"] is not None:
                ch, v = chan(o)
                final[ch] = max(final.get(ch, 0), v)
        block = stack.enter_context(nc.Block())
        deco = {"pe": block.tensor, "act": block.scalar, "dve": block.vector,
                "pool": block.gpsimd, "sp": block.sync}
        ops = self.ops

        def make(e):
            def body(engobj):
                seen = {}
                for o in per[e]:
                    need = {}
                    for j in o["deps"]:
                        ch, v = chan(ops[j])
                        if v > need.get(ch, 0):
                            need[ch] = v
                    for ch, v in need.items():
                        if seen.get(ch, 0) >= v:
                            continue
                        engobj.wait_ge(sems[ch], v)
                        seen[ch] = v
                    ins = o["fn"](engobj)
                    if o["sig"]:
                        ch, v = chan(o)
                        ins.then_inc(sems[ch], o["inc"] if o["dma"] is not None else 1)
                if e == final_wait_eng:
                    for ch, v in final.items():
                        if seen.get(ch, 0) < v:
                            engobj.wait_ge(sems[ch], v)
            return body

        for e in self.ENG:
            if per[e] or e == final_wait_eng:
                deco[e](make(e))


D_MODEL = 4096
SEQ = 4096
BATCH = 2
DEPTH = 2
NG = 4
WT = 256
NT = 28
NCOL = NT * WT
MIXW = 2048
RMS_EPS = 1e-6
LN_EPS = 1e-5
ALPHA = (2 * DEPTH) ** 0.25

PR_SDTB, PR_SALOG, PR_SD, PR_GDTB, PR_GALOG, PR_MIB, PR_MFB = 0, 16, 32, 48, 52, 56, 57
PR_MLNW, PR_GNW, PR_SNW = 64, 64 + 512, 64 + 512 + 128
PR_N = 64 + 512 + 128 + 1024
C_ID, C_TRI, C_U, C_ONES, C_BDS, C_BD, C_OFFT, C_NEG = range(8)
NCONST = 8


def make_consts():
    i = np.arange(128)
    k, l = i[:, None], i[None, :]
    c = np.zeros((NCONST, 128, 128), np.float32)
    c[C_ID] = np.eye(128)
    c[C_TRI] = (k <= l)
    c[C_U] = (k > l)
    c[C_ONES] = 1.0
    same = (k // 64) == (l // 64)
    c[C_BDS] = (k > l) & same
    c[C_BD] = same
    c[C_OFFT] = (k < 64) & (l >= 64)
    c[C_NEG] = np.where(k > l, -30000.0, 0.0)
    return np.ascontiguousarray(c.transpose(1, 0, 2).reshape(128, NCONST * 128))


def build_layer(D=D_MODEL, S=SEQ, TB=256, debug=False, parts=("ssm", "gdn", "ml")):
    KC = D // 128
    NB = S // TB
    NTT = TB // 128
    nc = bass.Bass("TRN2", target_bir_lowering=False)
    x_d = nc.dram_tensor("x", [S, D], F32, kind="ExternalInput").ap()
    wcat_d = nc.dram_tensor("wcat", [D, NCOL], F32, kind="ExternalInput").ap()
    wout_d = nc.dram_tensor("wout", [MIXW, D], F32, kind="ExternalInput").ap()
    convp_d = nc.dram_tensor("convp", [128, 24 * 5], F32, kind="ExternalInput").ap()
    prow_d = nc.dram_tensor("prow", [1, PR_N], F32, kind="ExternalInput").ap()
    const_d = nc.dram_tensor("consts", [128, NCONST * 128], F32, kind="ExternalInput").ap()
    ypart_d = nc.dram_tensor("ypart", [S, D], F32, kind="ExternalOutput").ap()
    wscr_d = nc.dram_tensor("wscr", [NT, 128, KC, WT], BF16, kind="Internal").ap()
    NOC = D // WT
    woscr_d = nc.dram_tensor("woscr", [NOC, 128, 16, WT], BF16, kind="Internal").ap()
    if debug:
        ymix_d = nc.dram_tensor("ymix", [16, 128, S], BF16, kind="ExternalOutput").ap()

    with ExitStack() as st:
        def T(name, shape, dt=F32):
            return st.enter_context(nc.sbuf_tensor("sb_" + name, shape, dt))
        P = Prog(nc, st)
        psb = [st.enter_context(nc.psum_tensor(f"psb{i}", [128, 512], F32)) for i in range(8)]

        class PSA:
            def __init__(self, banks):
                self.banks = banks
                self.pos = 0

            def get(self, nq):
                bank = self.banks[self.pos % len(self.banks)]
                self.pos += 1
                return psb[bank][:, 0:nq * 128], [("ps", bank)]
        ps_proj = PSA([0, 1])
        ps_mix = PSA([2, 3, 4, 5, 6, 7])

        consts = T("consts", [128, NCONST, 128])
        cid = consts[:, C_ID, :]
        ctri = consts[:, C_TRI, :]
        cU = consts[:, C_U, :]
        cones = consts[:, C_ONES, :]
        idb = T("idb", [128, 128], BF16)
        prow = T("prow", [128, PR_N])
        convp = T("convp", [128, 24, 5])
        nA_s = T("nA_s", [128, 16])
        nA_g = T("nA_g", [128, 4])
        XLW = 512 if D >= 512 else D
        xld = [T(f"xld{i}", [128, XLW]) for i in range(2)]
        xT = T("xT", [128, KC, TB], BF16)
        wt = [T(f"wt{i}", [128, max(KC, 16), WT], BF16) for i in range(2)]
        CK = 2 if KC >= 2 else 1
        assert CK == 2 and WT == 256
        HS = T("HS", [128, 9216])
        bar = T("bar", [128, 1])
        cin = [HS[:, i * 512:(i + 1) * 512].rearrange("p (k c) -> p k c", k=CK) for i in range(2)]
        cout = [HS[:, 1024 + i * 256:1024 + (i + 1) * 256].bitcast(BF16).rearrange("p (k c) -> p k c", k=CK) for i in range(2)]
        HS_KEYS = [("cin", 0), ("cin", 1), ("cout", 0), ("cout", 1), "s_gu", ("s_mte", 0), ("s_mte", 1), "s_xw"]
        for h_ in range(4):
            HS_KEYS += [(k_, h_) for k_ in ("hA", "hB", "hC", "hD", "hA0a", "hA0b", "hAToff", "hattn", "huw")]
            HS_KEYS += [(k_, h_, i_) for k_ in ("hPP", "hTT") for i_ in range(2)]

        def barrier():
            P.op("pool", lambda e: e.memset(bar[:], 0.0), w=HS_KEYS)
        U = T("U", [128, 12, 3 + TB])
        hist_s = T("hist_s", [128, 12, 3])
        hist_g = T("hist_g", [128, 12, 3])
        cv = T("cv", [128, 12, TB])
        ymixT = T("ymixT", [128, 16, TB], BF16)
        stage = [T(f"stage{i}", [128, WT]) for i in range(2)]
        junk = T("junk", [128, 512])
        epsc = T("epsc", [128, 1])

        P.op("sp", lambda e: e.dma_start(out=consts[:], in_=const_d.rearrange("p (c k) -> p c k", c=NCONST)), w=["consts"], dma="consts")
        P.op("sp", lambda e: e.dma_start(out=prow[:], in_=prow_d.partition_broadcast(128)[:, 0, :]), w=["prow"], dma="prow")
        P.op("sp", lambda e: e.dma_start(out=convp[:], in_=convp_d.rearrange("p (c k) -> p c k", c=24)), w=["convp"], dma="convp")
        P.op("dve", lambda e: e.tensor_copy(out=idb[:], in_=cid), r=["consts"], w=["idb"])
        P.op("pool", lambda e: e.memset(epsc[:], RMS_EPS), w=["epsc"])
        P.op("act", lambda e: e.activation(out=nA_s[:], in_=prow[:, PR_SALOG:PR_SALOG + 16], func=AF.Exp), r=["prow"], w=["nA_s"])
        P.op("dve", lambda e: e.tensor_scalar(out=nA_s[:], in0=nA_s[:], scalar1=-1.0, scalar2=None, op0=ALU.mult), r=["nA_s"], w=["nA_s"])
        P.op("act", lambda e: e.activation(out=nA_g[:], in_=prow[:, PR_GALOG:PR_GALOG + 4], func=AF.Exp), r=["prow"], w=["nA_g"])
        P.op("dve", lambda e: e.tensor_scalar(out=nA_g[:], in0=nA_g[:], scalar1=-1.0, scalar2=None, op0=ALU.mult), r=["nA_g"], w=["nA_g"])

        wv = wcat_d.rearrange("(kc p) c -> p kc c", p=128)
        n = 0
        cast_engs = ["dve", "pool", "act"]
        for t in range(NT):
            for k0 in range(0, KC, CK):
                s = n % 2
                P.op("sp", lambda e, s=s, t=t, k0=k0: e.dma_start(out=cin[s][:], in_=wv[:, k0:k0 + CK, t * WT:(t + 1) * WT]),
                     w=[("cin", s)], dma=("cin", s))
                ce = cast_engs[n % 3]
                if ce == "act":
                    P.op("act", lambda e, s=s: e.activation(out=cout[s][:], in_=cin[s][:], func=AF.Copy), r=[("cin", s)], w=[("cout", s)])
                else:
                    P.op(ce, lambda e, s=s: e.tensor_copy(out=cout[s][:], in_=cin[s][:]), r=[("cin", s)], w=[("cout", s)])
                P.op("pool", lambda e, s=s, t=t, k0=k0: e.dma_start(out=wscr_d[t][:, k0:k0 + CK, :], in_=cout[s][:]),
                     r=[("cout", s)], w=[("wscr", t)], dma=("cout", s))
                n += 1
        wov = wout_d.rearrange("(kc p) c -> p kc c", p=128)
        for t in range(NOC):
            for k0 in range(0, 16, CK):
                s = n % 2
                P.op("sp", lambda e, s=s, t=t, k0=k0: e.dma_start(out=cin[s][:], in_=wov[:, k0:k0 + CK, t * WT:(t + 1) * WT]),
                     w=[("cin", s)], dma=("cin", s))
                ce = cast_engs[n % 3]
                if ce == "act":
                    P.op("act", lambda e, s=s: e.activation(out=cout[s][:], in_=cin[s][:], func=AF.Copy), r=[("cin", s)], w=[("cout", s)])
                else:
                    P.op(ce, lambda e, s=s: e.tensor_copy(out=cout[s][:], in_=cin[s][:]), r=[("cin", s)], w=[("cout", s)])
                P.op("pool", lambda e, s=s, t=t, k0=k0: e.dma_start(out=woscr_d[t][:, k0:k0 + CK, :], in_=cout[s][:]),
                     r=[("cout", s)], w=[("woscr", t)], dma=("cout", s))
                n += 1

        barrier()
        wslot = [0]

        def load_w(t):
            s = wslot[0] % 2
            wslot[0] += 1
            P.op("sp", lambda e: e.dma_start(out=wt[s][:, 0:KC, :], in_=wscr_d[t]), r=[("wscr", t)], w=[("wt", s)], dma=("wt", s))
            return s

        def load_wo(t):
            s = wslot[0] % 2
            wslot[0] += 1
            P.op("sp", lambda e: e.dma_start(out=wt[s][:, 0:16, :], in_=woscr_d[t]), r=[("woscr", t)], w=[("wt", s)], dma=("wt", s))
            return s

        def proj_fm(s, j, M=128, c0=None):
            ps, keys = ps_proj.get(2)
            cc = j * 128 if c0 is None else c0

            def fn(e):
                for kc in range(KC):
                    ins = e.matmul(ps[0:M, 0:TB], lhsT=wt[s][:, kc, cc:cc + M], rhs=xT[:, kc, :], start=(kc == 0), stop=(kc == KC - 1))
                return ins
            P.op("pe", fn, r=[("wt", s), "xT"], w=keys)
            return ps, keys

        def proj_tm(s, tt):
            ps, keys = ps_proj.get(2)

            def fn(e):
                for kc in range(KC):
                    ins = e.matmul(ps[:, 0:WT], lhsT=xT[:, kc, tt * 128:(tt + 1) * 128], rhs=wt[s][:, kc, :], start=(kc == 0), stop=(kc == KC - 1))
                return ins
            P.op("pe", fn, r=[("wt", s), "xT"], w=keys)
            return ps, keys

        def load_xT(tb):
            n = 0
            for tt in range(NTT):
                r0 = tb * TB + tt * 128
                for c0 in range(0, D, XLW):
                    s = n % 2
                    n += 1
                    P.op("sp", lambda e, s=s, r0=r0, c0=c0: e.dma_start(out=xld[s][:], in_=x_d[r0:r0 + 128, c0:c0 + XLW]),
                         w=[("xld", s)], dma=("xld", s))
                    for q0 in range(0, XLW, 512):
                        nq = min(4, (XLW - q0) // 128)
                        ps, keys = ps_mix.get(nq)

                        def fn(e, s=s, q0=q0, nq=nq, ps=ps):
                            for q in range(nq):
                                ins = e.transpose(out=ps[:, q * 128:(q + 1) * 128], in_=xld[s][:, q0 + q * 128:q0 + (q + 1) * 128], identity=cid)
                            return ins
                        P.op("pe", fn, r=[("xld", s), "consts"], w=keys)
                        kc0 = (c0 + q0) // 128
                        P.op("act", lambda e, ps=ps, kc0=kc0, nq=nq, tt=tt: e.activation(
                            out=xT[:, kc0:kc0 + nq, tt * 128:(tt + 1) * 128],
                            in_=ps[:, 0:nq * 128].rearrange("p (q t) -> p q t", q=nq), func=AF.Copy),
                            r=keys, w=["xT"])

        def conv_blocks(Ubuf, hist, cbase, nblk, key):
            P.op("pool", lambda e: e.tensor_copy(out=Ubuf[:, 0:nblk, 0:3], in_=hist[:, 0:nblk, :]), r=[key + "hist"], w=[key + "Uh"])
            for b in range(nblk):
                eng = "dve"
                rk = [(key + "U", b), key + "Uh", "convp"]

                P.op(eng, lambda e, b=b: e.tensor_scalar(out=cv[:, b, :], in0=Ubuf[:, b, 3:3 + TB], scalar1=convp[:, cbase + b, 3:4], scalar2=convp[:, cbase + b, 4:5],
                                                         op0=ALU.mult, op1=ALU.add), r=rk, w=[("cvp", b)])
                for k in range(3):
                    P.op(eng, lambda e, b=b, k=k: e.scalar_tensor_tensor(out=cv[:, b, :], in0=Ubuf[:, b, k:k + TB], scalar=convp[:, cbase + b, k:k + 1], in1=cv[:, b, :],
                                                                         op0=ALU.mult, op1=ALU.add), r=rk + [("cvp", b)], w=[("cvp", b)])
                P.op("act", lambda e, b=b: e.activation(out=cv[:, b, :], in_=cv[:, b, :], func=AF.Silu), r=[("cvp", b)], w=[("cv", b)])
            P.op("pool", lambda e: e.tensor_copy(out=hist[:, 0:nblk, :], in_=Ubuf[:, 0:nblk, TB:TB + 3]),
                 r=[(key + "U", b) for b in range(nblk)] + [key + "Uh"], w=[key + "hist"])

        def softplus_(out, xin, nparts, shape_keys_r, wkey, tmpa, tmpb):
            P.op("act", lambda e: e.activation(out=tmpa, in_=xin, func=AF.Abs), r=shape_keys_r, w=[wkey + "_a"])
            P.op("act", lambda e: e.activation(out=tmpa, in_=tmpa, func=AF.Exp, scale=-1.0), r=[wkey + "_a"], w=[wkey + "_b"])
            P.op("act", lambda e: e.activation(out=tmpb, in_=tmpa, func=AF.Ln, bias=1.0), r=[wkey + "_b"], w=[wkey + "_c"])
            P.op("dve", lambda e: e.scalar_tensor_tensor(out=out, in0=xin, scalar=0.0, in1=tmpb, op0=ALU.max, op1=ALU.add),
                 r=list(shape_keys_r) + [wkey + "_c"], w=[wkey])

        def transpose_out_bf(src_bf, ncb, blk0, c, rkeys):
            ps, keys = ps_mix.get((ncb + 1) // 2)
            psv = ps.bitcast(BF16)

            def fn(e):
                for i in range(ncb):
                    ins = e.transpose(out=psv[:, i * 128:(i + 1) * 128], in_=src_bf[:, i * 128:(i + 1) * 128], identity=idb[:])
                return ins
            P.op("pe", fn, r=list(rkeys) + ["idb"], w=keys)
            P.op("act", lambda e: e.activation(out=ymixT[:, blk0:blk0 + ncb, c * 128:(c + 1) * 128],
                                                in_=psv[:, 0:ncb * 128].rearrange("p (b t) -> p b t", b=ncb), func=AF.Copy),
                 r=keys, w=[("ymixT", blk0)])

        s_small = T("s_small", [128, NTT, 32])
        big_tm = T("big_tm", [128, NTT, 1024])
        s_xtm = T("s_xtm", [128, 1024])
        s_xdt = T("s_xdt", [128, 1024])
        if "ssm" in parts:
            s_sz = big_tm
            s_dt = T("s_dt", [128, NTT, 16])
            s_a = T("s_a", [128, NTT, 16])
            s_t1 = T("s_t1", [128, NTT, 16])
            s_t2 = T("s_t2", [128, NTT, 16])
            s_xx = T("s_xx", [128, NTT, 16])
            s_state = T("s_state", [128, 2, 512])
            s_pre = T("s_pre", [128, 48])
            s_ex = T("s_ex", [128, 48])
            s_xw = HS[:, 2048:3072]
            s_btm = T("s_btm", [128, 2, 128])
            s_gu = HS[:, 0:1024].rearrange("p (h l) -> p h l", h=8)
            s_mt = HS[:, 1024:2048].rearrange("p (h l) -> p h l", h=8)
            s_cbm = T("s_cbm", [128, 128])
            s_y = T("s_y", [128, 512])
            s_ybf = T("s_ybf", [128, 512], BF16)
            s_ssq = T("s_ssq", [128, 1])
            s_rstd = T("s_rstd", [128, 1])
            P.op("pool", lambda e: e.memset(s_state[:], 0.0), w=["s_state0", "s_state1"])
            P.op("pool", lambda e: e.memset(hist_s[:], 0.0), w=["shist"])

        def ssm_proj(tb):
            for ti, t in enumerate(range(8, 14)):
                s = load_w(t)
                for j in range(2):
                    ps, keys = proj_fm(s, j)
                    b = ti * 2 + j
                    P.op("act", lambda e, ps=ps, b=b: e.activation(out=U[:, b, 3:3 + TB], in_=ps[:, 0:TB], func=AF.Copy), r=keys, w=[("sU", b)])
            for ti, t in enumerate(range(22, 26)):
                s = load_w(t)
                for tt in range(NTT):
                    ps, keys = proj_tm(s, tt)
                    P.op("act", lambda e, ps=ps, ti=ti, tt=tt: e.activation(out=s_sz[:, tt, ti * WT:(ti + 1) * WT], in_=ps[:, 0:WT], func=AF.Silu),
                         r=keys, w=[("s_sz", tt, ti)])

        def small_proj(tb):
            s = load_w(27)
            for tt in range(NTT):
                ps, keys = proj_tm(s, tt)
                P.op("dve", lambda e, ps=ps, tt=tt: e.tensor_copy(out=s_small[:, tt, :], in_=ps[:, 0:32]), r=keys, w=[("small", tt)])
            return s

        def ssm_mixer(tb):
            barrier()
            conv_blocks(U, hist_s, 12, 12, "s")
            smk = [("small", tt) for tt in range(NTT)]
            P.op("dve", lambda e: e.tensor_tensor(out=s_xx[:], in0=s_small[:, :, 8:24],
                                                  in1=prow[:, PR_SDTB:PR_SDTB + 16].unsqueeze(1).broadcast_to([128, NTT, 16]), op=ALU.add),
                 r=smk + ["prow"], w=["s_xx"])
            softplus_(s_dt[:], s_xx[:], 128, ["s_xx"], "s_dt", s_t1[:], s_t2[:])
            P.op("dve", lambda e: e.tensor_tensor(out=s_a[:], in0=s_dt[:], in1=nA_s[:].unsqueeze(1).broadcast_to([128, NTT, 16]), op=ALU.mult),
                 r=["s_dt", "nA_s"], w=["s_a"])
            for c in range(NTT):
                csl = slice(c * 128, (c + 1) * 128)
                ps, keys = ps_mix.get(1)

                def fn(e, ps=ps, c=c):
                    e.matmul(ps[:, 0:16], lhsT=ctri, rhs=s_a[:, c, :], start=True, stop=True)
                    return e.matmul(ps[:, 16:32], lhsT=cones, rhs=s_a[:, c, :], start=True, stop=True)
                P.op("pe", fn, r=["consts", "s_a"], w=keys)
                P.op("act", lambda e, ps=ps: e.activation(out=s_pre[:, 0:32], in_=ps[:, 0:32], func=AF.Copy), r=keys, w=["s_pre"])
                P.op("dve", lambda e: e.tensor_tensor(out=s_pre[:, 32:48], in0=s_pre[:, 16:32], in1=s_pre[:, 0:16], op=ALU.subtract), r=["s_pre"], w=["s_pre2"])
                P.op("act", lambda e: e.activation(out=s_ex[:], in_=s_pre[:], func=AF.Exp), r=["s_pre", "s_pre2"], w=["s_ex"])
                for half in range(2):
                    ps, keys = ps_mix.get(4)

                    def fn(e, ps=ps, half=half, csl=csl):
                        for q in range(4):
                            ins = e.transpose(out=ps[:, q * 128:(q + 1) * 128], in_=cv[:, half * 4 + q, csl], identity=cid)
                        return ins
                    P.op("pe", fn, r=[("cv", half * 4 + q) for q in range(4)] + ["consts"], w=keys)
                    P.op("act", lambda e, ps=ps, half=half: e.activation(out=s_xtm[:, half * 512:(half + 1) * 512], in_=ps[:, 0:512], func=AF.Copy),
                         r=keys, w=[("s_xtm", half)])
                ps, keys = ps_mix.get(2)

                def fn(e, ps=ps, csl=csl):
                    e.transpose(out=ps[:, 0:128], in_=cv[:, 8, csl], identity=cid)
                    return e.transpose(out=ps[:, 128:256], in_=cv[:, 9, csl], identity=cid)
                P.op("pe", fn, r=[("cv", 8), ("cv", 9), "consts"], w=keys)
                P.op("dve", lambda e, ps=ps: e.tensor_copy(out=s_btm[:], in_=ps[:, 0:256].rearrange("p (g n) -> p g n", g=2)), r=keys, w=["s_btm"])
                xk = [("s_xtm", 0), ("s_xtm", 1)]
                v3 = lambda t_: t_[:, 0:1024].rearrange("p (h q) -> p h q", h=16)
                P.op("pool", lambda e, c=c: e.tensor_tensor(out=v3(s_xdt), in0=v3(s_xtm), in1=s_dt[:, c, :].unsqueeze(2).broadcast_to([128, 16, 64]), op=ALU.mult),
                     r=xk + ["s_dt"], w=["s_xdt"])
                P.op("pool", lambda e: e.tensor_tensor(out=v3(s_xw), in0=v3(s_xdt), in1=s_ex[:, 32:48].unsqueeze(2).broadcast_to([128, 16, 64]), op=ALU.mult),
                     r=["s_xdt", "s_ex"], w=["s_xw"])
                for gi in range(2):
                    ps, keys = ps_mix.get(1)
                    P.op("pe", lambda e, ps=ps, gi=gi, csl=csl: e.matmul(ps[:, 0:128], lhsT=cv[:, 8 + gi, csl], rhs=cv[:, 10 + gi, csl], start=True, stop=True),
                         r=[("cv", 8 + gi), ("cv", 10 + gi)], w=keys)
                    P.op("dve", lambda e, ps=ps: e.tensor_tensor(out=s_cbm[:], in0=ps[:, 0:128], in1=ctri, op=ALU.mult), r=keys + ["consts"], w=["s_cbm"])
                    P.op("pool", lambda e, c=c, gi=gi: e.tensor_tensor(out=s_gu[:], in0=s_a[:, c, gi * 8:(gi + 1) * 8].unsqueeze(2).broadcast_to([128, 8, 128]),
                                                                       in1=cU.unsqueeze(1).broadcast_to([128, 8, 128]), op=ALU.mult),
                         r=["s_a", "consts"], w=["s_gu"])
                    for hh in range(2):
                        ps, keys = ps_mix.get(4)

                        def fn(e, ps=ps, hh=hh):
                            for q in range(4):
                                ins = e.matmul(ps[:, q * 128:(q + 1) * 128], lhsT=s_gu[:, hh * 4 + q, :], rhs=ctri, start=True, stop=True)
                            return ins
                        P.op("pe", fn, r=["s_gu", "consts"], w=keys)
                        P.op("act", lambda e, ps=ps, hh=hh: e.activation(out=s_mt[:, hh * 4:(hh + 1) * 4, :], in_=ps[:, 0:512].rearrange("p (h l) -> p h l", h=4), func=AF.Exp),
                             r=keys, w=[("s_mte", hh)])
                    P.op("dve", lambda e: e.tensor_tensor(out=s_mt[:], in0=s_mt[:], in1=s_cbm[:].unsqueeze(1).broadcast_to([128, 8, 128]), op=ALU.mult),
                         r=[("s_mte", 0), ("s_mte", 1), "s_cbm"], w=[("s_mte", 0), ("s_mte", 1)])
                    psd, kd = ps_mix.get(4)

                    def fn(e, psd=psd, gi=gi):
                        for h in range(8):
                            hg = gi * 8 + h
                            ins = e.matmul(psd[:, h * 64:(h + 1) * 64], lhsT=s_mt[:, h, :], rhs=s_xdt[:, hg * 64:(hg + 1) * 64], start=True, stop=True)
                        return ins
                    P.op("pe", fn, r=[("s_mte", 0), ("s_mte", 1), "s_xdt"], w=kd)
                    pso, ko = ps_mix.get(4)
                    P.op("pe", lambda e, pso=pso, gi=gi, csl=csl: e.matmul(pso[:, 0:512], lhsT=cv[:, 10 + gi, csl], rhs=s_state[:, gi, :], start=True, stop=True),
                         r=[("cv", 10 + gi), f"s_state{gi}"], w=ko)
                    y3 = s_y[:].rearrange("p (h q) -> p h q", h=8)
                    P.op("dve", lambda e, pso=pso, gi=gi: e.tensor_tensor(out=y3, in0=pso[:, 0:512].rearrange("p (h q) -> p h q", h=8),
                                                                          in1=s_ex[:, gi * 8:(gi + 1) * 8].unsqueeze(2).broadcast_to([128, 8, 64]), op=ALU.mult),
                         r=ko + ["s_ex"], w=["s_y"])
                    P.op("pool", lambda e, gi=gi: e.tensor_tensor(out=junk[:].rearrange("p (h q) -> p h q", h=8), in0=s_xtm[:, gi * 512:(gi + 1) * 512].rearrange("p (h q) -> p h q", h=8),
                                                                  in1=prow[:, PR_SD + gi * 8:PR_SD + (gi + 1) * 8].unsqueeze(2).broadcast_to([128, 8, 64]), op=ALU.mult),
                         r=[("s_xtm", gi), "prow"], w=["junk"])
                    P.op("pool", lambda e, gi=gi: e.tensor_tensor(out=s_y[:], in0=s_y[:], in1=junk[:], op=ALU.add), r=["s_y", "junk"], w=["s_y"])
                    P.op("dve", lambda e, psd=psd: e.tensor_tensor(out=s_y[:], in0=s_y[:], in1=psd[:, 0:512], op=ALU.add), r=kd + ["s_y"], w=["s_y"])
                    P.op("pool", lambda e, c=c, gi=gi: e.tensor_tensor(out=s_y[:], in0=s_y[:], in1=s_sz[:, c, gi * 512:(gi + 1) * 512], op=ALU.mult),
                         r=["s_y", ("s_sz", c, 2 * gi), ("s_sz", c, 2 * gi + 1)], w=["s_y"])
                    P.op("act", lambda e: e.activation(out=junk[:], in_=s_y[:], func=AF.Square, accum_out=s_ssq[:]), r=["s_y"], w=["junk", "s_ssq"])
                    P.op("act", lambda e: e.activation(out=s_rstd[:], in_=s_ssq[:], func=AF.Sqrt, scale=1.0 / 512, bias=epsc[:]), r=["s_ssq", "epsc"], w=["s_rstd0"])
                    P.op("dve", lambda e: e.reciprocal(out=s_rstd[:], in_=s_rstd[:]), r=["s_rstd0"], w=["s_rstd"])
                    P.op("dve", lambda e, gi=gi: e.scalar_tensor_tensor(out=s_ybf[:], in0=s_y[:], scalar=s_rstd[:], in1=prow[:, PR_SNW + gi * 512:PR_SNW + (gi + 1) * 512],
                                                                        op0=ALU.mult, op1=ALU.mult), r=["s_y", "s_rstd", "prow"], w=["s_ybf"])
                    transpose_out_bf(s_ybf, 4, 8 + gi * 4, c, ["s_ybf"])
                    pss, ks = ps_mix.get(4)
                    P.op("pe", lambda e, pss=pss, gi=gi: e.matmul(pss[:, 0:512], lhsT=s_btm[:, gi, :], rhs=s_xw[:, gi * 512:(gi + 1) * 512], start=True, stop=True),
                         r=["s_btm", "s_xw"], w=ks)
                    st3 = s_state[:, gi, :].rearrange("p (h q) -> p h q", h=8)
                    P.op("pool", lambda e, st3=st3, gi=gi: e.tensor_tensor(out=st3, in0=st3, in1=s_ex[:, 16 + gi * 8:16 + (gi + 1) * 8].unsqueeze(2).broadcast_to([128, 8, 64]), op=ALU.mult),
                         r=[f"s_state{gi}", "s_ex"], w=[f"s_state{gi}"])
                    P.op("dve", lambda e, pss=pss, gi=gi: e.tensor_tensor(out=s_state[:, gi, :], in0=s_state[:, gi, :], in1=pss[:, 0:512], op=ALU.add),
                         r=ks + [f"s_state{gi}"], w=[f"s_state{gi}"])

        if "gdn" in parts:
            g_sz = big_tm[:, :, 0:512]
            g_sq = T("g_sq", [128, TB])
            g_rn = T("g_rn", [128, TB])
            g_beta = T("g_beta", [128, NTT, 4])
            g_xx = T("g_xx", [128, NTT, 4])
            g_t1s = T("g_t1s", [128, NTT, 4])
            g_t2s = T("g_t2s", [128, NTT, 4])
            g_sp = T("g_sp", [128, NTT, 4])
            g_g = T("g_g", [128, NTT, 4])
            g_pre = T("g_pre", [128, 12])
            g_ex = T("g_ex", [128, 12])
            g_bg = T("g_bg", [128, 4])
            g_ktm = s_xtm[:, 0:512].rearrange("p (h d) -> p h d", h=4)
            g_vtm = s_xtm[:, 512:1024].rearrange("p (h d) -> p h d", h=4)
            g_R = s_xdt[:].rearrange("p (h c) -> p h c", h=4)
            g_kd = T("g_kd", [128, 4, 128])
            g_H = []
            for h_ in range(4):
                b0 = h_ * 2304
                v2 = lambda o_: HS[:, b0 + o_:b0 + o_ + 256].rearrange("p (a l) -> p a l", a=2)
                v1 = lambda o_: HS[:, b0 + o_:b0 + o_ + 128]
                g_H.append(dict(A=v2(0), B=v2(256), C=v1(512), D=v1(640), A0=v2(768), AToff=v1(1024), attnT=v1(1152),
                                PP=[v2(1280), v2(1536)], TT=[v1(1792), v1(1920)], uw=v2(2048)))
            g_o = T("g_o", [128, 4, 128])
            g_o2 = T("g_o2", [128, 4, 128])
            g_ssq = T("g_ssq", [128, 4])
            g_rstd = T("g_rstd", [128, 4])
            g_obf = T("g_obf", [128, 512], BF16)
            g_S = T("g_S", [128, 4, 128])
            P.op("pool", lambda e: e.memset(g_S[:], 0.0), w=[("g_S", h) for h in range(4)])
            P.op("pool", lambda e: e.memset(hist_g[:], 0.0), w=["ghist"])
            cBDm = consts[:, C_BD, :]
            cOFFT = consts[:, C_OFFT, :]
            cNEG = consts[:, C_NEG, :]

        def gdn_proj(tb):
            for ti, t in enumerate(range(2, 8)):
                s = load_w(t)
                for j in range(2):
                    ps, keys = proj_fm(s, j)
                    b = ti * 2 + j
                    P.op("act", lambda e, ps=ps, b=b: e.activation(out=U[:, b, 3:3 + TB], in_=ps[:, 0:TB], func=AF.Copy), r=keys, w=[("gU", b)])
            for ti, t in enumerate(range(20, 22)):
                s = load_w(t)
                for tt in range(NTT):
                    ps, keys = proj_tm(s, tt)
                    P.op("act", lambda e, ps=ps, ti=ti, tt=tt: e.activation(out=g_sz[:, tt, ti * WT:(ti + 1) * WT], in_=ps[:, 0:WT], func=AF.Silu),
                         r=keys, w=[("g_sz", tt, ti)])

        def head_gen(c, csl, h):
            Hh = g_H[h]
            kA, kB, kC, kD = ("hA", h), ("hB", h), ("hC", h), ("hD", h)
            kA0a, kA0b, kAo, kat, kuw = ("hA0a", h), ("hA0b", h), ("hAToff", h), ("hattn", h), ("huw", h)
            GU, E_, t1, Af, A0, AToff, attnT, PP, TT, uw = (Hh[k_] for k_ in ("A", "B", "C", "D", "A0", "AToff", "attnT", "PP", "TT", "uw"))
            X1 = GU.rearrange("p a l -> p (a l)")
            X2 = E_.rearrange("p a l -> p (a l)")
            vn, tmp_ = t1, Af
            gcol = g_g[:, c, h:h + 1]
            P.op("dve", lambda e: e.tensor_scalar(out=GU[:, 0, :], in0=cU, scalar1=gcol, scalar2=None, op0=ALU.mult), r=["consts", "g_g"], w=[kA])
            yield
            P.op("dve", lambda e: e.tensor_scalar(out=GU[:, 1, :], in0=ctri, scalar1=gcol, scalar2=None, op0=ALU.mult), r=["consts", "g_g"], w=[kA])
            yield
            ps, keys = ps_mix.get(4)
            kT = cv[:, 4 + h, csl]
            qT_ = cv[:, h, csl]

            def fn(e, ps=ps):
                e.matmul(ps[:, 0:128], lhsT=GU[:, 0, :], rhs=ctri, start=True, stop=False)
                e.matmul(ps[:, 0:128], lhsT=cid, rhs=cNEG, start=False, stop=True)
                e.matmul(ps[:, 128:256], lhsT=GU[:, 1, :], rhs=cU, start=True, stop=True)
                e.matmul(ps[:, 256:384], lhsT=kT, rhs=kT, start=True, stop=True)
                return e.matmul(ps[:, 384:512], lhsT=kT, rhs=qT_, start=True, stop=True)
            P.op("pe", fn, r=[kA, "consts", ("cv", 4 + h), ("cv", h)], w=keys)
            yield
            P.op("act", lambda e, ps=ps: e.activation(out=E_[:], in_=ps[:, 0:256].rearrange("p (a l) -> p a l", a=2), func=AF.Exp), r=keys, w=[kB])
            yield
            P.op("dve", lambda e, ps=ps: e.tensor_tensor(out=t1[:], in0=ps[:, 256:384], in1=E_[:, 1, :], op=ALU.mult), r=keys + [kB], w=[kC])
            yield
            P.op("dve", lambda e, ps=ps: e.tensor_tensor(out=attnT[:], in0=ps[:, 384:512], in1=E_[:, 0, :], op=ALU.mult), r=keys + [kB], w=[kat])
            yield
            P.op("dve", lambda e: e.scalar_tensor_tensor(out=Af[:], in0=t1[:], scalar=g_beta[:, c, h:h + 1], in1=cU, op0=ALU.mult, op1=ALU.mult),
                 r=[kC, "g_beta", "consts"], w=[kD])
            yield
            ps, keys = ps_mix.get(1)
            P.op("pe", lambda e, ps=ps: e.transpose(out=ps[:, 0:128], in_=Af[:], identity=cid), r=[kD, "consts"], w=keys)
            yield
            P.op("pool", lambda e: e.tensor_tensor(out=A0[:, 0, :], in0=Af[:], in1=cBDm, op=ALU.mult), r=[kD, "consts"], w=[kA0a])
            yield
            P.op("dve", lambda e, ps=ps: e.tensor_tensor(out=A0[:, 1, :], in0=ps[:, 0:128], in1=cBDm, op=ALU.mult), r=keys + ["consts"], w=[kA0b])
            yield
            P.op("dve", lambda e, ps=ps: e.tensor_tensor(out=AToff[:], in0=ps[:, 0:128], in1=cOFFT, op=ALU.mult), r=keys + ["consts"], w=[kAo])
            yield
            P.op("pool", lambda e: e.tensor_tensor(out=TT[0][:], in0=cid, in1=A0[:, 1, :], op=ALU.subtract), r=[kA0b, "consts"], w=[("hTT", h, 0)])
            yield
            Pc, Pk = A0, [kA0a, kA0b]
            tcur = 0
            for k in range(1, 6):
                ps, keys = ps_mix.get(2)

                def fn(e, ps=ps, Pc=Pc, k=k):
                    ins = e.matmul(ps[:, 0:128], lhsT=Pc[:, 1, :], rhs=Pc[:, 0, :], start=True, stop=True)
                    if k < 5:
                        ins = e.matmul(ps[:, 128:256], lhsT=Pc[:, 0, :], rhs=Pc[:, 1, :], start=True, stop=True)
                    return ins
                P.op("pe", fn, r=Pk, w=keys)
                yield
                Pn = PP[k % 2]
                nk = [("hPP", h, k % 2)]
                P.op("act", lambda e, ps=ps, Pn=Pn: e.activation(out=Pn[:], in_=ps[:, 0:256].rearrange("p (a l) -> p a l", a=2), func=AF.Copy), r=keys, w=nk)
                yield
                ps, keys = ps_mix.get(1)
                P.op("pe", lambda e, ps=ps, Pn=Pn, tcur=tcur: e.matmul(ps[:, 0:128], lhsT=Pn[:, 0, :], rhs=TT[tcur][:], start=True, stop=True),
                     r=nk + [("hTT", h, tcur)], w=keys)
                yield
                P.op("dve", lambda e, ps=ps, tcur=tcur: e.tensor_tensor(out=TT[1 - tcur][:], in0=TT[tcur][:], in1=ps[:, 0:128], op=ALU.add),
                     r=keys + [("hTT", h, tcur)], w=[("hTT", h, 1 - tcur)])
                yield
                tcur = 1 - tcur
                Pc, Pk = Pn, nk
            TTf = TT[tcur]
            tk = [("hTT", h, tcur)]
            ps, keys = ps_mix.get(2)
            P.op("pe", lambda e, ps=ps: e.matmul(ps[:, 0:256], lhsT=TTf[:], rhs=g_R[:, h, :], start=True, stop=True), r=tk + ["g_Rv", "g_Rk"], w=keys)
            yield
            P.op("act", lambda e, ps=ps: e.activation(out=X1, in_=ps[:, 0:256], func=AF.Copy), r=keys, w=[kA])
            yield
            ps, keys = ps_mix.get(2)
            P.op("pe", lambda e, ps=ps: e.matmul(ps[:, 0:256], lhsT=AToff[:], rhs=X1, start=True, stop=True), r=[kAo, kA], w=keys)
            yield
            P.op("dve", lambda e, ps=ps: e.tensor_tensor(out=X2, in0=g_R[:, h, :], in1=ps[:, 0:256], op=ALU.subtract), r=keys + ["g_Rv", "g_Rk"], w=[kB])
            yield
            ps, keys = ps_mix.get(2)

            def fn(e, ps=ps):
                e.matmul(ps[:, 0:128], lhsT=TTf[:], rhs=X2[:, 0:128], start=True, stop=True)
                return e.matmul(ps[:, 128:256], lhsT=X2[:, 128:256], rhs=TTf[:], start=True, stop=True)
            P.op("pe", fn, r=tk + [kB], w=keys)
            yield
            P.op("act", lambda e, ps=ps: e.activation(out=uw[:], in_=ps[:, 0:256].rearrange("p (a l) -> p a l", a=2), func=AF.Copy), r=keys, w=[kuw])
            yield
            ps, keys = ps_mix.get(2)

            def fn(e, ps=ps):
                e.matmul(ps[:, 0:128], lhsT=uw[:, 1, :], rhs=g_S[:, h, :], start=True, stop=True)
                return e.matmul(ps[:, 128:256], lhsT=qT_, rhs=g_S[:, h, :], start=True, stop=True)
            P.op("pe", fn, r=[kuw, ("g_S", h), ("cv", h)], w=keys)
            yield
            P.op("dve", lambda e, ps=ps: e.tensor_tensor(out=vn[:], in0=uw[:, 0, :], in1=ps[:, 0:128], op=ALU.subtract), r=keys + [kuw], w=[kC])
            yield
            P.op("dve", lambda e, ps=ps: e.tensor_scalar(out=tmp_[:], in0=ps[:, 128:256], scalar1=g_ex[:, h:h + 1], scalar2=None, op0=ALU.mult), r=keys + ["g_ex"], w=[kD])
            yield
            ps, keys = ps_mix.get(2)

            def fn(e, ps=ps):
                e.matmul(ps[:, 0:128], lhsT=attnT[:], rhs=vn[:], start=True, stop=True)
                return e.matmul(ps[:, 128:256], lhsT=g_kd[:, h, :], rhs=vn[:], start=True, stop=True)
            P.op("pe", fn, r=[kat, kC, "g_kd"], w=keys)
            yield
            P.op("dve", lambda e, ps=ps: e.tensor_tensor(out=g_o[:, h, :], in0=tmp_[:], in1=ps[:, 0:128], op=ALU.add), r=keys + [kD], w=[("g_o", h)])
            yield
            P.op("dve", lambda e, ps=ps: e.scalar_tensor_tensor(out=g_S[:, h, :], in0=g_S[:, h, :], scalar=g_ex[:, 4 + h:5 + h], in1=ps[:, 128:256],
                                                                op0=ALU.mult, op1=ALU.add), r=keys + [("g_S", h), "g_ex"], w=[("g_S", h)])
            yield

        def gdn_mixer(tb):
            barrier()
            conv_blocks(U, hist_g, 0, 12, "g")
            for b in range(8):
                P.op("act", lambda e, b=b: e.activation(out=g_sq[:], in_=cv[:, b, :], func=AF.Square), r=[("cv", b)], w=["g_sq"])
                ps, keys = ps_mix.get(2)
                P.op("pe", lambda e, ps=ps: e.matmul(ps[:, 0:TB], lhsT=cones, rhs=g_sq[:], start=True, stop=True), r=["g_sq", "consts"], w=keys)
                P.op("act", lambda e, ps=ps: e.activation(out=g_rn[:], in_=ps[:, 0:TB], func=AF.Sqrt, bias=epsc[:]), r=keys + ["epsc"], w=["g_rn0"])
                P.op("dve", lambda e: e.reciprocal(out=g_rn[:], in_=g_rn[:]), r=["g_rn0"], w=["g_rn"])
                sc_ = 128 ** -0.5 if b < 4 else 1.0
                P.op("dve", lambda e, b=b, sc_=sc_: e.scalar_tensor_tensor(out=cv[:, b, :], in0=cv[:, b, :], scalar=sc_, in1=g_rn[:], op0=ALU.mult, op1=ALU.mult),
                     r=[("cv", b), "g_rn"], w=[("cv", b)])
            if GSTOP <= 1:
                return
            smk = [("small", tt) for tt in range(NTT)]
            P.op("act", lambda e: e.activation(out=g_beta[:], in_=s_small[:, :, 0:4], func=AF.Sigmoid), r=smk, w=["g_beta"])
            P.op("dve", lambda e: e.tensor_tensor(out=g_xx[:], in0=s_small[:, :, 4:8],
                                                  in1=prow[:, PR_GDTB:PR_GDTB + 4].unsqueeze(1).broadcast_to([128, NTT, 4]), op=ALU.add),
                 r=smk + ["prow"], w=["g_xx"])
            softplus_(g_sp[:], g_xx[:], 128, ["g_xx"], "g_sp", g_t1s[:], g_t2s[:])
            P.op("dve", lambda e: e.tensor_tensor(out=g_g[:], in0=g_sp[:], in1=nA_g[:].unsqueeze(1).broadcast_to([128, NTT, 4]), op=ALU.mult),
                 r=["g_sp", "nA_g"], w=["g_g"])
            if GSTOP <= 2:
                return
            for c in range(NTT):
                csl = slice(c * 128, (c + 1) * 128)
                ps, keys = ps_mix.get(1)

                def fn(e, ps=ps, c=c):
                    e.matmul(ps[:, 0:4], lhsT=ctri, rhs=g_g[:, c, :], start=True, stop=True)
                    return e.matmul(ps[:, 4:8], lhsT=cones, rhs=g_g[:, c, :], start=True, stop=True)
                P.op("pe", fn, r=["consts", "g_g"], w=keys)
                P.op("act", lambda e, ps=ps: e.activation(out=g_pre[:, 0:8], in_=ps[:, 0:8], func=AF.Copy), r=keys, w=["g_pre"])
                P.op("dve", lambda e: e.tensor_tensor(out=g_pre[:, 8:12], in0=g_pre[:, 4:8], in1=g_pre[:, 0:4], op=ALU.subtract), r=["g_pre"], w=["g_pre2"])
                P.op("act", lambda e: e.activation(out=g_ex[:], in_=g_pre[:], func=AF.Exp), r=["g_pre", "g_pre2"], w=["g_ex"])
                P.op("dve", lambda e, c=c: e.tensor_tensor(out=g_bg[:], in0=g_beta[:, c, :], in1=g_ex[:, 0:4], op=ALU.mult), r=["g_beta", "g_ex"], w=["g_bg"])
                for which, dst, key in ((4, g_ktm, "g_ktm"), (8, g_vtm, "g_vtm")):
                    ps, keys = ps_mix.get(4)

                    def fn(e, ps=ps, which=which, csl=csl):
                        for q in range(4):
                            ins = e.transpose(out=ps[:, q * 128:(q + 1) * 128], in_=cv[:, which + q, csl], identity=cid)
                        return ins
                    P.op("pe", fn, r=[("cv", which + q) for q in range(4)] + ["consts"], w=keys)
                    P.op("act", lambda e, ps=ps, dst=dst: e.activation(out=dst[:], in_=ps[:, 0:512].rearrange("p (h d) -> p h d", h=4), func=AF.Copy), r=keys, w=[key])
                P.op("dve", lambda e, c=c: e.tensor_tensor(out=g_R[:, :, 0:128], in0=g_vtm[:], in1=g_beta[:, c, :].unsqueeze(2).broadcast_to([128, 4, 128]), op=ALU.mult),
                     r=["g_vtm", "g_beta"], w=["g_Rv"])
                P.op("pool", lambda e: e.tensor_tensor(out=g_R[:, :, 128:256], in0=g_ktm[:], in1=g_bg[:].unsqueeze(2).broadcast_to([128, 4, 128]), op=ALU.mult),
                     r=["g_ktm", "g_bg"], w=["g_Rk"])
                P.op("pool", lambda e: e.tensor_tensor(out=g_kd[:], in0=g_ktm[:], in1=g_ex[:, 8:12].unsqueeze(2).broadcast_to([128, 4, 128]), op=ALU.mult),
                     r=["g_ktm", "g_ex"], w=["g_kd"])
                if GSTOP <= 3:
                    continue
                gens = [head_gen(c, csl, h) for h in range(4)]
                live = list(gens)
                while live:
                    nxt = []
                    for g_ in live:
                        try:
                            next(g_)
                            nxt.append(g_)
                        except StopIteration:
                            pass
                    live = nxt
                if GSTOP <= 8:
                    continue
                ok = [("g_o", h) for h in range(4)]
                P.op("act", lambda e: e.activation(out=g_o2[:], in_=g_o[:], func=AF.Square), r=ok, w=["g_o2"])
                P.op("dve", lambda e: e.tensor_reduce(out=g_ssq[:], in_=g_o2[:], axis=mybir.AxisListType.X, op=ALU.add), r=["g_o2"], w=["g_ssq"])
                P.op("act", lambda e: e.activation(out=g_rstd[:], in_=g_ssq[:], func=AF.Sqrt, scale=1.0 / 128, bias=epsc[:]), r=["g_ssq", "epsc"], w=["g_rstd0"])
                P.op("dve", lambda e: e.reciprocal(out=g_rstd[:], in_=g_rstd[:]), r=["g_rstd0"], w=["g_rstd"])
                P.op("pool", lambda e: e.tensor_tensor(out=g_o2[:], in0=g_o[:], in1=g_rstd[:].unsqueeze(2).broadcast_to([128, 4, 128]), op=ALU.mult), r=ok + ["g_rstd", "g_o2"], w=["g_o2"])
                P.op("pool", lambda e: e.tensor_tensor(out=g_o2[:], in0=g_o2[:], in1=prow[:, PR_GNW:PR_GNW + 128].unsqueeze(1).broadcast_to([128, 4, 128]), op=ALU.mult),
                     r=["g_o2", "prow"], w=["g_o2"])
                P.op("dve", lambda e, c=c: e.tensor_tensor(out=g_obf[:], in0=g_o2[:].rearrange("p h d -> p (h d)"), in1=g_sz[:, c, :], op=ALU.mult),
                     r=["g_o2", ("g_sz", c, 0), ("g_sz", c, 1)], w=["g_obf"])
                transpose_out_bf(g_obf, 4, 4, c, ["g_obf"])

        if "ml" in parts:
            m_qT = T("m_qT", [128, 2, TB], BF16)
            m_kT = T("m_kT", [128, 2, TB], BF16)
            m_ktm = T("m_ktm", [128, NTT, 256])
            m_v = T("m_v", [128, NTT, 512], BF16)
            m_g = big_tm[:, :, 0:512]
            m_zt = big_tm[:, :, 512:1024]
            RN = ["li", "t", "a", "l1", "lf", "m", "F", "G", "H", "one"]
            m_rows = {nm: T("m_r_" + nm, [1, TB]) for nm in RN}
            m_rows["wi"] = m_rows["a"]
            m_rows["emm"] = m_rows["l1"]
            m_rows["nG"] = m_rows["t"]
            m_carry = T("m_carry", [1, 1])
            m_C = T("m_C", [128, 2, 512])
            m_Cb = T("m_Cb", [128, 2, 512], BF16)
            m_n = T("m_n", [128, 2])
            m_nb = T("m_nb", [128, 2], BF16)
            m_W = T("m_W", [128, 128])
            m_Wm = T("m_Wm", [128, 128])
            m_sc = T("m_sc", [128, 128], BF16)
            m_cols = T("m_cols", [128, 4])
            m_tmp = T("m_tmp", [128, 512])
            m_h = T("m_h", [128, 512])
            m_den = T("m_den", [128, 2])
            m_d1 = T("m_d1", [128, 1])
            m_rden = T("m_rden", [128, 1])
            m_ssq = T("m_ssq", [128, 1])
            m_rstd = T("m_rstd", [128, 1])
            m_comb = T("m_comb", [128, 1])
            m_g2 = T("m_g2", [128, 512])
            m_hbf = T("m_hbf", [128, 512], BF16)
            m_kws = T("m_kws", [128, 256], BF16)
            m_onesb = T("m_onesb", [128, 1], BF16)
            P.op("pool", lambda e: e.memset(m_C[:], 0.0), w=["m_C"])
            P.op("pool", lambda e: e.memset(m_Cb[:], 0.0), w=["m_Cb"])
            P.op("pool", lambda e: e.memset(m_n[:], 0.0), w=["m_n"])
            P.op("pool", lambda e: e.memset(m_nb[:], 0.0), w=["m_nb"])
            P.op("pool", lambda e: e.memset(m_carry[:], 0.0), w=["m_carry"])
            P.op("pool", lambda e: e.memset(m_rows["one"][:], 1.0), w=["m_one"])
            P.op("pool", lambda e: e.memset(m_onesb[:], 1.0), w=["m_onesb"])

        def ml_proj(tb):
            s = load_w(0)
            for j in range(2):
                ps, keys = proj_fm(s, j)
                P.op("act", lambda e, ps=ps, j=j: e.activation(out=m_qT[:, j, :], in_=ps[:, 0:TB], func=AF.Identity, scale=256 ** -0.5), r=keys, w=[("m_qT", j)])
            s = load_w(1)
            for j in range(2):
                ps, keys = proj_fm(s, j)
                P.op("act", lambda e, ps=ps, j=j: e.activation(out=m_kT[:, j, :], in_=ps[:, 0:TB], func=AF.Copy), r=keys, w=[("m_kT", j)])
            for ti, t in enumerate(range(14, 20)):
                s = load_w(t)
                for tt in range(NTT):
                    ps, keys = proj_tm(s, tt)
                    cs = slice((ti % 2) * WT, (ti % 2 + 1) * WT)
                    if ti < 2:
                        P.op("act", lambda e, ps=ps, tt=tt, cs=cs: e.activation(out=m_v[:, tt, cs], in_=ps[:, 0:WT], func=AF.Copy), r=keys, w=[("m_v", tt, ti)])
                    elif ti < 4:
                        P.op("act", lambda e, ps=ps, tt=tt, cs=cs: e.activation(out=m_g[:, tt, cs], in_=ps[:, 0:WT], func=AF.Sigmoid), r=keys, w=[("m_g", tt, ti % 2)])
                    else:
                        P.op("act", lambda e, ps=ps, tt=tt, cs=cs: e.activation(out=m_zt[:, tt, cs], in_=ps[:, 0:WT], func=AF.Silu), r=keys, w=[("m_zt", tt, ti % 2)])
            s = load_w(26)
            for tt in range(NTT):
                ps, keys = proj_tm(s, tt)
                P.op("act", lambda e, ps=ps, tt=tt: e.activation(out=m_ktm[:, tt, :], in_=ps[:, 0:WT], func=AF.Copy), r=keys, w=[("m_ktm", tt)])
            s = load_w(27)
            ps, keys = proj_fm(s, 0, M=1, c0=24)
            P.op("act", lambda e, ps=ps: e.activation(out=m_rows["li"][:], in_=ps[0:1, 0:TB], func=AF.Identity, bias=prow[0:1, PR_MIB:PR_MIB + 1]), r=keys + ["prow"], w=["m_li"])
            ps, keys = proj_fm(s, 0, M=1, c0=25)
            P.op("act", lambda e, ps=ps: e.activation(out=m_rows["t"][:], in_=ps[0:1, 0:TB], func=AF.Identity, bias=prow[0:1, PR_MFB:PR_MFB + 1]), r=keys + ["prow"], w=["m_t"])

        def ml_mixer(tb):
            R_ = m_rows
            for tt in range(NTT):
                P.op("pool", lambda e, tt=tt: e.tensor_tensor(out=m_g[:, tt, :], in0=m_g[:, tt, :], in1=m_zt[:, tt, :], op=ALU.mult),
                     r=[("m_g", tt, 0), ("m_g", tt, 1), ("m_zt", tt, 0), ("m_zt", tt, 1)], w=[("m_g", tt, 0), ("m_g", tt, 1)])
            P.op("act", lambda e: e.activation(out=R_["a"][:], in_=R_["t"][:], func=AF.Abs), r=["m_t"], w=["m_a"])
            P.op("act", lambda e: e.activation(out=R_["a"][:], in_=R_["a"][:], func=AF.Exp, scale=-1.0), r=["m_a"], w=["m_a2"])
            P.op("act", lambda e: e.activation(out=R_["l1"][:], in_=R_["a"][:], func=AF.Ln, bias=1.0), r=["m_a2"], w=["m_l1"])
            P.op("dve", lambda e: e.scalar_tensor_tensor(out=R_["lf"][:], in0=R_["t"][:], scalar=0.0, in1=R_["l1"][:], op0=ALU.min, op1=ALU.subtract), r=["m_t", "m_l1"], w=["m_lf"])
            P.op("dve", lambda e: e.tensor_tensor_scan(out=R_["m"][:], data0=R_["lf"][:], data1=R_["li"][:], initial=m_carry[:], op0=ALU.add, op1=ALU.max),
                 r=["m_lf", "m_li", "m_carry"], w=["m_m"])
            P.op("dve", lambda e: e.tensor_tensor_scan(out=R_["F"][:], data0=R_["one"][:], data1=R_["lf"][:], initial=0.0, op0=ALU.mult, op1=ALU.add),
                 r=["m_lf", "m_one"], w=["m_F"])
            P.op("dve", lambda e: e.tensor_tensor(out=R_["G"][:], in0=R_["F"][:], in1=R_["m"][:], op=ALU.subtract), r=["m_F", "m_m"], w=["m_G"])
            P.op("dve", lambda e: e.tensor_tensor(out=R_["H"][:], in0=R_["li"][:], in1=R_["F"][:], op=ALU.subtract), r=["m_F", "m_li"], w=["m_H"])
            P.op("dve", lambda e: e.tensor_scalar(out=R_["nG"][:], in0=R_["G"][:], scalar1=-1.0, scalar2=None, op0=ALU.mult), r=["m_G"], w=["m_nG"])
            P.op("act", lambda e: e.activation(out=R_["emm"][:], in_=R_["m"][:], func=AF.Exp, scale=-1.0), r=["m_m"], w=["m_emm"])
            one = R_["one"]
            for c in range(NTT):
                csl = slice(c * 128, (c + 1) * 128)
                bias_ap = m_carry[:] if c == 0 else R_["nG"][0:1, c * 128 - 1:c * 128]
                P.op("act", lambda e, csl=csl, bias_ap=bias_ap: e.activation(out=R_["wi"][0:1, csl], in_=R_["G"][0:1, csl], func=AF.Exp, bias=bias_ap),
                     r=["m_G", "m_nG", "m_carry"], w=["m_wi"])
                ps, keys = ps_mix.get(1)

                def fn(e, ps=ps, csl=csl):
                    e.matmul(ps[:, 0:128], lhsT=R_["H"][0:1, csl], rhs=one[0:1, 0:128], start=True, stop=False)
                    return e.matmul(ps[:, 0:128], lhsT=one[0:1, 0:128], rhs=R_["G"][0:1, csl], start=False, stop=True)
                P.op("pe", fn, r=["m_H", "m_G", "m_one"], w=keys)
                P.op("dve", lambda e, ps=ps: e.tensor_scalar(out=m_W[:], in0=ps[:, 0:128], scalar1=0.0, scalar2=None, op0=ALU.min), r=keys, w=["m_W0"])
                P.op("act", lambda e: e.activation(out=m_W[:], in_=m_W[:], func=AF.Exp), r=["m_W0"], w=["m_W"])
                P.op("pool", lambda e: e.tensor_tensor(out=m_Wm[:], in0=m_W[:], in1=ctri, op=ALU.mult), r=["m_W", "consts"], w=["m_Wm"])
                ps, keys = ps_mix.get(1)

                def fn(e, ps=ps, csl=csl, c=c):
                    e.matmul(ps[:, 0:1], lhsT=R_["wi"][0:1, csl], rhs=one[0:1, 0:1], start=True, stop=True)
                    e.matmul(ps[:, 1:2], lhsT=R_["emm"][0:1, csl], rhs=one[0:1, 0:1], start=True, stop=True)
                    return e.matmul(ps[:, 2:3], lhsT=one[0:1, 0:128], rhs=R_["wi"][0:1, c * 128 + 127:c * 128 + 128], start=True, stop=True)
                P.op("pe", fn, r=["m_wi", "m_emm", "m_one"], w=keys)
                P.op("act", lambda e, ps=ps: e.activation(out=m_cols[:, 0:3], in_=ps[:, 0:3], func=AF.Copy), r=keys, w=["m_cols"])
                ps, keys = ps_mix.get(1)

                def fn(e, ps=ps, csl=csl):
                    e.matmul(ps[:, 0:128], lhsT=m_kT[:, 0, csl], rhs=m_qT[:, 0, csl], start=True, stop=False)
                    return e.matmul(ps[:, 0:128], lhsT=m_kT[:, 1, csl], rhs=m_qT[:, 1, csl], start=False, stop=True)
                P.op("pe", fn, r=[("m_kT", 0), ("m_kT", 1), ("m_qT", 0), ("m_qT", 1)], w=keys)
                P.op("dve", lambda e, ps=ps: e.tensor_tensor(out=m_sc[:], in0=ps[:, 0:128], in1=m_Wm[:], op=ALU.mult), r=keys + ["m_Wm"], w=["m_sc"])
                psi, ki = ps_mix.get(4)

                def fn(e, psi=psi, csl=csl):
                    e.matmul(psi[:, 0:512], lhsT=m_qT[:, 0, csl], rhs=m_Cb[:, 0, :], start=True, stop=False)
                    return e.matmul(psi[:, 0:512], lhsT=m_qT[:, 1, csl], rhs=m_Cb[:, 1, :], start=False, stop=True)
                P.op("pe", fn, r=[("m_qT", 0), ("m_qT", 1), "m_Cb"], w=ki)
                P.op("dve", lambda e, psi=psi: e.tensor_scalar(out=m_tmp[:], in0=psi[:, 0:512], scalar1=m_cols[:, 0:1], scalar2=None, op0=ALU.mult), r=ki + ["m_cols"], w=["m_tmp"])
                psa, ka = ps_mix.get(4)
                P.op("pe", lambda e, psa=psa, c=c: e.matmul(psa[:, 0:512], lhsT=m_sc[:], rhs=m_v[:, c, :], start=True, stop=True), r=["m_sc", ("m_v", c, 0), ("m_v", c, 1)], w=ka)
                P.op("dve", lambda e, psa=psa: e.tensor_tensor(out=m_h[:], in0=m_tmp[:], in1=psa[:, 0:512], op=ALU.add), r=ka + ["m_tmp"], w=["m_h"])
                ps, keys = ps_mix.get(1)

                def fn(e, ps=ps, csl=csl):
                    e.matmul(ps[:, 0:1], lhsT=m_sc[:], rhs=m_onesb[:], start=True, stop=True)
                    e.matmul(ps[:, 1:2], lhsT=m_qT[:, 0, csl], rhs=m_nb[:, 0:1], start=True, stop=False)
                    return e.matmul(ps[:, 1:2], lhsT=m_qT[:, 1, csl], rhs=m_nb[:, 1:2], start=False, stop=True)
                P.op("pe", fn, r=["m_sc", "m_onesb", ("m_qT", 0), ("m_qT", 1), "m_nb"], w=keys)
                P.op("act", lambda e, ps=ps: e.activation(out=m_den[:], in_=ps[:, 0:2], func=AF.Copy), r=keys, w=["m_den"])
                P.op("dve", lambda e: e.scalar_tensor_tensor(out=m_d1[:], in0=m_den[:, 1:2], scalar=m_cols[:, 0:1], in1=m_den[:, 0:1], op0=ALU.mult, op1=ALU.add),
                     r=["m_den", "m_cols"], w=["m_d1"])
                P.op("act", lambda e: e.activation(out=m_d1[:], in_=m_d1[:], func=AF.Abs), r=["m_d1"], w=["m_d1b"])
                P.op("dve", lambda e: e.tensor_tensor(out=m_rden[:], in0=m_d1[:], in1=m_cols[:, 1:2], op=ALU.max), r=["m_d1b", "m_cols"], w=["m_rden0"])
                P.op("dve", lambda e: e.reciprocal(out=m_rden[:], in_=m_rden[:]), r=["m_rden0"], w=["m_rden"])
                P.op("dve", lambda e: e.tensor_scalar(out=m_h[:], in0=m_h[:], scalar1=m_rden[:], scalar2=None, op0=ALU.mult), r=["m_h", "m_rden"], w=["m_h"])
                P.op("act", lambda e: e.activation(out=junk[:], in_=m_h[:], func=AF.Square, accum_out=m_ssq[:]), r=["m_h"], w=["junk", "m_ssq"])
                P.op("act", lambda e: e.activation(out=m_rstd[:], in_=m_ssq[:], func=AF.Sqrt, scale=1.0 / 512, bias=epsc[:]), r=["m_ssq", "epsc"], w=["m_rstd0"])
                P.op("dve", lambda e: e.reciprocal(out=m_rstd[:], in_=m_rstd[:]), r=["m_rstd0"], w=["m_rstd"])
                P.op("dve", lambda e: e.tensor_copy(out=m_comb[:], in_=m_rstd[:]), r=["m_rstd"], w=["m_comb"])
                P.op("pool", lambda e, c=c: e.tensor_tensor(out=m_g2[:], in0=m_g[:, c, :], in1=prow[:, PR_MLNW:PR_MLNW + 512], op=ALU.mult), r=[("m_g", c, 0), ("m_g", c, 1), "prow"], w=["m_g2"])
                P.op("dve", lambda e: e.scalar_tensor_tensor(out=m_hbf[:], in0=m_h[:], scalar=m_comb[:], in1=m_g2[:], op0=ALU.mult, op1=ALU.mult),
                     r=["m_h", "m_comb", "m_g2"], w=["m_hbf"])
                transpose_out_bf(m_hbf, 4, 0, c, ["m_hbf"])
                P.op("dve", lambda e, c=c: e.tensor_scalar(out=m_kws[:], in0=m_ktm[:, c, :], scalar1=m_W[:, 127:128], scalar2=None, op0=ALU.mult), r=[("m_ktm", c), "m_W"], w=["m_kws"])
                for kc in range(2):
                    ps, keys = ps_mix.get(4)
                    P.op("pe", lambda e, ps=ps, kc=kc, c=c: e.matmul(ps[:, 0:512], lhsT=m_kws[:, kc * 128:(kc + 1) * 128], rhs=m_v[:, c, :], start=True, stop=True),
                         r=["m_kws", ("m_v", c, 0), ("m_v", c, 1)], w=keys)
                    P.op("dve", lambda e, ps=ps, kc=kc: e.scalar_tensor_tensor(out=m_C[:, kc, :], in0=m_C[:, kc, :], scalar=m_cols[:, 2:3], in1=ps[:, 0:512], op0=ALU.mult, op1=ALU.add),
                         r=keys + ["m_C", "m_cols", "m_Cb"], w=["m_C"])
                ps, keys = ps_mix.get(1)

                def fn(e, ps=ps):
                    e.matmul(ps[:, 0:1], lhsT=m_kws[:, 0:128], rhs=m_onesb[:], start=True, stop=True)
                    return e.matmul(ps[:, 1:2], lhsT=m_kws[:, 128:256], rhs=m_onesb[:], start=True, stop=True)
                P.op("pe", fn, r=["m_kws", "m_onesb"], w=keys)
                P.op("dve", lambda e, ps=ps: e.scalar_tensor_tensor(out=m_n[:], in0=m_n[:], scalar=m_cols[:, 2:3], in1=ps[:, 0:2], op0=ALU.mult, op1=ALU.add),
                     r=keys + ["m_n", "m_cols"], w=["m_n"])
                P.op("pool", lambda e: e.tensor_copy(out=m_Cb[:], in_=m_C[:]), r=["m_C"], w=["m_Cb"])
                P.op("pool", lambda e: e.tensor_copy(out=m_nb[:], in_=m_n[:]), r=["m_n"], w=["m_nb"])
            P.op("dve", lambda e: e.tensor_copy(out=m_carry[:], in_=R_["m"][0:1, TB - 1:TB]), r=["m_m", "m_wi"], w=["m_carry"])

        if "ssm" not in parts or "gdn" not in parts or "ml" not in parts:
            P.op("pool", lambda e: e.memset(ymixT[:], 0.0), w=[("ymixT", b) for b in range(16)])
        for tb in range(NB):
            load_xT(tb)
            if "ssm" in parts:
                ssm_proj(tb)
            if "ssm" in parts or "gdn" in parts:
                small_proj(tb)
            if "ssm" in parts:
                ssm_mixer(tb)
            if "gdn" in parts:
                gdn_proj(tb)
                gdn_mixer(tb)
            if "ml" in parts:
                ml_proj(tb)
                ml_mixer(tb)
            ymk = [("ymixT", b) for b in (0, 4, 8, 12)] + [("ymixT", b) for b in range(16)]
            if debug:
                P.op("pool", lambda e, tb=tb: e.dma_start(out=ymix_d[:, :, tb * TB:(tb + 1) * TB].rearrange("b p t -> p b t"), in_=ymixT[:]),
                     r=ymk, w=[("ymixd", tb)], dma="ymixd")
            n = 0
            for t in range(NOC):
                s = load_wo(t)
                for tt in range(NTT):
                    ps, keys = ps_proj.get(2)

                    def fn(e, ps=ps, s=s, tt=tt):
                        for kc in range(16):
                            ins = e.matmul(ps[:, 0:WT], lhsT=ymixT[:, kc, tt * 128:(tt + 1) * 128], rhs=wt[s][:, kc, :], start=(kc == 0), stop=(kc == 15))
                        return ins
                    P.op("pe", fn, r=ymk + [("wt", s)], w=keys)
                    ss = n % 2
                    n += 1
                    P.op("act", lambda e, ps=ps, ss=ss: e.activation(out=stage[ss][:], in_=ps[:, 0:WT], func=AF.Copy), r=keys, w=[("stage", ss)])
                    r0 = tb * TB + tt * 128
                    P.op("pool", lambda e, ss=ss, r0=r0, t=t: e.dma_start(out=ypart_d[r0:r0 + 128, t * WT:(t + 1) * WT], in_=stage[ss][:]),
                         r=[("stage", ss)], w=[("ypart", r0, t)], dma=("stage", ss))
        P.emit()
        build_layer.last_prog = P
    return nc


ML_OFF, GDN_OFF, SSM_OFF = 0, 8200, 16424


def core_columns(g):
    r = np.arange
    ml = ML_OFF
    gd = GDN_OFF
    ss = SSM_OFF
    cols = [
        ml + g * 256 + r(256),
        ml + 1024 + g * 256 + r(256),
        gd + g * 512 + r(512),
        gd + 2048 + g * 512 + r(512),
        gd + 4096 + g * 512 + r(512),
        ss + 4096 + g * 1024 + r(1024),
        ss + 8192 + g * 256 + r(256),
        ss + 9216 + g * 256 + r(256),
        ml + 2048 + g * 512 + r(512),
        ml + 4096 + g * 512 + r(512),
        ml + 6144 + g * 512 + r(512),
        gd + 6144 + g * 512 + r(512),
        ss + g * 1024 + r(1024),
        ml + 1024 + g * 256 + r(256),
        gd + 8192 + g * 4 + r(4),
        gd + 8208 + g * 4 + r(4),
        ss + 10240 + g * 16 + r(16),
        np.array([ml + 8192 + g]),
        np.array([ml + 8196 + g]),
    ]
    return np.concatenate(cols)


def pack_core(inp, l, g):
    cols = core_columns(g)
    D = inp["w_in"].shape[1]
    wcat = np.zeros((D, NCOL), np.float32)
    wcat[:, :cols.size] = inp["w_in"][l][:, cols]
    mixrows = np.concatenate([g * 512 + np.arange(512), 2048 + g * 512 + np.arange(512), 4096 + g * 1024 + np.arange(1024)])
    wout = np.ascontiguousarray(inp["w_out"][l][mixrows, :])
    gch = np.concatenate([g * 512 + np.arange(512), 2048 + g * 512 + np.arange(512), 4096 + g * 512 + np.arange(512)])
    sch = np.concatenate([g * 1024 + np.arange(1024), 4096 + g * 256 + np.arange(256), 5120 + g * 256 + np.arange(256)])
    cp = np.zeros((24 * 128, 5), np.float32)
    cp[:1536, 0:4] = inp["gdn_conv_w"][l][:, gch].T
    cp[1536:, 0:4] = inp["ssm_conv_w"][l][:, sch].T
    cp[1536:, 4] = inp["ssm_conv_b"][l][sch]
    convp = np.ascontiguousarray(cp.reshape(24, 128, 5).transpose(1, 0, 2).reshape(128, 120))
    pr = np.zeros((1, PR_N), np.float32)
    pr[0, PR_SDTB:PR_SDTB + 16] = inp["ssm_dt_bias"][l][16 * g:16 * g + 16]
    pr[0, PR_SALOG:PR_SALOG + 16] = inp["ssm_A_log"][l][16 * g:16 * g + 16]
    pr[0, PR_SD:PR_SD + 16] = inp["ssm_D"][l][16 * g:16 * g + 16]
    pr[0, PR_GDTB:PR_GDTB + 4] = inp["gdn_dt_bias"][l][4 * g:4 * g + 4]
    pr[0, PR_GALOG:PR_GALOG + 4] = inp["gdn_A_log"][l][4 * g:4 * g + 4]
    pr[0, PR_MIB] = inp["ml_i_bias"][l][g]
    pr[0, PR_MFB] = inp["ml_f_bias"][l][g]
    pr[0, PR_MLNW:PR_MLNW + 512] = inp["ml_norm_w"][l][g * 512:(g + 1) * 512]
    pr[0, PR_GNW:PR_GNW + 128] = inp["gdn_norm_w"][l]
    pr[0, PR_SNW:PR_SNW + 1024] = inp["ssm_norm_w"][l][g * 1024:(g + 1) * 1024]
    return {"wcat": wcat, "wout": wout, "convp": convp, "prow": pr, "consts": make_consts()}


def build_reduce_ln(ROWS=1024, D=D_MODEL):
    nc = bass.Bass("TRN2", target_bir_lowering=False)
    x_d = nc.dram_tensor("xin", [ROWS, D], F32, kind="ExternalInput").ap()
    p_d = [nc.dram_tensor(f"p{j}", [ROWS, D], F32, kind="ExternalInput").ap() for j in range(4)]
    g_d = nc.dram_tensor("lng", [1, D], F32, kind="ExternalInput").ap()
    b_d = nc.dram_tensor("lnb", [1, D], F32, kind="ExternalInput").ap()
    o_d = nc.dram_tensor("out", [ROWS, D], F32, kind="ExternalOutput").ap()
    CW = 1024 if D >= 1024 else D
    NCH = D // CW
    with ExitStack() as st:
        def T(name, shape, dt=F32):
            return st.enter_context(nc.sbuf_tensor("sb_" + name, shape, dt))
        P = Prog(nc, st)
        gb = T("gb", [128, D])
        bb = T("bb", [128, D])
        acc = [T(f"acc{i}", [128, D]) for i in range(2)]
        ld = [[T(f"ld{i}_{j}", [128, CW]) for j in range(5)] for i in range(2)]
        junk = T("junk", [128, D])
        stt = T("stt", [128, 8])
        epsl = T("epsl", [128, 1])
        P.op("sp", lambda e: e.dma_start(out=gb[:], in_=g_d.partition_broadcast(128)[:, 0, :]), w=["gb"], dma="gb")
        P.op("sp", lambda e: e.dma_start(out=bb[:], in_=b_d.partition_broadcast(128)[:, 0, :]), w=["bb"], dma="bb")
        P.op("pool", lambda e: e.memset(epsl[:], LN_EPS), w=["epsl"])
        n = 0
        for ti in range(ROWS // 128):
            r0 = ti * 128
            a = acc[ti % 2]
            ak = ("acc", ti % 2)
            for ch in range(NCH):
                s = n % 2
                n += 1
                cs = slice(ch * CW, (ch + 1) * CW)
                srcs = [x_d] + p_d
                for j in range(5):
                    q = "sp" if j % 2 == 0 else "act"
                    P.op(q, lambda e, s=s, j=j, cs=cs, r0=r0, srcs=srcs: e.dma_start(out=ld[s][j][:], in_=srcs[j][r0:r0 + 128, cs]),
                         w=[("ld", s, j)], dma=("ld", s, j))
                P.op("dve", lambda e, s=s, cs=cs, a=a: e.scalar_tensor_tensor(out=a[:, cs], in0=ld[s][0][:], scalar=float(ALPHA), in1=ld[s][1][:], op0=ALU.mult, op1=ALU.add),
                     r=[("ld", s, 0), ("ld", s, 1)], w=[ak])
                P.op("pool", lambda e, s=s: e.tensor_tensor(out=ld[s][2][:], in0=ld[s][2][:], in1=ld[s][3][:], op=ALU.add), r=[("ld", s, 2), ("ld", s, 3)], w=[("ld", s, 2)])
                P.op("pool", lambda e, s=s: e.tensor_tensor(out=ld[s][2][:], in0=ld[s][2][:], in1=ld[s][4][:], op=ALU.add), r=[("ld", s, 2), ("ld", s, 4)], w=[("ld", s, 2)])
                P.op("dve", lambda e, s=s, cs=cs, a=a: e.tensor_tensor(out=a[:, cs], in0=a[:, cs], in1=ld[s][2][:], op=ALU.add), r=[ak, ("ld", s, 2)], w=[ak])
            P.op("act", lambda e, a=a: e.activation(out=junk[:], in_=a[:], func=AF.Identity, accum_out=stt[:, 0:1]), r=[ak], w=["junk", "st0"])
            P.op("dve", lambda e: e.tensor_scalar(out=stt[:, 1:2], in0=stt[:, 0:1], scalar1=-1.0 / D, scalar2=None, op0=ALU.mult), r=["st0"], w=["st1"])
            P.op("dve", lambda e, a=a: e.tensor_scalar(out=a[:], in0=a[:], scalar1=stt[:, 1:2], scalar2=None, op0=ALU.add), r=[ak, "st1"], w=[ak])
            P.op("act", lambda e, a=a: e.activation(out=junk[:], in_=a[:], func=AF.Square, accum_out=stt[:, 2:3]), r=[ak], w=["junk", "st2"])
            P.op("act", lambda e: e.activation(out=stt[:, 3:4], in_=stt[:, 2:3], func=AF.Sqrt, scale=1.0 / D, bias=epsl[:]), r=["st2", "epsl"], w=["st3"])
            P.op("dve", lambda e: e.reciprocal(out=stt[:, 4:5], in_=stt[:, 3:4]), r=["st3"], w=["st4"])
            P.op("dve", lambda e, a=a: e.scalar_tensor_tensor(out=a[:], in0=a[:], scalar=stt[:, 4:5], in1=gb[:], op0=ALU.mult, op1=ALU.mult), r=[ak, "st4", "gb"], w=[ak])
            P.op("pool", lambda e, a=a: e.tensor_tensor(out=a[:], in0=a[:], in1=bb[:], op=ALU.add), r=[ak, "bb"], w=[ak])
            P.op("pool", lambda e, a=a, r0=r0: e.dma_start(out=o_d[r0:r0 + 128, :], in_=a[:]), r=[ak], w=[("out", ti)], dma=ak)
        P.emit()
    return nc


_PROGS = {}


def kernel(x, w_in, w_out, ml_i_bias, ml_f_bias, ml_norm_w, gdn_conv_w, gdn_A_log, gdn_dt_bias,
           gdn_norm_w, ssm_conv_w, ssm_conv_b, ssm_A_log, ssm_dt_bias, ssm_D, ssm_norm_w, ln_g, ln_b):
    inp = dict(w_in=np.asarray(w_in), w_out=np.asarray(w_out), ml_i_bias=np.asarray(ml_i_bias), ml_f_bias=np.asarray(ml_f_bias),
               ml_norm_w=np.asarray(ml_norm_w), gdn_conv_w=np.asarray(gdn_conv_w), gdn_A_log=np.asarray(gdn_A_log),
               gdn_dt_bias=np.asarray(gdn_dt_bias), gdn_norm_w=np.asarray(gdn_norm_w), ssm_conv_w=np.asarray(ssm_conv_w),
               ssm_conv_b=np.asarray(ssm_conv_b), ssm_A_log=np.asarray(ssm_A_log), ssm_dt_bias=np.asarray(ssm_dt_bias),
               ssm_D=np.asarray(ssm_D), ssm_norm_w=np.asarray(ssm_norm_w))
    ln_g = np.asarray(ln_g, np.float32)
    ln_b = np.asarray(ln_b, np.float32)
    xcur = np.ascontiguousarray(np.asarray(x, np.float32))
    B, S, D = xcur.shape
    if "layer" not in _PROGS:
        _PROGS["layer"] = build_layer(D=D, S=S)
        _PROGS["ln"] = build_reduce_ln(ROWS=B * S // 8, D=D)
    RPC = B * S // 8
    for l in range(DEPTH):
        packs = [pack_core(inp, l, g) for g in range(NG)]
        in_maps = [dict(packs[c % NG], x=xcur[c // NG]) for c in range(8)]
        res = run_bass_kernel_spmd(_PROGS["layer"], in_maps, core_ids=list(range(8)))
        yp = [r["ypart"] for r in res.results]
        del in_maps, packs
        xf = xcur.reshape(B * S, D)
        in_maps = []
        for c in range(8):
            b = (c * RPC) // S
            r0 = (c * RPC) % S
            m = {"xin": np.ascontiguousarray(xf[c * RPC:(c + 1) * RPC]), "lng": ln_g[l][None, :], "lnb": ln_b[l][None, :]}
            for j in range(NG):
                m[f"p{j}"] = np.ascontiguousarray(yp[b * NG + j][r0:r0 + RPC])
            in_maps.append(m)
        res = run_bass_kernel_spmd(_PROGS["ln"], in_maps, core_ids=list(range(8)))
        xcur = np.concatenate([r["out"] for r in res.results], axis=0).reshape(B, S, D)
    return xcur.astype(np.float32)
```

```python
from contextlib import ExitStack
import os
GSTOP = int(os.environ.get('GSTOP', '99'))
MERGE = os.environ.get('MERGE', '123')
import numpy as np
import concourse.bass as bass
import concourse.mybir as mybir
from concourse.bass_utils import run_bass_kernel_spmd

F32 = mybir.dt.float32
BF16 = mybir.dt.bfloat16
ALU = mybir.AluOpType
AF = mybir.ActivationFunctionType

SIG_ROT = 30000


class Prog:
    ENG = ("pe", "act", "dve", "pool", "sp")

    def __init__(self, nc, stack):
        self._stack = stack
        self.nc = nc
        self.ops = []
        self.lw = {}
        self.rs = {}
        self.dma_cnt = {}

    cap = None

    def capture(self, f, *a):
        assert self.cap is None
        self.cap = []
        f(*a)
        L, self.cap = self.cap, None
        return L

    def replay(self, L):
        for a in L:
            if a is not None:
                self.op(*a)

    def mark(self):
        if self.cap is not None:
            self.cap.append(None)

    def merge(self, A, B):
        A = [a for a in A if a is not None]
        B = [b for b in B if b is not None]
        ia = ib = 0
        while ia < len(A) or ib < len(B):
            if ib >= len(B) or (ia < len(A) and ia * len(B) <= ib * len(A)):
                self.op(*A[ia])
                ia += 1
            else:
                self.op(*B[ib])
                ib += 1

    def op(self, eng, fn, r=(), w=(), dma=None, inc=16):
        if self.cap is not None:
            self.cap.append((eng, fn, list(r), list(w), dma, inc))
            return
        r = [self.canon(k) for k in r]
        w = [self.canon(k) for k in w]
        i = len(self.ops)
        deps = set()
        raw = set()
        for k in r:
            j = self.lw.get(k)
            if j is not None:
                deps.add(j)
                raw.add(j)
        for k in w:
            j = self.lw.get(k)
            if j is not None:
                deps.add(j)
            for j in self.rs.get(k, ()):
                deps.add(j)
        keep = set()
        for j in deps:
            pj = self.ops[j]
            if pj["dma"] is None and dma is None and pj["eng"] == eng:
                if eng == "pe":
                    continue
            keep.add(j)
        o = dict(eng=eng, fn=fn, deps=keep, dma=dma, sig=False, sidx=None, inc=inc)
        if dma is not None:
            c = self.dma_cnt.get(dma, 0) + 1
            self.dma_cnt[dma] = c
            o["sidx"] = c
            o["sig"] = True
        self.ops.append(o)
        for j in keep:
            self.ops[j]["sig"] = True
        for k in r:
            self.rs.setdefault(k, []).append(i)
        for k in w:
            self.lw[k] = i
            self.rs[k] = []
        return i

    ALIAS = {"g_rn0": "g_rn", "s_rstd0": "s_rstd", "g_rstd0": "g_rstd", "m_rstd0": "m_rstd", "m_rden0": "m_rden",
             "m_W0": "m_W", "m_d1b": "m_d1", "m_a2": "m_a", "s_dt_b": "s_dt_a", "g_sp_b": "g_sp_a", "sUh": "Uh", "gUh": "Uh",
             "m_wi": "m_a", "m_emm": "m_l1", "m_nG": "m_t", "g_ktm": ("s_xtm", 0), "g_vtm": ("s_xtm", 1),
             "g_Rv": "s_xdt", "g_Rk": "s_xdt"}

    def canon(self, k):
        if isinstance(k, tuple):
            if k[0] == "cvp":
                return ("cv",) + k[1:]
            if k[0] in ("sU", "gU"):
                return ("U",) + k[1:]
            if k[0] in ("g_sz", "m_g"):
                return ("s_sz",) + k[1:]
            if k[0] == "m_zt":
                return ("s_sz", k[1], 2 + k[2])
            return k
        return self.ALIAS.get(k, k)

    def emit(self, final_wait_eng="sp"):
        nc = self.nc
        cnt = {e: 0 for e in self.ENG}
        for o in self.ops:
            if o["dma"] is None and o["sig"]:
                cnt[o["eng"]] += 1
                o["sidx"] = cnt[o["eng"]]
        sems = {}
        stack = self._stack
        for e in self.ENG:
            n = (cnt[e] + SIG_ROT - 1) // SIG_ROT
            for q in range(max(n, 1)):
                sems[("E", e, q)] = stack.enter_context(nc.semaphore(f"s_{e}_{q}"))
        for k in self.dma_cnt:
            nm = "d_" + "_".join(str(x) for x in (k if isinstance(k, tuple) else (k,)))
            sems[("D", k)] = stack.enter_context(nc.semaphore(nm))
        self.nsem = len(sems)

        def chan(o):
            if o["dma"] is not None:
                return ("D", o["dma"]), o["inc"] * o["sidx"]
            q, rr = divmod(o["sidx"] - 1, SIG_ROT)
            return ("E", o["eng"], q), rr + 1

        per = {e: [] for e in self.ENG}
        for o in self.ops:
            per[o["eng"]].append(o)
        final = {}
        for o in self.ops:
            if o["dma"] is not None:
                ch, v = chan(o)
                final[ch] = max(final.get(ch, 0), v)
        block = stack.enter_context(nc.Block())
        deco = {"pe": block.tensor, "act": block.scalar, "dve": block.vector,
                "pool": block.gpsimd, "sp": block.sync}
        ops = self.ops

        def make(e):
            def body(engobj):
                seen = {}
                for o in per[e]:
                    need = {}
                    for j in o["deps"]:
                        ch, v = chan(ops[j])
                        if v > need.get(ch, 0):
                            need[ch] = v
                    for ch, v in need.items():
                        if seen.get(ch, 0) >= v:
                            continue
                        engobj.wait_ge(sems[ch], v)
                        seen[ch] = v
                    ins = o["fn"](engobj)
                    if o["sig"]:
                        ch, v = chan(o)
                        ins.then_inc(sems[ch], o["inc"] if o["dma"] is not None else 1)
                if e == final_wait_eng:
                    for ch, v in final.items():
                        if seen.get(ch, 0) < v:
                            engobj.wait_ge(sems[ch], v)
            return body

        for e in self.ENG:
            if per[e] or e == final_wait_eng:
                deco[e](make(e))


D_MODEL = 4096
SEQ = 4096
BATCH = 2
DEPTH = 2
NG = 4
WT = 256
NT = 28
NCOL = NT * WT
MIXW = 2048
RMS_EPS = 1e-6
LN_EPS = 1e-5
ALPHA = (2 * DEPTH) ** 0.25

PR_SDTB, PR_SALOG, PR_SD, PR_GDTB, PR_GALOG, PR_MIB, PR_MFB = 0, 16, 32, 48, 52, 56, 57
PR_MLNW, PR_GNW, PR_SNW = 64, 64 + 512, 64 + 512 + 128
PR_N = 64 + 512 + 128 + 1024
C_ID, C_TRI, C_U, C_ONES, C_BDS, C_BD, C_OFFT, C_NEG = range(8)
NCONST = 8


def make_consts():
    i = np.arange(128)
    k, l = i[:, None], i[None, :]
    c = np.zeros((NCONST, 128, 128), np.float32)
    c[C_ID] = np.eye(128)
    c[C_TRI] = (k <= l)
    c[C_U] = (k > l)
    c[C_ONES] = 1.0
    same = (k // 64) == (l // 64)
    c[C_BDS] = (k > l) & same
    c[C_BD] = same
    c[C_OFFT] = (k < 64) & (l >= 64)
    c[C_NEG] = np.where(k > l, -30000.0, 0.0)
    return np.ascontiguousarray(c.transpose(1, 0, 2).reshape(128, NCONST * 128))


def build_layer(D=D_MODEL, S=SEQ, TB=256, debug=False, parts=("ssm", "gdn", "ml")):
    KC = D // 128
    NB = S // TB
    NTT = TB // 128
    nc = bass.Bass("TRN2", target_bir_lowering=False)
    x_d = nc.dram_tensor("x", [S, D], F32, kind="ExternalInput").ap()
    wcat_d = nc.dram_tensor("wcat", [D, NCOL], F32, kind="ExternalInput").ap()
    wout_d = nc.dram_tensor("wout", [MIXW, D], F32, kind="ExternalInput").ap()
    convp_d = nc.dram_tensor("convp", [128, 24 * 5], F32, kind="ExternalInput").ap()
    prow_d = nc.dram_tensor("prow", [1, PR_N], F32, kind="ExternalInput").ap()
    const_d = nc.dram_tensor("consts", [128, NCONST * 128], F32, kind="ExternalInput").ap()
    ypart_d = nc.dram_tensor("ypart", [S, D], F32, kind="ExternalOutput").ap()
    wscr_d = nc.dram_tensor("wscr", [NT, 128, KC, WT], BF16, kind="Internal").ap()
    NOC = D // WT
    woscr_d = nc.dram_tensor("woscr", [NOC, 128, 16, WT], BF16, kind="Internal").ap()
    if debug:
        ymix_d = nc.dram_tensor("ymix", [16, 128, S], BF16, kind="ExternalOutput").ap()

    with ExitStack() as st:
        def T(name, shape, dt=F32):
            return st.enter_context(nc.sbuf_tensor("sb_" + name, shape, dt))
        P = Prog(nc, st)
        psb = [st.enter_context(nc.psum_tensor(f"psb{i}", [128, 512], F32)) for i in range(8)]

        class PSA:
            def __init__(self, banks):
                self.banks = banks
                self.pos = 0

            def get(self, nq):
                bank = self.banks[self.pos % len(self.banks)]
                self.pos += 1
                return psb[bank][:, 0:nq * 128], [("ps", bank)]
        ps_proj = PSA([0, 1])
        ps_mix = PSA([2, 3, 4, 5, 6, 7])

        consts = T("consts", [128, NCONST, 128])
        cid = consts[:, C_ID, :]
        ctri = consts[:, C_TRI, :]
        cU = consts[:, C_U, :]
        cones = consts[:, C_ONES, :]
        idb = T("idb", [128, 128], BF16)
        prow = T("prow", [128, PR_N])
        convp = T("convp", [128, 24, 5])
        nA_s = T("nA_s", [128, 16])
        nA_g = T("nA_g", [128, 4])
        XLW = 512 if D >= 512 else D
        xld = [T(f"xld{i}", [128, XLW]) for i in range(2)]
        xT = T("xT", [128, KC, TB], BF16)
        wt = [T(f"wt{i}", [128, max(KC, 16), WT], BF16) for i in range(2)]
        CK = 2 if KC >= 2 else 1
        assert CK == 2 and WT == 256
        HS = T("HS", [128, 9216])
        bar = T("bar", [128, 1])
        cin = [HS[:, i * 512:(i + 1) * 512].rearrange("p (k c) -> p k c", k=CK) for i in range(2)]
        cout = [HS[:, 1024 + i * 256:1024 + (i + 1) * 256].bitcast(BF16).rearrange("p (k c) -> p k c", k=CK) for i in range(2)]
        HS_KEYS = [("cin", 0), ("cin", 1), ("cout", 0), ("cout", 1), "s_gu", ("s_mte", 0), ("s_mte", 1), "s_xw"]
        for h_ in range(4):
            HS_KEYS += [(k_, h_) for k_ in ("hA", "hB", "hC", "hD", "hA0a", "hA0b", "hAToff", "hattn", "huw")]
            HS_KEYS += [(k_, h_, i_) for k_ in ("hPP", "hTT") for i_ in range(2)]

        def barrier():
            P.op("pool", lambda e: e.memset(bar[:], 0.0), w=HS_KEYS)
        U = T("U", [128, 12, 3 + TB])
        hist_s = T("hist_s", [128, 12, 3])
        hist_g = T("hist_g", [128, 12, 3])
        cv = T("cv", [128, 12, TB])
        ymixT = T("ymixT", [128, 16, TB], BF16)
        stage = [T(f"stage{i}", [128, WT]) for i in range(2)]
        junk = T("junk", [128, 512])
        epsc = T("epsc", [128, 1])

        P.op("sp", lambda e: e.dma_start(out=consts[:], in_=const_d.rearrange("p (c k) -> p c k", c=NCONST)), w=["consts"], dma="consts")
        P.op("sp", lambda e: e.dma_start(out=prow[:], in_=prow_d.partition_broadcast(128)[:, 0, :]), w=["prow"], dma="prow")
        P.op("sp", lambda e: e.dma_start(out=convp[:], in_=convp_d.rearrange("p (c k) -> p c k", c=24)), w=["convp"], dma="convp")
        P.op("dve", lambda e: e.tensor_copy(out=idb[:], in_=cid), r=["consts"], w=["idb"])
        P.op("pool", lambda e: e.memset(epsc[:], RMS_EPS), w=["epsc"])
        P.op("act", lambda e: e.activation(out=nA_s[:], in_=prow[:, PR_SALOG:PR_SALOG + 16], func=AF.Exp), r=["prow"], w=["nA_s"])
        P.op("dve", lambda e: e.tensor_scalar(out=nA_s[:], in0=nA_s[:], scalar1=-1.0, scalar2=None, op0=ALU.mult), r=["nA_s"], w=["nA_s"])
        P.op("act", lambda e: e.activation(out=nA_g[:], in_=prow[:, PR_GALOG:PR_GALOG + 4], func=AF.Exp), r=["prow"], w=["nA_g"])
        P.op("dve", lambda e: e.tensor_scalar(out=nA_g[:], in0=nA_g[:], scalar1=-1.0, scalar2=None, op0=ALU.mult), r=["nA_g"], w=["nA_g"])

        wv = wcat_d.rearrange("(kc p) c -> p kc c", p=128)
        n = 0
        cast_engs = ["dve", "pool", "act"]
        for t in range(NT):
            for k0 in range(0, KC, CK):
                s = n % 2
                P.op("sp", lambda e, s=s, t=t, k0=k0: e.dma_start(out=cin[s][:], in_=wv[:, k0:k0 + CK, t * WT:(t + 1) * WT]),
                     w=[("cin", s)], dma=("cin", s))
                ce = cast_engs[n % 3]
                if ce == "act":
                    P.op("act", lambda e, s=s: e.activation(out=cout[s][:], in_=cin[s][:], func=AF.Copy), r=[("cin", s)], w=[("cout", s)])
                else:
                    P.op(ce, lambda e, s=s: e.tensor_copy(out=cout[s][:], in_=cin[s][:]), r=[("cin", s)], w=[("cout", s)])
                P.op("pool", lambda e, s=s, t=t, k0=k0: e.dma_start(out=wscr_d[t][:, k0:k0 + CK, :], in_=cout[s][:]),
                     r=[("cout", s)], w=[("wscr", t)], dma=("cout", s))
                n += 1
        wov = wout_d.rearrange("(kc p) c -> p kc c", p=128)
        for t in range(NOC):
            for k0 in range(0, 16, CK):
                s = n % 2
                P.op("sp", lambda e, s=s, t=t, k0=k0: e.dma_start(out=cin[s][:], in_=wov[:, k0:k0 + CK, t * WT:(t + 1) * WT]),
                     w=[("cin", s)], dma=("cin", s))
                ce = cast_engs[n % 3]
                if ce == "act":
                    P.op("act", lambda e, s=s: e.activation(out=cout[s][:], in_=cin[s][:], func=AF.Copy), r=[("cin", s)], w=[("cout", s)])
                else:
                    P.op(ce, lambda e, s=s: e.tensor_copy(out=cout[s][:], in_=cin[s][:]), r=[("cin", s)], w=[("cout", s)])
                P.op("pool", lambda e, s=s, t=t, k0=k0: e.dma_start(out=woscr_d[t][:, k0:k0 + CK, :], in_=cout[s][:]),
                     r=[("cout", s)], w=[("woscr", t)], dma=("cout", s))
                n += 1

        barrier()
        wslot = [0]

        def load_w(t):
            s = wslot[0] % 2
            wslot[0] += 1
            P.op("sp", lambda e: e.dma_start(out=wt[s][:, 0:KC, :], in_=wscr_d[t]), r=[("wscr", t)], w=[("wt", s)], dma=("wt", s))
            return s

        def load_wo(t):
            s = wslot[0] % 2
            wslot[0] += 1
            P.op("sp", lambda e: e.dma_start(out=wt[s][:, 0:16, :], in_=woscr_d[t]), r=[("woscr", t)], w=[("wt", s)], dma=("wt", s))
            return s

        def proj_fm(s, j, M=128, c0=None):
            ps, keys = ps_proj.get(2)
            cc = j * 128 if c0 is None else c0

            def fn(e):
                for kc in range(KC):
                    ins = e.matmul(ps[0:M, 0:TB], lhsT=wt[s][:, kc, cc:cc + M], rhs=xT[:, kc, :], start=(kc == 0), stop=(kc == KC - 1))
                return ins
            P.op("pe", fn, r=[("wt", s), "xT"], w=keys)
            return ps, keys

        def proj_tm(s, tt):
            ps, keys = ps_proj.get(2)

            def fn(e):
                for kc in range(KC):
                    ins = e.matmul(ps[:, 0:WT], lhsT=xT[:, kc, tt * 128:(tt + 1) * 128], rhs=wt[s][:, kc, :], start=(kc == 0), stop=(kc == KC - 1))
                return ins
            P.op("pe", fn, r=[("wt", s), "xT"], w=keys)
            return ps, keys

        def load_xT(tb):
            n = 0
            for tt in range(NTT):
                r0 = tb * TB + tt * 128
                for c0 in range(0, D, XLW):
                    s = n % 2
                    n += 1
                    P.op("sp", lambda e, s=s, r0=r0, c0=c0: e.dma_start(out=xld[s][:], in_=x_d[r0:r0 + 128, c0:c0 + XLW]),
                         w=[("xld", s)], dma=("xld", s))
                    for q0 in range(0, XLW, 512):
                        nq = min(4, (XLW - q0) // 128)
                        ps, keys = ps_mix.get(nq)

                        def fn(e, s=s, q0=q0, nq=nq, ps=ps):
                            for q in range(nq):
                                ins = e.transpose(out=ps[:, q * 128:(q + 1) * 128], in_=xld[s][:, q0 + q * 128:q0 + (q + 1) * 128], identity=cid)
                            return ins
                        P.op("pe", fn, r=[("xld", s), "consts"], w=keys)
                        kc0 = (c0 + q0) // 128
                        P.op("act", lambda e, ps=ps, kc0=kc0, nq=nq, tt=tt: e.activation(
                            out=xT[:, kc0:kc0 + nq, tt * 128:(tt + 1) * 128],
                            in_=ps[:, 0:nq * 128].rearrange("p (q t) -> p q t", q=nq), func=AF.Copy),
                            r=keys, w=["xT"])

        def conv_blocks(Ubuf, hist, cbase, nblk, key):
            P.op("pool", lambda e: e.tensor_copy(out=Ubuf[:, 0:nblk, 0:3], in_=hist[:, 0:nblk, :]), r=[key + "hist"], w=[key + "Uh"])
            for b in range(nblk):
                eng = "dve"
                rk = [(key + "U", b), key + "Uh", "convp"]

                P.op(eng, lambda e, b=b: e.tensor_scalar(out=cv[:, b, :], in0=Ubuf[:, b, 3:3 + TB], scalar1=convp[:, cbase + b, 3:4], scalar2=convp[:, cbase + b, 4:5],
                                                         op0=ALU.mult, op1=ALU.add), r=rk, w=[("cvp", b)])
                for k in range(3):
                    P.op(eng, lambda e, b=b, k=k: e.scalar_tensor_tensor(out=cv[:, b, :], in0=Ubuf[:, b, k:k + TB], scalar=convp[:, cbase + b, k:k + 1], in1=cv[:, b, :],
                                                                         op0=ALU.mult, op1=ALU.add), r=rk + [("cvp", b)], w=[("cvp", b)])
                P.op("act", lambda e, b=b: e.activation(out=cv[:, b, :], in_=cv[:, b, :], func=AF.Silu), r=[("cvp", b)], w=[("cv", b)])
            P.op("pool", lambda e: e.tensor_copy(out=hist[:, 0:nblk, :], in_=Ubuf[:, 0:nblk, TB:TB + 3]),
                 r=[(key + "U", b) for b in range(nblk)] + [key + "Uh"], w=[key + "hist"])

        def softplus_(out, xin, nparts, shape_keys_r, wkey, tmpa, tmpb):
            P.op("act", lambda e: e.activation(out=tmpa, in_=xin, func=AF.Abs), r=shape_keys_r, w=[wkey + "_a"])
            P.op("act", lambda e: e.activation(out=tmpa, in_=tmpa, func=AF.Exp, scale=-1.0), r=[wkey + "_a"], w=[wkey + "_b"])
            P.op("act", lambda e: e.activation(out=tmpb, in_=tmpa, func=AF.Ln, bias=1.0), r=[wkey + "_b"], w=[wkey + "_c"])
            P.op("dve", lambda e: e.scalar_tensor_tensor(out=out, in0=xin, scalar=0.0, in1=tmpb, op0=ALU.max, op1=ALU.add),
                 r=list(shape_keys_r) + [wkey + "_c"], w=[wkey])

        def transpose_out_bf(src_bf, ncb, blk0, c, rkeys):
            ps, keys = ps_mix.get((ncb + 1) // 2)
            psv = ps.bitcast(BF16)

            def fn(e):
                for i in range(ncb):
                    ins = e.transpose(out=psv[:, i * 128:(i + 1) * 128], in_=src_bf[:, i * 128:(i + 1) * 128], identity=idb[:])
                return ins
            P.op("pe", fn, r=list(rkeys) + ["idb"], w=keys)
            P.op("act", lambda e: e.activation(out=ymixT[:, blk0:blk0 + ncb, c * 128:(c + 1) * 128],
                                                in_=psv[:, 0:ncb * 128].rearrange("p (b t) -> p b t", b=ncb), func=AF.Copy),
                 r=keys, w=[("ymixT", blk0)])

        s_small = T("s_small", [128, NTT, 32])
        big_tm = T("big_tm", [128, NTT, 1024])
        s_xtm = T("s_xtm", [128, 1024])
        s_xdt = T("s_xdt", [128, 1024])
        if "ssm" in parts:
            s_sz = big_tm
            s_dt = T("s_dt", [128, NTT, 16])
            s_a = T("s_a", [128, NTT, 16])
            s_t1 = T("s_t1", [128, NTT, 16])
            s_t2 = T("s_t2", [128, NTT, 16])
            s_xx = T("s_xx", [128, NTT, 16])
            s_state = T("s_state", [128, 2, 512])
            s_pre = T("s_pre", [128, 48])
            s_ex = T("s_ex", [128, 48])
            s_xw = HS[:, 2048:3072]
            s_btm = T("s_btm", [128, 2, 128])
            s_gu = HS[:, 0:1024].rearrange("p (h l) -> p h l", h=8)
            s_mt = HS[:, 1024:2048].rearrange("p (h l) -> p h l", h=8)
            s_cbm = T("s_cbm", [128, 128])
            s_y = T("s_y", [128, 512])
            s_ybf = T("s_ybf", [128, 512], BF16)
            s_ssq = T("s_ssq", [128, 1])
            s_rstd = T("s_rstd", [128, 1])
            P.op("pool", lambda e: e.memset(s_state[:], 0.0), w=["s_state0", "s_state1"])
            P.op("pool", lambda e: e.memset(hist_s[:], 0.0), w=["shist"])

        def ssm_proj(tb, part="all"):
            for ti, t in enumerate(range(8, 14) if part in ("all", "fm") else ()):
                s = load_w(t)
                for j in range(2):
                    ps, keys = proj_fm(s, j)
                    b = ti * 2 + j
                    P.op("act", lambda e, ps=ps, b=b: e.activation(out=U[:, b, 3:3 + TB], in_=ps[:, 0:TB], func=AF.Copy), r=keys, w=[("sU", b)])
            for ti, t in enumerate(range(22, 26) if part in ("all", "z") else ()):
                s = load_w(t)
                for tt in range(NTT):
                    ps, keys = proj_tm(s, tt)
                    P.op("act", lambda e, ps=ps, ti=ti, tt=tt: e.activation(out=s_sz[:, tt, ti * WT:(ti + 1) * WT], in_=ps[:, 0:WT], func=AF.Silu),
                         r=keys, w=[("s_sz", tt, ti)])

        def small_proj(tb):
            s = load_w(27)
            for tt in range(NTT):
                ps, keys = proj_tm(s, tt)
                P.op("dve", lambda e, ps=ps, tt=tt: e.tensor_copy(out=s_small[:, tt, :], in_=ps[:, 0:32]), r=keys, w=[("small", tt)])
            return s

        def ssm_mixer(tb):
            barrier()
            conv_blocks(U, hist_s, 12, 12, "s")
            P.mark()
            smk = [("small", tt) for tt in range(NTT)]
            P.op("dve", lambda e: e.tensor_tensor(out=s_xx[:], in0=s_small[:, :, 8:24],
                                                  in1=prow[:, PR_SDTB:PR_SDTB + 16].unsqueeze(1).broadcast_to([128, NTT, 16]), op=ALU.add),
                 r=smk + ["prow"], w=["s_xx"])
            softplus_(s_dt[:], s_xx[:], 128, ["s_xx"], "s_dt", s_t1[:], s_t2[:])
            P.op("dve", lambda e: e.tensor_tensor(out=s_a[:], in0=s_dt[:], in1=nA_s[:].unsqueeze(1).broadcast_to([128, NTT, 16]), op=ALU.mult),
                 r=["s_dt", "nA_s"], w=["s_a"])
            for c in range(NTT):
                csl = slice(c * 128, (c + 1) * 128)
                ps, keys = ps_mix.get(1)

                def fn(e, ps=ps, c=c):
                    e.matmul(ps[:, 0:16], lhsT=ctri, rhs=s_a[:, c, :], start=True, stop=True)
                    return e.matmul(ps[:, 16:32], lhsT=cones, rhs=s_a[:, c, :], start=True, stop=True)
                P.op("pe", fn, r=["consts", "s_a"], w=keys)
                P.op("act", lambda e, ps=ps: e.activation(out=s_pre[:, 0:32], in_=ps[:, 0:32], func=AF.Copy), r=keys, w=["s_pre"])
                P.op("dve", lambda e: e.tensor_tensor(out=s_pre[:, 32:48], in0=s_pre[:, 16:32], in1=s_pre[:, 0:16], op=ALU.subtract), r=["s_pre"], w=["s_pre2"])
                P.op("act", lambda e: e.activation(out=s_ex[:], in_=s_pre[:], func=AF.Exp), r=["s_pre", "s_pre2"], w=["s_ex"])
                for half in range(2):
                    ps, keys = ps_mix.get(4)

                    def fn(e, ps=ps, half=half, csl=csl):
                        for q in range(4):
                            ins = e.transpose(out=ps[:, q * 128:(q + 1) * 128], in_=cv[:, half * 4 + q, csl], identity=cid)
                        return ins
                    P.op("pe", fn, r=[("cv", half * 4 + q) for q in range(4)] + ["consts"], w=keys)
                    P.op("act", lambda e, ps=ps, half=half: e.activation(out=s_xtm[:, half * 512:(half + 1) * 512], in_=ps[:, 0:512], func=AF.Copy),
                         r=keys, w=[("s_xtm", half)])
                ps, keys = ps_mix.get(2)

                def fn(e, ps=ps, csl=csl):
                    e.transpose(out=ps[:, 0:128], in_=cv[:, 8, csl], identity=cid)
                    return e.transpose(out=ps[:, 128:256], in_=cv[:, 9, csl], identity=cid)
                P.op("pe", fn, r=[("cv", 8), ("cv", 9), "consts"], w=keys)
                P.op("dve", lambda e, ps=ps: e.tensor_copy(out=s_btm[:], in_=ps[:, 0:256].rearrange("p (g n) -> p g n", g=2)), r=keys, w=["s_btm"])
                xk = [("s_xtm", 0), ("s_xtm", 1)]
                v3 = lambda t_: t_[:, 0:1024].rearrange("p (h q) -> p h q", h=16)
                P.op("pool", lambda e, c=c: e.tensor_tensor(out=v3(s_xdt), in0=v3(s_xtm), in1=s_dt[:, c, :].unsqueeze(2).broadcast_to([128, 16, 64]), op=ALU.mult),
                     r=xk + ["s_dt"], w=["s_xdt"])
                P.op("pool", lambda e: e.tensor_tensor(out=v3(s_xw), in0=v3(s_xdt), in1=s_ex[:, 32:48].unsqueeze(2).broadcast_to([128, 16, 64]), op=ALU.mult),
                     r=["s_xdt", "s_ex"], w=["s_xw"])
                for gi in range(2):
                    ps, keys = ps_mix.get(1)
                    P.op("pe", lambda e, ps=ps, gi=gi, csl=csl: e.matmul(ps[:, 0:128], lhsT=cv[:, 8 + gi, csl], rhs=cv[:, 10 + gi, csl], start=True, stop=True),
                         r=[("cv", 8 + gi), ("cv", 10 + gi)], w=keys)
                    P.op("dve", lambda e, ps=ps: e.tensor_tensor(out=s_cbm[:], in0=ps[:, 0:128], in1=ctri, op=ALU.mult), r=keys + ["consts"], w=["s_cbm"])
                    P.op("pool", lambda e, c=c, gi=gi: e.tensor_tensor(out=s_gu[:], in0=s_a[:, c, gi * 8:(gi + 1) * 8].unsqueeze(2).broadcast_to([128, 8, 128]),
                                                                       in1=cU.unsqueeze(1).broadcast_to([128, 8, 128]), op=ALU.mult),
                         r=["s_a", "consts"], w=["s_gu"])
                    for hh in range(2):
                        ps, keys = ps_mix.get(4)

                        def fn(e, ps=ps, hh=hh):
                            for q in range(4):
                                ins = e.matmul(ps[:, q * 128:(q + 1) * 128], lhsT=s_gu[:, hh * 4 + q, :], rhs=ctri, start=True, stop=True)
                            return ins
                        P.op("pe", fn, r=["s_gu", "consts"], w=keys)
                        P.op("act", lambda e, ps=ps, hh=hh: e.activation(out=s_mt[:, hh * 4:(hh + 1) * 4, :], in_=ps[:, 0:512].rearrange("p (h l) -> p h l", h=4), func=AF.Exp),
                             r=keys, w=[("s_mte", hh)])
                    P.op("dve", lambda e: e.tensor_tensor(out=s_mt[:], in0=s_mt[:], in1=s_cbm[:].unsqueeze(1).broadcast_to([128, 8, 128]), op=ALU.mult),
                         r=[("s_mte", 0), ("s_mte", 1), "s_cbm"], w=[("s_mte", 0), ("s_mte", 1)])
                    psd, kd = ps_mix.get(4)

                    def fn(e, psd=psd, gi=gi):
                        for h in range(8):
                            hg = gi * 8 + h
                            ins = e.matmul(psd[:, h * 64:(h + 1) * 64], lhsT=s_mt[:, h, :], rhs=s_xdt[:, hg * 64:(hg + 1) * 64], start=True, stop=True)
                        return ins
                    P.op("pe", fn, r=[("s_mte", 0), ("s_mte", 1), "s_xdt"], w=kd)
                    pso, ko = ps_mix.get(4)
                    P.op("pe", lambda e, pso=pso, gi=gi, csl=csl: e.matmul(pso[:, 0:512], lhsT=cv[:, 10 + gi, csl], rhs=s_state[:, gi, :], start=True, stop=True),
                         r=[("cv", 10 + gi), f"s_state{gi}"], w=ko)
                    y3 = s_y[:].rearrange("p (h q) -> p h q", h=8)
                    P.op("dve", lambda e, pso=pso, gi=gi: e.tensor_tensor(out=y3, in0=pso[:, 0:512].rearrange("p (h q) -> p h q", h=8),
                                                                          in1=s_ex[:, gi * 8:(gi + 1) * 8].unsqueeze(2).broadcast_to([128, 8, 64]), op=ALU.mult),
                         r=ko + ["s_ex"], w=["s_y"])
                    P.op("pool", lambda e, gi=gi: e.tensor_tensor(out=junk[:].rearrange("p (h q) -> p h q", h=8), in0=s_xtm[:, gi * 512:(gi + 1) * 512].rearrange("p (h q) -> p h q", h=8),
                                                                  in1=prow[:, PR_SD + gi * 8:PR_SD + (gi + 1) * 8].unsqueeze(2).broadcast_to([128, 8, 64]), op=ALU.mult),
                         r=[("s_xtm", gi), "prow"], w=["junk"])
                    P.op("pool", lambda e, gi=gi: e.tensor_tensor(out=s_y[:], in0=s_y[:], in1=junk[:], op=ALU.add), r=["s_y", "junk"], w=["s_y"])
                    P.op("dve", lambda e, psd=psd: e.tensor_tensor(out=s_y[:], in0=s_y[:], in1=psd[:, 0:512], op=ALU.add), r=kd + ["s_y"], w=["s_y"])
                    P.op("pool", lambda e, c=c, gi=gi: e.tensor_tensor(out=s_y[:], in0=s_y[:], in1=s_sz[:, c, gi * 512:(gi + 1) * 512], op=ALU.mult),
                         r=["s_y", ("s_sz", c, 2 * gi), ("s_sz", c, 2 * gi + 1)], w=["s_y"])
                    P.op("act", lambda e: e.activation(out=junk[:], in_=s_y[:], func=AF.Square, accum_out=s_ssq[:]), r=["s_y"], w=["junk", "s_ssq"])
                    P.op("act", lambda e: e.activation(out=s_rstd[:], in_=s_ssq[:], func=AF.Sqrt, scale=1.0 / 512, bias=epsc[:]), r=["s_ssq", "epsc"], w=["s_rstd0"])
                    P.op("dve", lambda e: e.reciprocal(out=s_rstd[:], in_=s_rstd[:]), r=["s_rstd0"], w=["s_rstd"])
                    P.op("dve", lambda e, gi=gi: e.scalar_tensor_tensor(out=s_ybf[:], in0=s_y[:], scalar=s_rstd[:], in1=prow[:, PR_SNW + gi * 512:PR_SNW + (gi + 1) * 512],
                                                                        op0=ALU.mult, op1=ALU.mult), r=["s_y", "s_rstd", "prow"], w=["s_ybf"])
                    transpose_out_bf(s_ybf, 4, 8 + gi * 4, c, ["s_ybf"])
                    pss, ks = ps_mix.get(4)
                    P.op("pe", lambda e, pss=pss, gi=gi: e.matmul(pss[:, 0:512], lhsT=s_btm[:, gi, :], rhs=s_xw[:, gi * 512:(gi + 1) * 512], start=True, stop=True),
                         r=["s_btm", "s_xw"], w=ks)
                    st3 = s_state[:, gi, :].rearrange("p (h q) -> p h q", h=8)
                    P.op("pool", lambda e, st3=st3, gi=gi: e.tensor_tensor(out=st3, in0=st3, in1=s_ex[:, 16 + gi * 8:16 + (gi + 1) * 8].unsqueeze(2).broadcast_to([128, 8, 64]), op=ALU.mult),
                         r=[f"s_state{gi}", "s_ex"], w=[f"s_state{gi}"])
                    P.op("dve", lambda e, pss=pss, gi=gi: e.tensor_tensor(out=s_state[:, gi, :], in0=s_state[:, gi, :], in1=pss[:, 0:512], op=ALU.add),
                         r=ks + [f"s_state{gi}"], w=[f"s_state{gi}"])

        if "gdn" in parts:
            g_sz = big_tm[:, :, 0:512]
            g_sq = T("g_sq", [128, TB])
            g_rn = T("g_rn", [128, TB])
            g_beta = T("g_beta", [128, NTT, 4])
            g_xx = T("g_xx", [128, NTT, 4])
            g_t1s = T("g_t1s", [128, NTT, 4])
            g_t2s = T("g_t2s", [128, NTT, 4])
            g_sp = T("g_sp", [128, NTT, 4])
            g_g = T("g_g", [128, NTT, 4])
            g_pre = T("g_pre", [128, 12])
            g_ex = T("g_ex", [128, 12])
            g_bg = T("g_bg", [128, 4])
            g_ktm = s_xtm[:, 0:512].rearrange("p (h d) -> p h d", h=4)
            g_vtm = s_xtm[:, 512:1024].rearrange("p (h d) -> p h d", h=4)
            g_R = s_xdt[:].rearrange("p (h c) -> p h c", h=4)
            g_kd = T("g_kd", [128, 4, 128])
            g_H = []
            for h_ in range(4):
                b0 = h_ * 2304
                v2 = lambda o_: HS[:, b0 + o_:b0 + o_ + 256].rearrange("p (a l) -> p a l", a=2)
                v1 = lambda o_: HS[:, b0 + o_:b0 + o_ + 128]
                g_H.append(dict(A=v2(0), B=v2(256), C=v1(512), D=v1(640), A0=v2(768), AToff=v1(1024), attnT=v1(1152),
                                PP=[v2(1280), v2(1536)], TT=[v1(1792), v1(1920)], uw=v2(2048)))
            g_o = T("g_o", [128, 4, 128])
            g_o2 = T("g_o2", [128, 4, 128])
            g_ssq = T("g_ssq", [128, 4])
            g_rstd = T("g_rstd", [128, 4])
            g_obf = T("g_obf", [128, 512], BF16)
            g_S = T("g_S", [128, 4, 128])
            P.op("pool", lambda e: e.memset(g_S[:], 0.0), w=[("g_S", h) for h in range(4)])
            P.op("pool", lambda e: e.memset(hist_g[:], 0.0), w=["ghist"])
            cBDm = consts[:, C_BD, :]
            cOFFT = consts[:, C_OFFT, :]
            cNEG = consts[:, C_NEG, :]

        def gdn_proj(tb, part="all"):
            for ti, t in enumerate(range(2, 8) if part in ("all", "fm") else ()):
                s = load_w(t)
                for j in range(2):
                    ps, keys = proj_fm(s, j)
                    b = ti * 2 + j
                    P.op("act", lambda e, ps=ps, b=b: e.activation(out=U[:, b, 3:3 + TB], in_=ps[:, 0:TB], func=AF.Copy), r=keys, w=[("gU", b)])
            for ti, t in enumerate(range(20, 22) if part in ("all", "z") else ()):
                s = load_w(t)
                for tt in range(NTT):
                    ps, keys = proj_tm(s, tt)
                    P.op("act", lambda e, ps=ps, ti=ti, tt=tt: e.activation(out=g_sz[:, tt, ti * WT:(ti + 1) * WT], in_=ps[:, 0:WT], func=AF.Silu),
                         r=keys, w=[("g_sz", tt, ti)])

        def head_gen(c, csl, h):
            Hh = g_H[h]
            kA, kB, kC, kD = ("hA", h), ("hB", h), ("hC", h), ("hD", h)
            kA0a, kA0b, kAo, kat, kuw = ("hA0a", h), ("hA0b", h), ("hAToff", h), ("hattn", h), ("huw", h)
            GU, E_, t1, Af, A0, AToff, attnT, PP, TT, uw = (Hh[k_] for k_ in ("A", "B", "C", "D", "A0", "AToff", "attnT", "PP", "TT", "uw"))
            X1 = GU.rearrange("p a l -> p (a l)")
            X2 = E_.rearrange("p a l -> p (a l)")
            vn, tmp_ = t1, Af
            gcol = g_g[:, c, h:h + 1]
            P.op("dve", lambda e: e.tensor_scalar(out=GU[:, 0, :], in0=cU, scalar1=gcol, scalar2=None, op0=ALU.mult), r=["consts", "g_g"], w=[kA])
            yield
            P.op("dve", lambda e: e.tensor_scalar(out=GU[:, 1, :], in0=ctri, scalar1=gcol, scalar2=None, op0=ALU.mult), r=["consts", "g_g"], w=[kA])
            yield
            ps, keys = ps_mix.get(4)
            kT = cv[:, 4 + h, csl]
            qT_ = cv[:, h, csl]

            def fn(e, ps=ps):
                e.matmul(ps[:, 0:128], lhsT=GU[:, 0, :], rhs=ctri, start=True, stop=False)
                e.matmul(ps[:, 0:128], lhsT=cid, rhs=cNEG, start=False, stop=True)
                e.matmul(ps[:, 128:256], lhsT=GU[:, 1, :], rhs=cU, start=True, stop=True)
                e.matmul(ps[:, 256:384], lhsT=kT, rhs=kT, start=True, stop=True)
                return e.matmul(ps[:, 384:512], lhsT=kT, rhs=qT_, start=True, stop=True)
            P.op("pe", fn, r=[kA, "consts", ("cv", 4 + h), ("cv", h)], w=keys)
            yield
            P.op("act", lambda e, ps=ps: e.activation(out=E_[:], in_=ps[:, 0:256].rearrange("p (a l) -> p a l", a=2), func=AF.Exp), r=keys, w=[kB])
            yield
            P.op("dve", lambda e, ps=ps: e.tensor_tensor(out=t1[:], in0=ps[:, 256:384], in1=E_[:, 1, :], op=ALU.mult), r=keys + [kB], w=[kC])
            yield
            P.op("dve", lambda e, ps=ps: e.tensor_tensor(out=attnT[:], in0=ps[:, 384:512], in1=E_[:, 0, :], op=ALU.mult), r=keys + [kB], w=[kat])
            yield
            P.op("dve", lambda e: e.scalar_tensor_tensor(out=Af[:], in0=t1[:], scalar=g_beta[:, c, h:h + 1], in1=cU, op0=ALU.mult, op1=ALU.mult),
                 r=[kC, "g_beta", "consts"], w=[kD])
            yield
            ps, keys = ps_mix.get(1)
            P.op("pe", lambda e, ps=ps: e.transpose(out=ps[:, 0:128], in_=Af[:], identity=cid), r=[kD, "consts"], w=keys)
            yield
            P.op("pool", lambda e: e.tensor_tensor(out=A0[:, 0, :], in0=Af[:], in1=cBDm, op=ALU.mult), r=[kD, "consts"], w=[kA0a])
            yield
            P.op("dve", lambda e, ps=ps: e.tensor_tensor(out=A0[:, 1, :], in0=ps[:, 0:128], in1=cBDm, op=ALU.mult), r=keys + ["consts"], w=[kA0b])
            yield
            P.op("dve", lambda e, ps=ps: e.tensor_tensor(out=AToff[:], in0=ps[:, 0:128], in1=cOFFT, op=ALU.mult), r=keys + ["consts"], w=[kAo])
            yield
            P.op("pool", lambda e: e.tensor_tensor(out=TT[0][:], in0=cid, in1=A0[:, 1, :], op=ALU.subtract), r=[kA0b, "consts"], w=[("hTT", h, 0)])
            yield
            Pc, Pk = A0, [kA0a, kA0b]
            tcur = 0
            for k in range(1, 6):
                ps, keys = ps_mix.get(2)

                def fn(e, ps=ps, Pc=Pc, k=k):
                    ins = e.matmul(ps[:, 0:128], lhsT=Pc[:, 1, :], rhs=Pc[:, 0, :], start=True, stop=True)
                    if k < 5:
                        ins = e.matmul(ps[:, 128:256], lhsT=Pc[:, 0, :], rhs=Pc[:, 1, :], start=True, stop=True)
                    return ins
                P.op("pe", fn, r=Pk, w=keys)
                yield
                Pn = PP[k % 2]
                nk = [("hPP", h, k % 2)]
                P.op("act", lambda e, ps=ps, Pn=Pn: e.activation(out=Pn[:], in_=ps[:, 0:256].rearrange("p (a l) -> p a l", a=2), func=AF.Copy), r=keys, w=nk)
                yield
                ps, keys = ps_mix.get(1)
                P.op("pe", lambda e, ps=ps, Pn=Pn, tcur=tcur: e.matmul(ps[:, 0:128], lhsT=Pn[:, 0, :], rhs=TT[tcur][:], start=True, stop=True),
                     r=nk + [("hTT", h, tcur)], w=keys)
                yield
                P.op("dve", lambda e, ps=ps, tcur=tcur: e.tensor_tensor(out=TT[1 - tcur][:], in0=TT[tcur][:], in1=ps[:, 0:128], op=ALU.add),
                     r=keys + [("hTT", h, tcur)], w=[("hTT", h, 1 - tcur)])
                yield
                tcur = 1 - tcur
                Pc, Pk = Pn, nk
            TTf = TT[tcur]
            tk = [("hTT", h, tcur)]
            ps, keys = ps_mix.get(2)
            P.op("pe", lambda e, ps=ps: e.matmul(ps[:, 0:256], lhsT=TTf[:], rhs=g_R[:, h, :], start=True, stop=True), r=tk + ["g_Rv", "g_Rk"], w=keys)
            yield
            P.op("act", lambda e, ps=ps: e.activation(out=X1, in_=ps[:, 0:256], func=AF.Copy), r=keys, w=[kA])
            yield
            ps, keys = ps_mix.get(2)
            P.op("pe", lambda e, ps=ps: e.matmul(ps[:, 0:256], lhsT=AToff[:], rhs=X1, start=True, stop=True), r=[kAo, kA], w=keys)
            yield
            P.op("dve", lambda e, ps=ps: e.tensor_tensor(out=X2, in0=g_R[:, h, :], in1=ps[:, 0:256], op=ALU.subtract), r=keys + ["g_Rv", "g_Rk"], w=[kB])
            yield
            ps, keys = ps_mix.get(2)

            def fn(e, ps=ps):
                e.matmul(ps[:, 0:128], lhsT=TTf[:], rhs=X2[:, 0:128], start=True, stop=True)
                return e.matmul(ps[:, 128:256], lhsT=X2[:, 128:256], rhs=TTf[:], start=True, stop=True)
            P.op("pe", fn, r=tk + [kB], w=keys)
            yield
            P.op("act", lambda e, ps=ps: e.activation(out=uw[:], in_=ps[:, 0:256].rearrange("p (a l) -> p a l", a=2), func=AF.Copy), r=keys, w=[kuw])
            yield
            ps, keys = ps_mix.get(2)

            def fn(e, ps=ps):
                e.matmul(ps[:, 0:128], lhsT=uw[:, 1, :], rhs=g_S[:, h, :], start=True, stop=True)
                return e.matmul(ps[:, 128:256], lhsT=qT_, rhs=g_S[:, h, :], start=True, stop=True)
            P.op("pe", fn, r=[kuw, ("g_S", h), ("cv", h)], w=keys)
            yield
            P.op("dve", lambda e, ps=ps: e.tensor_tensor(out=vn[:], in0=uw[:, 0, :], in1=ps[:, 0:128], op=ALU.subtract), r=keys + [kuw], w=[kC])
            yield
            P.op("dve", lambda e, ps=ps: e.tensor_scalar(out=tmp_[:], in0=ps[:, 128:256], scalar1=g_ex[:, h:h + 1], scalar2=None, op0=ALU.mult), r=keys + ["g_ex"], w=[kD])
            yield
            ps, keys = ps_mix.get(2)

            def fn(e, ps=ps):
                e.matmul(ps[:, 0:128], lhsT=attnT[:], rhs=vn[:], start=True, stop=True)
                return e.matmul(ps[:, 128:256], lhsT=g_kd[:, h, :], rhs=vn[:], start=True, stop=True)
            P.op("pe", fn, r=[kat, kC, "g_kd"], w=keys)
            yield
            P.op("dve", lambda e, ps=ps: e.tensor_tensor(out=g_o[:, h, :], in0=tmp_[:], in1=ps[:, 0:128], op=ALU.add), r=keys + [kD], w=[("g_o", h)])
            yield
            P.op("dve", lambda e, ps=ps: e.scalar_tensor_tensor(out=g_S[:, h, :], in0=g_S[:, h, :], scalar=g_ex[:, 4 + h:5 + h], in1=ps[:, 128:256],
                                                                op0=ALU.mult, op1=ALU.add), r=keys + [("g_S", h), "g_ex"], w=[("g_S", h)])
            yield

        def gdn_mixer(tb):
            barrier()
            conv_blocks(U, hist_g, 0, 12, "g")
            for b in range(8):
                P.op("act", lambda e, b=b: e.activation(out=g_sq[:], in_=cv[:, b, :], func=AF.Square), r=[("cv", b)], w=["g_sq"])
                ps, keys = ps_mix.get(2)
                P.op("pe", lambda e, ps=ps: e.matmul(ps[:, 0:TB], lhsT=cones, rhs=g_sq[:], start=True, stop=True), r=["g_sq", "consts"], w=keys)
                P.op("act", lambda e, ps=ps: e.activation(out=g_rn[:], in_=ps[:, 0:TB], func=AF.Sqrt, bias=epsc[:]), r=keys + ["epsc"], w=["g_rn0"])
                P.op("dve", lambda e: e.reciprocal(out=g_rn[:], in_=g_rn[:]), r=["g_rn0"], w=["g_rn"])
                sc_ = 128 ** -0.5 if b < 4 else 1.0
                P.op("dve", lambda e, b=b, sc_=sc_: e.scalar_tensor_tensor(out=cv[:, b, :], in0=cv[:, b, :], scalar=sc_, in1=g_rn[:], op0=ALU.mult, op1=ALU.mult),
                     r=[("cv", b), "g_rn"], w=[("cv", b)])
            if GSTOP <= 1:
                return
            smk = [("small", tt) for tt in range(NTT)]
            P.op("act", lambda e: e.activation(out=g_beta[:], in_=s_small[:, :, 0:4], func=AF.Sigmoid), r=smk, w=["g_beta"])
            P.op("dve", lambda e: e.tensor_tensor(out=g_xx[:], in0=s_small[:, :, 4:8],
                                                  in1=prow[:, PR_GDTB:PR_GDTB + 4].unsqueeze(1).broadcast_to([128, NTT, 4]), op=ALU.add),
                 r=smk + ["prow"], w=["g_xx"])
            softplus_(g_sp[:], g_xx[:], 128, ["g_xx"], "g_sp", g_t1s[:], g_t2s[:])
            P.op("dve", lambda e: e.tensor_tensor(out=g_g[:], in0=g_sp[:], in1=nA_g[:].unsqueeze(1).broadcast_to([128, NTT, 4]), op=ALU.mult),
                 r=["g_sp", "nA_g"], w=["g_g"])
            if GSTOP <= 2:
                return
            for c in range(NTT):
                csl = slice(c * 128, (c + 1) * 128)
                ps, keys = ps_mix.get(1)

                def fn(e, ps=ps, c=c):
                    e.matmul(ps[:, 0:4], lhsT=ctri, rhs=g_g[:, c, :], start=True, stop=True)
                    return e.matmul(ps[:, 4:8], lhsT=cones, rhs=g_g[:, c, :], start=True, stop=True)
                P.op("pe", fn, r=["consts", "g_g"], w=keys)
                P.op("act", lambda e, ps=ps: e.activation(out=g_pre[:, 0:8], in_=ps[:, 0:8], func=AF.Copy), r=keys, w=["g_pre"])
                P.op("dve", lambda e: e.tensor_tensor(out=g_pre[:, 8:12], in0=g_pre[:, 4:8], in1=g_pre[:, 0:4], op=ALU.subtract), r=["g_pre"], w=["g_pre2"])
                P.op("act", lambda e: e.activation(out=g_ex[:], in_=g_pre[:], func=AF.Exp), r=["g_pre", "g_pre2"], w=["g_ex"])
                P.op("dve", lambda e, c=c: e.tensor_tensor(out=g_bg[:], in0=g_beta[:, c, :], in1=g_ex[:, 0:4], op=ALU.mult), r=["g_beta", "g_ex"], w=["g_bg"])
                for which, dst, key in ((4, g_ktm, "g_ktm"), (8, g_vtm, "g_vtm")):
                    ps, keys = ps_mix.get(4)

                    def fn(e, ps=ps, which=which, csl=csl):
                        for q in range(4):
                            ins = e.transpose(out=ps[:, q * 128:(q + 1) * 128], in_=cv[:, which + q, csl], identity=cid)
                        return ins
                    P.op("pe", fn, r=[("cv", which + q) for q in range(4)] + ["consts"], w=keys)
                    P.op("act", lambda e, ps=ps, dst=dst: e.activation(out=dst[:], in_=ps[:, 0:512].rearrange("p (h d) -> p h d", h=4), func=AF.Copy), r=keys, w=[key])
                P.op("dve", lambda e, c=c: e.tensor_tensor(out=g_R[:, :, 0:128], in0=g_vtm[:], in1=g_beta[:, c, :].unsqueeze(2).broadcast_to([128, 4, 128]), op=ALU.mult),
                     r=["g_vtm", "g_beta"], w=["g_Rv"])
                P.op("pool", lambda e: e.tensor_tensor(out=g_R[:, :, 128:256], in0=g_ktm[:], in1=g_bg[:].unsqueeze(2).broadcast_to([128, 4, 128]), op=ALU.mult),
                     r=["g_ktm", "g_bg"], w=["g_Rk"])
                P.op("pool", lambda e: e.tensor_tensor(out=g_kd[:], in0=g_ktm[:], in1=g_ex[:, 8:12].unsqueeze(2).broadcast_to([128, 4, 128]), op=ALU.mult),
                     r=["g_ktm", "g_ex"], w=["g_kd"])
                if GSTOP <= 3:
                    continue
                gens = [head_gen(c, csl, h) for h in range(4)]
                live = list(gens)
                while live:
                    nxt = []
                    for g_ in live:
                        try:
                            next(g_)
                            nxt.append(g_)
                        except StopIteration:
                            pass
                    live = nxt
                if GSTOP <= 8:
                    continue
                ok = [("g_o", h) for h in range(4)]
                P.op("act", lambda e: e.activation(out=g_o2[:], in_=g_o[:], func=AF.Square), r=ok, w=["g_o2"])
                P.op("dve", lambda e: e.tensor_reduce(out=g_ssq[:], in_=g_o2[:], axis=mybir.AxisListType.X, op=ALU.add), r=["g_o2"], w=["g_ssq"])
                P.op("act", lambda e: e.activation(out=g_rstd[:], in_=g_ssq[:], func=AF.Sqrt, scale=1.0 / 128, bias=epsc[:]), r=["g_ssq", "epsc"], w=["g_rstd0"])
                P.op("dve", lambda e: e.reciprocal(out=g_rstd[:], in_=g_rstd[:]), r=["g_rstd0"], w=["g_rstd"])
                P.op("pool", lambda e: e.tensor_tensor(out=g_o2[:], in0=g_o[:], in1=g_rstd[:].unsqueeze(2).broadcast_to([128, 4, 128]), op=ALU.mult), r=ok + ["g_rstd", "g_o2"], w=["g_o2"])
                P.op("pool", lambda e: e.tensor_tensor(out=g_o2[:], in0=g_o2[:], in1=prow[:, PR_GNW:PR_GNW + 128].unsqueeze(1).broadcast_to([128, 4, 128]), op=ALU.mult),
                     r=["g_o2", "prow"], w=["g_o2"])
                P.op("dve", lambda e, c=c: e.tensor_tensor(out=g_obf[:], in0=g_o2[:].rearrange("p h d -> p (h d)"), in1=g_sz[:, c, :], op=ALU.mult),
                     r=["g_o2", ("g_sz", c, 0), ("g_sz", c, 1)], w=["g_obf"])
                transpose_out_bf(g_obf, 4, 4, c, ["g_obf"])

        if "ml" in parts:
            m_qT = T("m_qT", [128, 2, TB], BF16)
            m_kT = T("m_kT", [128, 2, TB], BF16)
            m_ktm = T("m_ktm", [128, NTT, 256])
            m_v = T("m_v", [128, NTT, 512], BF16)
            m_g = big_tm[:, :, 0:512]
            m_zt = big_tm[:, :, 512:1024]
            RN = ["li", "t", "a", "l1", "lf", "m", "F", "G", "H", "one"]
            m_rows = {nm: T("m_r_" + nm, [1, TB]) for nm in RN}
            m_rows["wi"] = m_rows["a"]
            m_rows["emm"] = m_rows["l1"]
            m_rows["nG"] = m_rows["t"]
            m_carry = T("m_carry", [1, 1])
            m_C = T("m_C", [128, 2, 512])
            m_Cb = T("m_Cb", [128, 2, 512], BF16)
            m_n = T("m_n", [128, 2])
            m_nb = T("m_nb", [128, 2], BF16)
            m_W = T("m_W", [128, 128])
            m_Wm = T("m_Wm", [128, 128])
            m_sc = T("m_sc", [128, 128], BF16)
            m_cols = T("m_cols", [128, 4])
            m_tmp = T("m_tmp", [128, 512])
            m_h = T("m_h", [128, 512])
            m_den = T("m_den", [128, 2])
            m_d1 = T("m_d1", [128, 1])
            m_rden = T("m_rden", [128, 1])
            m_ssq = T("m_ssq", [128, 1])
            m_rstd = T("m_rstd", [128, 1])
            m_comb = T("m_comb", [128, 1])
            m_g2 = T("m_g2", [128, 512])
            m_hbf = T("m_hbf", [128, 512], BF16)
            m_kws = T("m_kws", [128, 256], BF16)
            m_onesb = T("m_onesb", [128, 1], BF16)
            P.op("pool", lambda e: e.memset(m_C[:], 0.0), w=["m_C"])
            P.op("pool", lambda e: e.memset(m_Cb[:], 0.0), w=["m_Cb"])
            P.op("pool", lambda e: e.memset(m_n[:], 0.0), w=["m_n"])
            P.op("pool", lambda e: e.memset(m_nb[:], 0.0), w=["m_nb"])
            P.op("pool", lambda e: e.memset(m_carry[:], 0.0), w=["m_carry"])
            P.op("pool", lambda e: e.memset(m_rows["one"][:], 1.0), w=["m_one"])
            P.op("pool", lambda e: e.memset(m_onesb[:], 1.0), w=["m_onesb"])

        def ml_proj(tb, part="all"):
            if part == "o":
                for ti, t in ((2, 16), (3, 17)):
                    s = load_w(t)
                    for tt in range(NTT):
                        ps, keys = proj_tm(s, tt)
                        cs = slice((ti % 2) * WT, (ti % 2 + 1) * WT)
                        P.op("act", lambda e, ps=ps, tt=tt, cs=cs: e.activation(out=m_g[:, tt, cs], in_=ps[:, 0:WT], func=AF.Sigmoid), r=keys, w=[("m_g", tt, ti % 2)])
                return
            s = load_w(0)
            for j in range(2):
                ps, keys = proj_fm(s, j)
                P.op("act", lambda e, ps=ps, j=j: e.activation(out=m_qT[:, j, :], in_=ps[:, 0:TB], func=AF.Identity, scale=256 ** -0.5), r=keys, w=[("m_qT", j)])
            s = load_w(1)
            for j in range(2):
                ps, keys = proj_fm(s, j)
                P.op("act", lambda e, ps=ps, j=j: e.activation(out=m_kT[:, j, :], in_=ps[:, 0:TB], func=AF.Copy), r=keys, w=[("m_kT", j)])
            for ti, t in enumerate(range(14, 20)):
                if part == "a" and ti in (2, 3):
                    continue
                s = load_w(t)
                for tt in range(NTT):
                    ps, keys = proj_tm(s, tt)
                    cs = slice((ti % 2) * WT, (ti % 2 + 1) * WT)
                    if ti < 2:
                        P.op("act", lambda e, ps=ps, tt=tt, cs=cs: e.activation(out=m_v[:, tt, cs], in_=ps[:, 0:WT], func=AF.Copy), r=keys, w=[("m_v", tt, ti)])
                    elif ti < 4:
                        P.op("act", lambda e, ps=ps, tt=tt, cs=cs: e.activation(out=m_g[:, tt, cs], in_=ps[:, 0:WT], func=AF.Sigmoid), r=keys, w=[("m_g", tt, ti % 2)])
                    else:
                        P.op("act", lambda e, ps=ps, tt=tt, cs=cs: e.activation(out=m_zt[:, tt, cs], in_=ps[:, 0:WT], func=AF.Silu), r=keys, w=[("m_zt", tt, ti % 2)])
            s = load_w(26)
            for tt in range(NTT):
                ps, keys = proj_tm(s, tt)
                P.op("act", lambda e, ps=ps, tt=tt: e.activation(out=m_ktm[:, tt, :], in_=ps[:, 0:WT], func=AF.Copy), r=keys, w=[("m_ktm", tt)])
            s = load_w(27)
            ps, keys = proj_fm(s, 0, M=1, c0=24)
            P.op("act", lambda e, ps=ps: e.activation(out=m_rows["li"][:], in_=ps[0:1, 0:TB], func=AF.Identity, bias=prow[0:1, PR_MIB:PR_MIB + 1]), r=keys + ["prow"], w=["m_li"])
            ps, keys = proj_fm(s, 0, M=1, c0=25)
            P.op("act", lambda e, ps=ps: e.activation(out=m_rows["t"][:], in_=ps[0:1, 0:TB], func=AF.Identity, bias=prow[0:1, PR_MFB:PR_MFB + 1]), r=keys + ["prow"], w=["m_t"])

        def ml_mixer(tb):
            R_ = m_rows
            for tt in range(NTT):
                P.op("pool", lambda e, tt=tt: e.tensor_tensor(out=m_g[:, tt, :], in0=m_g[:, tt, :], in1=m_zt[:, tt, :], op=ALU.mult),
                     r=[("m_g", tt, 0), ("m_g", tt, 1), ("m_zt", tt, 0), ("m_zt", tt, 1)], w=[("m_g", tt, 0), ("m_g", tt, 1)])
            P.op("act", lambda e: e.activation(out=R_["a"][:], in_=R_["t"][:], func=AF.Abs), r=["m_t"], w=["m_a"])
            P.op("act", lambda e: e.activation(out=R_["a"][:], in_=R_["a"][:], func=AF.Exp, scale=-1.0), r=["m_a"], w=["m_a2"])
            P.op("act", lambda e: e.activation(out=R_["l1"][:], in_=R_["a"][:], func=AF.Ln, bias=1.0), r=["m_a2"], w=["m_l1"])
            P.op("dve", lambda e: e.scalar_tensor_tensor(out=R_["lf"][:], in0=R_["t"][:], scalar=0.0, in1=R_["l1"][:], op0=ALU.min, op1=ALU.subtract), r=["m_t", "m_l1"], w=["m_lf"])
            P.op("dve", lambda e: e.tensor_tensor_scan(out=R_["m"][:], data0=R_["lf"][:], data1=R_["li"][:], initial=m_carry[:], op0=ALU.add, op1=ALU.max),
                 r=["m_lf", "m_li", "m_carry"], w=["m_m"])
            P.op("dve", lambda e: e.tensor_tensor_scan(out=R_["F"][:], data0=R_["one"][:], data1=R_["lf"][:], initial=0.0, op0=ALU.mult, op1=ALU.add),
                 r=["m_lf", "m_one"], w=["m_F"])
            P.op("dve", lambda e: e.tensor_tensor(out=R_["G"][:], in0=R_["F"][:], in1=R_["m"][:], op=ALU.subtract), r=["m_F", "m_m"], w=["m_G"])
            P.op("dve", lambda e: e.tensor_tensor(out=R_["H"][:], in0=R_["li"][:], in1=R_["F"][:], op=ALU.subtract), r=["m_F", "m_li"], w=["m_H"])
            P.op("dve", lambda e: e.tensor_scalar(out=R_["nG"][:], in0=R_["G"][:], scalar1=-1.0, scalar2=None, op0=ALU.mult), r=["m_G"], w=["m_nG"])
            P.op("act", lambda e: e.activation(out=R_["emm"][:], in_=R_["m"][:], func=AF.Exp, scale=-1.0), r=["m_m"], w=["m_emm"])
            one = R_["one"]
            for c in range(NTT):
                csl = slice(c * 128, (c + 1) * 128)
                bias_ap = m_carry[:] if c == 0 else R_["nG"][0:1, c * 128 - 1:c * 128]
                P.op("act", lambda e, csl=csl, bias_ap=bias_ap: e.activation(out=R_["wi"][0:1, csl], in_=R_["G"][0:1, csl], func=AF.Exp, bias=bias_ap),
                     r=["m_G", "m_nG", "m_carry"], w=["m_wi"])
                ps, keys = ps_mix.get(1)

                def fn(e, ps=ps, csl=csl):
                    e.matmul(ps[:, 0:128], lhsT=R_["H"][0:1, csl], rhs=one[0:1, 0:128], start=True, stop=False)
                    return e.matmul(ps[:, 0:128], lhsT=one[0:1, 0:128], rhs=R_["G"][0:1, csl], start=False, stop=True)
                P.op("pe", fn, r=["m_H", "m_G", "m_one"], w=keys)
                P.op("dve", lambda e, ps=ps: e.tensor_scalar(out=m_W[:], in0=ps[:, 0:128], scalar1=0.0, scalar2=None, op0=ALU.min), r=keys, w=["m_W0"])
                P.op("act", lambda e: e.activation(out=m_W[:], in_=m_W[:], func=AF.Exp), r=["m_W0"], w=["m_W"])
                P.op("pool", lambda e: e.tensor_tensor(out=m_Wm[:], in0=m_W[:], in1=ctri, op=ALU.mult), r=["m_W", "consts"], w=["m_Wm"])
                ps, keys = ps_mix.get(1)

                def fn(e, ps=ps, csl=csl, c=c):
                    e.matmul(ps[:, 0:1], lhsT=R_["wi"][0:1, csl], rhs=one[0:1, 0:1], start=True, stop=True)
                    e.matmul(ps[:, 1:2], lhsT=R_["emm"][0:1, csl], rhs=one[0:1, 0:1], start=True, stop=True)
                    return e.matmul(ps[:, 2:3], lhsT=one[0:1, 0:128], rhs=R_["wi"][0:1, c * 128 + 127:c * 128 + 128], start=True, stop=True)
                P.op("pe", fn, r=["m_wi", "m_emm", "m_one"], w=keys)
                P.op("act", lambda e, ps=ps: e.activation(out=m_cols[:, 0:3], in_=ps[:, 0:3], func=AF.Copy), r=keys, w=["m_cols"])
                ps, keys = ps_mix.get(1)

                def fn(e, ps=ps, csl=csl):
                    e.matmul(ps[:, 0:128], lhsT=m_kT[:, 0, csl], rhs=m_qT[:, 0, csl], start=True, stop=False)
                    return e.matmul(ps[:, 0:128], lhsT=m_kT[:, 1, csl], rhs=m_qT[:, 1, csl], start=False, stop=True)
                P.op("pe", fn, r=[("m_kT", 0), ("m_kT", 1), ("m_qT", 0), ("m_qT", 1)], w=keys)
                P.op("dve", lambda e, ps=ps: e.tensor_tensor(out=m_sc[:], in0=ps[:, 0:128], in1=m_Wm[:], op=ALU.mult), r=keys + ["m_Wm"], w=["m_sc"])
                psi, ki = ps_mix.get(4)

                def fn(e, psi=psi, csl=csl):
                    e.matmul(psi[:, 0:512], lhsT=m_qT[:, 0, csl], rhs=m_Cb[:, 0, :], start=True, stop=False)
                    return e.matmul(psi[:, 0:512], lhsT=m_qT[:, 1, csl], rhs=m_Cb[:, 1, :], start=False, stop=True)
                P.op("pe", fn, r=[("m_qT", 0), ("m_qT", 1), "m_Cb"], w=ki)
                P.op("dve", lambda e, psi=psi: e.tensor_scalar(out=m_tmp[:], in0=psi[:, 0:512], scalar1=m_cols[:, 0:1], scalar2=None, op0=ALU.mult), r=ki + ["m_cols"], w=["m_tmp"])
                psa, ka = ps_mix.get(4)
                P.op("pe", lambda e, psa=psa, c=c: e.matmul(psa[:, 0:512], lhsT=m_sc[:], rhs=m_v[:, c, :], start=True, stop=True), r=["m_sc", ("m_v", c, 0), ("m_v", c, 1)], w=ka)
                P.op("dve", lambda e, psa=psa: e.tensor_tensor(out=m_h[:], in0=m_tmp[:], in1=psa[:, 0:512], op=ALU.add), r=ka + ["m_tmp"], w=["m_h"])
                ps, keys = ps_mix.get(1)

                def fn(e, ps=ps, csl=csl):
                    e.matmul(ps[:, 0:1], lhsT=m_sc[:], rhs=m_onesb[:], start=True, stop=True)
                    e.matmul(ps[:, 1:2], lhsT=m_qT[:, 0, csl], rhs=m_nb[:, 0:1], start=True, stop=False)
                    return e.matmul(ps[:, 1:2], lhsT=m_qT[:, 1, csl], rhs=m_nb[:, 1:2], start=False, stop=True)
                P.op("pe", fn, r=["m_sc", "m_onesb", ("m_qT", 0), ("m_qT", 1), "m_nb"], w=keys)
                P.op("act", lambda e, ps=ps: e.activation(out=m_den[:], in_=ps[:, 0:2], func=AF.Copy), r=keys, w=["m_den"])
                P.op("dve", lambda e: e.scalar_tensor_tensor(out=m_d1[:], in0=m_den[:, 1:2], scalar=m_cols[:, 0:1], in1=m_den[:, 0:1], op0=ALU.mult, op1=ALU.add),
                     r=["m_den", "m_cols"], w=["m_d1"])
                P.op("act", lambda e: e.activation(out=m_d1[:], in_=m_d1[:], func=AF.Abs), r=["m_d1"], w=["m_d1b"])
                P.op("dve", lambda e: e.tensor_tensor(out=m_rden[:], in0=m_d1[:], in1=m_cols[:, 1:2], op=ALU.max), r=["m_d1b", "m_cols"], w=["m_rden0"])
                P.op("dve", lambda e: e.reciprocal(out=m_rden[:], in_=m_rden[:]), r=["m_rden0"], w=["m_rden"])
                P.op("dve", lambda e: e.tensor_scalar(out=m_h[:], in0=m_h[:], scalar1=m_rden[:], scalar2=None, op0=ALU.mult), r=["m_h", "m_rden"], w=["m_h"])
                P.op("act", lambda e: e.activation(out=junk[:], in_=m_h[:], func=AF.Square, accum_out=m_ssq[:]), r=["m_h"], w=["junk", "m_ssq"])
                P.op("act", lambda e: e.activation(out=m_rstd[:], in_=m_ssq[:], func=AF.Sqrt, scale=1.0 / 512, bias=epsc[:]), r=["m_ssq", "epsc"], w=["m_rstd0"])
                P.op("dve", lambda e: e.reciprocal(out=m_rstd[:], in_=m_rstd[:]), r=["m_rstd0"], w=["m_rstd"])
                P.op("dve", lambda e: e.tensor_copy(out=m_comb[:], in_=m_rstd[:]), r=["m_rstd"], w=["m_comb"])
                P.op("pool", lambda e, c=c: e.tensor_tensor(out=m_g2[:], in0=m_g[:, c, :], in1=prow[:, PR_MLNW:PR_MLNW + 512], op=ALU.mult), r=[("m_g", c, 0), ("m_g", c, 1), "prow"], w=["m_g2"])
                P.op("dve", lambda e: e.scalar_tensor_tensor(out=m_hbf[:], in0=m_h[:], scalar=m_comb[:], in1=m_g2[:], op0=ALU.mult, op1=ALU.mult),
                     r=["m_h", "m_comb", "m_g2"], w=["m_hbf"])
                transpose_out_bf(m_hbf, 4, 0, c, ["m_hbf"])
                P.op("dve", lambda e, c=c: e.tensor_scalar(out=m_kws[:], in0=m_ktm[:, c, :], scalar1=m_W[:, 127:128], scalar2=None, op0=ALU.mult), r=[("m_ktm", c), "m_W"], w=["m_kws"])
                for kc in range(2):
                    ps, keys = ps_mix.get(4)
                    P.op("pe", lambda e, ps=ps, kc=kc, c=c: e.matmul(ps[:, 0:512], lhsT=m_kws[:, kc * 128:(kc + 1) * 128], rhs=m_v[:, c, :], start=True, stop=True),
                         r=["m_kws", ("m_v", c, 0), ("m_v", c, 1)], w=keys)
                    P.op("dve", lambda e, ps=ps, kc=kc: e.scalar_tensor_tensor(out=m_C[:, kc, :], in0=m_C[:, kc, :], scalar=m_cols[:, 2:3], in1=ps[:, 0:512], op0=ALU.mult, op1=ALU.add),
                         r=keys + ["m_C", "m_cols", "m_Cb"], w=["m_C"])
                ps, keys = ps_mix.get(1)

                def fn(e, ps=ps):
                    e.matmul(ps[:, 0:1], lhsT=m_kws[:, 0:128], rhs=m_onesb[:], start=True, stop=True)
                    return e.matmul(ps[:, 1:2], lhsT=m_kws[:, 128:256], rhs=m_onesb[:], start=True, stop=True)
                P.op("pe", fn, r=["m_kws", "m_onesb"], w=keys)
                P.op("dve", lambda e, ps=ps: e.scalar_tensor_tensor(out=m_n[:], in0=m_n[:], scalar=m_cols[:, 2:3], in1=ps[:, 0:2], op0=ALU.mult, op1=ALU.add),
                     r=keys + ["m_n", "m_cols"], w=["m_n"])
                P.op("pool", lambda e: e.tensor_copy(out=m_Cb[:], in_=m_C[:]), r=["m_C"], w=["m_Cb"])
                P.op("pool", lambda e: e.tensor_copy(out=m_nb[:], in_=m_n[:]), r=["m_n"], w=["m_nb"])
            P.op("dve", lambda e: e.tensor_copy(out=m_carry[:], in_=R_["m"][0:1, TB - 1:TB]), r=["m_m", "m_wi"], w=["m_carry"])

        if "ssm" not in parts or "gdn" not in parts or "ml" not in parts:
            P.op("pool", lambda e: e.memset(ymixT[:], 0.0), w=[("ymixT", b) for b in range(16)])
        ymk = [("ymixT", b) for b in (0, 4, 8, 12)] + [("ymixT", b) for b in range(16)]

        def outproj(tb):
            if debug:
                P.op("pool", lambda e, tb=tb: e.dma_start(out=ymix_d[:, :, tb * TB:(tb + 1) * TB].rearrange("b p t -> p b t"), in_=ymixT[:]),
                     r=ymk, w=[("ymixd", tb)], dma="ymixd")
            n = 0
            for t in range(NOC):
                s = load_wo(t)
                for tt in range(NTT):
                    ps, keys = ps_proj.get(2)

                    def fn(e, ps=ps, s=s, tt=tt):
                        for kc in range(16):
                            ins = e.matmul(ps[:, 0:WT], lhsT=ymixT[:, kc, tt * 128:(tt + 1) * 128], rhs=wt[s][:, kc, :], start=(kc == 0), stop=(kc == 15))
                        return ins
                    P.op("pe", fn, r=ymk + [("wt", s)], w=keys)
                    ss = n % 2
                    n += 1
                    P.op("act", lambda e, ps=ps, ss=ss: e.activation(out=stage[ss][:], in_=ps[:, 0:WT], func=AF.Copy), r=keys, w=[("stage", ss)])
                    r0 = tb * TB + tt * 128
                    P.op("pool", lambda e, ss=ss, r0=r0, t=t: e.dma_start(out=ypart_d[r0:r0 + 128, t * WT:(t + 1) * WT], in_=stage[ss][:]),
                         r=[("stage", ss)], w=[("ypart", r0, t)], dma=("stage", ss))

        full = all(p_ in parts for p_ in ("ssm", "gdn", "ml"))
        if not full or os.environ.get("NOMERGE"):
            for tb in range(NB):
                load_xT(tb)
                if "ssm" in parts:
                    ssm_proj(tb)
                if "ssm" in parts or "gdn" in parts:
                    small_proj(tb)
                if "ssm" in parts:
                    ssm_mixer(tb)
                if "gdn" in parts:
                    gdn_proj(tb)
                    gdn_mixer(tb)
                if "ml" in parts:
                    ml_proj(tb)
                    ml_mixer(tb)
                outproj(tb)
        else:
            def head_phase(tb):
                ssm_proj(tb, "fm")
                small_proj(tb)

            def tail_phase(tb):
                ml_mixer(tb)
                outproj(tb)
            load_xT(0)
            head_phase(0)
            ssm_proj(0, "z")
            for tb in range(NB):
                A = P.capture(ssm_mixer, tb)
                mk = A.index(None)
                B = P.capture(gdn_proj, tb, "fm")
                P.replay(A[:mk])
                if "1" in MERGE:
                    P.merge(A[mk:], B)
                else:
                    P.replay(A[mk:])
                    P.replay(B)
                gdn_proj(tb, "z")
                A = P.capture(gdn_mixer, tb)
                B = P.capture(ml_proj, tb, "a")
                if "2" in MERGE:
                    P.merge(A, B)
                else:
                    P.replay(B)
                    P.replay(A)
                ml_proj(tb, "o")
                A = P.capture(ml_mixer, tb)
                if tb + 1 < NB:
                    load_xT(tb + 1)
                    B = P.capture(head_phase, tb + 1)
                    if "3" in MERGE:
                        P.merge(A, B)
                    else:
                        P.replay(A)
                        P.replay(B)
                    outproj(tb)
                    ssm_proj(tb + 1, "z")
                else:
                    P.replay(A)
                    outproj(tb)
        P.emit()
        build_layer.last_prog = P
    return nc


ML_OFF, GDN_OFF, SSM_OFF = 0, 8200, 16424


def core_columns(g):
    r = np.arange
    ml = ML_OFF
    gd = GDN_OFF
    ss = SSM_OFF
    cols = [
        ml + g * 256 + r(256),
        ml + 1024 + g * 256 + r(256),
        gd + g * 512 + r(512),
        gd + 2048 + g * 512 + r(512),
        gd + 4096 + g * 512 + r(512),
        ss + 4096 + g * 1024 + r(1024),
        ss + 8192 + g * 256 + r(256),
        ss + 9216 + g * 256 + r(256),
        ml + 2048 + g * 512 + r(512),
        ml + 4096 + g * 512 + r(512),
        ml + 6144 + g * 512 + r(512),
        gd + 6144 + g * 512 + r(512),
        ss + g * 1024 + r(1024),
        ml + 1024 + g * 256 + r(256),
        gd + 8192 + g * 4 + r(4),
        gd + 8208 + g * 4 + r(4),
        ss + 10240 + g * 16 + r(16),
        np.array([ml + 8192 + g]),
        np.array([ml + 8196 + g]),
    ]
    return np.concatenate(cols)


def pack_core(inp, l, g):
    cols = core_columns(g)
    D = inp["w_in"].shape[1]
    wcat = np.zeros((D, NCOL), np.float32)
    wcat[:, :cols.size] = inp["w_in"][l][:, cols]
    mixrows = np.concatenate([g * 512 + np.arange(512), 2048 + g * 512 + np.arange(512), 4096 + g * 1024 + np.arange(1024)])
    wout = np.ascontiguousarray(inp["w_out"][l][mixrows, :])
    gch = np.concatenate([g * 512 + np.arange(512), 2048 + g * 512 + np.arange(512), 4096 + g * 512 + np.arange(512)])
    sch = np.concatenate([g * 1024 + np.arange(1024), 4096 + g * 256 + np.arange(256), 5120 + g * 256 + np.arange(256)])
    cp = np.zeros((24 * 128, 5), np.float32)
    cp[:1536, 0:4] = inp["gdn_conv_w"][l][:, gch].T
    cp[1536:, 0:4] = inp["ssm_conv_w"][l][:, sch].T
    cp[1536:, 4] = inp["ssm_conv_b"][l][sch]
    convp = np.ascontiguousarray(cp.reshape(24, 128, 5).transpose(1, 0, 2).reshape(128, 120))
    pr = np.zeros((1, PR_N), np.float32)
    pr[0, PR_SDTB:PR_SDTB + 16] = inp["ssm_dt_bias"][l][16 * g:16 * g + 16]
    pr[0, PR_SALOG:PR_SALOG + 16] = inp["ssm_A_log"][l][16 * g:16 * g + 16]
    pr[0, PR_SD:PR_SD + 16] = inp["ssm_D"][l][16 * g:16 * g + 16]
    pr[0, PR_GDTB:PR_GDTB + 4] = inp["gdn_dt_bias"][l][4 * g:4 * g + 4]
    pr[0, PR_GALOG:PR_GALOG + 4] = inp["gdn_A_log"][l][4 * g:4 * g + 4]
    pr[0, PR_MIB] = inp["ml_i_bias"][l][g]
    pr[0, PR_MFB] = inp["ml_f_bias"][l][g]
    pr[0, PR_MLNW:PR_MLNW + 512] = inp["ml_norm_w"][l][g * 512:(g + 1) * 512]
    pr[0, PR_GNW:PR_GNW + 128] = inp["gdn_norm_w"][l]
    pr[0, PR_SNW:PR_SNW + 1024] = inp["ssm_norm_w"][l][g * 1024:(g + 1) * 1024]
    return {"wcat": wcat, "wout": wout, "convp": convp, "prow": pr, "consts": make_consts()}


def build_reduce_ln(ROWS=1024, D=D_MODEL):
    nc = bass.Bass("TRN2", target_bir_lowering=False)
    x_d = nc.dram_tensor("xin", [ROWS, D], F32, kind="ExternalInput").ap()
    p_d = [nc.dram_tensor(f"p{j}", [ROWS, D], F32, kind="ExternalInput").ap() for j in range(4)]
    g_d = nc.dram_tensor("lng", [1, D], F32, kind="ExternalInput").ap()
    b_d = nc.dram_tensor("lnb", [1, D], F32, kind="ExternalInput").ap()
    o_d = nc.dram_tensor("out", [ROWS, D], F32, kind="ExternalOutput").ap()
    CW = 1024 if D >= 1024 else D
    NCH = D // CW
    with ExitStack() as st:
        def T(name, shape, dt=F32):
            return st.enter_context(nc.sbuf_tensor("sb_" + name, shape, dt))
        P = Prog(nc, st)
        gb = T("gb", [128, D])
        bb = T("bb", [128, D])
        acc = [T(f"acc{i}", [128, D]) for i in range(2)]
        ld = [[T(f"ld{i}_{j}", [128, CW]) for j in range(5)] for i in range(2)]
        junk = T("junk", [128, D])
        stt = T("stt", [128, 8])
        epsl = T("epsl", [128, 1])
        P.op("sp", lambda e: e.dma_start(out=gb[:], in_=g_d.partition_broadcast(128)[:, 0, :]), w=["gb"], dma="gb")
        P.op("sp", lambda e: e.dma_start(out=bb[:], in_=b_d.partition_broadcast(128)[:, 0, :]), w=["bb"], dma="bb")
        P.op("pool", lambda e: e.memset(epsl[:], LN_EPS), w=["epsl"])
        n = 0
        for ti in range(ROWS // 128):
            r0 = ti * 128
            a = acc[ti % 2]
            ak = ("acc", ti % 2)
            for ch in range(NCH):
                s = n % 2
                n += 1
                cs = slice(ch * CW, (ch + 1) * CW)
                srcs = [x_d] + p_d
                for j in range(5):
                    q = "sp" if j % 2 == 0 else "act"
                    P.op(q, lambda e, s=s, j=j, cs=cs, r0=r0, srcs=srcs: e.dma_start(out=ld[s][j][:], in_=srcs[j][r0:r0 + 128, cs]),
                         w=[("ld", s, j)], dma=("ld", s, j))
                P.op("dve", lambda e, s=s, cs=cs, a=a: e.scalar_tensor_tensor(out=a[:, cs], in0=ld[s][0][:], scalar=float(ALPHA), in1=ld[s][1][:], op0=ALU.mult, op1=ALU.add),
                     r=[("ld", s, 0), ("ld", s, 1)], w=[ak])
                P.op("pool", lambda e, s=s: e.tensor_tensor(out=ld[s][2][:], in0=ld[s][2][:], in1=ld[s][3][:], op=ALU.add), r=[("ld", s, 2), ("ld", s, 3)], w=[("ld", s, 2)])
                P.op("pool", lambda e, s=s: e.tensor_tensor(out=ld[s][2][:], in0=ld[s][2][:], in1=ld[s][4][:], op=ALU.add), r=[("ld", s, 2), ("ld", s, 4)], w=[("ld", s, 2)])
                P.op("dve", lambda e, s=s, cs=cs, a=a: e.tensor_tensor(out=a[:, cs], in0=a[:, cs], in1=ld[s][2][:], op=ALU.add), r=[ak, ("ld", s, 2)], w=[ak])
            P.op("act", lambda e, a=a: e.activation(out=junk[:], in_=a[:], func=AF.Identity, accum_out=stt[:, 0:1]), r=[ak], w=["junk", "st0"])
            P.op("dve", lambda e: e.tensor_scalar(out=stt[:, 1:2], in0=stt[:, 0:1], scalar1=-1.0 / D, scalar2=None, op0=ALU.mult), r=["st0"], w=["st1"])
            P.op("dve", lambda e, a=a: e.tensor_scalar(out=a[:], in0=a[:], scalar1=stt[:, 1:2], scalar2=None, op0=ALU.add), r=[ak, "st1"], w=[ak])
            P.op("act", lambda e, a=a: e.activation(out=junk[:], in_=a[:], func=AF.Square, accum_out=stt[:, 2:3]), r=[ak], w=["junk", "st2"])
            P.op("act", lambda e: e.activation(out=stt[:, 3:4], in_=stt[:, 2:3], func=AF.Sqrt, scale=1.0 / D, bias=epsl[:]), r=["st2", "epsl"], w=["st3"])
            P.op("dve", lambda e: e.reciprocal(out=stt[:, 4:5], in_=stt[:, 3:4]), r=["st3"], w=["st4"])
            P.op("dve", lambda e, a=a: e.scalar_tensor_tensor(out=a[:], in0=a[:], scalar=stt[:, 4:5], in1=gb[:], op0=ALU.mult, op1=ALU.mult), r=[ak, "st4", "gb"], w=[ak])
            P.op("pool", lambda e, a=a: e.tensor_tensor(out=a[:], in0=a[:], in1=bb[:], op=ALU.add), r=[ak, "bb"], w=[ak])
            P.op("pool", lambda e, a=a, r0=r0: e.dma_start(out=o_d[r0:r0 + 128, :], in_=a[:]), r=[ak], w=[("out", ti)], dma=ak)
        P.emit()
    return nc


_PROGS = {}


def kernel(x, w_in, w_out, ml_i_bias, ml_f_bias, ml_norm_w, gdn_conv_w, gdn_A_log, gdn_dt_bias,
           gdn_norm_w, ssm_conv_w, ssm_conv_b, ssm_A_log, ssm_dt_bias, ssm_D, ssm_norm_w, ln_g, ln_b):
    inp = dict(w_in=np.asarray(w_in), w_out=np.asarray(w_out), ml_i_bias=np.asarray(ml_i_bias), ml_f_bias=np.asarray(ml_f_bias),
               ml_norm_w=np.asarray(ml_norm_w), gdn_conv_w=np.asarray(gdn_conv_w), gdn_A_log=np.asarray(gdn_A_log),
               gdn_dt_bias=np.asarray(gdn_dt_bias), gdn_norm_w=np.asarray(gdn_norm_w), ssm_conv_w=np.asarray(ssm_conv_w),
               ssm_conv_b=np.asarray(ssm_conv_b), ssm_A_log=np.asarray(ssm_A_log), ssm_dt_bias=np.asarray(ssm_dt_bias),
               ssm_D=np.asarray(ssm_D), ssm_norm_w=np.asarray(ssm_norm_w))
    ln_g = np.asarray(ln_g, np.float32)
    ln_b = np.asarray(ln_b, np.float32)
    xcur = np.ascontiguousarray(np.asarray(x, np.float32))
    B, S, D = xcur.shape
    if "layer" not in _PROGS:
        _PROGS["layer"] = build_layer(D=D, S=S)
        _PROGS["ln"] = build_reduce_ln(ROWS=B * S // 8, D=D)
    RPC = B * S // 8
    for l in range(DEPTH):
        packs = [pack_core(inp, l, g) for g in range(NG)]
        in_maps = [dict(packs[c % NG], x=xcur[c // NG]) for c in range(8)]
        res = run_bass_kernel_spmd(_PROGS["layer"], in_maps, core_ids=list(range(8)))
        yp = [r["ypart"] for r in res.results]
        del in_maps, packs
        xf = xcur.reshape(B * S, D)
        in_maps = []
        for c in range(8):
            b = (c * RPC) // S
            r0 = (c * RPC) % S
            m = {"xin": np.ascontiguousarray(xf[c * RPC:(c + 1) * RPC]), "lng": ln_g[l][None, :], "lnb": ln_b[l][None, :]}
            for j in range(NG):
                m[f"p{j}"] = np.ascontiguousarray(yp[b * NG + j][r0:r0 + RPC])
            in_maps.append(m)
        res = run_bass_kernel_spmd(_PROGS["ln"], in_maps, core_ids=list(range(8)))
        xcur = np.concatenate([r["out"] for r in res.results], axis=0).reshape(B, S, D)
    return xcur.astype(np.float32)
```

```python
from contextlib import ExitStack
import os
GSTOP = int(os.environ.get('GSTOP', '99'))
MERGE = os.environ.get('MERGE', '123')
import numpy as np
import concourse.bass as bass
import concourse.mybir as mybir
from concourse.bass_utils import run_bass_kernel_spmd

F32 = mybir.dt.float32
BF16 = mybir.dt.bfloat16
ALU = mybir.AluOpType
AF = mybir.ActivationFunctionType

SIG_ROT = 30000


class Prog:
    ENG = ("pe", "act", "dve", "pool", "sp")

    def __init__(self, nc, stack):
        self._stack = stack
        self.nc = nc
        self.ops = []
        self.lw = {}
        self.rs = {}
        self.dma_cnt = {}

    cap = None

    def capture(self, f, *a):
        assert self.cap is None
        self.cap = []
        f(*a)
        L, self.cap = self.cap, None
        return L

    def replay(self, L):
        for a in L:
            if a is not None:
                self.op(*a)

    def mark(self):
        if self.cap is not None:
            self.cap.append(None)

    def merge(self, A, B):
        A = [a for a in A if a is not None]
        B = [b for b in B if b is not None]
        ia = ib = 0
        while ia < len(A) or ib < len(B):
            if ib >= len(B) or (ia < len(A) and ia * len(B) <= ib * len(A)):
                self.op(*A[ia])
                ia += 1
            else:
                self.op(*B[ib])
                ib += 1

    def op(self, eng, fn, r=(), w=(), dma=None, inc=16):
        if self.cap is not None:
            self.cap.append((eng, fn, list(r), list(w), dma, inc))
            return
        r = [self.canon(k) for k in r]
        w = [self.canon(k) for k in w]
        i = len(self.ops)
        deps = set()
        raw = set()
        for k in r:
            j = self.lw.get(k)
            if j is not None:
                deps.add(j)
                raw.add(j)
        for k in w:
            j = self.lw.get(k)
            if j is not None:
                deps.add(j)
            for j in self.rs.get(k, ()):
                deps.add(j)
        keep = set()
        for j in deps:
            pj = self.ops[j]
            if pj["dma"] is None and dma is None and pj["eng"] == eng:
                if eng == "pe":
                    continue
            keep.add(j)
        o = dict(eng=eng, fn=fn, deps=keep, dma=dma, sig=False, sidx=None, inc=inc)
        if dma is not None:
            c = self.dma_cnt.get(dma, 0) + 1
            self.dma_cnt[dma] = c
            o["sidx"] = c
            o["sig"] = True
        self.ops.append(o)
        for j in keep:
            self.ops[j]["sig"] = True
        for k in r:
            self.rs.setdefault(k, []).append(i)
        for k in w:
            self.lw[k] = i
            self.rs[k] = []
        return i

    ALIAS = {"g_rn0": "g_rn", "s_rstd0": "s_rstd", "g_rstd0": "g_rstd", "m_rstd0": "m_rstd", "m_rden0": "m_rden",
             "m_W0": "m_W", "m_d1b": "m_d1", "m_a2": "m_a", "s_dt_b": "s_dt_a", "g_sp_b": "g_sp_a", "sUh": "Uh", "gUh": "Uh",
             "m_wi": "m_a", "m_emm": "m_l1", "m_nG": "m_t", "g_ktm": ("s_xtm", 0), "g_vtm": ("s_xtm", 1),
             "g_Rv": "s_xdt", "g_Rk": "s_xdt"}

    def canon(self, k):
        if isinstance(k, tuple):
            if k[0] == "cvp":
                return ("cv",) + k[1:]
            if k[0] in ("sU", "gU"):
                return ("U",) + k[1:]
            if k[0] in ("g_sz", "m_g"):
                return ("s_sz",) + k[1:]
            if k[0] == "m_zt":
                return ("s_sz", k[1], 2 + k[2])
            return k
        return self.ALIAS.get(k, k)

    def emit(self, final_wait_eng="sp"):
        nc = self.nc
        cnt = {e: 0 for e in self.ENG}
        for o in self.ops:
            if o["dma"] is None and o["sig"]:
                cnt[o["eng"]] += 1
                o["sidx"] = cnt[o["eng"]]
        sems = {}
        stack = self._stack
        for e in self.ENG:
            n = (cnt[e] + SIG_ROT - 1) // SIG_ROT
            for q in range(max(n, 1)):
                sems[("E", e, q)] = stack.enter_context(nc.semaphore(f"s_{e}_{q}"))
        for k in self.dma_cnt:
            nm = "d_" + "_".join(str(x) for x in (k if isinstance(k, tuple) else (k,)))
            sems[("D", k)] = stack.enter_context(nc.semaphore(nm))
        self.nsem = len(sems)

        def chan(o):
            if o["dma"] is not None:
                return ("D", o["dma"]), o["inc"] * o["sidx"]
            q, rr = divmod(o["sidx"] - 1, SIG_ROT)
            return ("E", o["eng"], q), rr + 1

        per = {e: [] for e in self.ENG}
        for o in self.ops:
            per[o["eng"]].append(o)
        final = {}
        for o in self.ops:
            if o["dma"] is not None:
                ch, v = chan(o)
                final[ch] = max(final.get(ch, 0), v)
        block = stack.enter_context(nc.Block())
        deco = {"pe": block.tensor, "act": block.scalar, "dve": block.vector,
                "pool": block.gpsimd, "sp": block.sync}
        ops = self.ops

        def make(e):
            def body(engobj):
                seen = {}
                for o in per[e]:
                    need = {}
                    for j in o["deps"]:
                        ch, v = chan(ops[j])
                        if v > need.get(ch, 0):
                            need[ch] = v
                    for ch, v in need.items():
                        if seen.get(ch, 0) >= v:
                            continue
                        engobj.wait_ge(sems[ch], v)
                        seen[ch] = v
                    ins = o["fn"](engobj)
                    if o["sig"]:
                        ch, v = chan(o)
                        ins.then_inc(sems[ch], o["inc"] if o["dma"] is not None else 1)
                if e == final_wait_eng:
                    for ch, v in final.items():
                        if seen.get(ch, 0) < v:
                            engobj.wait_ge(sems[ch], v)
            return body

        for e in self.ENG:
            if per[e] or e == final_wait_eng:
                deco[e](make(e))


D_MODEL = 4096
SEQ = 4096
BATCH = 2
DEPTH = 2
NG = 4
WT = 256
NT = 28
NCOL = NT * WT
MIXW = 2048
RMS_EPS = 1e-6
LN_EPS = 1e-5
ALPHA = (2 * DEPTH) ** 0.25

PR_SDTB, PR_SALOG, PR_SD, PR_GDTB, PR_GALOG, PR_MIB, PR_MFB = 0, 16, 32, 48, 52, 56, 57
PR_MLNW, PR_GNW, PR_SNW = 64, 64 + 512, 64 + 512 + 128
PR_N = 64 + 512 + 128 + 1024
C_ID, C_TRI, C_U, C_ONES, C_BDS, C_BD, C_OFFT, C_NEG = range(8)
NCONST = 8


def make_consts():
    i = np.arange(128)
    k, l = i[:, None], i[None, :]
    c = np.zeros((NCONST, 128, 128), np.float32)
    c[C_ID] = np.eye(128)
    c[C_TRI] = (k <= l)
    c[C_U] = (k > l)
    c[C_ONES] = 1.0
    same = (k // 64) == (l // 64)
    c[C_BDS] = (k > l) & same
    c[C_BD] = same
    c[C_OFFT] = (k < 64) & (l >= 64)
    c[C_NEG] = np.where(k > l, -30000.0, 0.0)
    return np.ascontiguousarray(c.transpose(1, 0, 2).reshape(128, NCONST * 128))


def build_layer(D=D_MODEL, S=SEQ, TB=256, debug=False, parts=("ssm", "gdn", "ml")):
    KC = D // 128
    NB = S // TB
    NTT = TB // 128
    nc = bass.Bass("TRN2", target_bir_lowering=False)
    x_d = nc.dram_tensor("x", [S, D], F32, kind="ExternalInput").ap()
    wcat_d = nc.dram_tensor("wcat", [D, NCOL], F32, kind="ExternalInput").ap()
    wout_d = nc.dram_tensor("wout", [MIXW, D], F32, kind="ExternalInput").ap()
    convp_d = nc.dram_tensor("convp", [128, 24 * 5], F32, kind="ExternalInput").ap()
    prow_d = nc.dram_tensor("prow", [1, PR_N], F32, kind="ExternalInput").ap()
    const_d = nc.dram_tensor("consts", [128, NCONST * 128], F32, kind="ExternalInput").ap()
    ypart_d = nc.dram_tensor("ypart", [S, D], F32, kind="ExternalOutput").ap()
    wscr_d = nc.dram_tensor("wscr", [NT, 128, KC, WT], BF16, kind="Internal").ap()
    NOC = D // WT
    woscr_d = nc.dram_tensor("woscr", [NOC, 128, 16, WT], BF16, kind="Internal").ap()
    if debug:
        ymix_d = nc.dram_tensor("ymix", [16, 128, S], BF16, kind="ExternalOutput").ap()

    with ExitStack() as st:
        def T(name, shape, dt=F32):
            return st.enter_context(nc.sbuf_tensor("sb_" + name, shape, dt))
        P = Prog(nc, st)
        psb = [st.enter_context(nc.psum_tensor(f"psb{i}", [128, 512], F32)) for i in range(8)]

        class PSA:
            def __init__(self, banks):
                self.banks = banks
                self.pos = 0

            def get(self, nq):
                bank = self.banks[self.pos % len(self.banks)]
                self.pos += 1
                return psb[bank][:, 0:nq * 128], [("ps", bank)]
        ps_proj = PSA([0, 1])
        ps_mix = PSA([2, 3, 4, 5, 6, 7])

        consts = T("consts", [128, NCONST, 128])
        cid = consts[:, C_ID, :]
        ctri = consts[:, C_TRI, :]
        cU = consts[:, C_U, :]
        cones = consts[:, C_ONES, :]
        idb = T("idb", [128, 128], BF16)
        prow = T("prow", [128, PR_N])
        convp = T("convp", [128, 24, 5])
        nA_s = T("nA_s", [128, 16])
        nA_g = T("nA_g", [128, 4])
        XLW = 512 if D >= 512 else D
        xld = [T(f"xld{i}", [128, XLW]) for i in range(2)]
        xT = T("xT", [128, KC, TB], BF16)
        wt = [T(f"wt{i}", [128, max(KC, 16), WT], BF16) for i in range(2)]
        CK = 2 if KC >= 2 else 1
        assert CK == 2 and WT == 256
        HS = T("HS", [128, 9216])
        bar = T("bar", [128, 1])
        cin = [HS[:, i * 512:(i + 1) * 512].rearrange("p (k c) -> p k c", k=CK) for i in range(2)]
        cout = [HS[:, 1024 + i * 256:1024 + (i + 1) * 256].bitcast(BF16).rearrange("p (k c) -> p k c", k=CK) for i in range(2)]
        HS_KEYS = [("cin", 0), ("cin", 1), ("cout", 0), ("cout", 1), "s_gu", ("s_mte", 0), ("s_mte", 1), "s_xw"]
        for h_ in range(4):
            HS_KEYS += [(k_, h_) for k_ in ("hA", "hB", "hC", "hD", "hA0a", "hA0b", "hAToff", "hattn", "huw")]
            HS_KEYS += [(k_, h_, i_) for k_ in ("hPP", "hTT") for i_ in range(2)]

        def barrier():
            P.op("pool", lambda e: e.memset(bar[:], 0.0), w=HS_KEYS)
        U = T("U", [128, 12, 3 + TB])
        hist_s = T("hist_s", [128, 12, 3])
        hist_g = T("hist_g", [128, 12, 3])
        cv = T("cv", [128, 12, TB])
        ymixT = T("ymixT", [128, 16, TB], BF16)
        stage = [T(f"stage{i}", [128, WT]) for i in range(2)]
        junk = T("junk", [128, 512])
        epsc = T("epsc", [128, 1])

        P.op("sp", lambda e: e.dma_start(out=consts[:], in_=const_d.rearrange("p (c k) -> p c k", c=NCONST)), w=["consts"], dma="consts")
        P.op("sp", lambda e: e.dma_start(out=prow[:], in_=prow_d.partition_broadcast(128)[:, 0, :]), w=["prow"], dma="prow")
        P.op("sp", lambda e: e.dma_start(out=convp[:], in_=convp_d.rearrange("p (c k) -> p c k", c=24)), w=["convp"], dma="convp")
        P.op("dve", lambda e: e.tensor_copy(out=idb[:], in_=cid), r=["consts"], w=["idb"])
        P.op("pool", lambda e: e.memset(epsc[:], RMS_EPS), w=["epsc"])
        P.op("act", lambda e: e.activation(out=nA_s[:], in_=prow[:, PR_SALOG:PR_SALOG + 16], func=AF.Exp), r=["prow"], w=["nA_s"])
        P.op("dve", lambda e: e.tensor_scalar(out=nA_s[:], in0=nA_s[:], scalar1=-1.0, scalar2=None, op0=ALU.mult), r=["nA_s"], w=["nA_s"])
        P.op("act", lambda e: e.activation(out=nA_g[:], in_=prow[:, PR_GALOG:PR_GALOG + 4], func=AF.Exp), r=["prow"], w=["nA_g"])
        P.op("dve", lambda e: e.tensor_scalar(out=nA_g[:], in0=nA_g[:], scalar1=-1.0, scalar2=None, op0=ALU.mult), r=["nA_g"], w=["nA_g"])

        wv = wcat_d.rearrange("(kc p) c -> p kc c", p=128)
        wov = wout_d.rearrange("(kc p) c -> p kc c", p=128)
        PK = 8 if KC >= 8 else KC
        NPW = KC // PK
        NPO = 16 // PK if PK <= 16 else 1
        order = [8, 9, 10, 11, 12, 13, 27, 22, 23, 24, 25, 2, 3, 4, 5, 6, 7, 20, 21, 0, 1, 14, 15, 18, 19, 26, 16, 17]
        assert sorted(order) == list(range(NT))
        for t in order:
            for pi in range(NPW):
                k0 = pi * PK
                P.op("pool", lambda e, t=t, k0=k0: e.dma_start(out=wscr_d[t][:, k0:k0 + PK, :], in_=wv[:, k0:k0 + PK, t * WT:(t + 1) * WT]),
                     w=[("wscr", t, pi)], dma=("prew", t))
        for t in range(NOC):
            for pi in range(NPO):
                k0 = pi * PK
                P.op("pool", lambda e, t=t, k0=k0: e.dma_start(out=woscr_d[t][:, k0:k0 + PK, :], in_=wov[:, k0:k0 + PK, t * WT:(t + 1) * WT]),
                     w=[("woscr", t, pi)], dma=("preo", t))

        barrier()
        wslot = [0]

        def load_w(t):
            s = wslot[0] % 2
            wslot[0] += 1
            P.op("sp", lambda e: e.dma_start(out=wt[s][:, 0:KC, :], in_=wscr_d[t]), r=[("wscr", t, pi) for pi in range(NPW)], w=[("wt", s)], dma=("wt", s))
            return s

        def load_wo(t):
            s = wslot[0] % 2
            wslot[0] += 1
            P.op("sp", lambda e: e.dma_start(out=wt[s][:, 0:16, :], in_=woscr_d[t]), r=[("woscr", t, pi) for pi in range(NPO)], w=[("wt", s)], dma=("wt", s))
            return s

        def proj_fm(s, j, M=128, c0=None):
            ps, keys = ps_proj.get(2)
            cc = j * 128 if c0 is None else c0

            def fn(e):
                for kc in range(KC):
                    ins = e.matmul(ps[0:M, 0:TB], lhsT=wt[s][:, kc, cc:cc + M], rhs=xT[:, kc, :], start=(kc == 0), stop=(kc == KC - 1))
                return ins
            P.op("pe", fn, r=[("wt", s), "xT"], w=keys)
            return ps, keys

        def proj_tm(s, tt):
            ps, keys = ps_proj.get(2)

            def fn(e):
                for kc in range(KC):
                    ins = e.matmul(ps[:, 0:WT], lhsT=xT[:, kc, tt * 128:(tt + 1) * 128], rhs=wt[s][:, kc, :], start=(kc == 0), stop=(kc == KC - 1))
                return ins
            P.op("pe", fn, r=[("wt", s), "xT"], w=keys)
            return ps, keys

        def load_xT(tb):
            n = 0
            for tt in range(NTT):
                r0 = tb * TB + tt * 128
                for c0 in range(0, D, XLW):
                    s = n % 2
                    n += 1
                    P.op("sp", lambda e, s=s, r0=r0, c0=c0: e.dma_start(out=xld[s][:], in_=x_d[r0:r0 + 128, c0:c0 + XLW]),
                         w=[("xld", s)], dma=("xld", s))
                    for q0 in range(0, XLW, 512):
                        nq = min(4, (XLW - q0) // 128)
                        ps, keys = ps_mix.get(nq)

                        def fn(e, s=s, q0=q0, nq=nq, ps=ps):
                            for q in range(nq):
                                ins = e.transpose(out=ps[:, q * 128:(q + 1) * 128], in_=xld[s][:, q0 + q * 128:q0 + (q + 1) * 128], identity=cid)
                            return ins
                        P.op("pe", fn, r=[("xld", s), "consts"], w=keys)
                        kc0 = (c0 + q0) // 128
                        P.op("act", lambda e, ps=ps, kc0=kc0, nq=nq, tt=tt: e.activation(
                            out=xT[:, kc0:kc0 + nq, tt * 128:(tt + 1) * 128],
                            in_=ps[:, 0:nq * 128].rearrange("p (q t) -> p q t", q=nq), func=AF.Copy),
                            r=keys, w=["xT"])

        def conv_blocks(Ubuf, hist, cbase, nblk, key):
            P.op("pool", lambda e: e.tensor_copy(out=Ubuf[:, 0:nblk, 0:3], in_=hist[:, 0:nblk, :]), r=[key + "hist"], w=[key + "Uh"])
            for b in range(nblk):
                eng = "dve"
                rk = [(key + "U", b), key + "Uh", "convp"]

                P.op(eng, lambda e, b=b: e.tensor_scalar(out=cv[:, b, :], in0=Ubuf[:, b, 3:3 + TB], scalar1=convp[:, cbase + b, 3:4], scalar2=convp[:, cbase + b, 4:5],
                                                         op0=ALU.mult, op1=ALU.add), r=rk, w=[("cvp", b)])
                for k in range(3):
                    P.op(eng, lambda e, b=b, k=k: e.scalar_tensor_tensor(out=cv[:, b, :], in0=Ubuf[:, b, k:k + TB], scalar=convp[:, cbase + b, k:k + 1], in1=cv[:, b, :],
                                                                         op0=ALU.mult, op1=ALU.add), r=rk + [("cvp", b)], w=[("cvp", b)])
                P.op("act", lambda e, b=b: e.activation(out=cv[:, b, :], in_=cv[:, b, :], func=AF.Silu), r=[("cvp", b)], w=[("cv", b)])
            P.op("pool", lambda e: e.tensor_copy(out=hist[:, 0:nblk, :], in_=Ubuf[:, 0:nblk, TB:TB + 3]),
                 r=[(key + "U", b) for b in range(nblk)] + [key + "Uh"], w=[key + "hist"])

        def softplus_(out, xin, nparts, shape_keys_r, wkey, tmpa, tmpb):
            P.op("act", lambda e: e.activation(out=tmpa, in_=xin, func=AF.Abs), r=shape_keys_r, w=[wkey + "_a"])
            P.op("act", lambda e: e.activation(out=tmpa, in_=tmpa, func=AF.Exp, scale=-1.0), r=[wkey + "_a"], w=[wkey + "_b"])
            P.op("act", lambda e: e.activation(out=tmpb, in_=tmpa, func=AF.Ln, bias=1.0), r=[wkey + "_b"], w=[wkey + "_c"])
            P.op("dve", lambda e: e.scalar_tensor_tensor(out=out, in0=xin, scalar=0.0, in1=tmpb, op0=ALU.max, op1=ALU.add),
                 r=list(shape_keys_r) + [wkey + "_c"], w=[wkey])

        def transpose_out_bf(src_bf, ncb, blk0, c, rkeys):
            ps, keys = ps_mix.get((ncb + 1) // 2)
            psv = ps.bitcast(BF16)

            def fn(e):
                for i in range(ncb):
                    ins = e.transpose(out=psv[:, i * 128:(i + 1) * 128], in_=src_bf[:, i * 128:(i + 1) * 128], identity=idb[:])
                return ins
            P.op("pe", fn, r=list(rkeys) + ["idb"], w=keys)
            P.op("act", lambda e: e.activation(out=ymixT[:, blk0:blk0 + ncb, c * 128:(c + 1) * 128],
                                                in_=psv[:, 0:ncb * 128].rearrange("p (b t) -> p b t", b=ncb), func=AF.Copy),
                 r=keys, w=[("ymixT", blk0)])

        s_small = T("s_small", [128, NTT, 32])
        big_tm = T("big_tm", [128, NTT, 1024])
        s_xtm = T("s_xtm", [128, 1024])
        s_xdt = T("s_xdt", [128, 1024])
        if "ssm" in parts:
            s_sz = big_tm
            s_dt = T("s_dt", [128, NTT, 16])
            s_a = T("s_a", [128, NTT, 16])
            s_t1 = T("s_t1", [128, NTT, 16])
            s_t2 = T("s_t2", [128, NTT, 16])
            s_xx = T("s_xx", [128, NTT, 16])
            s_state = T("s_state", [128, 2, 512])
            s_pre = T("s_pre", [128, 48])
            s_ex = T("s_ex", [128, 48])
            s_xw = HS[:, 2048:3072]
            s_btm = T("s_btm", [128, 2, 128])
            s_gu = HS[:, 0:1024].rearrange("p (h l) -> p h l", h=8)
            s_mt = HS[:, 1024:2048].rearrange("p (h l) -> p h l", h=8)
            s_cbm = T("s_cbm", [128, 128])
            s_y = T("s_y", [128, 512])
            s_ybf = T("s_ybf", [128, 512], BF16)
            s_ssq = T("s_ssq", [128, 1])
            s_rstd = T("s_rstd", [128, 1])
            P.op("pool", lambda e: e.memset(s_state[:], 0.0), w=["s_state0", "s_state1"])
            P.op("pool", lambda e: e.memset(hist_s[:], 0.0), w=["shist"])

        def ssm_proj(tb, part="all"):
            for ti, t in enumerate(range(8, 14) if part in ("all", "fm") else ()):
                s = load_w(t)
                for j in range(2):
                    ps, keys = proj_fm(s, j)
                    b = ti * 2 + j
                    P.op("act", lambda e, ps=ps, b=b: e.activation(out=U[:, b, 3:3 + TB], in_=ps[:, 0:TB], func=AF.Copy), r=keys, w=[("sU", b)])
            for ti, t in enumerate(range(22, 26) if part in ("all", "z") else ()):
                s = load_w(t)
                for tt in range(NTT):
                    ps, keys = proj_tm(s, tt)
                    P.op("act", lambda e, ps=ps, ti=ti, tt=tt: e.activation(out=s_sz[:, tt, ti * WT:(ti + 1) * WT], in_=ps[:, 0:WT], func=AF.Silu),
                         r=keys, w=[("s_sz", tt, ti)])

        def small_proj(tb):
            s = load_w(27)
            for tt in range(NTT):
                ps, keys = proj_tm(s, tt)
                P.op("dve", lambda e, ps=ps, tt=tt: e.tensor_copy(out=s_small[:, tt, :], in_=ps[:, 0:32]), r=keys, w=[("small", tt)])
            return s

        def ssm_mixer(tb):
            barrier()
            conv_blocks(U, hist_s, 12, 12, "s")
            P.mark()
            smk = [("small", tt) for tt in range(NTT)]
            P.op("dve", lambda e: e.tensor_tensor(out=s_xx[:], in0=s_small[:, :, 8:24],
                                                  in1=prow[:, PR_SDTB:PR_SDTB + 16].unsqueeze(1).broadcast_to([128, NTT, 16]), op=ALU.add),
                 r=smk + ["prow"], w=["s_xx"])
            softplus_(s_dt[:], s_xx[:], 128, ["s_xx"], "s_dt", s_t1[:], s_t2[:])
            P.op("dve", lambda e: e.tensor_tensor(out=s_a[:], in0=s_dt[:], in1=nA_s[:].unsqueeze(1).broadcast_to([128, NTT, 16]), op=ALU.mult),
                 r=["s_dt", "nA_s"], w=["s_a"])
            for c in range(NTT):
                csl = slice(c * 128, (c + 1) * 128)
                ps, keys = ps_mix.get(1)

                def fn(e, ps=ps, c=c):
                    e.matmul(ps[:, 0:16], lhsT=ctri, rhs=s_a[:, c, :], start=True, stop=True)
                    return e.matmul(ps[:, 16:32], lhsT=cones, rhs=s_a[:, c, :], start=True, stop=True)
                P.op("pe", fn, r=["consts", "s_a"], w=keys)
                P.op("act", lambda e, ps=ps: e.activation(out=s_pre[:, 0:32], in_=ps[:, 0:32], func=AF.Copy), r=keys, w=["s_pre"])
                P.op("dve", lambda e: e.tensor_tensor(out=s_pre[:, 32:48], in0=s_pre[:, 16:32], in1=s_pre[:, 0:16], op=ALU.subtract), r=["s_pre"], w=["s_pre2"])
                P.op("act", lambda e: e.activation(out=s_ex[:], in_=s_pre[:], func=AF.Exp), r=["s_pre", "s_pre2"], w=["s_ex"])
                for half in range(2):
                    ps, keys = ps_mix.get(4)

                    def fn(e, ps=ps, half=half, csl=csl):
                        for q in range(4):
                            ins = e.transpose(out=ps[:, q * 128:(q + 1) * 128], in_=cv[:, half * 4 + q, csl], identity=cid)
                        return ins
                    P.op("pe", fn, r=[("cv", half * 4 + q) for q in range(4)] + ["consts"], w=keys)
                    P.op("act", lambda e, ps=ps, half=half: e.activation(out=s_xtm[:, half * 512:(half + 1) * 512], in_=ps[:, 0:512], func=AF.Copy),
                         r=keys, w=[("s_xtm", half)])
                ps, keys = ps_mix.get(2)

                def fn(e, ps=ps, csl=csl):
                    e.transpose(out=ps[:, 0:128], in_=cv[:, 8, csl], identity=cid)
                    return e.transpose(out=ps[:, 128:256], in_=cv[:, 9, csl], identity=cid)
                P.op("pe", fn, r=[("cv", 8), ("cv", 9), "consts"], w=keys)
                P.op("dve", lambda e, ps=ps: e.tensor_copy(out=s_btm[:], in_=ps[:, 0:256].rearrange("p (g n) -> p g n", g=2)), r=keys, w=["s_btm"])
                xk = [("s_xtm", 0), ("s_xtm", 1)]
                v3 = lambda t_: t_[:, 0:1024].rearrange("p (h q) -> p h q", h=16)
                P.op("pool", lambda e, c=c: e.tensor_tensor(out=v3(s_xdt), in0=v3(s_xtm), in1=s_dt[:, c, :].unsqueeze(2).broadcast_to([128, 16, 64]), op=ALU.mult),
                     r=xk + ["s_dt"], w=["s_xdt"])
                P.op("pool", lambda e: e.tensor_tensor(out=v3(s_xw), in0=v3(s_xdt), in1=s_ex[:, 32:48].unsqueeze(2).broadcast_to([128, 16, 64]), op=ALU.mult),
                     r=["s_xdt", "s_ex"], w=["s_xw"])
                for gi in range(2):
                    ps, keys = ps_mix.get(1)
                    P.op("pe", lambda e, ps=ps, gi=gi, csl=csl: e.matmul(ps[:, 0:128], lhsT=cv[:, 8 + gi, csl], rhs=cv[:, 10 + gi, csl], start=True, stop=True),
                         r=[("cv", 8 + gi), ("cv", 10 + gi)], w=keys)
                    P.op("dve", lambda e, ps=ps: e.tensor_tensor(out=s_cbm[:], in0=ps[:, 0:128], in1=ctri, op=ALU.mult), r=keys + ["consts"], w=["s_cbm"])
                    P.op("pool", lambda e, c=c, gi=gi: e.tensor_tensor(out=s_gu[:], in0=s_a[:, c, gi * 8:(gi + 1) * 8].unsqueeze(2).broadcast_to([128, 8, 128]),
                                                                       in1=cU.unsqueeze(1).broadcast_to([128, 8, 128]), op=ALU.mult),
                         r=["s_a", "consts"], w=["s_gu"])
                    for hh in range(2):
                        ps, keys = ps_mix.get(4)

                        def fn(e, ps=ps, hh=hh):
                            for q in range(4):
                                ins = e.matmul(ps[:, q * 128:(q + 1) * 128], lhsT=s_gu[:, hh * 4 + q, :], rhs=ctri, start=True, stop=True)
                            return ins
                        P.op("pe", fn, r=["s_gu", "consts"], w=keys)
                        P.op("act", lambda e, ps=ps, hh=hh: e.activation(out=s_mt[:, hh * 4:(hh + 1) * 4, :], in_=ps[:, 0:512].rearrange("p (h l) -> p h l", h=4), func=AF.Exp),
                             r=keys, w=[("s_mte", hh)])
                    P.op("dve", lambda e: e.tensor_tensor(out=s_mt[:], in0=s_mt[:], in1=s_cbm[:].unsqueeze(1).broadcast_to([128, 8, 128]), op=ALU.mult),
                         r=[("s_mte", 0), ("s_mte", 1), "s_cbm"], w=[("s_mte", 0), ("s_mte", 1)])
                    psd, kd = ps_mix.get(4)

                    def fn(e, psd=psd, gi=gi):
                        for h in range(8):
                            hg = gi * 8 + h
                            ins = e.matmul(psd[:, h * 64:(h + 1) * 64], lhsT=s_mt[:, h, :], rhs=s_xdt[:, hg * 64:(hg + 1) * 64], start=True, stop=True)
                        return ins
                    P.op("pe", fn, r=[("s_mte", 0), ("s_mte", 1), "s_xdt"], w=kd)
                    pso, ko = ps_mix.get(4)
                    P.op("pe", lambda e, pso=pso, gi=gi, csl=csl: e.matmul(pso[:, 0:512], lhsT=cv[:, 10 + gi, csl], rhs=s_state[:, gi, :], start=True, stop=True),
                         r=[("cv", 10 + gi), f"s_state{gi}"], w=ko)
                    y3 = s_y[:].rearrange("p (h q) -> p h q", h=8)
                    P.op("dve", lambda e, pso=pso, gi=gi: e.tensor_tensor(out=y3, in0=pso[:, 0:512].rearrange("p (h q) -> p h q", h=8),
                                                                          in1=s_ex[:, gi * 8:(gi + 1) * 8].unsqueeze(2).broadcast_to([128, 8, 64]), op=ALU.mult),
                         r=ko + ["s_ex"], w=["s_y"])
                    P.op("pool", lambda e, gi=gi: e.tensor_tensor(out=junk[:].rearrange("p (h q) -> p h q", h=8), in0=s_xtm[:, gi * 512:(gi + 1) * 512].rearrange("p (h q) -> p h q", h=8),
                                                                  in1=prow[:, PR_SD + gi * 8:PR_SD + (gi + 1) * 8].unsqueeze(2).broadcast_to([128, 8, 64]), op=ALU.mult),
                         r=[("s_xtm", gi), "prow"], w=["junk"])
                    P.op("pool", lambda e, gi=gi: e.tensor_tensor(out=s_y[:], in0=s_y[:], in1=junk[:], op=ALU.add), r=["s_y", "junk"], w=["s_y"])
                    P.op("dve", lambda e, psd=psd: e.tensor_tensor(out=s_y[:], in0=s_y[:], in1=psd[:, 0:512], op=ALU.add), r=kd + ["s_y"], w=["s_y"])
                    P.op("pool", lambda e, c=c, gi=gi: e.tensor_tensor(out=s_y[:], in0=s_y[:], in1=s_sz[:, c, gi * 512:(gi + 1) * 512], op=ALU.mult),
                         r=["s_y", ("s_sz", c, 2 * gi), ("s_sz", c, 2 * gi + 1)], w=["s_y"])
                    P.op("act", lambda e: e.activation(out=junk[:], in_=s_y[:], func=AF.Square, accum_out=s_ssq[:]), r=["s_y"], w=["junk", "s_ssq"])
                    P.op("act", lambda e: e.activation(out=s_rstd[:], in_=s_ssq[:], func=AF.Sqrt, scale=1.0 / 512, bias=epsc[:]), r=["s_ssq", "epsc"], w=["s_rstd0"])
                    P.op("dve", lambda e: e.reciprocal(out=s_rstd[:], in_=s_rstd[:]), r=["s_rstd0"], w=["s_rstd"])
                    P.op("dve", lambda e, gi=gi: e.scalar_tensor_tensor(out=s_ybf[:], in0=s_y[:], scalar=s_rstd[:], in1=prow[:, PR_SNW + gi * 512:PR_SNW + (gi + 1) * 512],
                                                                        op0=ALU.mult, op1=ALU.mult), r=["s_y", "s_rstd", "prow"], w=["s_ybf"])
                    transpose_out_bf(s_ybf, 4, 8 + gi * 4, c, ["s_ybf"])
                    pss, ks = ps_mix.get(4)
                    P.op("pe", lambda e, pss=pss, gi=gi: e.matmul(pss[:, 0:512], lhsT=s_btm[:, gi, :], rhs=s_xw[:, gi * 512:(gi + 1) * 512], start=True, stop=True),
                         r=["s_btm", "s_xw"], w=ks)
                    st3 = s_state[:, gi, :].rearrange("p (h q) -> p h q", h=8)
                    P.op("pool", lambda e, st3=st3, gi=gi: e.tensor_tensor(out=st3, in0=st3, in1=s_ex[:, 16 + gi * 8:16 + (gi + 1) * 8].unsqueeze(2).broadcast_to([128, 8, 64]), op=ALU.mult),
                         r=[f"s_state{gi}", "s_ex"], w=[f"s_state{gi}"])
                    P.op("dve", lambda e, pss=pss, gi=gi: e.tensor_tensor(out=s_state[:, gi, :], in0=s_state[:, gi, :], in1=pss[:, 0:512], op=ALU.add),
                         r=ks + [f"s_state{gi}"], w=[f"s_state{gi}"])

        if "gdn" in parts:
            g_sz = big_tm[:, :, 0:512]
            g_sq = T("g_sq", [128, TB])
            g_rn = T("g_rn", [128, TB])
            g_beta = T("g_beta", [128, NTT, 4])
            g_xx = T("g_xx", [128, NTT, 4])
            g_t1s = T("g_t1s", [128, NTT, 4])
            g_t2s = T("g_t2s", [128, NTT, 4])
            g_sp = T("g_sp", [128, NTT, 4])
            g_g = T("g_g", [128, NTT, 4])
            g_pre = T("g_pre", [128, 12])
            g_ex = T("g_ex", [128, 12])
            g_bg = T("g_bg", [128, 4])
            g_ktm = s_xtm[:, 0:512].rearrange("p (h d) -> p h d", h=4)
            g_vtm = s_xtm[:, 512:1024].rearrange("p (h d) -> p h d", h=4)
            g_R = s_xdt[:].rearrange("p (h c) -> p h c", h=4)
            g_kd = T("g_kd", [128, 4, 128])
            g_H = []
            for h_ in range(4):
                b0 = h_ * 2304
                v2 = lambda o_: HS[:, b0 + o_:b0 + o_ + 256].rearrange("p (a l) -> p a l", a=2)
                v1 = lambda o_: HS[:, b0 + o_:b0 + o_ + 128]
                g_H.append(dict(A=v2(0), B=v2(256), C=v1(512), D=v1(640), A0=v2(768), AToff=v1(1024), attnT=v1(1152),
                                PP=[v2(1280), v2(1536)], TT=[v1(1792), v1(1920)], uw=v2(2048)))
            g_o = T("g_o", [128, 4, 128])
            g_o2 = T("g_o2", [128, 4, 128])
            g_ssq = T("g_ssq", [128, 4])
            g_rstd = T("g_rstd", [128, 4])
            g_obf = T("g_obf", [128, 512], BF16)
            g_S = T("g_S", [128, 4, 128])
            P.op("pool", lambda e: e.memset(g_S[:], 0.0), w=[("g_S", h) for h in range(4)])
            P.op("pool", lambda e: e.memset(hist_g[:], 0.0), w=["ghist"])
            cBDm = consts[:, C_BD, :]
            cOFFT = consts[:, C_OFFT, :]
            cNEG = consts[:, C_NEG, :]

        def gdn_proj(tb, part="all"):
            for ti, t in enumerate(range(2, 8) if part in ("all", "fm") else ()):
                s = load_w(t)
                for j in range(2):
                    ps, keys = proj_fm(s, j)
                    b = ti * 2 + j
                    P.op("act", lambda e, ps=ps, b=b: e.activation(out=U[:, b, 3:3 + TB], in_=ps[:, 0:TB], func=AF.Copy), r=keys, w=[("gU", b)])
            for ti, t in enumerate(range(20, 22) if part in ("all", "z") else ()):
                s = load_w(t)
                for tt in range(NTT):
                    ps, keys = proj_tm(s, tt)
                    P.op("act", lambda e, ps=ps, ti=ti, tt=tt: e.activation(out=g_sz[:, tt, ti * WT:(ti + 1) * WT], in_=ps[:, 0:WT], func=AF.Silu),
                         r=keys, w=[("g_sz", tt, ti)])

        def head_gen(c, csl, h):
            Hh = g_H[h]
            kA, kB, kC, kD = ("hA", h), ("hB", h), ("hC", h), ("hD", h)
            kA0a, kA0b, kAo, kat, kuw = ("hA0a", h), ("hA0b", h), ("hAToff", h), ("hattn", h), ("huw", h)
            GU, E_, t1, Af, A0, AToff, attnT, PP, TT, uw = (Hh[k_] for k_ in ("A", "B", "C", "D", "A0", "AToff", "attnT", "PP", "TT", "uw"))
            X1 = GU.rearrange("p a l -> p (a l)")
            X2 = E_.rearrange("p a l -> p (a l)")
            vn, tmp_ = t1, Af
            gcol = g_g[:, c, h:h + 1]
            P.op("dve", lambda e: e.tensor_scalar(out=GU[:, 0, :], in0=cU, scalar1=gcol, scalar2=None, op0=ALU.mult), r=["consts", "g_g"], w=[kA])
            yield
            P.op("dve", lambda e: e.tensor_scalar(out=GU[:, 1, :], in0=ctri, scalar1=gcol, scalar2=None, op0=ALU.mult), r=["consts", "g_g"], w=[kA])
            yield
            ps, keys = ps_mix.get(4)
            kT = cv[:, 4 + h, csl]
            qT_ = cv[:, h, csl]

            def fn(e, ps=ps):
                e.matmul(ps[:, 0:128], lhsT=GU[:, 0, :], rhs=ctri, start=True, stop=False)
                e.matmul(ps[:, 0:128], lhsT=cid, rhs=cNEG, start=False, stop=True)
                e.matmul(ps[:, 128:256], lhsT=GU[:, 1, :], rhs=cU, start=True, stop=True)
                e.matmul(ps[:, 256:384], lhsT=kT, rhs=kT, start=True, stop=True)
                return e.matmul(ps[:, 384:512], lhsT=kT, rhs=qT_, start=True, stop=True)
            P.op("pe", fn, r=[kA, "consts", ("cv", 4 + h), ("cv", h)], w=keys)
            yield
            P.op("act", lambda e, ps=ps: e.activation(out=E_[:], in_=ps[:, 0:256].rearrange("p (a l) -> p a l", a=2), func=AF.Exp), r=keys, w=[kB])
            yield
            P.op("dve", lambda e, ps=ps: e.tensor_tensor(out=t1[:], in0=ps[:, 256:384], in1=E_[:, 1, :], op=ALU.mult), r=keys + [kB], w=[kC])
            yield
            P.op("dve", lambda e, ps=ps: e.tensor_tensor(out=attnT[:], in0=ps[:, 384:512], in1=E_[:, 0, :], op=ALU.mult), r=keys + [kB], w=[kat])
            yield
            P.op("dve", lambda e: e.scalar_tensor_tensor(out=Af[:], in0=t1[:], scalar=g_beta[:, c, h:h + 1], in1=cU, op0=ALU.mult, op1=ALU.mult),
                 r=[kC, "g_beta", "consts"], w=[kD])
            yield
            ps, keys = ps_mix.get(1)
            P.op("pe", lambda e, ps=ps: e.transpose(out=ps[:, 0:128], in_=Af[:], identity=cid), r=[kD, "consts"], w=keys)
            yield
            P.op("pool", lambda e: e.tensor_tensor(out=A0[:, 0, :], in0=Af[:], in1=cBDm, op=ALU.mult), r=[kD, "consts"], w=[kA0a])
            yield
            P.op("dve", lambda e, ps=ps: e.tensor_tensor(out=A0[:, 1, :], in0=ps[:, 0:128], in1=cBDm, op=ALU.mult), r=keys + ["consts"], w=[kA0b])
            yield
            P.op("dve", lambda e, ps=ps: e.tensor_tensor(out=AToff[:], in0=ps[:, 0:128], in1=cOFFT, op=ALU.mult), r=keys + ["consts"], w=[kAo])
            yield
            P.op("pool", lambda e: e.tensor_tensor(out=TT[0][:], in0=cid, in1=A0[:, 1, :], op=ALU.subtract), r=[kA0b, "consts"], w=[("hTT", h, 0)])
            yield
            Pc, Pk = A0, [kA0a, kA0b]
            tcur = 0
            for k in range(1, 6):
                ps, keys = ps_mix.get(2)

                def fn(e, ps=ps, Pc=Pc, k=k):
                    ins = e.matmul(ps[:, 0:128], lhsT=Pc[:, 1, :], rhs=Pc[:, 0, :], start=True, stop=True)
                    if k < 5:
                        ins = e.matmul(ps[:, 128:256], lhsT=Pc[:, 0, :], rhs=Pc[:, 1, :], start=True, stop=True)
                    return ins
                P.op("pe", fn, r=Pk, w=keys)
                yield
                Pn = PP[k % 2]
                nk = [("hPP", h, k % 2)]
                P.op("act", lambda e, ps=ps, Pn=Pn: e.activation(out=Pn[:], in_=ps[:, 0:256].rearrange("p (a l) -> p a l", a=2), func=AF.Copy), r=keys, w=nk)
                yield
                ps, keys = ps_mix.get(1)
                P.op("pe", lambda e, ps=ps, Pn=Pn, tcur=tcur: e.matmul(ps[:, 0:128], lhsT=Pn[:, 0, :], rhs=TT[tcur][:], start=True, stop=True),
                     r=nk + [("hTT", h, tcur)], w=keys)
                yield
                P.op("dve", lambda e, ps=ps, tcur=tcur: e.tensor_tensor(out=TT[1 - tcur][:], in0=TT[tcur][:], in1=ps[:, 0:128], op=ALU.add),
                     r=keys + [("hTT", h, tcur)], w=[("hTT", h, 1 - tcur)])
                yield
                tcur = 1 - tcur
                Pc, Pk = Pn, nk
            TTf = TT[tcur]
            tk = [("hTT", h, tcur)]
            ps, keys = ps_mix.get(2)
            P.op("pe", lambda e, ps=ps: e.matmul(ps[:, 0:256], lhsT=TTf[:], rhs=g_R[:, h, :], start=True, stop=True), r=tk + ["g_Rv", "g_Rk"], w=keys)
            yield
            P.op("act", lambda e, ps=ps: e.activation(out=X1, in_=ps[:, 0:256], func=AF.Copy), r=keys, w=[kA])
            yield
            ps, keys = ps_mix.get(2)
            P.op("pe", lambda e, ps=ps: e.matmul(ps[:, 0:256], lhsT=AToff[:], rhs=X1, start=True, stop=True), r=[kAo, kA], w=keys)
            yield
            P.op("dve", lambda e, ps=ps: e.tensor_tensor(out=X2, in0=g_R[:, h, :], in1=ps[:, 0:256], op=ALU.subtract), r=keys + ["g_Rv", "g_Rk"], w=[kB])
            yield
            ps, keys = ps_mix.get(2)

            def fn(e, ps=ps):
                e.matmul(ps[:, 0:128], lhsT=TTf[:], rhs=X2[:, 0:128], start=True, stop=True)
                return e.matmul(ps[:, 128:256], lhsT=X2[:, 128:256], rhs=TTf[:], start=True, stop=True)
            P.op("pe", fn, r=tk + [kB], w=keys)
            yield
            P.op("act", lambda e, ps=ps: e.activation(out=uw[:], in_=ps[:, 0:256].rearrange("p (a l) -> p a l", a=2), func=AF.Copy), r=keys, w=[kuw])
            yield
            ps, keys = ps_mix.get(2)

            def fn(e, ps=ps):
                e.matmul(ps[:, 0:128], lhsT=uw[:, 1, :], rhs=g_S[:, h, :], start=True, stop=True)
                return e.matmul(ps[:, 128:256], lhsT=qT_, rhs=g_S[:, h, :], start=True, stop=True)
            P.op("pe", fn, r=[kuw, ("g_S", h), ("cv", h)], w=keys)
            yield
            P.op("dve", lambda e, ps=ps: e.tensor_tensor(out=vn[:], in0=uw[:, 0, :], in1=ps[:, 0:128], op=ALU.subtract), r=keys + [kuw], w=[kC])
            yield
            P.op("dve", lambda e, ps=ps: e.tensor_scalar(out=tmp_[:], in0=ps[:, 128:256], scalar1=g_ex[:, h:h + 1], scalar2=None, op0=ALU.mult), r=keys + ["g_ex"], w=[kD])
            yield
            ps, keys = ps_mix.get(2)

            def fn(e, ps=ps):
                e.matmul(ps[:, 0:128], lhsT=attnT[:], rhs=vn[:], start=True, stop=True)
                return e.matmul(ps[:, 128:256], lhsT=g_kd[:, h, :], rhs=vn[:], start=True, stop=True)
            P.op("pe", fn, r=[kat, kC, "g_kd"], w=keys)
            yield
            P.op("dve", lambda e, ps=ps: e.tensor_tensor(out=g_o[:, h, :], in0=tmp_[:], in1=ps[:, 0:128], op=ALU.add), r=keys + [kD], w=[("g_o", h)])
            yield
            P.op("dve", lambda e, ps=ps: e.scalar_tensor_tensor(out=g_S[:, h, :], in0=g_S[:, h, :], scalar=g_ex[:, 4 + h:5 + h], in1=ps[:, 128:256],
                                                                op0=ALU.mult, op1=ALU.add), r=keys + [("g_S", h), "g_ex"], w=[("g_S", h)])
            yield

        def gdn_mixer(tb):
            barrier()
            conv_blocks(U, hist_g, 0, 12, "g")
            for b in range(8):
                P.op("act", lambda e, b=b: e.activation(out=g_sq[:], in_=cv[:, b, :], func=AF.Square), r=[("cv", b)], w=["g_sq"])
                ps, keys = ps_mix.get(2)
                P.op("pe", lambda e, ps=ps: e.matmul(ps[:, 0:TB], lhsT=cones, rhs=g_sq[:], start=True, stop=True), r=["g_sq", "consts"], w=keys)
                P.op("act", lambda e, ps=ps: e.activation(out=g_rn[:], in_=ps[:, 0:TB], func=AF.Sqrt, bias=epsc[:]), r=keys + ["epsc"], w=["g_rn0"])
                P.op("dve", lambda e: e.reciprocal(out=g_rn[:], in_=g_rn[:]), r=["g_rn0"], w=["g_rn"])
                sc_ = 128 ** -0.5 if b < 4 else 1.0
                P.op("dve", lambda e, b=b, sc_=sc_: e.scalar_tensor_tensor(out=cv[:, b, :], in0=cv[:, b, :], scalar=sc_, in1=g_rn[:], op0=ALU.mult, op1=ALU.mult),
                     r=[("cv", b), "g_rn"], w=[("cv", b)])
            if GSTOP <= 1:
                return
            smk = [("small", tt) for tt in range(NTT)]
            P.op("act", lambda e: e.activation(out=g_beta[:], in_=s_small[:, :, 0:4], func=AF.Sigmoid), r=smk, w=["g_beta"])
            P.op("dve", lambda e: e.tensor_tensor(out=g_xx[:], in0=s_small[:, :, 4:8],
                                                  in1=prow[:, PR_GDTB:PR_GDTB + 4].unsqueeze(1).broadcast_to([128, NTT, 4]), op=ALU.add),
                 r=smk + ["prow"], w=["g_xx"])
            softplus_(g_sp[:], g_xx[:], 128, ["g_xx"], "g_sp", g_t1s[:], g_t2s[:])
            P.op("dve", lambda e: e.tensor_tensor(out=g_g[:], in0=g_sp[:], in1=nA_g[:].unsqueeze(1).broadcast_to([128, NTT, 4]), op=ALU.mult),
                 r=["g_sp", "nA_g"], w=["g_g"])
            if GSTOP <= 2:
                return
            for c in range(NTT):
                csl = slice(c * 128, (c + 1) * 128)
                ps, keys = ps_mix.get(1)

                def fn(e, ps=ps, c=c):
                    e.matmul(ps[:, 0:4], lhsT=ctri, rhs=g_g[:, c, :], start=True, stop=True)
                    return e.matmul(ps[:, 4:8], lhsT=cones, rhs=g_g[:, c, :], start=True, stop=True)
                P.op("pe", fn, r=["consts", "g_g"], w=keys)
                P.op("act", lambda e, ps=ps: e.activation(out=g_pre[:, 0:8], in_=ps[:, 0:8], func=AF.Copy), r=keys, w=["g_pre"])
                P.op("dve", lambda e: e.tensor_tensor(out=g_pre[:, 8:12], in0=g_pre[:, 4:8], in1=g_pre[:, 0:4], op=ALU.subtract), r=["g_pre"], w=["g_pre2"])
                P.op("act", lambda e: e.activation(out=g_ex[:], in_=g_pre[:], func=AF.Exp), r=["g_pre", "g_pre2"], w=["g_ex"])
                P.op("dve", lambda e, c=c: e.tensor_tensor(out=g_bg[:], in0=g_beta[:, c, :], in1=g_ex[:, 0:4], op=ALU.mult), r=["g_beta", "g_ex"], w=["g_bg"])
                for which, dst, key in ((4, g_ktm, "g_ktm"), (8, g_vtm, "g_vtm")):
                    ps, keys = ps_mix.get(4)

                    def fn(e, ps=ps, which=which, csl=csl):
                        for q in range(4):
                            ins = e.transpose(out=ps[:, q * 128:(q + 1) * 128], in_=cv[:, which + q, csl], identity=cid)
                        return ins
                    P.op("pe", fn, r=[("cv", which + q) for q in range(4)] + ["consts"], w=keys)
                    P.op("act", lambda e, ps=ps, dst=dst: e.activation(out=dst[:], in_=ps[:, 0:512].rearrange("p (h d) -> p h d", h=4), func=AF.Copy), r=keys, w=[key])
                P.op("dve", lambda e, c=c: e.tensor_tensor(out=g_R[:, :, 0:128], in0=g_vtm[:], in1=g_beta[:, c, :].unsqueeze(2).broadcast_to([128, 4, 128]), op=ALU.mult),
                     r=["g_vtm", "g_beta"], w=["g_Rv"])
                P.op("pool", lambda e: e.tensor_tensor(out=g_R[:, :, 128:256], in0=g_ktm[:], in1=g_bg[:].unsqueeze(2).broadcast_to([128, 4, 128]), op=ALU.mult),
                     r=["g_ktm", "g_bg"], w=["g_Rk"])
                P.op("pool", lambda e: e.tensor_tensor(out=g_kd[:], in0=g_ktm[:], in1=g_ex[:, 8:12].unsqueeze(2).broadcast_to([128, 4, 128]), op=ALU.mult),
                     r=["g_ktm", "g_ex"], w=["g_kd"])
                if GSTOP <= 3:
                    continue
                gens = [head_gen(c, csl, h) for h in range(4)]
                live = list(gens)
                while live:
                    nxt = []
                    for g_ in live:
                        try:
                            next(g_)
                            nxt.append(g_)
                        except StopIteration:
                            pass
                    live = nxt
                if GSTOP <= 8:
                    continue
                ok = [("g_o", h) for h in range(4)]
                P.op("act", lambda e: e.activation(out=g_o2[:], in_=g_o[:], func=AF.Square), r=ok, w=["g_o2"])
                P.op("dve", lambda e: e.tensor_reduce(out=g_ssq[:], in_=g_o2[:], axis=mybir.AxisListType.X, op=ALU.add), r=["g_o2"], w=["g_ssq"])
                P.op("act", lambda e: e.activation(out=g_rstd[:], in_=g_ssq[:], func=AF.Sqrt, scale=1.0 / 128, bias=epsc[:]), r=["g_ssq", "epsc"], w=["g_rstd0"])
                P.op("dve", lambda e: e.reciprocal(out=g_rstd[:], in_=g_rstd[:]), r=["g_rstd0"], w=["g_rstd"])
                P.op("pool", lambda e: e.tensor_tensor(out=g_o2[:], in0=g_o[:], in1=g_rstd[:].unsqueeze(2).broadcast_to([128, 4, 128]), op=ALU.mult), r=ok + ["g_rstd", "g_o2"], w=["g_o2"])
                P.op("pool", lambda e: e.tensor_tensor(out=g_o2[:], in0=g_o2[:], in1=prow[:, PR_GNW:PR_GNW + 128].unsqueeze(1).broadcast_to([128, 4, 128]), op=ALU.mult),
                     r=["g_o2", "prow"], w=["g_o2"])
                P.op("dve", lambda e, c=c: e.tensor_tensor(out=g_obf[:], in0=g_o2[:].rearrange("p h d -> p (h d)"), in1=g_sz[:, c, :], op=ALU.mult),
                     r=["g_o2", ("g_sz", c, 0), ("g_sz", c, 1)], w=["g_obf"])
                transpose_out_bf(g_obf, 4, 4, c, ["g_obf"])

        if "ml" in parts:
            m_qT = T("m_qT", [128, 2, TB], BF16)
            m_kT = T("m_kT", [128, 2, TB], BF16)
            m_ktm = T("m_ktm", [128, NTT, 256])
            m_v = T("m_v", [128, NTT, 512], BF16)
            m_g = big_tm[:, :, 0:512]
            m_zt = big_tm[:, :, 512:1024]
            RN = ["li", "t", "a", "l1", "lf", "m", "F", "G", "H", "one"]
            m_rows = {nm: T("m_r_" + nm, [1, TB]) for nm in RN}
            m_rows["wi"] = m_rows["a"]
            m_rows["emm"] = m_rows["l1"]
            m_rows["nG"] = m_rows["t"]
            m_carry = T("m_carry", [1, 1])
            m_C = T("m_C", [128, 2, 512])
            m_Cb = T("m_Cb", [128, 2, 512], BF16)
            m_n = T("m_n", [128, 2])
            m_nb = T("m_nb", [128, 2], BF16)
            m_W = T("m_W", [128, 128])
            m_Wm = T("m_Wm", [128, 128])
            m_sc = T("m_sc", [128, 128], BF16)
            m_cols = T("m_cols", [128, 4])
            m_tmp = T("m_tmp", [128, 512])
            m_h = T("m_h", [128, 512])
            m_den = T("m_den", [128, 2])
            m_d1 = T("m_d1", [128, 1])
            m_rden = T("m_rden", [128, 1])
            m_ssq = T("m_ssq", [128, 1])
            m_rstd = T("m_rstd", [128, 1])
            m_comb = T("m_comb", [128, 1])
            m_g2 = T("m_g2", [128, 512])
            m_hbf = T("m_hbf", [128, 512], BF16)
            m_kws = T("m_kws", [128, 256], BF16)
            m_onesb = T("m_onesb", [128, 1], BF16)
            P.op("pool", lambda e: e.memset(m_C[:], 0.0), w=["m_C"])
            P.op("pool", lambda e: e.memset(m_Cb[:], 0.0), w=["m_Cb"])
            P.op("pool", lambda e: e.memset(m_n[:], 0.0), w=["m_n"])
            P.op("pool", lambda e: e.memset(m_nb[:], 0.0), w=["m_nb"])
            P.op("pool", lambda e: e.memset(m_carry[:], 0.0), w=["m_carry"])
            P.op("pool", lambda e: e.memset(m_rows["one"][:], 1.0), w=["m_one"])
            P.op("pool", lambda e: e.memset(m_onesb[:], 1.0), w=["m_onesb"])

        def ml_proj(tb, part="all"):
            if part == "o":
                for ti, t in ((2, 16), (3, 17)):
                    s = load_w(t)
                    for tt in range(NTT):
                        ps, keys = proj_tm(s, tt)
                        cs = slice((ti % 2) * WT, (ti % 2 + 1) * WT)
                        P.op("act", lambda e, ps=ps, tt=tt, cs=cs: e.activation(out=m_g[:, tt, cs], in_=ps[:, 0:WT], func=AF.Sigmoid), r=keys, w=[("m_g", tt, ti % 2)])
                return
            s = load_w(0)
            for j in range(2):
                ps, keys = proj_fm(s, j)
                P.op("act", lambda e, ps=ps, j=j: e.activation(out=m_qT[:, j, :], in_=ps[:, 0:TB], func=AF.Identity, scale=256 ** -0.5), r=keys, w=[("m_qT", j)])
            s = load_w(1)
            for j in range(2):
                ps, keys = proj_fm(s, j)
                P.op("act", lambda e, ps=ps, j=j: e.activation(out=m_kT[:, j, :], in_=ps[:, 0:TB], func=AF.Copy), r=keys, w=[("m_kT", j)])
            for ti, t in enumerate(range(14, 20)):
                if part == "a" and ti in (2, 3):
                    continue
                s = load_w(t)
                for tt in range(NTT):
                    ps, keys = proj_tm(s, tt)
                    cs = slice((ti % 2) * WT, (ti % 2 + 1) * WT)
                    if ti < 2:
                        P.op("act", lambda e, ps=ps, tt=tt, cs=cs: e.activation(out=m_v[:, tt, cs], in_=ps[:, 0:WT], func=AF.Copy), r=keys, w=[("m_v", tt, ti)])
                    elif ti < 4:
                        P.op("act", lambda e, ps=ps, tt=tt, cs=cs: e.activation(out=m_g[:, tt, cs], in_=ps[:, 0:WT], func=AF.Sigmoid), r=keys, w=[("m_g", tt, ti % 2)])
                    else:
                        P.op("act", lambda e, ps=ps, tt=tt, cs=cs: e.activation(out=m_zt[:, tt, cs], in_=ps[:, 0:WT], func=AF.Silu), r=keys, w=[("m_zt", tt, ti % 2)])
            s = load_w(26)
            for tt in range(NTT):
                ps, keys = proj_tm(s, tt)
                P.op("act", lambda e, ps=ps, tt=tt: e.activation(out=m_ktm[:, tt, :], in_=ps[:, 0:WT], func=AF.Copy), r=keys, w=[("m_ktm", tt)])
            s = load_w(27)
            ps, keys = proj_fm(s, 0, M=1, c0=24)
            P.op("act", lambda e, ps=ps: e.activation(out=m_rows["li"][:], in_=ps[0:1, 0:TB], func=AF.Identity, bias=prow[0:1, PR_MIB:PR_MIB + 1]), r=keys + ["prow"], w=["m_li"])
            ps, keys = proj_fm(s, 0, M=1, c0=25)
            P.op("act", lambda e, ps=ps: e.activation(out=m_rows["t"][:], in_=ps[0:1, 0:TB], func=AF.Identity, bias=prow[0:1, PR_MFB:PR_MFB + 1]), r=keys + ["prow"], w=["m_t"])

        def ml_mixer(tb):
            R_ = m_rows
            for tt in range(NTT):
                P.op("pool", lambda e, tt=tt: e.tensor_tensor(out=m_g[:, tt, :], in0=m_g[:, tt, :], in1=m_zt[:, tt, :], op=ALU.mult),
                     r=[("m_g", tt, 0), ("m_g", tt, 1), ("m_zt", tt, 0), ("m_zt", tt, 1)], w=[("m_g", tt, 0), ("m_g", tt, 1)])
            P.op("act", lambda e: e.activation(out=R_["a"][:], in_=R_["t"][:], func=AF.Abs), r=["m_t"], w=["m_a"])
            P.op("act", lambda e: e.activation(out=R_["a"][:], in_=R_["a"][:], func=AF.Exp, scale=-1.0), r=["m_a"], w=["m_a2"])
            P.op("act", lambda e: e.activation(out=R_["l1"][:], in_=R_["a"][:], func=AF.Ln, bias=1.0), r=["m_a2"], w=["m_l1"])
            P.op("dve", lambda e: e.scalar_tensor_tensor(out=R_["lf"][:], in0=R_["t"][:], scalar=0.0, in1=R_["l1"][:], op0=ALU.min, op1=ALU.subtract), r=["m_t", "m_l1"], w=["m_lf"])
            P.op("dve", lambda e: e.tensor_tensor_scan(out=R_["m"][:], data0=R_["lf"][:], data1=R_["li"][:], initial=m_carry[:], op0=ALU.add, op1=ALU.max),
                 r=["m_lf", "m_li", "m_carry"], w=["m_m"])
            P.op("dve", lambda e: e.tensor_tensor_scan(out=R_["F"][:], data0=R_["one"][:], data1=R_["lf"][:], initial=0.0, op0=ALU.mult, op1=ALU.add),
                 r=["m_lf", "m_one"], w=["m_F"])
            P.op("dve", lambda e: e.tensor_tensor(out=R_["G"][:], in0=R_["F"][:], in1=R_["m"][:], op=ALU.subtract), r=["m_F", "m_m"], w=["m_G"])
            P.op("dve", lambda e: e.tensor_tensor(out=R_["H"][:], in0=R_["li"][:], in1=R_["F"][:], op=ALU.subtract), r=["m_F", "m_li"], w=["m_H"])
            P.op("dve", lambda e: e.tensor_scalar(out=R_["nG"][:], in0=R_["G"][:], scalar1=-1.0, scalar2=None, op0=ALU.mult), r=["m_G"], w=["m_nG"])
            P.op("act", lambda e: e.activation(out=R_["emm"][:], in_=R_["m"][:], func=AF.Exp, scale=-1.0), r=["m_m"], w=["m_emm"])
            one = R_["one"]
            for c in range(NTT):
                csl = slice(c * 128, (c + 1) * 128)
                bias_ap = m_carry[:] if c == 0 else R_["nG"][0:1, c * 128 - 1:c * 128]
                P.op("act", lambda e, csl=csl, bias_ap=bias_ap: e.activation(out=R_["wi"][0:1, csl], in_=R_["G"][0:1, csl], func=AF.Exp, bias=bias_ap),
                     r=["m_G", "m_nG", "m_carry"], w=["m_wi"])
                ps, keys = ps_mix.get(1)

                def fn(e, ps=ps, csl=csl):
                    e.matmul(ps[:, 0:128], lhsT=R_["H"][0:1, csl], rhs=one[0:1, 0:128], start=True, stop=False)
                    return e.matmul(ps[:, 0:128], lhsT=one[0:1, 0:128], rhs=R_["G"][0:1, csl], start=False, stop=True)
                P.op("pe", fn, r=["m_H", "m_G", "m_one"], w=keys)
                P.op("dve", lambda e, ps=ps: e.tensor_scalar(out=m_W[:], in0=ps[:, 0:128], scalar1=0.0, scalar2=None, op0=ALU.min), r=keys, w=["m_W0"])
                P.op("act", lambda e: e.activation(out=m_W[:], in_=m_W[:], func=AF.Exp), r=["m_W0"], w=["m_W"])
                P.op("pool", lambda e: e.tensor_tensor(out=m_Wm[:], in0=m_W[:], in1=ctri, op=ALU.mult), r=["m_W", "consts"], w=["m_Wm"])
                ps, keys = ps_mix.get(1)

                def fn(e, ps=ps, csl=csl, c=c):
                    e.matmul(ps[:, 0:1], lhsT=R_["wi"][0:1, csl], rhs=one[0:1, 0:1], start=True, stop=True)
                    e.matmul(ps[:, 1:2], lhsT=R_["emm"][0:1, csl], rhs=one[0:1, 0:1], start=True, stop=True)
                    return e.matmul(ps[:, 2:3], lhsT=one[0:1, 0:128], rhs=R_["wi"][0:1, c * 128 + 127:c * 128 + 128], start=True, stop=True)
                P.op("pe", fn, r=["m_wi", "m_emm", "m_one"], w=keys)
                P.op("act", lambda e, ps=ps: e.activation(out=m_cols[:, 0:3], in_=ps[:, 0:3], func=AF.Copy), r=keys, w=["m_cols"])
                ps, keys = ps_mix.get(1)

                def fn(e, ps=ps, csl=csl):
                    e.matmul(ps[:, 0:128], lhsT=m_kT[:, 0, csl], rhs=m_qT[:, 0, csl], start=True, stop=False)
                    return e.matmul(ps[:, 0:128], lhsT=m_kT[:, 1, csl], rhs=m_qT[:, 1, csl], start=False, stop=True)
                P.op("pe", fn, r=[("m_kT", 0), ("m_kT", 1), ("m_qT", 0), ("m_qT", 1)], w=keys)
                P.op("dve", lambda e, ps=ps: e.tensor_tensor(out=m_sc[:], in0=ps[:, 0:128], in1=m_Wm[:], op=ALU.mult), r=keys + ["m_Wm"], w=["m_sc"])
                psi, ki = ps_mix.get(4)

                def fn(e, psi=psi, csl=csl):
                    e.matmul(psi[:, 0:512], lhsT=m_qT[:, 0, csl], rhs=m_Cb[:, 0, :], start=True, stop=False)
                    return e.matmul(psi[:, 0:512], lhsT=m_qT[:, 1, csl], rhs=m_Cb[:, 1, :], start=False, stop=True)
                P.op("pe", fn, r=[("m_qT", 0), ("m_qT", 1), "m_Cb"], w=ki)
                P.op("dve", lambda e, psi=psi: e.tensor_scalar(out=m_tmp[:], in0=psi[:, 0:512], scalar1=m_cols[:, 0:1], scalar2=None, op0=ALU.mult), r=ki + ["m_cols"], w=["m_tmp"])
                psa, ka = ps_mix.get(4)
                P.op("pe", lambda e, psa=psa, c=c: e.matmul(psa[:, 0:512], lhsT=m_sc[:], rhs=m_v[:, c, :], start=True, stop=True), r=["m_sc", ("m_v", c, 0), ("m_v", c, 1)], w=ka)
                P.op("dve", lambda e, psa=psa: e.tensor_tensor(out=m_h[:], in0=m_tmp[:], in1=psa[:, 0:512], op=ALU.add), r=ka + ["m_tmp"], w=["m_h"])
                ps, keys = ps_mix.get(1)

                def fn(e, ps=ps, csl=csl):
                    e.matmul(ps[:, 0:1], lhsT=m_sc[:], rhs=m_onesb[:], start=True, stop=True)
                    e.matmul(ps[:, 1:2], lhsT=m_qT[:, 0, csl], rhs=m_nb[:, 0:1], start=True, stop=False)
                    return e.matmul(ps[:, 1:2], lhsT=m_qT[:, 1, csl], rhs=m_nb[:, 1:2], start=False, stop=True)
                P.op("pe", fn, r=["m_sc", "m_onesb", ("m_qT", 0), ("m_qT", 1), "m_nb"], w=keys)
                P.op("act", lambda e, ps=ps: e.activation(out=m_den[:], in_=ps[:, 0:2], func=AF.Copy), r=keys, w=["m_den"])
                P.op("dve", lambda e: e.scalar_tensor_tensor(out=m_d1[:], in0=m_den[:, 1:2], scalar=m_cols[:, 0:1], in1=m_den[:, 0:1], op0=ALU.mult, op1=ALU.add),
                     r=["m_den", "m_cols"], w=["m_d1"])
                P.op("act", lambda e: e.activation(out=m_d1[:], in_=m_d1[:], func=AF.Abs), r=["m_d1"], w=["m_d1b"])
                P.op("dve", lambda e: e.tensor_tensor(out=m_rden[:], in0=m_d1[:], in1=m_cols[:, 1:2], op=ALU.max), r=["m_d1b", "m_cols"], w=["m_rden0"])
                P.op("dve", lambda e: e.reciprocal(out=m_rden[:], in_=m_rden[:]), r=["m_rden0"], w=["m_rden"])
                P.op("dve", lambda e: e.tensor_scalar(out=m_h[:], in0=m_h[:], scalar1=m_rden[:], scalar2=None, op0=ALU.mult), r=["m_h", "m_rden"], w=["m_h"])
                P.op("act", lambda e: e.activation(out=junk[:], in_=m_h[:], func=AF.Square, accum_out=m_ssq[:]), r=["m_h"], w=["junk", "m_ssq"])
                P.op("act", lambda e: e.activation(out=m_rstd[:], in_=m_ssq[:], func=AF.Sqrt, scale=1.0 / 512, bias=epsc[:]), r=["m_ssq", "epsc"], w=["m_rstd0"])
                P.op("dve", lambda e: e.reciprocal(out=m_rstd[:], in_=m_rstd[:]), r=["m_rstd0"], w=["m_rstd"])
                P.op("dve", lambda e: e.tensor_copy(out=m_comb[:], in_=m_rstd[:]), r=["m_rstd"], w=["m_comb"])
                P.op("pool", lambda e, c=c: e.tensor_tensor(out=m_g2[:], in0=m_g[:, c, :], in1=prow[:, PR_MLNW:PR_MLNW + 512], op=ALU.mult), r=[("m_g", c, 0), ("m_g", c, 1), "prow"], w=["m_g2"])
                P.op("dve", lambda e: e.scalar_tensor_tensor(out=m_hbf[:], in0=m_h[:], scalar=m_comb[:], in1=m_g2[:], op0=ALU.mult, op1=ALU.mult),
                     r=["m_h", "m_comb", "m_g2"], w=["m_hbf"])
                transpose_out_bf(m_hbf, 4, 0, c, ["m_hbf"])
                P.op("dve", lambda e, c=c: e.tensor_scalar(out=m_kws[:], in0=m_ktm[:, c, :], scalar1=m_W[:, 127:128], scalar2=None, op0=ALU.mult), r=[("m_ktm", c), "m_W"], w=["m_kws"])
                for kc in range(2):
                    ps, keys = ps_mix.get(4)
                    P.op("pe", lambda e, ps=ps, kc=kc, c=c: e.matmul(ps[:, 0:512], lhsT=m_kws[:, kc * 128:(kc + 1) * 128], rhs=m_v[:, c, :], start=True, stop=True),
                         r=["m_kws", ("m_v", c, 0), ("m_v", c, 1)], w=keys)
                    P.op("dve", lambda e, ps=ps, kc=kc: e.scalar_tensor_tensor(out=m_C[:, kc, :], in0=m_C[:, kc, :], scalar=m_cols[:, 2:3], in1=ps[:, 0:512], op0=ALU.mult, op1=ALU.add),
                         r=keys + ["m_C", "m_cols", "m_Cb"], w=["m_C"])
                ps, keys = ps_mix.get(1)

                def fn(e, ps=ps):
                    e.matmul(ps[:, 0:1], lhsT=m_kws[:, 0:128], rhs=m_onesb[:], start=True, stop=True)
                    return e.matmul(ps[:, 1:2], lhsT=m_kws[:, 128:256], rhs=m_onesb[:], start=True, stop=True)
                P.op("pe", fn, r=["m_kws", "m_onesb"], w=keys)
                P.op("dve", lambda e, ps=ps: e.scalar_tensor_tensor(out=m_n[:], in0=m_n[:], scalar=m_cols[:, 2:3], in1=ps[:, 0:2], op0=ALU.mult, op1=ALU.add),
                     r=keys + ["m_n", "m_cols"], w=["m_n"])
                P.op("pool", lambda e: e.tensor_copy(out=m_Cb[:], in_=m_C[:]), r=["m_C"], w=["m_Cb"])
                P.op("pool", lambda e: e.tensor_copy(out=m_nb[:], in_=m_n[:]), r=["m_n"], w=["m_nb"])
            P.op("dve", lambda e: e.tensor_copy(out=m_carry[:], in_=R_["m"][0:1, TB - 1:TB]), r=["m_m", "m_wi"], w=["m_carry"])

        if "ssm" not in parts or "gdn" not in parts or "ml" not in parts:
            P.op("pool", lambda e: e.memset(ymixT[:], 0.0), w=[("ymixT", b) for b in range(16)])
        ymk = [("ymixT", b) for b in (0, 4, 8, 12)] + [("ymixT", b) for b in range(16)]

        def outproj(tb):
            if debug:
                P.op("pool", lambda e, tb=tb: e.dma_start(out=ymix_d[:, :, tb * TB:(tb + 1) * TB].rearrange("b p t -> p b t"), in_=ymixT[:]),
                     r=ymk, w=[("ymixd", tb)], dma="ymixd")
            n = 0
            for t in range(NOC):
                s = load_wo(t)
                for tt in range(NTT):
                    ps, keys = ps_proj.get(2)

                    def fn(e, ps=ps, s=s, tt=tt):
                        for kc in range(16):
                            ins = e.matmul(ps[:, 0:WT], lhsT=ymixT[:, kc, tt * 128:(tt + 1) * 128], rhs=wt[s][:, kc, :], start=(kc == 0), stop=(kc == 15))
                        return ins
                    P.op("pe", fn, r=ymk + [("wt", s)], w=keys)
                    ss = n % 2
                    n += 1
                    P.op("act", lambda e, ps=ps, ss=ss: e.activation(out=stage[ss][:], in_=ps[:, 0:WT], func=AF.Copy), r=keys, w=[("stage", ss)])
                    r0 = tb * TB + tt * 128
                    P.op("pool", lambda e, ss=ss, r0=r0, t=t: e.dma_start(out=ypart_d[r0:r0 + 128, t * WT:(t + 1) * WT], in_=stage[ss][:]),
                         r=[("stage", ss)], w=[("ypart", r0, t)], dma=("stage", ss))

        full = all(p_ in parts for p_ in ("ssm", "gdn", "ml"))
        if not full or os.environ.get("NOMERGE"):
            for tb in range(NB):
                load_xT(tb)
                if "ssm" in parts:
                    ssm_proj(tb)
                if "ssm" in parts or "gdn" in parts:
                    small_proj(tb)
                if "ssm" in parts:
                    ssm_mixer(tb)
                if "gdn" in parts:
                    gdn_proj(tb)
                    gdn_mixer(tb)
                if "ml" in parts:
                    ml_proj(tb)
                    ml_mixer(tb)
                outproj(tb)
        else:
            def head_phase(tb):
                ssm_proj(tb, "fm")
                small_proj(tb)

            def tail_phase(tb):
                ml_mixer(tb)
                outproj(tb)
            load_xT(0)
            head_phase(0)
            ssm_proj(0, "z")
            for tb in range(NB):
                A = P.capture(ssm_mixer, tb)
                mk = A.index(None)
                B = P.capture(gdn_proj, tb, "fm")
                P.replay(A[:mk])
                if "1" in MERGE:
                    P.merge(A[mk:], B)
                else:
                    P.replay(A[mk:])
                    P.replay(B)
                gdn_proj(tb, "z")
                A = P.capture(gdn_mixer, tb)
                B = P.capture(ml_proj, tb, "a")
                if "2" in MERGE:
                    P.merge(A, B)
                else:
                    P.replay(B)
                    P.replay(A)
                ml_proj(tb, "o")
                A = P.capture(ml_mixer, tb)
                if tb + 1 < NB:
                    load_xT(tb + 1)
                    B = P.capture(head_phase, tb + 1)
                    if "3" in MERGE:
                        P.merge(A, B)
                    else:
                        P.replay(A)
                        P.replay(B)
                    outproj(tb)
                    ssm_proj(tb + 1, "z")
                else:
                    P.replay(A)
                    outproj(tb)
        P.emit()
        build_layer.last_prog = P
    return nc


ML_OFF, GDN_OFF, SSM_OFF = 0, 8200, 16424


def core_columns(g):
    r = np.arange
    ml = ML_OFF
    gd = GDN_OFF
    ss = SSM_OFF
    cols = [
        ml + g * 256 + r(256),
        ml + 1024 + g * 256 + r(256),
        gd + g * 512 + r(512),
        gd + 2048 + g * 512 + r(512),
        gd + 4096 + g * 512 + r(512),
        ss + 4096 + g * 1024 + r(1024),
        ss + 8192 + g * 256 + r(256),
        ss + 9216 + g * 256 + r(256),
        ml + 2048 + g * 512 + r(512),
        ml + 4096 + g * 512 + r(512),
        ml + 6144 + g * 512 + r(512),
        gd + 6144 + g * 512 + r(512),
        ss + g * 1024 + r(1024),
        ml + 1024 + g * 256 + r(256),
        gd + 8192 + g * 4 + r(4),
        gd + 8208 + g * 4 + r(4),
        ss + 10240 + g * 16 + r(16),
        np.array([ml + 8192 + g]),
        np.array([ml + 8196 + g]),
    ]
    return np.concatenate(cols)


def pack_core(inp, l, g):
    cols = core_columns(g)
    D = inp["w_in"].shape[1]
    wcat = np.zeros((D, NCOL), np.float32)
    wcat[:, :cols.size] = inp["w_in"][l][:, cols]
    mixrows = np.concatenate([g * 512 + np.arange(512), 2048 + g * 512 + np.arange(512), 4096 + g * 1024 + np.arange(1024)])
    wout = np.ascontiguousarray(inp["w_out"][l][mixrows, :])
    gch = np.concatenate([g * 512 + np.arange(512), 2048 + g * 512 + np.arange(512), 4096 + g * 512 + np.arange(512)])
    sch = np.concatenate([g * 1024 + np.arange(1024), 4096 + g * 256 + np.arange(256), 5120 + g * 256 + np.arange(256)])
    cp = np.zeros((24 * 128, 5), np.float32)
    cp[:1536, 0:4] = inp["gdn_conv_w"][l][:, gch].T
    cp[1536:, 0:4] = inp["ssm_conv_w"][l][:, sch].T
    cp[1536:, 4] = inp["ssm_conv_b"][l][sch]
    convp = np.ascontiguousarray(cp.reshape(24, 128, 5).transpose(1, 0, 2).reshape(128, 120))
    pr = np.zeros((1, PR_N), np.float32)
    pr[0, PR_SDTB:PR_SDTB + 16] = inp["ssm_dt_bias"][l][16 * g:16 * g + 16]
    pr[0, PR_SALOG:PR_SALOG + 16] = inp["ssm_A_log"][l][16 * g:16 * g + 16]
    pr[0, PR_SD:PR_SD + 16] = inp["ssm_D"][l][16 * g:16 * g + 16]
    pr[0, PR_GDTB:PR_GDTB + 4] = inp["gdn_dt_bias"][l][4 * g:4 * g + 4]
    pr[0, PR_GALOG:PR_GALOG + 4] = inp["gdn_A_log"][l][4 * g:4 * g + 4]
    pr[0, PR_MIB] = inp["ml_i_bias"][l][g]
    pr[0, PR_MFB] = inp["ml_f_bias"][l][g]
    pr[0, PR_MLNW:PR_MLNW + 512] = inp["ml_norm_w"][l][g * 512:(g + 1) * 512]
    pr[0, PR_GNW:PR_GNW + 128] = inp["gdn_norm_w"][l]
    pr[0, PR_SNW:PR_SNW + 1024] = inp["ssm_norm_w"][l][g * 1024:(g + 1) * 1024]
    return {"wcat": wcat, "wout": wout, "convp": convp, "prow": pr, "consts": make_consts()}


def build_reduce_ln(ROWS=1024, D=D_MODEL):
    nc = bass.Bass("TRN2", target_bir_lowering=False)
    x_d = nc.dram_tensor("xin", [ROWS, D], F32, kind="ExternalInput").ap()
    p_d = [nc.dram_tensor(f"p{j}", [ROWS, D], F32, kind="ExternalInput").ap() for j in range(4)]
    g_d = nc.dram_tensor("lng", [1, D], F32, kind="ExternalInput").ap()
    b_d = nc.dram_tensor("lnb", [1, D], F32, kind="ExternalInput").ap()
    o_d = nc.dram_tensor("out", [ROWS, D], F32, kind="ExternalOutput").ap()
    CW = 1024 if D >= 1024 else D
    NCH = D // CW
    with ExitStack() as st:
        def T(name, shape, dt=F32):
            return st.enter_context(nc.sbuf_tensor("sb_" + name, shape, dt))
        P = Prog(nc, st)
        gb = T("gb", [128, D])
        bb = T("bb", [128, D])
        acc = [T(f"acc{i}", [128, D]) for i in range(2)]
        ld = [[T(f"ld{i}_{j}", [128, CW]) for j in range(5)] for i in range(2)]
        junk = T("junk", [128, D])
        stt = T("stt", [128, 8])
        epsl = T("epsl", [128, 1])
        P.op("sp", lambda e: e.dma_start(out=gb[:], in_=g_d.partition_broadcast(128)[:, 0, :]), w=["gb"], dma="gb")
        P.op("sp", lambda e: e.dma_start(out=bb[:], in_=b_d.partition_broadcast(128)[:, 0, :]), w=["bb"], dma="bb")
        P.op("pool", lambda e: e.memset(epsl[:], LN_EPS), w=["epsl"])
        n = 0
        for ti in range(ROWS // 128):
            r0 = ti * 128
            a = acc[ti % 2]
            ak = ("acc", ti % 2)
            for ch in range(NCH):
                s = n % 2
                n += 1
                cs = slice(ch * CW, (ch + 1) * CW)
                srcs = [x_d] + p_d
                for j in range(5):
                    q = "sp" if j % 2 == 0 else "act"
                    P.op(q, lambda e, s=s, j=j, cs=cs, r0=r0, srcs=srcs: e.dma_start(out=ld[s][j][:], in_=srcs[j][r0:r0 + 128, cs]),
                         w=[("ld", s, j)], dma=("ld", s, j))
                P.op("dve", lambda e, s=s, cs=cs, a=a: e.scalar_tensor_tensor(out=a[:, cs], in0=ld[s][0][:], scalar=float(ALPHA), in1=ld[s][1][:], op0=ALU.mult, op1=ALU.add),
                     r=[("ld", s, 0), ("ld", s, 1)], w=[ak])
                P.op("pool", lambda e, s=s: e.tensor_tensor(out=ld[s][2][:], in0=ld[s][2][:], in1=ld[s][3][:], op=ALU.add), r=[("ld", s, 2), ("ld", s, 3)], w=[("ld", s, 2)])
                P.op("pool", lambda e, s=s: e.tensor_tensor(out=ld[s][2][:], in0=ld[s][2][:], in1=ld[s][4][:], op=ALU.add), r=[("ld", s, 2), ("ld", s, 4)], w=[("ld", s, 2)])
                P.op("dve", lambda e, s=s, cs=cs, a=a: e.tensor_tensor(out=a[:, cs], in0=a[:, cs], in1=ld[s][2][:], op=ALU.add), r=[ak, ("ld", s, 2)], w=[ak])
            P.op("act", lambda e, a=a: e.activation(out=junk[:], in_=a[:], func=AF.Identity, accum_out=stt[:, 0:1]), r=[ak], w=["junk", "st0"])
            P.op("dve", lambda e: e.tensor_scalar(out=stt[:, 1:2], in0=stt[:, 0:1], scalar1=-1.0 / D, scalar2=None, op0=ALU.mult), r=["st0"], w=["st1"])
            P.op("dve", lambda e, a=a: e.tensor_scalar(out=a[:], in0=a[:], scalar1=stt[:, 1:2], scalar2=None, op0=ALU.add), r=[ak, "st1"], w=[ak])
            P.op("act", lambda e, a=a: e.activation(out=junk[:], in_=a[:], func=AF.Square, accum_out=stt[:, 2:3]), r=[ak], w=["junk", "st2"])
            P.op("act", lambda e: e.activation(out=stt[:, 3:4], in_=stt[:, 2:3], func=AF.Sqrt, scale=1.0 / D, bias=epsl[:]), r=["st2", "epsl"], w=["st3"])
            P.op("dve", lambda e: e.reciprocal(out=stt[:, 4:5], in_=stt[:, 3:4]), r=["st3"], w=["st4"])
            P.op("dve", lambda e, a=a: e.scalar_tensor_tensor(out=a[:], in0=a[:], scalar=stt[:, 4:5], in1=gb[:], op0=ALU.mult, op1=ALU.mult), r=[ak, "st4", "gb"], w=[ak])
            P.op("pool", lambda e, a=a: e.tensor_tensor(out=a[:], in0=a[:], in1=bb[:], op=ALU.add), r=[ak, "bb"], w=[ak])
            P.op("pool", lambda e, a=a, r0=r0: e.dma_start(out=o_d[r0:r0 + 128, :], in_=a[:]), r=[ak], w=[("out", ti)], dma=ak)
        P.emit()
    return nc


_PROGS = {}


def kernel(x, w_in, w_out, ml_i_bias, ml_f_bias, ml_norm_w, gdn_conv_w, gdn_A_log, gdn_dt_bias,
           gdn_norm_w, ssm_conv_w, ssm_conv_b, ssm_A_log, ssm_dt_bias, ssm_D, ssm_norm_w, ln_g, ln_b):
    inp = dict(w_in=np.asarray(w_in), w_out=np.asarray(w_out), ml_i_bias=np.asarray(ml_i_bias), ml_f_bias=np.asarray(ml_f_bias),
               ml_norm_w=np.asarray(ml_norm_w), gdn_conv_w=np.asarray(gdn_conv_w), gdn_A_log=np.asarray(gdn_A_log),
               gdn_dt_bias=np.asarray(gdn_dt_bias), gdn_norm_w=np.asarray(gdn_norm_w), ssm_conv_w=np.asarray(ssm_conv_w),
               ssm_conv_b=np.asarray(ssm_conv_b), ssm_A_log=np.asarray(ssm_A_log), ssm_dt_bias=np.asarray(ssm_dt_bias),
               ssm_D=np.asarray(ssm_D), ssm_norm_w=np.asarray(ssm_norm_w))
    ln_g = np.asarray(ln_g, np.float32)
    ln_b = np.asarray(ln_b, np.float32)
    xcur = np.ascontiguousarray(np.asarray(x, np.float32))
    B, S, D = xcur.shape
    if "layer" not in _PROGS:
        _PROGS["layer"] = build_layer(D=D, S=S)
        _PROGS["ln"] = build_reduce_ln(ROWS=B * S // 8, D=D)
    RPC = B * S // 8
    for l in range(DEPTH):
        packs = [pack_core(inp, l, g) for g in range(NG)]
        in_maps = [dict(packs[c % NG], x=xcur[c // NG]) for c in range(8)]
        res = run_bass_kernel_spmd(_PROGS["layer"], in_maps, core_ids=list(range(8)))
        yp = [r["ypart"] for r in res.results]
        del in_maps, packs
        xf = xcur.reshape(B * S, D)
        in_maps = []
        for c in range(8):
            b = (c * RPC) // S
            r0 = (c * RPC) % S
            m = {"xin": np.ascontiguousarray(xf[c * RPC:(c + 1) * RPC]), "lng": ln_g[l][None, :], "lnb": ln_b[l][None, :]}
            for j in range(NG):
                m[f"p{j}"] = np.ascontiguousarray(yp[b * NG + j][r0:r0 + RPC])
            in_maps.append(m)
        res = run_bass_kernel_spmd(_PROGS["ln"], in_maps, core_ids=list(range(8)))
        xcur = np.concatenate([r["out"] for r in res.results], axis=0).reshape(B, S, D)
    return xcur.astype(np.float32)
```

```python
from contextlib import ExitStack
import os
GSTOP = int(os.environ.get('GSTOP', '99'))
MERGE = os.environ.get('MERGE', '123')
import numpy as np
import concourse.bass as bass
import concourse.mybir as mybir
from concourse.bass_utils import run_bass_kernel_spmd

F32 = mybir.dt.float32
BF16 = mybir.dt.bfloat16
ALU = mybir.AluOpType
AF = mybir.ActivationFunctionType

SIG_ROT = 30000


class Prog:
    ENG = ("pe", "act", "dve", "pool", "sp")

    def __init__(self, nc, stack):
        self._stack = stack
        self.nc = nc
        self.ops = []
        self.lw = {}
        self.rs = {}
        self.dma_cnt = {}

    cap = None

    def capture(self, f, *a):
        assert self.cap is None
        self.cap = []
        f(*a)
        L, self.cap = self.cap, None
        return L

    def replay(self, L):
        for a in L:
            if a is not None:
                self.op(*a)

    def mark(self):
        if self.cap is not None:
            self.cap.append(None)

    def merge(self, A, B):
        A = [a for a in A if a is not None]
        B = [b for b in B if b is not None]
        ia = ib = 0
        while ia < len(A) or ib < len(B):
            if ib >= len(B) or (ia < len(A) and ia * len(B) <= ib * len(A)):
                self.op(*A[ia])
                ia += 1
            else:
                self.op(*B[ib])
                ib += 1

    def op(self, eng, fn, r=(), w=(), dma=None, inc=16):
        if self.cap is not None:
            self.cap.append((eng, fn, list(r), list(w), dma, inc))
            return
        r = [self.canon(k) for k in r]
        w = [self.canon(k) for k in w]
        i = len(self.ops)
        deps = set()
        raw = set()
        for k in r:
            j = self.lw.get(k)
            if j is not None:
                deps.add(j)
                raw.add(j)
        for k in w:
            j = self.lw.get(k)
            if j is not None:
                deps.add(j)
            for j in self.rs.get(k, ()):
                deps.add(j)
        keep = set()
        for j in deps:
            pj = self.ops[j]
            if pj["dma"] is None and dma is None and pj["eng"] == eng:
                if eng == "pe":
                    continue
            keep.add(j)
        o = dict(eng=eng, fn=fn, deps=keep, dma=dma, sig=False, sidx=None, inc=inc)
        if dma is not None:
            c = self.dma_cnt.get(dma, 0) + 1
            self.dma_cnt[dma] = c
            o["sidx"] = c
            o["sig"] = True
        self.ops.append(o)
        for j in keep:
            self.ops[j]["sig"] = True
        for k in r:
            self.rs.setdefault(k, []).append(i)
        for k in w:
            self.lw[k] = i
            self.rs[k] = []
        return i

    ALIAS = {"g_rn0": "g_rn", "s_rstd0": "s_rstd", "g_rstd0": "g_rstd", "m_rstd0": "m_rstd", "m_rden0": "m_rden",
             "m_W0": "m_W", "m_d1b": "m_d1", "m_a2": "m_a", "s_dt_b": "s_dt_a", "g_sp_b": "g_sp_a", "sUh": "Uh", "gUh": "Uh",
             "m_wi": "m_a", "m_emm": "m_l1", "m_nG": "m_t", "g_ktm": ("s_xtm", 0), "g_vtm": ("s_xtm", 1),
             "g_Rv": "s_xdt", "g_Rk": "s_xdt"}

    def canon(self, k):
        if isinstance(k, tuple):
            if k[0] == "cvp":
                return ("cv",) + k[1:]
            if k[0] in ("sU", "gU"):
                return ("U",) + k[1:]
            if k[0] in ("g_sz", "m_g"):
                return ("s_sz",) + k[1:]
            if k[0] == "m_zt":
                return ("s_sz", k[1], 2 + k[2])
            return k
        return self.ALIAS.get(k, k)

    def emit(self, final_wait_eng="sp"):
        nc = self.nc
        cnt = {e: 0 for e in self.ENG}
        for o in self.ops:
            if o["dma"] is None and o["sig"]:
                cnt[o["eng"]] += 1
                o["sidx"] = cnt[o["eng"]]
        sems = {}
        stack = self._stack
        for e in self.ENG:
            n = (cnt[e] + SIG_ROT - 1) // SIG_ROT
            for q in range(max(n, 1)):
                sems[("E", e, q)] = stack.enter_context(nc.semaphore(f"s_{e}_{q}"))
        for k in self.dma_cnt:
            nm = "d_" + "_".join(str(x) for x in (k if isinstance(k, tuple) else (k,)))
            sems[("D", k)] = stack.enter_context(nc.semaphore(nm))
        self.nsem = len(sems)

        def chan(o):
            if o["dma"] is not None:
                return ("D", o["dma"]), o["inc"] * o["sidx"]
            q, rr = divmod(o["sidx"] - 1, SIG_ROT)
            return ("E", o["eng"], q), rr + 1

        per = {e: [] for e in self.ENG}
        for o in self.ops:
            per[o["eng"]].append(o)
        final = {}
        for o in self.ops:
            if o["dma"] is not None:
                ch, v = chan(o)
                final[ch] = max(final.get(ch, 0), v)
        block = stack.enter_context(nc.Block())
        deco = {"pe": block.tensor, "act": block.scalar, "dve": block.vector,
                "pool": block.gpsimd, "sp": block.sync}
        ops = self.ops

        def make(e):
            def body(engobj):
                seen = {}
                for o in per[e]:
                    need = {}
                    for j in o["deps"]:
                        ch, v = chan(ops[j])
                        if v > need.get(ch, 0):
                            need[ch] = v
                    for ch, v in need.items():
                        if seen.get(ch, 0) >= v:
                            continue
                        engobj.wait_ge(sems[ch], v)
                        seen[ch] = v
                    ins = o["fn"](engobj)
                    if o["sig"]:
                        ch, v = chan(o)
                        ins.then_inc(sems[ch], o["inc"] if o["dma"] is not None else 1)
                if e == final_wait_eng:
                    for ch, v in final.items():
                        if seen.get(ch, 0) < v:
                            engobj.wait_ge(sems[ch], v)
            return body

        for e in self.ENG:
            if per[e] or e == final_wait_eng:
                deco[e](make(e))


D_MODEL = 4096
SEQ = 4096
BATCH = 2
DEPTH = 2
NG = 4
WT = 256
NT = 28
NCOL = NT * WT
MIXW = 2048
RMS_EPS = 1e-6
LN_EPS = 1e-5
ALPHA = (2 * DEPTH) ** 0.25

PR_SDTB, PR_SALOG, PR_SD, PR_GDTB, PR_GALOG, PR_MIB, PR_MFB = 0, 16, 32, 48, 52, 56, 57
PR_MLNW, PR_GNW, PR_SNW = 64, 64 + 512, 64 + 512 + 128
PR_N = 64 + 512 + 128 + 1024
C_ID, C_TRI, C_U, C_ONES, C_BDS, C_BD, C_OFFT, C_NEG = range(8)
NCONST = 8


def make_consts():
    i = np.arange(128)
    k, l = i[:, None], i[None, :]
    c = np.zeros((NCONST, 128, 128), np.float32)
    c[C_ID] = np.eye(128)
    c[C_TRI] = (k <= l)
    c[C_U] = (k > l)
    c[C_ONES] = 1.0
    same = (k // 64) == (l // 64)
    c[C_BDS] = (k > l) & same
    c[C_BD] = same
    c[C_OFFT] = (k < 64) & (l >= 64)
    c[C_NEG] = np.where(k > l, -30000.0, 0.0)
    return np.ascontiguousarray(c.transpose(1, 0, 2).reshape(128, NCONST * 128))


def build_layer(D=D_MODEL, S=SEQ, TB=256, debug=False, parts=("ssm", "gdn", "ml")):
    KC = D // 128
    NB = S // TB
    NTT = TB // 128
    nc = bass.Bass("TRN2", target_bir_lowering=False)
    x_d = nc.dram_tensor("x", [S, D], F32, kind="ExternalInput").ap()
    wcat_d = nc.dram_tensor("wcat", [D, NCOL], F32, kind="ExternalInput").ap()
    wout_d = nc.dram_tensor("wout", [MIXW, D], F32, kind="ExternalInput").ap()
    convp_d = nc.dram_tensor("convp", [128, 24 * 5], F32, kind="ExternalInput").ap()
    prow_d = nc.dram_tensor("prow", [1, PR_N], F32, kind="ExternalInput").ap()
    const_d = nc.dram_tensor("consts", [128, NCONST * 128], F32, kind="ExternalInput").ap()
    ypart_d = nc.dram_tensor("ypart", [S, D], F32, kind="ExternalOutput").ap()
    wscr_d = nc.dram_tensor("wscr", [NT, 128, KC, WT], BF16, kind="Internal").ap()
    OW = 512 if KC >= 32 else WT
    NOC = D // OW
    woscr_d = nc.dram_tensor("woscr", [NOC, 128, 16, OW], BF16, kind="Internal").ap()
    if debug:
        ymix_d = nc.dram_tensor("ymix", [16, 128, S], BF16, kind="ExternalOutput").ap()

    with ExitStack() as st:
        def T(name, shape, dt=F32):
            return st.enter_context(nc.sbuf_tensor("sb_" + name, shape, dt))
        P = Prog(nc, st)
        psb = [st.enter_context(nc.psum_tensor(f"psb{i}", [128, 512], F32)) for i in range(8)]

        class PSA:
            def __init__(self, banks):
                self.banks = banks
                self.pos = 0

            def get(self, nq):
                bank = self.banks[self.pos % len(self.banks)]
                self.pos += 1
                return psb[bank][:, 0:nq * 128], [("ps", bank)]
        ps_proj = PSA([0, 1])
        ps_mix = PSA([2, 3, 4, 5, 6, 7])

        consts = T("consts", [128, NCONST, 128])
        cid = consts[:, C_ID, :]
        ctri = consts[:, C_TRI, :]
        cU = consts[:, C_U, :]
        cones = consts[:, C_ONES, :]
        idb = T("idb", [128, 128], BF16)
        prow = T("prow", [128, PR_N])
        convp = T("convp", [128, 24, 5])
        nA_s = T("nA_s", [128, 16])
        nA_g = T("nA_g", [128, 4])
        XLW = 512 if D >= 512 else D
        xld = [T(f"xld{i}", [128, XLW]) for i in range(2)]
        xT = T("xT", [128, KC, TB], BF16)
        wt = [T(f"wt{i}", [128, max(KC, 16), WT], BF16) for i in range(2)]
        CK = 2 if KC >= 2 else 1
        assert CK == 2 and WT == 256
        HS = T("HS", [128, 9216])
        bar = T("bar", [128, 1])
        cin = [HS[:, i * 512:(i + 1) * 512].rearrange("p (k c) -> p k c", k=CK) for i in range(2)]
        cout = [HS[:, 1024 + i * 256:1024 + (i + 1) * 256].bitcast(BF16).rearrange("p (k c) -> p k c", k=CK) for i in range(2)]
        HS_KEYS = [("cin", 0), ("cin", 1), ("cout", 0), ("cout", 1), "s_gu", ("s_mte", 0), ("s_mte", 1), "s_xw"]
        for h_ in range(4):
            HS_KEYS += [(k_, h_) for k_ in ("hA", "hB", "hC", "hD", "hA0a", "hA0b", "hAToff", "hattn", "huw")]
            HS_KEYS += [(k_, h_, i_) for k_ in ("hPP", "hTT") for i_ in range(2)]

        def barrier():
            P.op("pool", lambda e: e.memset(bar[:], 0.0), w=HS_KEYS)
        U = T("U", [128, 12, 3 + TB])
        hist_s = T("hist_s", [128, 12, 3])
        hist_g = T("hist_g", [128, 12, 3])
        cv = T("cv", [128, 12, TB])
        ymixT = T("ymixT", [128, 16, TB], BF16)
        stage = [T(f"stage{i}", [128, WT]) for i in range(2)]
        junk = T("junk", [128, 512])
        epsc = T("epsc", [128, 1])

        P.op("sp", lambda e: e.dma_start(out=consts[:], in_=const_d.rearrange("p (c k) -> p c k", c=NCONST)), w=["consts"], dma="consts")
        P.op("sp", lambda e: e.dma_start(out=prow[:], in_=prow_d.partition_broadcast(128)[:, 0, :]), w=["prow"], dma="prow")
        P.op("sp", lambda e: e.dma_start(out=convp[:], in_=convp_d.rearrange("p (c k) -> p c k", c=24)), w=["convp"], dma="convp")
        P.op("dve", lambda e: e.tensor_copy(out=idb[:], in_=cid), r=["consts"], w=["idb"])
        P.op("pool", lambda e: e.memset(epsc[:], RMS_EPS), w=["epsc"])
        P.op("act", lambda e: e.activation(out=nA_s[:], in_=prow[:, PR_SALOG:PR_SALOG + 16], func=AF.Exp), r=["prow"], w=["nA_s"])
        P.op("dve", lambda e: e.tensor_scalar(out=nA_s[:], in0=nA_s[:], scalar1=-1.0, scalar2=None, op0=ALU.mult), r=["nA_s"], w=["nA_s"])
        P.op("act", lambda e: e.activation(out=nA_g[:], in_=prow[:, PR_GALOG:PR_GALOG + 4], func=AF.Exp), r=["prow"], w=["nA_g"])
        P.op("dve", lambda e: e.tensor_scalar(out=nA_g[:], in0=nA_g[:], scalar1=-1.0, scalar2=None, op0=ALU.mult), r=["nA_g"], w=["nA_g"])

        wv = wcat_d.rearrange("(kc p) c -> p kc c", p=128)
        wov = wout_d.rearrange("(kc p) c -> p kc c", p=128)
        PK = 8 if KC >= 8 else KC
        NPW = KC // PK
        NPO = 16 // PK if PK <= 16 else 1
        order = [8, 9, 10, 11, 12, 13, 27, 22, 23, 24, 25, 2, 3, 4, 5, 6, 7, 20, 21, 0, 1, 14, 15, 18, 19, 16, 17]
        for t in order:
            for pi in range(NPW):
                k0 = pi * PK
                P.op("pool", lambda e, t=t, k0=k0: e.dma_start(out=wscr_d[t][:, k0:k0 + PK, :], in_=wv[:, k0:k0 + PK, t * WT:(t + 1) * WT]),
                     w=[("wscr", t, pi)], dma=("prew", t))
        for t in range(NOC):
            for pi in range(NPO):
                k0 = pi * PK
                P.op("pool", lambda e, t=t, k0=k0: e.dma_start(out=woscr_d[t][:, k0:k0 + PK, :], in_=wov[:, k0:k0 + PK, t * OW:(t + 1) * OW]),
                     w=[("woscr", t, pi)], dma=("preo", t))

        barrier()
        wslot = [0]

        def load_w(t):
            s = wslot[0] % 2
            wslot[0] += 1
            P.op("sp", lambda e: e.dma_start(out=wt[s][:, 0:KC, :], in_=wscr_d[t]), r=[("wscr", t, pi) for pi in range(NPW)], w=[("wt", s)], dma=("wt", s))
            return s

        def wo_view(s):
            if OW == WT:
                return wt[s][:, 0:16, :]
            return wt[s][:].rearrange("p k c -> p (k c)")[:, 0:16 * OW].rearrange("p (k c) -> p k c", k=16)

        def load_wo(t):
            s = wslot[0] % 2
            wslot[0] += 1
            P.op("sp", lambda e: e.dma_start(out=wo_view(s), in_=woscr_d[t]), r=[("woscr", t, pi) for pi in range(NPO)], w=[("wt", s)], dma=("wt", s))
            return s

        def proj_fm(s, j, M=128, c0=None):
            ps, keys = ps_proj.get(2)
            cc = j * 128 if c0 is None else c0

            def fn(e):
                for kc in range(KC):
                    ins = e.matmul(ps[0:M, 0:TB], lhsT=wt[s][:, kc, cc:cc + M], rhs=xT[:, kc, :], start=(kc == 0), stop=(kc == KC - 1))
                return ins
            P.op("pe", fn, r=[("wt", s), "xT"], w=keys)
            return ps, keys

        def proj_tm(s, tt):
            ps, keys = ps_proj.get(2)

            def fn(e):
                for kc in range(KC):
                    ins = e.matmul(ps[:, 0:WT], lhsT=xT[:, kc, tt * 128:(tt + 1) * 128], rhs=wt[s][:, kc, :], start=(kc == 0), stop=(kc == KC - 1))
                return ins
            P.op("pe", fn, r=[("wt", s), "xT"], w=keys)
            return ps, keys

        def load_xT(tb):
            n = 0
            for tt in range(NTT):
                r0 = tb * TB + tt * 128
                for c0 in range(0, D, XLW):
                    s = n % 2
                    n += 1
                    P.op("sp", lambda e, s=s, r0=r0, c0=c0: e.dma_start(out=xld[s][:], in_=x_d[r0:r0 + 128, c0:c0 + XLW]),
                         w=[("xld", s)], dma=("xld", s))
                    for q0 in range(0, XLW, 512):
                        nq = min(4, (XLW - q0) // 128)
                        ps, keys = ps_mix.get(nq)

                        def fn(e, s=s, q0=q0, nq=nq, ps=ps):
                            for q in range(nq):
                                ins = e.transpose(out=ps[:, q * 128:(q + 1) * 128], in_=xld[s][:, q0 + q * 128:q0 + (q + 1) * 128], identity=cid)
                            return ins
                        P.op("pe", fn, r=[("xld", s), "consts"], w=keys)
                        kc0 = (c0 + q0) // 128
                        P.op("act", lambda e, ps=ps, kc0=kc0, nq=nq, tt=tt: e.activation(
                            out=xT[:, kc0:kc0 + nq, tt * 128:(tt + 1) * 128],
                            in_=ps[:, 0:nq * 128].rearrange("p (q t) -> p q t", q=nq), func=AF.Copy),
                            r=keys, w=["xT"])

        def conv_blocks(Ubuf, hist, cbase, nblk, key):
            P.op("pool", lambda e: e.tensor_copy(out=Ubuf[:, 0:nblk, 0:3], in_=hist[:, 0:nblk, :]), r=[key + "hist"], w=[key + "Uh"])
            for b in range(nblk):
                eng = "dve"
                rk = [(key + "U", b), key + "Uh", "convp"]

                P.op(eng, lambda e, b=b: e.tensor_scalar(out=cv[:, b, :], in0=Ubuf[:, b, 3:3 + TB], scalar1=convp[:, cbase + b, 3:4], scalar2=convp[:, cbase + b, 4:5],
                                                         op0=ALU.mult, op1=ALU.add), r=rk, w=[("cvp", b)])
                for k in range(3):
                    P.op(eng, lambda e, b=b, k=k: e.scalar_tensor_tensor(out=cv[:, b, :], in0=Ubuf[:, b, k:k + TB], scalar=convp[:, cbase + b, k:k + 1], in1=cv[:, b, :],
                                                                         op0=ALU.mult, op1=ALU.add), r=rk + [("cvp", b)], w=[("cvp", b)])
                P.op("act", lambda e, b=b: e.activation(out=cv[:, b, :], in_=cv[:, b, :], func=AF.Silu), r=[("cvp", b)], w=[("cv", b)])
            P.op("pool", lambda e: e.tensor_copy(out=hist[:, 0:nblk, :], in_=Ubuf[:, 0:nblk, TB:TB + 3]),
                 r=[(key + "U", b) for b in range(nblk)] + [key + "Uh"], w=[key + "hist"])

        def softplus_(out, xin, nparts, shape_keys_r, wkey, tmpa, tmpb):
            P.op("act", lambda e: e.activation(out=tmpa, in_=xin, func=AF.Abs), r=shape_keys_r, w=[wkey + "_a"])
            P.op("act", lambda e: e.activation(out=tmpa, in_=tmpa, func=AF.Exp, scale=-1.0), r=[wkey + "_a"], w=[wkey + "_b"])
            P.op("act", lambda e: e.activation(out=tmpb, in_=tmpa, func=AF.Ln, bias=1.0), r=[wkey + "_b"], w=[wkey + "_c"])
            P.op("dve", lambda e: e.scalar_tensor_tensor(out=out, in0=xin, scalar=0.0, in1=tmpb, op0=ALU.max, op1=ALU.add),
                 r=list(shape_keys_r) + [wkey + "_c"], w=[wkey])

        def transpose_out_bf(src_bf, ncb, blk0, c, rkeys):
            ps, keys = ps_mix.get((ncb + 1) // 2)
            psv = ps.bitcast(BF16)

            def fn(e):
                for i in range(ncb):
                    ins = e.transpose(out=psv[:, i * 128:(i + 1) * 128], in_=src_bf[:, i * 128:(i + 1) * 128], identity=idb[:])
                return ins
            P.op("pe", fn, r=list(rkeys) + ["idb"], w=keys)
            P.op("act", lambda e: e.activation(out=ymixT[:, blk0:blk0 + ncb, c * 128:(c + 1) * 128],
                                                in_=psv[:, 0:ncb * 128].rearrange("p (b t) -> p b t", b=ncb), func=AF.Copy),
                 r=keys, w=[("ymixT", blk0)])

        s_small = T("s_small", [128, NTT, 32])
        big_tm = T("big_tm", [128, NTT, 1024])
        s_xtm = T("s_xtm", [128, 1024])
        s_xdt = T("s_xdt", [128, 1024])
        if "ssm" in parts:
            s_sz = big_tm
            s_dt = T("s_dt", [128, NTT, 16])
            s_a = T("s_a", [128, NTT, 16])
            s_t1 = T("s_t1", [128, NTT, 16])
            s_t2 = T("s_t2", [128, NTT, 16])
            s_xx = T("s_xx", [128, NTT, 16])
            s_state = T("s_state", [128, 2, 512])
            s_pre = T("s_pre", [128, 48])
            s_ex = T("s_ex", [128, 48])
            s_xw = HS[:, 2048:3072]
            s_btm = T("s_btm", [128, 2, 128])
            s_gu = HS[:, 0:1024].rearrange("p (h l) -> p h l", h=8)
            s_mt = HS[:, 1024:2048].rearrange("p (h l) -> p h l", h=8)
            s_cbm = T("s_cbm", [128, 128])
            s_y = T("s_y", [128, 512])
            s_ybf = T("s_ybf", [128, 512], BF16)
            s_ssq = T("s_ssq", [128, 1])
            s_rstd = T("s_rstd", [128, 1])
            P.op("pool", lambda e: e.memset(s_state[:], 0.0), w=["s_state0", "s_state1"])
            P.op("pool", lambda e: e.memset(hist_s[:], 0.0), w=["shist"])

        def ssm_proj(tb, part="all"):
            for ti, t in enumerate(range(8, 14) if part in ("all", "fm") else ()):
                s = load_w(t)
                for j in range(2):
                    ps, keys = proj_fm(s, j)
                    b = ti * 2 + j
                    P.op("act", lambda e, ps=ps, b=b: e.activation(out=U[:, b, 3:3 + TB], in_=ps[:, 0:TB], func=AF.Copy), r=keys, w=[("sU", b)])
            for ti, t in enumerate(range(22, 26) if part in ("all", "z") else ()):
                s = load_w(t)
                for tt in range(NTT):
                    ps, keys = proj_tm(s, tt)
                    P.op("act", lambda e, ps=ps, ti=ti, tt=tt: e.activation(out=s_sz[:, tt, ti * WT:(ti + 1) * WT], in_=ps[:, 0:WT], func=AF.Silu),
                         r=keys, w=[("s_sz", tt, ti)])

        def small_proj(tb):
            s = load_w(27)
            for tt in range(NTT):
                ps, keys = proj_tm(s, tt)
                P.op("dve", lambda e, ps=ps, tt=tt: e.tensor_copy(out=s_small[:, tt, :], in_=ps[:, 0:32]), r=keys, w=[("small", tt)])
            return s

        def ssm_mixer(tb):
            barrier()
            conv_blocks(U, hist_s, 12, 12, "s")
            P.mark()
            smk = [("small", tt) for tt in range(NTT)]
            P.op("dve", lambda e: e.tensor_tensor(out=s_xx[:], in0=s_small[:, :, 8:24],
                                                  in1=prow[:, PR_SDTB:PR_SDTB + 16].unsqueeze(1).broadcast_to([128, NTT, 16]), op=ALU.add),
                 r=smk + ["prow"], w=["s_xx"])
            softplus_(s_dt[:], s_xx[:], 128, ["s_xx"], "s_dt", s_t1[:], s_t2[:])
            P.op("dve", lambda e: e.tensor_tensor(out=s_a[:], in0=s_dt[:], in1=nA_s[:].unsqueeze(1).broadcast_to([128, NTT, 16]), op=ALU.mult),
                 r=["s_dt", "nA_s"], w=["s_a"])
            for c in range(NTT):
                csl = slice(c * 128, (c + 1) * 128)
                ps, keys = ps_mix.get(1)

                def fn(e, ps=ps, c=c):
                    e.matmul(ps[:, 0:16], lhsT=ctri, rhs=s_a[:, c, :], start=True, stop=True)
                    return e.matmul(ps[:, 16:32], lhsT=cones, rhs=s_a[:, c, :], start=True, stop=True)
                P.op("pe", fn, r=["consts", "s_a"], w=keys)
                P.op("act", lambda e, ps=ps: e.activation(out=s_pre[:, 0:32], in_=ps[:, 0:32], func=AF.Copy), r=keys, w=["s_pre"])
                P.op("dve", lambda e: e.tensor_tensor(out=s_pre[:, 32:48], in0=s_pre[:, 16:32], in1=s_pre[:, 0:16], op=ALU.subtract), r=["s_pre"], w=["s_pre2"])
                P.op("act", lambda e: e.activation(out=s_ex[:], in_=s_pre[:], func=AF.Exp), r=["s_pre", "s_pre2"], w=["s_ex"])
                for half in range(2):
                    ps, keys = ps_mix.get(4)

                    def fn(e, ps=ps, half=half, csl=csl):
                        for q in range(4):
                            ins = e.transpose(out=ps[:, q * 128:(q + 1) * 128], in_=cv[:, half * 4 + q, csl], identity=cid)
                        return ins
                    P.op("pe", fn, r=[("cv", half * 4 + q) for q in range(4)] + ["consts"], w=keys)
                    P.op("act", lambda e, ps=ps, half=half: e.activation(out=s_xtm[:, half * 512:(half + 1) * 512], in_=ps[:, 0:512], func=AF.Copy),
                         r=keys, w=[("s_xtm", half)])
                ps, keys = ps_mix.get(2)

                def fn(e, ps=ps, csl=csl):
                    e.transpose(out=ps[:, 0:128], in_=cv[:, 8, csl], identity=cid)
                    return e.transpose(out=ps[:, 128:256], in_=cv[:, 9, csl], identity=cid)
                P.op("pe", fn, r=[("cv", 8), ("cv", 9), "consts"], w=keys)
                P.op("dve", lambda e, ps=ps: e.tensor_copy(out=s_btm[:], in_=ps[:, 0:256].rearrange("p (g n) -> p g n", g=2)), r=keys, w=["s_btm"])
                xk = [("s_xtm", 0), ("s_xtm", 1)]
                v3 = lambda t_: t_[:, 0:1024].rearrange("p (h q) -> p h q", h=16)
                P.op("pool", lambda e, c=c: e.tensor_tensor(out=v3(s_xdt), in0=v3(s_xtm), in1=s_dt[:, c, :].unsqueeze(2).broadcast_to([128, 16, 64]), op=ALU.mult),
                     r=xk + ["s_dt"], w=["s_xdt"])
                P.op("pool", lambda e: e.tensor_tensor(out=v3(s_xw), in0=v3(s_xdt), in1=s_ex[:, 32:48].unsqueeze(2).broadcast_to([128, 16, 64]), op=ALU.mult),
                     r=["s_xdt", "s_ex"], w=["s_xw"])
                for gi in range(2):
                    ps, keys = ps_mix.get(1)
                    P.op("pe", lambda e, ps=ps, gi=gi, csl=csl: e.matmul(ps[:, 0:128], lhsT=cv[:, 8 + gi, csl], rhs=cv[:, 10 + gi, csl], start=True, stop=True),
                         r=[("cv", 8 + gi), ("cv", 10 + gi)], w=keys)
                    P.op("dve", lambda e, ps=ps: e.tensor_tensor(out=s_cbm[:], in0=ps[:, 0:128], in1=ctri, op=ALU.mult), r=keys + ["consts"], w=["s_cbm"])
                    P.op("pool", lambda e, c=c, gi=gi: e.tensor_tensor(out=s_gu[:], in0=s_a[:, c, gi * 8:(gi + 1) * 8].unsqueeze(2).broadcast_to([128, 8, 128]),
                                                                       in1=cU.unsqueeze(1).broadcast_to([128, 8, 128]), op=ALU.mult),
                         r=["s_a", "consts"], w=["s_gu"])
                    for hh in range(2):
                        ps, keys = ps_mix.get(4)

                        def fn(e, ps=ps, hh=hh):
                            for q in range(4):
                                ins = e.matmul(ps[:, q * 128:(q + 1) * 128], lhsT=s_gu[:, hh * 4 + q, :], rhs=ctri, start=True, stop=True)
                            return ins
                        P.op("pe", fn, r=["s_gu", "consts"], w=keys)
                        P.op("act", lambda e, ps=ps, hh=hh: e.activation(out=s_mt[:, hh * 4:(hh + 1) * 4, :], in_=ps[:, 0:512].rearrange("p (h l) -> p h l", h=4), func=AF.Exp),
                             r=keys, w=[("s_mte", hh)])
                    P.op("dve", lambda e: e.tensor_tensor(out=s_mt[:], in0=s_mt[:], in1=s_cbm[:].unsqueeze(1).broadcast_to([128, 8, 128]), op=ALU.mult),
                         r=[("s_mte", 0), ("s_mte", 1), "s_cbm"], w=[("s_mte", 0), ("s_mte", 1)])
                    psd, kd = ps_mix.get(4)

                    def fn(e, psd=psd, gi=gi):
                        for h in range(8):
                            hg = gi * 8 + h
                            ins = e.matmul(psd[:, h * 64:(h + 1) * 64], lhsT=s_mt[:, h, :], rhs=s_xdt[:, hg * 64:(hg + 1) * 64], start=True, stop=True)
                        return ins
                    P.op("pe", fn, r=[("s_mte", 0), ("s_mte", 1), "s_xdt"], w=kd)
                    pso, ko = ps_mix.get(4)
                    P.op("pe", lambda e, pso=pso, gi=gi, csl=csl: e.matmul(pso[:, 0:512], lhsT=cv[:, 10 + gi, csl], rhs=s_state[:, gi, :], start=True, stop=True),
                         r=[("cv", 10 + gi), f"s_state{gi}"], w=ko)
                    y3 = s_y[:].rearrange("p (h q) -> p h q", h=8)
                    P.op("dve", lambda e, pso=pso, gi=gi: e.tensor_tensor(out=y3, in0=pso[:, 0:512].rearrange("p (h q) -> p h q", h=8),
                                                                          in1=s_ex[:, gi * 8:(gi + 1) * 8].unsqueeze(2).broadcast_to([128, 8, 64]), op=ALU.mult),
                         r=ko + ["s_ex"], w=["s_y"])
                    P.op("pool", lambda e, gi=gi: e.tensor_tensor(out=junk[:].rearrange("p (h q) -> p h q", h=8), in0=s_xtm[:, gi * 512:(gi + 1) * 512].rearrange("p (h q) -> p h q", h=8),
                                                                  in1=prow[:, PR_SD + gi * 8:PR_SD + (gi + 1) * 8].unsqueeze(2).broadcast_to([128, 8, 64]), op=ALU.mult),
                         r=[("s_xtm", gi), "prow"], w=["junk"])
                    P.op("pool", lambda e, gi=gi: e.tensor_tensor(out=s_y[:], in0=s_y[:], in1=junk[:], op=ALU.add), r=["s_y", "junk"], w=["s_y"])
                    P.op("dve", lambda e, psd=psd: e.tensor_tensor(out=s_y[:], in0=s_y[:], in1=psd[:, 0:512], op=ALU.add), r=kd + ["s_y"], w=["s_y"])
                    P.op("pool", lambda e, c=c, gi=gi: e.tensor_tensor(out=s_y[:], in0=s_y[:], in1=s_sz[:, c, gi * 512:(gi + 1) * 512], op=ALU.mult),
                         r=["s_y", ("s_sz", c, 2 * gi), ("s_sz", c, 2 * gi + 1)], w=["s_y"])
                    P.op("act", lambda e: e.activation(out=junk[:], in_=s_y[:], func=AF.Square, accum_out=s_ssq[:]), r=["s_y"], w=["junk", "s_ssq"])
                    P.op("act", lambda e: e.activation(out=s_rstd[:], in_=s_ssq[:], func=AF.Sqrt, scale=1.0 / 512, bias=epsc[:]), r=["s_ssq", "epsc"], w=["s_rstd0"])
                    P.op("dve", lambda e: e.reciprocal(out=s_rstd[:], in_=s_rstd[:]), r=["s_rstd0"], w=["s_rstd"])
                    P.op("dve", lambda e, gi=gi: e.scalar_tensor_tensor(out=s_ybf[:], in0=s_y[:], scalar=s_rstd[:], in1=prow[:, PR_SNW + gi * 512:PR_SNW + (gi + 1) * 512],
                                                                        op0=ALU.mult, op1=ALU.mult), r=["s_y", "s_rstd", "prow"], w=["s_ybf"])
                    transpose_out_bf(s_ybf, 4, 8 + gi * 4, c, ["s_ybf"])
                    pss, ks = ps_mix.get(4)
                    P.op("pe", lambda e, pss=pss, gi=gi: e.matmul(pss[:, 0:512], lhsT=s_btm[:, gi, :], rhs=s_xw[:, gi * 512:(gi + 1) * 512], start=True, stop=True),
                         r=["s_btm", "s_xw"], w=ks)
                    st3 = s_state[:, gi, :].rearrange("p (h q) -> p h q", h=8)
                    P.op("pool", lambda e, st3=st3, gi=gi: e.tensor_tensor(out=st3, in0=st3, in1=s_ex[:, 16 + gi * 8:16 + (gi + 1) * 8].unsqueeze(2).broadcast_to([128, 8, 64]), op=ALU.mult),
                         r=[f"s_state{gi}", "s_ex"], w=[f"s_state{gi}"])
                    P.op("dve", lambda e, pss=pss, gi=gi: e.tensor_tensor(out=s_state[:, gi, :], in0=s_state[:, gi, :], in1=pss[:, 0:512], op=ALU.add),
                         r=ks + [f"s_state{gi}"], w=[f"s_state{gi}"])

        if "gdn" in parts:
            g_sz = big_tm[:, :, 0:512]
            g_sq = T("g_sq", [128, TB])
            g_rn = T("g_rn", [128, TB])
            g_beta = T("g_beta", [128, NTT, 4])
            g_xx = T("g_xx", [128, NTT, 4])
            g_t1s = T("g_t1s", [128, NTT, 4])
            g_t2s = T("g_t2s", [128, NTT, 4])
            g_sp = T("g_sp", [128, NTT, 4])
            g_g = T("g_g", [128, NTT, 4])
            g_pre = T("g_pre", [128, 12])
            g_ex = T("g_ex", [128, 12])
            g_bg = T("g_bg", [128, 4])
            g_ktm = s_xtm[:, 0:512].rearrange("p (h d) -> p h d", h=4)
            g_vtm = s_xtm[:, 512:1024].rearrange("p (h d) -> p h d", h=4)
            g_R = s_xdt[:].rearrange("p (h c) -> p h c", h=4)
            g_kd = T("g_kd", [128, 4, 128])
            g_H = []
            for h_ in range(4):
                b0 = h_ * 2304
                v2 = lambda o_: HS[:, b0 + o_:b0 + o_ + 256].rearrange("p (a l) -> p a l", a=2)
                v1 = lambda o_: HS[:, b0 + o_:b0 + o_ + 128]
                g_H.append(dict(A=v2(0), B=v2(256), C=v1(512), D=v1(640), A0=v2(768), AToff=v1(1024), attnT=v1(1152),
                                PP=[v2(1280), v2(1536)], TT=[v1(1792), v1(1920)], uw=v2(2048)))
            g_o = T("g_o", [128, 4, 128])
            g_o2 = T("g_o2", [128, 4, 128])
            g_ssq = T("g_ssq", [128, 4])
            g_rstd = T("g_rstd", [128, 4])
            g_obf = T("g_obf", [128, 512], BF16)
            g_S = T("g_S", [128, 4, 128])
            P.op("pool", lambda e: e.memset(g_S[:], 0.0), w=[("g_S", h) for h in range(4)])
            P.op("pool", lambda e: e.memset(hist_g[:], 0.0), w=["ghist"])
            cBDm = consts[:, C_BD, :]
            cOFFT = consts[:, C_OFFT, :]
            cNEG = consts[:, C_NEG, :]

        def gdn_proj(tb, part="all"):
            for ti, t in enumerate(range(2, 8) if part in ("all", "fm") else ()):
                s = load_w(t)
                for j in range(2):
                    ps, keys = proj_fm(s, j)
                    b = ti * 2 + j
                    P.op("act", lambda e, ps=ps, b=b: e.activation(out=U[:, b, 3:3 + TB], in_=ps[:, 0:TB], func=AF.Copy), r=keys, w=[("gU", b)])
            for ti, t in enumerate(range(20, 22) if part in ("all", "z") else ()):
                s = load_w(t)
                for tt in range(NTT):
                    ps, keys = proj_tm(s, tt)
                    P.op("act", lambda e, ps=ps, ti=ti, tt=tt: e.activation(out=g_sz[:, tt, ti * WT:(ti + 1) * WT], in_=ps[:, 0:WT], func=AF.Silu),
                         r=keys, w=[("g_sz", tt, ti)])

        def head_gen(c, csl, h):
            Hh = g_H[h]
            kA, kB, kC, kD = ("hA", h), ("hB", h), ("hC", h), ("hD", h)
            kA0a, kA0b, kAo, kat, kuw = ("hA0a", h), ("hA0b", h), ("hAToff", h), ("hattn", h), ("huw", h)
            GU, E_, t1, Af, A0, AToff, attnT, PP, TT, uw = (Hh[k_] for k_ in ("A", "B", "C", "D", "A0", "AToff", "attnT", "PP", "TT", "uw"))
            X1 = GU.rearrange("p a l -> p (a l)")
            X2 = E_.rearrange("p a l -> p (a l)")
            vn, tmp_ = t1, Af
            gcol = g_g[:, c, h:h + 1]
            P.op("dve", lambda e: e.tensor_scalar(out=GU[:, 0, :], in0=cU, scalar1=gcol, scalar2=None, op0=ALU.mult), r=["consts", "g_g"], w=[kA])
            yield
            P.op("dve", lambda e: e.tensor_scalar(out=GU[:, 1, :], in0=ctri, scalar1=gcol, scalar2=None, op0=ALU.mult), r=["consts", "g_g"], w=[kA])
            yield
            ps, keys = ps_mix.get(4)
            kT = cv[:, 4 + h, csl]
            qT_ = cv[:, h, csl]

            def fn(e, ps=ps):
                e.matmul(ps[:, 0:128], lhsT=GU[:, 0, :], rhs=ctri, start=True, stop=False)
                e.matmul(ps[:, 0:128], lhsT=cid, rhs=cNEG, start=False, stop=True)
                e.matmul(ps[:, 128:256], lhsT=GU[:, 1, :], rhs=cU, start=True, stop=True)
                e.matmul(ps[:, 256:384], lhsT=kT, rhs=kT, start=True, stop=True)
                return e.matmul(ps[:, 384:512], lhsT=kT, rhs=qT_, start=True, stop=True)
            P.op("pe", fn, r=[kA, "consts", ("cv", 4 + h), ("cv", h)], w=keys)
            yield
            P.op("act", lambda e, ps=ps: e.activation(out=E_[:], in_=ps[:, 0:256].rearrange("p (a l) -> p a l", a=2), func=AF.Exp), r=keys, w=[kB])
            yield
            P.op("dve", lambda e, ps=ps: e.tensor_tensor(out=t1[:], in0=ps[:, 256:384], in1=E_[:, 1, :], op=ALU.mult), r=keys + [kB], w=[kC])
            yield
            P.op("dve", lambda e, ps=ps: e.tensor_tensor(out=attnT[:], in0=ps[:, 384:512], in1=E_[:, 0, :], op=ALU.mult), r=keys + [kB], w=[kat])
            yield
            P.op("dve", lambda e: e.scalar_tensor_tensor(out=Af[:], in0=t1[:], scalar=g_beta[:, c, h:h + 1], in1=cU, op0=ALU.mult, op1=ALU.mult),
                 r=[kC, "g_beta", "consts"], w=[kD])
            yield
            ps, keys = ps_mix.get(1)
            P.op("pe", lambda e, ps=ps: e.transpose(out=ps[:, 0:128], in_=Af[:], identity=cid), r=[kD, "consts"], w=keys)
            yield
            P.op("pool", lambda e: e.tensor_tensor(out=A0[:, 0, :], in0=Af[:], in1=cBDm, op=ALU.mult), r=[kD, "consts"], w=[kA0a])
            yield
            P.op("dve", lambda e, ps=ps: e.tensor_tensor(out=A0[:, 1, :], in0=ps[:, 0:128], in1=cBDm, op=ALU.mult), r=keys + ["consts"], w=[kA0b])
            yield
            P.op("dve", lambda e, ps=ps: e.tensor_tensor(out=AToff[:], in0=ps[:, 0:128], in1=cOFFT, op=ALU.mult), r=keys + ["consts"], w=[kAo])
            yield
            P.op("pool", lambda e: e.tensor_tensor(out=TT[0][:], in0=cid, in1=A0[:, 1, :], op=ALU.subtract), r=[kA0b, "consts"], w=[("hTT", h, 0)])
            yield
            Pc, Pk = A0, [kA0a, kA0b]
            tcur = 0
            for k in range(1, 6):
                ps, keys = ps_mix.get(2)

                def fn(e, ps=ps, Pc=Pc, k=k):
                    ins = e.matmul(ps[:, 0:128], lhsT=Pc[:, 1, :], rhs=Pc[:, 0, :], start=True, stop=True)
                    if k < 5:
                        ins = e.matmul(ps[:, 128:256], lhsT=Pc[:, 0, :], rhs=Pc[:, 1, :], start=True, stop=True)
                    return ins
                P.op("pe", fn, r=Pk, w=keys)
                yield
                Pn = PP[k % 2]
                nk = [("hPP", h, k % 2)]
                P.op("act", lambda e, ps=ps, Pn=Pn: e.activation(out=Pn[:], in_=ps[:, 0:256].rearrange("p (a l) -> p a l", a=2), func=AF.Copy), r=keys, w=nk)
                yield
                ps, keys = ps_mix.get(1)
                P.op("pe", lambda e, ps=ps, Pn=Pn, tcur=tcur: e.matmul(ps[:, 0:128], lhsT=Pn[:, 0, :], rhs=TT[tcur][:], start=True, stop=True),
                     r=nk + [("hTT", h, tcur)], w=keys)
                yield
                P.op("dve", lambda e, ps=ps, tcur=tcur: e.tensor_tensor(out=TT[1 - tcur][:], in0=TT[tcur][:], in1=ps[:, 0:128], op=ALU.add),
                     r=keys + [("hTT", h, tcur)], w=[("hTT", h, 1 - tcur)])
                yield
                tcur = 1 - tcur
                Pc, Pk = Pn, nk
            TTf = TT[tcur]
            tk = [("hTT", h, tcur)]
            ps, keys = ps_mix.get(2)
            P.op("pe", lambda e, ps=ps: e.matmul(ps[:, 0:256], lhsT=TTf[:], rhs=g_R[:, h, :], start=True, stop=True), r=tk + ["g_Rv", "g_Rk"], w=keys)
            yield
            P.op("act", lambda e, ps=ps: e.activation(out=X1, in_=ps[:, 0:256], func=AF.Copy), r=keys, w=[kA])
            yield
            ps, keys = ps_mix.get(2)
            P.op("pe", lambda e, ps=ps: e.matmul(ps[:, 0:256], lhsT=AToff[:], rhs=X1, start=True, stop=True), r=[kAo, kA], w=keys)
            yield
            P.op("dve", lambda e, ps=ps: e.tensor_tensor(out=X2, in0=g_R[:, h, :], in1=ps[:, 0:256], op=ALU.subtract), r=keys + ["g_Rv", "g_Rk"], w=[kB])
            yield
            ps, keys = ps_mix.get(2)

            def fn(e, ps=ps):
                e.matmul(ps[:, 0:128], lhsT=TTf[:], rhs=X2[:, 0:128], start=True, stop=True)
                return e.matmul(ps[:, 128:256], lhsT=X2[:, 128:256], rhs=TTf[:], start=True, stop=True)
            P.op("pe", fn, r=tk + [kB], w=keys)
            yield
            P.op("act", lambda e, ps=ps: e.activation(out=uw[:], in_=ps[:, 0:256].rearrange("p (a l) -> p a l", a=2), func=AF.Copy), r=keys, w=[kuw])
            yield
            ps, keys = ps_mix.get(2)

            def fn(e, ps=ps):
                e.matmul(ps[:, 0:128], lhsT=uw[:, 1, :], rhs=g_S[:, h, :], start=True, stop=True)
                return e.matmul(ps[:, 128:256], lhsT=qT_, rhs=g_S[:, h, :], start=True, stop=True)
            P.op("pe", fn, r=[kuw, ("g_S", h), ("cv", h)], w=keys)
            yield
            P.op("dve", lambda e, ps=ps: e.tensor_tensor(out=vn[:], in0=uw[:, 0, :], in1=ps[:, 0:128], op=ALU.subtract), r=keys + [kuw], w=[kC])
            yield
            P.op("dve", lambda e, ps=ps: e.tensor_scalar(out=tmp_[:], in0=ps[:, 128:256], scalar1=g_ex[:, h:h + 1], scalar2=None, op0=ALU.mult), r=keys + ["g_ex"], w=[kD])
            yield
            ps, keys = ps_mix.get(2)

            def fn(e, ps=ps):
                e.matmul(ps[:, 0:128], lhsT=attnT[:], rhs=vn[:], start=True, stop=True)
                return e.matmul(ps[:, 128:256], lhsT=g_kd[:, h, :], rhs=vn[:], start=True, stop=True)
            P.op("pe", fn, r=[kat, kC, "g_kd"], w=keys)
            yield
            P.op("dve", lambda e, ps=ps: e.tensor_tensor(out=g_o[:, h, :], in0=tmp_[:], in1=ps[:, 0:128], op=ALU.add), r=keys + [kD], w=[("g_o", h)])
            yield
            P.op("dve", lambda e, ps=ps: e.scalar_tensor_tensor(out=g_S[:, h, :], in0=g_S[:, h, :], scalar=g_ex[:, 4 + h:5 + h], in1=ps[:, 128:256],
                                                                op0=ALU.mult, op1=ALU.add), r=keys + [("g_S", h), "g_ex"], w=[("g_S", h)])
            yield

        def gdn_mixer(tb):
            barrier()
            conv_blocks(U, hist_g, 0, 12, "g")
            for b in range(8):
                P.op("act", lambda e, b=b: e.activation(out=g_sq[:], in_=cv[:, b, :], func=AF.Square), r=[("cv", b)], w=["g_sq"])
                ps, keys = ps_mix.get(2)
                P.op("pe", lambda e, ps=ps: e.matmul(ps[:, 0:TB], lhsT=cones, rhs=g_sq[:], start=True, stop=True), r=["g_sq", "consts"], w=keys)
                P.op("act", lambda e, ps=ps: e.activation(out=g_rn[:], in_=ps[:, 0:TB], func=AF.Sqrt, bias=epsc[:]), r=keys + ["epsc"], w=["g_rn0"])
                P.op("dve", lambda e: e.reciprocal(out=g_rn[:], in_=g_rn[:]), r=["g_rn0"], w=["g_rn"])
                sc_ = 128 ** -0.5 if b < 4 else 1.0
                P.op("dve", lambda e, b=b, sc_=sc_: e.scalar_tensor_tensor(out=cv[:, b, :], in0=cv[:, b, :], scalar=sc_, in1=g_rn[:], op0=ALU.mult, op1=ALU.mult),
                     r=[("cv", b), "g_rn"], w=[("cv", b)])
            if GSTOP <= 1:
                return
            smk = [("small", tt) for tt in range(NTT)]
            P.op("act", lambda e: e.activation(out=g_beta[:], in_=s_small[:, :, 0:4], func=AF.Sigmoid), r=smk, w=["g_beta"])
            P.op("dve", lambda e: e.tensor_tensor(out=g_xx[:], in0=s_small[:, :, 4:8],
                                                  in1=prow[:, PR_GDTB:PR_GDTB + 4].unsqueeze(1).broadcast_to([128, NTT, 4]), op=ALU.add),
                 r=smk + ["prow"], w=["g_xx"])
            softplus_(g_sp[:], g_xx[:], 128, ["g_xx"], "g_sp", g_t1s[:], g_t2s[:])
            P.op("dve", lambda e: e.tensor_tensor(out=g_g[:], in0=g_sp[:], in1=nA_g[:].unsqueeze(1).broadcast_to([128, NTT, 4]), op=ALU.mult),
                 r=["g_sp", "nA_g"], w=["g_g"])
            if GSTOP <= 2:
                return
            for c in range(NTT):
                csl = slice(c * 128, (c + 1) * 128)
                ps, keys = ps_mix.get(1)

                def fn(e, ps=ps, c=c):
                    e.matmul(ps[:, 0:4], lhsT=ctri, rhs=g_g[:, c, :], start=True, stop=True)
                    return e.matmul(ps[:, 4:8], lhsT=cones, rhs=g_g[:, c, :], start=True, stop=True)
                P.op("pe", fn, r=["consts", "g_g"], w=keys)
                P.op("act", lambda e, ps=ps: e.activation(out=g_pre[:, 0:8], in_=ps[:, 0:8], func=AF.Copy), r=keys, w=["g_pre"])
                P.op("dve", lambda e: e.tensor_tensor(out=g_pre[:, 8:12], in0=g_pre[:, 4:8], in1=g_pre[:, 0:4], op=ALU.subtract), r=["g_pre"], w=["g_pre2"])
                P.op("act", lambda e: e.activation(out=g_ex[:], in_=g_pre[:], func=AF.Exp), r=["g_pre", "g_pre2"], w=["g_ex"])
                P.op("dve", lambda e, c=c: e.tensor_tensor(out=g_bg[:], in0=g_beta[:, c, :], in1=g_ex[:, 0:4], op=ALU.mult), r=["g_beta", "g_ex"], w=["g_bg"])
                for which, dst, key in ((4, g_ktm, "g_ktm"), (8, g_vtm, "g_vtm")):
                    ps, keys = ps_mix.get(4)

                    def fn(e, ps=ps, which=which, csl=csl):
                        for q in range(4):
                            ins = e.transpose(out=ps[:, q * 128:(q + 1) * 128], in_=cv[:, which + q, csl], identity=cid)
                        return ins
                    P.op("pe", fn, r=[("cv", which + q) for q in range(4)] + ["consts"], w=keys)
                    P.op("act", lambda e, ps=ps, dst=dst: e.activation(out=dst[:], in_=ps[:, 0:512].rearrange("p (h d) -> p h d", h=4), func=AF.Copy), r=keys, w=[key])
                P.op("dve", lambda e, c=c: e.tensor_tensor(out=g_R[:, :, 0:128], in0=g_vtm[:], in1=g_beta[:, c, :].unsqueeze(2).broadcast_to([128, 4, 128]), op=ALU.mult),
                     r=["g_vtm", "g_beta"], w=["g_Rv"])
                P.op("pool", lambda e: e.tensor_tensor(out=g_R[:, :, 128:256], in0=g_ktm[:], in1=g_bg[:].unsqueeze(2).broadcast_to([128, 4, 128]), op=ALU.mult),
                     r=["g_ktm", "g_bg"], w=["g_Rk"])
                P.op("pool", lambda e: e.tensor_tensor(out=g_kd[:], in0=g_ktm[:], in1=g_ex[:, 8:12].unsqueeze(2).broadcast_to([128, 4, 128]), op=ALU.mult),
                     r=["g_ktm", "g_ex"], w=["g_kd"])
                if GSTOP <= 3:
                    continue
                gens = [head_gen(c, csl, h) for h in range(4)]
                live = list(gens)
                while live:
                    nxt = []
                    for g_ in live:
                        try:
                            next(g_)
                            nxt.append(g_)
                        except StopIteration:
                            pass
                    live = nxt
                if GSTOP <= 8:
                    continue
                ok = [("g_o", h) for h in range(4)]
                P.op("act", lambda e: e.activation(out=g_o2[:], in_=g_o[:], func=AF.Square), r=ok, w=["g_o2"])
                P.op("dve", lambda e: e.tensor_reduce(out=g_ssq[:], in_=g_o2[:], axis=mybir.AxisListType.X, op=ALU.add), r=["g_o2"], w=["g_ssq"])
                P.op("act", lambda e: e.activation(out=g_rstd[:], in_=g_ssq[:], func=AF.Sqrt, scale=1.0 / 128, bias=epsc[:]), r=["g_ssq", "epsc"], w=["g_rstd0"])
                P.op("dve", lambda e: e.reciprocal(out=g_rstd[:], in_=g_rstd[:]), r=["g_rstd0"], w=["g_rstd"])
                P.op("pool", lambda e: e.tensor_tensor(out=g_o2[:], in0=g_o[:], in1=g_rstd[:].unsqueeze(2).broadcast_to([128, 4, 128]), op=ALU.mult), r=ok + ["g_rstd", "g_o2"], w=["g_o2"])
                P.op("pool", lambda e: e.tensor_tensor(out=g_o2[:], in0=g_o2[:], in1=prow[:, PR_GNW:PR_GNW + 128].unsqueeze(1).broadcast_to([128, 4, 128]), op=ALU.mult),
                     r=["g_o2", "prow"], w=["g_o2"])
                P.op("dve", lambda e, c=c: e.tensor_tensor(out=g_obf[:], in0=g_o2[:].rearrange("p h d -> p (h d)"), in1=g_sz[:, c, :], op=ALU.mult),
                     r=["g_o2", ("g_sz", c, 0), ("g_sz", c, 1)], w=["g_obf"])
                transpose_out_bf(g_obf, 4, 4, c, ["g_obf"])

        if "ml" in parts:
            m_qT = T("m_qT", [128, 2, TB], BF16)
            m_kT = T("m_kT", [128, 2, TB], BF16)
            m_ktm = T("m_ktm", [128, NTT, 256])
            m_v = T("m_v", [128, NTT, 512], BF16)
            m_g = big_tm[:, :, 0:512]
            m_zt = big_tm[:, :, 512:1024]
            RN = ["li", "t", "a", "l1", "lf", "m", "F", "G", "H", "one"]
            m_rows = {nm: T("m_r_" + nm, [1, TB]) for nm in RN}
            m_rows["wi"] = m_rows["a"]
            m_rows["emm"] = m_rows["l1"]
            m_rows["nG"] = m_rows["t"]
            m_carry = T("m_carry", [1, 1])
            m_C = T("m_C", [128, 2, 512])
            m_Cb = T("m_Cb", [128, 2, 512], BF16)
            m_n = T("m_n", [128, 2])
            m_nb = T("m_nb", [128, 2], BF16)
            m_W = T("m_W", [128, 128])
            m_Wm = T("m_Wm", [128, 128])
            m_sc = T("m_sc", [128, 128], BF16)
            m_cols = T("m_cols", [128, 4])
            m_tmp = T("m_tmp", [128, 512])
            m_h = T("m_h", [128, 512])
            m_den = T("m_den", [128, 2])
            m_d1 = T("m_d1", [128, 1])
            m_rden = T("m_rden", [128, 1])
            m_ssq = T("m_ssq", [128, 1])
            m_rstd = T("m_rstd", [128, 1])
            m_comb = T("m_comb", [128, 1])
            m_g2 = T("m_g2", [128, 512])
            m_hbf = T("m_hbf", [128, 512], BF16)
            m_kws = T("m_kws", [128, 256], BF16)
            m_onesb = T("m_onesb", [128, 1], BF16)
            P.op("pool", lambda e: e.memset(m_C[:], 0.0), w=["m_C"])
            P.op("pool", lambda e: e.memset(m_Cb[:], 0.0), w=["m_Cb"])
            P.op("pool", lambda e: e.memset(m_n[:], 0.0), w=["m_n"])
            P.op("pool", lambda e: e.memset(m_nb[:], 0.0), w=["m_nb"])
            P.op("pool", lambda e: e.memset(m_carry[:], 0.0), w=["m_carry"])
            P.op("pool", lambda e: e.memset(m_rows["one"][:], 1.0), w=["m_one"])
            P.op("pool", lambda e: e.memset(m_onesb[:], 1.0), w=["m_onesb"])

        def ml_proj(tb, part="all"):
            if part == "o":
                for ti, t in ((2, 16), (3, 17)):
                    s = load_w(t)
                    for tt in range(NTT):
                        ps, keys = proj_tm(s, tt)
                        cs = slice((ti % 2) * WT, (ti % 2 + 1) * WT)
                        P.op("act", lambda e, ps=ps, tt=tt, cs=cs: e.activation(out=m_g[:, tt, cs], in_=ps[:, 0:WT], func=AF.Sigmoid), r=keys, w=[("m_g", tt, ti % 2)])
                return
            s = load_w(0)
            for j in range(2):
                ps, keys = proj_fm(s, j)
                P.op("act", lambda e, ps=ps, j=j: e.activation(out=m_qT[:, j, :], in_=ps[:, 0:TB], func=AF.Identity, scale=256 ** -0.5), r=keys, w=[("m_qT", j)])
            s = load_w(1)
            for j in range(2):
                ps, keys = proj_fm(s, j)
                P.op("act", lambda e, ps=ps, j=j: e.activation(out=m_kT[:, j, :], in_=ps[:, 0:TB], func=AF.Copy), r=keys, w=[("m_kT", j)])
            for ti, t in enumerate(range(14, 20)):
                if part == "a" and ti in (2, 3):
                    continue
                s = load_w(t)
                for tt in range(NTT):
                    ps, keys = proj_tm(s, tt)
                    cs = slice((ti % 2) * WT, (ti % 2 + 1) * WT)
                    if ti < 2:
                        P.op("act", lambda e, ps=ps, tt=tt, cs=cs: e.activation(out=m_v[:, tt, cs], in_=ps[:, 0:WT], func=AF.Copy), r=keys, w=[("m_v", tt, ti)])
                    elif ti < 4:
                        P.op("act", lambda e, ps=ps, tt=tt, cs=cs: e.activation(out=m_g[:, tt, cs], in_=ps[:, 0:WT], func=AF.Sigmoid), r=keys, w=[("m_g", tt, ti % 2)])
                    else:
                        P.op("act", lambda e, ps=ps, tt=tt, cs=cs: e.activation(out=m_zt[:, tt, cs], in_=ps[:, 0:WT], func=AF.Silu), r=keys, w=[("m_zt", tt, ti % 2)])
            for tt in range(NTT):
                ps, keys = ps_proj.get(1)
                psv = ps.bitcast(BF16)

                def fn(e, psv=psv, tt=tt):
                    e.transpose(out=psv[:, 0:128], in_=m_kT[:, 0, tt * 128:(tt + 1) * 128], identity=idb[:])
                    return e.transpose(out=psv[:, 128:256], in_=m_kT[:, 1, tt * 128:(tt + 1) * 128], identity=idb[:])
                P.op("pe", fn, r=[("m_kT", 0), ("m_kT", 1), "idb"], w=keys)
                P.op("act", lambda e, psv=psv, tt=tt: e.activation(out=m_ktm[:, tt, :], in_=psv[:, 0:256], func=AF.Copy), r=keys, w=[("m_ktm", tt)])
            smk_ = [("small", tt) for tt in range(NTT)]
            for col, nm, boff in ((24, "li", PR_MIB), (25, "t", PR_MFB)):
                ps, keys = ps_proj.get(2)

                def fn(e, ps=ps, col=col):
                    for tt in range(NTT):
                        ins = e.transpose(out=ps[0:1, tt * 128:(tt + 1) * 128], in_=s_small[:, tt, col:col + 1], identity=cid)
                    return ins
                P.op("pe", fn, r=smk_ + ["consts"], w=keys)
                P.op("act", lambda e, ps=ps, nm=nm, boff=boff: e.activation(out=m_rows[nm][:], in_=ps[0:1, 0:TB], func=AF.Identity, bias=prow[0:1, boff:boff + 1]),
                     r=keys + ["prow"], w=["m_" + nm])

        def ml_mixer(tb):
            R_ = m_rows
            for tt in range(NTT):
                P.op("pool", lambda e, tt=tt: e.tensor_tensor(out=m_g[:, tt, :], in0=m_g[:, tt, :], in1=m_zt[:, tt, :], op=ALU.mult),
                     r=[("m_g", tt, 0), ("m_g", tt, 1), ("m_zt", tt, 0), ("m_zt", tt, 1)], w=[("m_g", tt, 0), ("m_g", tt, 1)])
            P.op("act", lambda e: e.activation(out=R_["a"][:], in_=R_["t"][:], func=AF.Abs), r=["m_t"], w=["m_a"])
            P.op("act", lambda e: e.activation(out=R_["a"][:], in_=R_["a"][:], func=AF.Exp, scale=-1.0), r=["m_a"], w=["m_a2"])
            P.op("act", lambda e: e.activation(out=R_["l1"][:], in_=R_["a"][:], func=AF.Ln, bias=1.0), r=["m_a2"], w=["m_l1"])
            P.op("dve", lambda e: e.scalar_tensor_tensor(out=R_["lf"][:], in0=R_["t"][:], scalar=0.0, in1=R_["l1"][:], op0=ALU.min, op1=ALU.subtract), r=["m_t", "m_l1"], w=["m_lf"])
            P.op("dve", lambda e: e.tensor_tensor_scan(out=R_["m"][:], data0=R_["lf"][:], data1=R_["li"][:], initial=m_carry[:], op0=ALU.add, op1=ALU.max),
                 r=["m_lf", "m_li", "m_carry"], w=["m_m"])
            P.op("dve", lambda e: e.tensor_tensor_scan(out=R_["F"][:], data0=R_["one"][:], data1=R_["lf"][:], initial=0.0, op0=ALU.mult, op1=ALU.add),
                 r=["m_lf", "m_one"], w=["m_F"])
            P.op("dve", lambda e: e.tensor_tensor(out=R_["G"][:], in0=R_["F"][:], in1=R_["m"][:], op=ALU.subtract), r=["m_F", "m_m"], w=["m_G"])
            P.op("dve", lambda e: e.tensor_tensor(out=R_["H"][:], in0=R_["li"][:], in1=R_["F"][:], op=ALU.subtract), r=["m_F", "m_li"], w=["m_H"])
            P.op("dve", lambda e: e.tensor_scalar(out=R_["nG"][:], in0=R_["G"][:], scalar1=-1.0, scalar2=None, op0=ALU.mult), r=["m_G"], w=["m_nG"])
            P.op("act", lambda e: e.activation(out=R_["emm"][:], in_=R_["m"][:], func=AF.Exp, scale=-1.0), r=["m_m"], w=["m_emm"])
            one = R_["one"]
            for c in range(NTT):
                csl = slice(c * 128, (c + 1) * 128)
                bias_ap = m_carry[:] if c == 0 else R_["nG"][0:1, c * 128 - 1:c * 128]
                P.op("act", lambda e, csl=csl, bias_ap=bias_ap: e.activation(out=R_["wi"][0:1, csl], in_=R_["G"][0:1, csl], func=AF.Exp, bias=bias_ap),
                     r=["m_G", "m_nG", "m_carry"], w=["m_wi"])
                ps, keys = ps_mix.get(1)

                def fn(e, ps=ps, csl=csl):
                    e.matmul(ps[:, 0:128], lhsT=R_["H"][0:1, csl], rhs=one[0:1, 0:128], start=True, stop=False)
                    return e.matmul(ps[:, 0:128], lhsT=one[0:1, 0:128], rhs=R_["G"][0:1, csl], start=False, stop=True)
                P.op("pe", fn, r=["m_H", "m_G", "m_one"], w=keys)
                P.op("dve", lambda e, ps=ps: e.tensor_scalar(out=m_W[:], in0=ps[:, 0:128], scalar1=0.0, scalar2=None, op0=ALU.min), r=keys, w=["m_W0"])
                P.op("act", lambda e: e.activation(out=m_W[:], in_=m_W[:], func=AF.Exp), r=["m_W0"], w=["m_W"])
                P.op("pool", lambda e: e.tensor_tensor(out=m_Wm[:], in0=m_W[:], in1=ctri, op=ALU.mult), r=["m_W", "consts"], w=["m_Wm"])
                ps, keys = ps_mix.get(1)

                def fn(e, ps=ps, csl=csl, c=c):
                    e.matmul(ps[:, 0:1], lhsT=R_["wi"][0:1, csl], rhs=one[0:1, 0:1], start=True, stop=True)
                    e.matmul(ps[:, 1:2], lhsT=R_["emm"][0:1, csl], rhs=one[0:1, 0:1], start=True, stop=True)
                    return e.matmul(ps[:, 2:3], lhsT=one[0:1, 0:128], rhs=R_["wi"][0:1, c * 128 + 127:c * 128 + 128], start=True, stop=True)
                P.op("pe", fn, r=["m_wi", "m_emm", "m_one"], w=keys)
                P.op("act", lambda e, ps=ps: e.activation(out=m_cols[:, 0:3], in_=ps[:, 0:3], func=AF.Copy), r=keys, w=["m_cols"])
                ps, keys = ps_mix.get(1)

                def fn(e, ps=ps, csl=csl):
                    e.matmul(ps[:, 0:128], lhsT=m_kT[:, 0, csl], rhs=m_qT[:, 0, csl], start=True, stop=False)
                    return e.matmul(ps[:, 0:128], lhsT=m_kT[:, 1, csl], rhs=m_qT[:, 1, csl], start=False, stop=True)
                P.op("pe", fn, r=[("m_kT", 0), ("m_kT", 1), ("m_qT", 0), ("m_qT", 1)], w=keys)
                P.op("dve", lambda e, ps=ps: e.tensor_tensor(out=m_sc[:], in0=ps[:, 0:128], in1=m_Wm[:], op=ALU.mult), r=keys + ["m_Wm"], w=["m_sc"])
                psi, ki = ps_mix.get(4)

                def fn(e, psi=psi, csl=csl):
                    e.matmul(psi[:, 0:512], lhsT=m_qT[:, 0, csl], rhs=m_Cb[:, 0, :], start=True, stop=False)
                    return e.matmul(psi[:, 0:512], lhsT=m_qT[:, 1, csl], rhs=m_Cb[:, 1, :], start=False, stop=True)
                P.op("pe", fn, r=[("m_qT", 0), ("m_qT", 1), "m_Cb"], w=ki)
                P.op("dve", lambda e, psi=psi: e.tensor_scalar(out=m_tmp[:], in0=psi[:, 0:512], scalar1=m_cols[:, 0:1], scalar2=None, op0=ALU.mult), r=ki + ["m_cols"], w=["m_tmp"])
                psa, ka = ps_mix.get(4)
                P.op("pe", lambda e, psa=psa, c=c: e.matmul(psa[:, 0:512], lhsT=m_sc[:], rhs=m_v[:, c, :], start=True, stop=True), r=["m_sc", ("m_v", c, 0), ("m_v", c, 1)], w=ka)
                P.op("dve", lambda e, psa=psa: e.tensor_tensor(out=m_h[:], in0=m_tmp[:], in1=psa[:, 0:512], op=ALU.add), r=ka + ["m_tmp"], w=["m_h"])
                ps, keys = ps_mix.get(1)

                def fn(e, ps=ps, csl=csl):
                    e.matmul(ps[:, 0:1], lhsT=m_sc[:], rhs=m_onesb[:], start=True, stop=True)
                    e.matmul(ps[:, 1:2], lhsT=m_qT[:, 0, csl], rhs=m_nb[:, 0:1], start=True, stop=False)
                    return e.matmul(ps[:, 1:2], lhsT=m_qT[:, 1, csl], rhs=m_nb[:, 1:2], start=False, stop=True)
                P.op("pe", fn, r=["m_sc", "m_onesb", ("m_qT", 0), ("m_qT", 1), "m_nb"], w=keys)
                P.op("act", lambda e, ps=ps: e.activation(out=m_den[:], in_=ps[:, 0:2], func=AF.Copy), r=keys, w=["m_den"])
                P.op("dve", lambda e: e.scalar_tensor_tensor(out=m_d1[:], in0=m_den[:, 1:2], scalar=m_cols[:, 0:1], in1=m_den[:, 0:1], op0=ALU.mult, op1=ALU.add),
                     r=["m_den", "m_cols"], w=["m_d1"])
                P.op("act", lambda e: e.activation(out=m_d1[:], in_=m_d1[:], func=AF.Abs), r=["m_d1"], w=["m_d1b"])
                P.op("dve", lambda e: e.tensor_tensor(out=m_rden[:], in0=m_d1[:], in1=m_cols[:, 1:2], op=ALU.max), r=["m_d1b", "m_cols"], w=["m_rden0"])
                P.op("dve", lambda e: e.reciprocal(out=m_rden[:], in_=m_rden[:]), r=["m_rden0"], w=["m_rden"])
                P.op("dve", lambda e: e.tensor_scalar(out=m_h[:], in0=m_h[:], scalar1=m_rden[:], scalar2=None, op0=ALU.mult), r=["m_h", "m_rden"], w=["m_h"])
                P.op("act", lambda e: e.activation(out=junk[:], in_=m_h[:], func=AF.Square, accum_out=m_ssq[:]), r=["m_h"], w=["junk", "m_ssq"])
                P.op("act", lambda e: e.activation(out=m_rstd[:], in_=m_ssq[:], func=AF.Sqrt, scale=1.0 / 512, bias=epsc[:]), r=["m_ssq", "epsc"], w=["m_rstd0"])
                P.op("dve", lambda e: e.reciprocal(out=m_rstd[:], in_=m_rstd[:]), r=["m_rstd0"], w=["m_rstd"])
                P.op("dve", lambda e: e.tensor_copy(out=m_comb[:], in_=m_rstd[:]), r=["m_rstd"], w=["m_comb"])
                P.op("pool", lambda e, c=c: e.tensor_tensor(out=m_g2[:], in0=m_g[:, c, :], in1=prow[:, PR_MLNW:PR_MLNW + 512], op=ALU.mult), r=[("m_g", c, 0), ("m_g", c, 1), "prow"], w=["m_g2"])
                P.op("dve", lambda e: e.scalar_tensor_tensor(out=m_hbf[:], in0=m_h[:], scalar=m_comb[:], in1=m_g2[:], op0=ALU.mult, op1=ALU.mult),
                     r=["m_h", "m_comb", "m_g2"], w=["m_hbf"])
                transpose_out_bf(m_hbf, 4, 0, c, ["m_hbf"])
                P.op("dve", lambda e, c=c: e.tensor_scalar(out=m_kws[:], in0=m_ktm[:, c, :], scalar1=m_W[:, 127:128], scalar2=None, op0=ALU.mult), r=[("m_ktm", c), "m_W"], w=["m_kws"])
                for kc in range(2):
                    ps, keys = ps_mix.get(4)
                    P.op("pe", lambda e, ps=ps, kc=kc, c=c: e.matmul(ps[:, 0:512], lhsT=m_kws[:, kc * 128:(kc + 1) * 128], rhs=m_v[:, c, :], start=True, stop=True),
                         r=["m_kws", ("m_v", c, 0), ("m_v", c, 1)], w=keys)
                    P.op("dve", lambda e, ps=ps, kc=kc: e.scalar_tensor_tensor(out=m_C[:, kc, :], in0=m_C[:, kc, :], scalar=m_cols[:, 2:3], in1=ps[:, 0:512], op0=ALU.mult, op1=ALU.add),
                         r=keys + ["m_C", "m_cols", "m_Cb"], w=["m_C"])
                ps, keys = ps_mix.get(1)

                def fn(e, ps=ps):
                    e.matmul(ps[:, 0:1], lhsT=m_kws[:, 0:128], rhs=m_onesb[:], start=True, stop=True)
                    return e.matmul(ps[:, 1:2], lhsT=m_kws[:, 128:256], rhs=m_onesb[:], start=True, stop=True)
                P.op("pe", fn, r=["m_kws", "m_onesb"], w=keys)
                P.op("dve", lambda e, ps=ps: e.scalar_tensor_tensor(out=m_n[:], in0=m_n[:], scalar=m_cols[:, 2:3], in1=ps[:, 0:2], op0=ALU.mult, op1=ALU.add),
                     r=keys + ["m_n", "m_cols"], w=["m_n"])
                P.op("pool", lambda e: e.tensor_copy(out=m_Cb[:], in_=m_C[:]), r=["m_C"], w=["m_Cb"])
                P.op("pool", lambda e: e.tensor_copy(out=m_nb[:], in_=m_n[:]), r=["m_n"], w=["m_nb"])
            P.op("dve", lambda e: e.tensor_copy(out=m_carry[:], in_=R_["m"][0:1, TB - 1:TB]), r=["m_m", "m_wi"], w=["m_carry"])

        if "ssm" not in parts or "gdn" not in parts or "ml" not in parts:
            P.op("pool", lambda e: e.memset(ymixT[:], 0.0), w=[("ymixT", b) for b in range(16)])
        ymk = [("ymixT", b) for b in (0, 4, 8, 12)] + [("ymixT", b) for b in range(16)]

        def outproj(tb):
            if debug:
                P.op("pool", lambda e, tb=tb: e.dma_start(out=ymix_d[:, :, tb * TB:(tb + 1) * TB].rearrange("b p t -> p b t"), in_=ymixT[:]),
                     r=ymk, w=[("ymixd", tb)], dma="ymixd")
            n = 0
            for t in range(NOC):
                s = load_wo(t)
                wv_ = wo_view(s)
                for tt in range(NTT):
                    ps, keys = ps_proj.get(OW // 128)

                    def fn(e, ps=ps, wv_=wv_, tt=tt):
                        for kc in range(16):
                            ins = e.matmul(ps[:, 0:OW], lhsT=ymixT[:, kc, tt * 128:(tt + 1) * 128], rhs=wv_[:, kc, :], start=(kc == 0), stop=(kc == 15))
                        return ins
                    P.op("pe", fn, r=ymk + [("wt", s)], w=keys)
                    r0 = tb * TB + tt * 128
                    for hf in range(OW // WT):
                        ss = n % 2
                        n += 1
                        P.op("act", lambda e, ps=ps, ss=ss, hf=hf: e.activation(out=stage[ss][:], in_=ps[:, hf * WT:(hf + 1) * WT], func=AF.Copy), r=keys, w=[("stage", ss)])
                        c0 = t * OW + hf * WT
                        P.op("pool", lambda e, ss=ss, r0=r0, c0=c0: e.dma_start(out=ypart_d[r0:r0 + 128, c0:c0 + WT], in_=stage[ss][:]),
                             r=[("stage", ss)], w=[("ypart", r0, c0)], dma=("stage", ss))

        full = all(p_ in parts for p_ in ("ssm", "gdn", "ml"))
        if not full or os.environ.get("NOMERGE"):
            for tb in range(NB):
                load_xT(tb)
                if "ssm" in parts:
                    ssm_proj(tb)
                if "ssm" in parts or "gdn" in parts:
                    small_proj(tb)
                if "ssm" in parts:
                    ssm_mixer(tb)
                if "gdn" in parts:
                    gdn_proj(tb)
                    gdn_mixer(tb)
                if "ml" in parts:
                    ml_proj(tb)
                    ml_mixer(tb)
                outproj(tb)
        else:
            def head_phase(tb):
                ssm_proj(tb, "fm")
                small_proj(tb)

            def tail_phase(tb):
                ml_mixer(tb)
                outproj(tb)
            load_xT(0)
            head_phase(0)
            ssm_proj(0, "z")
            for tb in range(NB):
                A = P.capture(ssm_mixer, tb)
                mk = A.index(None)
                B = P.capture(gdn_proj, tb, "fm")
                P.replay(A[:mk])
                if "1" in MERGE:
                    P.merge(A[mk:], B)
                else:
                    P.replay(A[mk:])
                    P.replay(B)
                gdn_proj(tb, "z")
                A = P.capture(gdn_mixer, tb)
                B = P.capture(ml_proj, tb, "a")
                if "2" in MERGE:
                    P.merge(A, B)
                else:
                    P.replay(B)
                    P.replay(A)
                ml_proj(tb, "o")
                A = P.capture(ml_mixer, tb)
                if tb + 1 < NB:
                    load_xT(tb + 1)
                    B = P.capture(head_phase, tb + 1)
                    if "3" in MERGE:
                        P.merge(A, B)
                    else:
                        P.replay(A)
                        P.replay(B)
                    outproj(tb)
                    ssm_proj(tb + 1, "z")
                else:
                    P.replay(A)
                    outproj(tb)
        P.emit()
        build_layer.last_prog = P
    return nc


ML_OFF, GDN_OFF, SSM_OFF = 0, 8200, 16424


def core_columns(g):
    r = np.arange
    ml = ML_OFF
    gd = GDN_OFF
    ss = SSM_OFF
    cols = [
        ml + g * 256 + r(256),
        ml + 1024 + g * 256 + r(256),
        gd + g * 512 + r(512),
        gd + 2048 + g * 512 + r(512),
        gd + 4096 + g * 512 + r(512),
        ss + 4096 + g * 1024 + r(1024),
        ss + 8192 + g * 256 + r(256),
        ss + 9216 + g * 256 + r(256),
        ml + 2048 + g * 512 + r(512),
        ml + 4096 + g * 512 + r(512),
        ml + 6144 + g * 512 + r(512),
        gd + 6144 + g * 512 + r(512),
        ss + g * 1024 + r(1024),
        ml + 1024 + g * 256 + r(256),
        gd + 8192 + g * 4 + r(4),
        gd + 8208 + g * 4 + r(4),
        ss + 10240 + g * 16 + r(16),
        np.array([ml + 8192 + g]),
        np.array([ml + 8196 + g]),
    ]
    return np.concatenate(cols)


def pack_core(inp, l, g):
    cols = core_columns(g)
    D = inp["w_in"].shape[1]
    wcat = np.zeros((D, NCOL), np.float32)
    wcat[:, :cols.size] = inp["w_in"][l][:, cols]
    mixrows = np.concatenate([g * 512 + np.arange(512), 2048 + g * 512 + np.arange(512), 4096 + g * 1024 + np.arange(1024)])
    wout = np.ascontiguousarray(inp["w_out"][l][mixrows, :])
    gch = np.concatenate([g * 512 + np.arange(512), 2048 + g * 512 + np.arange(512), 4096 + g * 512 + np.arange(512)])
    sch = np.concatenate([g * 1024 + np.arange(1024), 4096 + g * 256 + np.arange(256), 5120 + g * 256 + np.arange(256)])
    cp = np.zeros((24 * 128, 5), np.float32)
    cp[:1536, 0:4] = inp["gdn_conv_w"][l][:, gch].T
    cp[1536:, 0:4] = inp["ssm_conv_w"][l][:, sch].T
    cp[1536:, 4] = inp["ssm_conv_b"][l][sch]
    convp = np.ascontiguousarray(cp.reshape(24, 128, 5).transpose(1, 0, 2).reshape(128, 120))
    pr = np.zeros((1, PR_N), np.float32)
    pr[0, PR_SDTB:PR_SDTB + 16] = inp["ssm_dt_bias"][l][16 * g:16 * g + 16]
    pr[0, PR_SALOG:PR_SALOG + 16] = inp["ssm_A_log"][l][16 * g:16 * g + 16]
    pr[0, PR_SD:PR_SD + 16] = inp["ssm_D"][l][16 * g:16 * g + 16]
    pr[0, PR_GDTB:PR_GDTB + 4] = inp["gdn_dt_bias"][l][4 * g:4 * g + 4]
    pr[0, PR_GALOG:PR_GALOG + 4] = inp["gdn_A_log"][l][4 * g:4 * g + 4]
    pr[0, PR_MIB] = inp["ml_i_bias"][l][g]
    pr[0, PR_MFB] = inp["ml_f_bias"][l][g]
    pr[0, PR_MLNW:PR_MLNW + 512] = inp["ml_norm_w"][l][g * 512:(g + 1) * 512]
    pr[0, PR_GNW:PR_GNW + 128] = inp["gdn_norm_w"][l]
    pr[0, PR_SNW:PR_SNW + 1024] = inp["ssm_norm_w"][l][g * 1024:(g + 1) * 1024]
    return {"wcat": wcat, "wout": wout, "convp": convp, "prow": pr, "consts": make_consts()}


def build_reduce_ln(ROWS=1024, D=D_MODEL):
    nc = bass.Bass("TRN2", target_bir_lowering=False)
    x_d = nc.dram_tensor("xin", [ROWS, D], F32, kind="ExternalInput").ap()
    p_d = [nc.dram_tensor(f"p{j}", [ROWS, D], F32, kind="ExternalInput").ap() for j in range(4)]
    g_d = nc.dram_tensor("lng", [1, D], F32, kind="ExternalInput").ap()
    b_d = nc.dram_tensor("lnb", [1, D], F32, kind="ExternalInput").ap()
    o_d = nc.dram_tensor("out", [ROWS, D], F32, kind="ExternalOutput").ap()
    CW = 1024 if D >= 1024 else D
    NCH = D // CW
    with ExitStack() as st:
        def T(name, shape, dt=F32):
            return st.enter_context(nc.sbuf_tensor("sb_" + name, shape, dt))
        P = Prog(nc, st)
        gb = T("gb", [128, D])
        bb = T("bb", [128, D])
        acc = [T(f"acc{i}", [128, D]) for i in range(2)]
        ld = [[T(f"ld{i}_{j}", [128, CW]) for j in range(5)] for i in range(2)]
        junk = T("junk", [128, D])
        stt = T("stt", [128, 8])
        epsl = T("epsl", [128, 1])
        P.op("sp", lambda e: e.dma_start(out=gb[:], in_=g_d.partition_broadcast(128)[:, 0, :]), w=["gb"], dma="gb")
        P.op("sp", lambda e: e.dma_start(out=bb[:], in_=b_d.partition_broadcast(128)[:, 0, :]), w=["bb"], dma="bb")
        P.op("pool", lambda e: e.memset(epsl[:], LN_EPS), w=["epsl"])
        n = 0
        for ti in range(ROWS // 128):
            r0 = ti * 128
            a = acc[ti % 2]
            ak = ("acc", ti % 2)
            for ch in range(NCH):
                s = n % 2
                n += 1
                cs = slice(ch * CW, (ch + 1) * CW)
                srcs = [x_d] + p_d
                for j in range(5):
                    q = "sp" if j % 2 == 0 else "act"
                    P.op(q, lambda e, s=s, j=j, cs=cs, r0=r0, srcs=srcs: e.dma_start(out=ld[s][j][:], in_=srcs[j][r0:r0 + 128, cs]),
                         w=[("ld", s, j)], dma=("ld", s, j))
                P.op("dve", lambda e, s=s, cs=cs, a=a: e.scalar_tensor_tensor(out=a[:, cs], in0=ld[s][0][:], scalar=float(ALPHA), in1=ld[s][1][:], op0=ALU.mult, op1=ALU.add),
                     r=[("ld", s, 0), ("ld", s, 1)], w=[ak])
                P.op("pool", lambda e, s=s: e.tensor_tensor(out=ld[s][2][:], in0=ld[s][2][:], in1=ld[s][3][:], op=ALU.add), r=[("ld", s, 2), ("ld", s, 3)], w=[("ld", s, 2)])
                P.op("pool", lambda e, s=s: e.tensor_tensor(out=ld[s][2][:], in0=ld[s][2][:], in1=ld[s][4][:], op=ALU.add), r=[("ld", s, 2), ("ld", s, 4)], w=[("ld", s, 2)])
                P.op("dve", lambda e, s=s, cs=cs, a=a: e.tensor_tensor(out=a[:, cs], in0=a[:, cs], in1=ld[s][2][:], op=ALU.add), r=[ak, ("ld", s, 2)], w=[ak])
            P.op("act", lambda e, a=a: e.activation(out=junk[:], in_=a[:], func=AF.Identity, accum_out=stt[:, 0:1]), r=[ak], w=["junk", "st0"])
            P.op("dve", lambda e: e.tensor_scalar(out=stt[:, 1:2], in0=stt[:, 0:1], scalar1=-1.0 / D, scalar2=None, op0=ALU.mult), r=["st0"], w=["st1"])
            P.op("dve", lambda e, a=a: e.tensor_scalar(out=a[:], in0=a[:], scalar1=stt[:, 1:2], scalar2=None, op0=ALU.add), r=[ak, "st1"], w=[ak])
            P.op("act", lambda e, a=a: e.activation(out=junk[:], in_=a[:], func=AF.Square, accum_out=stt[:, 2:3]), r=[ak], w=["junk", "st2"])
            P.op("act", lambda e: e.activation(out=stt[:, 3:4], in_=stt[:, 2:3], func=AF.Sqrt, scale=1.0 / D, bias=epsl[:]), r=["st2", "epsl"], w=["st3"])
            P.op("dve", lambda e: e.reciprocal(out=stt[:, 4:5], in_=stt[:, 3:4]), r=["st3"], w=["st4"])
            P.op("dve", lambda e, a=a: e.scalar_tensor_tensor(out=a[:], in0=a[:], scalar=stt[:, 4:5], in1=gb[:], op0=ALU.mult, op1=ALU.mult), r=[ak, "st4", "gb"], w=[ak])
            P.op("pool", lambda e, a=a: e.tensor_tensor(out=a[:], in0=a[:], in1=bb[:], op=ALU.add), r=[ak, "bb"], w=[ak])
            P.op("pool", lambda e, a=a, r0=r0: e.dma_start(out=o_d[r0:r0 + 128, :], in_=a[:]), r=[ak], w=[("out", ti)], dma=ak)
        P.emit()
    return nc


_PROGS = {}


def kernel(x, w_in, w_out, ml_i_bias, ml_f_bias, ml_norm_w, gdn_conv_w, gdn_A_log, gdn_dt_bias,
           gdn_norm_w, ssm_conv_w, ssm_conv_b, ssm_A_log, ssm_dt_bias, ssm_D, ssm_norm_w, ln_g, ln_b):
    inp = dict(w_in=np.asarray(w_in), w_out=np.asarray(w_out), ml_i_bias=np.asarray(ml_i_bias), ml_f_bias=np.asarray(ml_f_bias),
               ml_norm_w=np.asarray(ml_norm_w), gdn_conv_w=np.asarray(gdn_conv_w), gdn_A_log=np.asarray(gdn_A_log),
               gdn_dt_bias=np.asarray(gdn_dt_bias), gdn_norm_w=np.asarray(gdn_norm_w), ssm_conv_w=np.asarray(ssm_conv_w),
               ssm_conv_b=np.asarray(ssm_conv_b), ssm_A_log=np.asarray(ssm_A_log), ssm_dt_bias=np.asarray(ssm_dt_bias),
               ssm_D=np.asarray(ssm_D), ssm_norm_w=np.asarray(ssm_norm_w))
    ln_g = np.asarray(ln_g, np.float32)
    ln_b = np.asarray(ln_b, np.float32)
    xcur = np.ascontiguousarray(np.asarray(x, np.float32))
    B, S, D = xcur.shape
    if "layer" not in _PROGS:
        _PROGS["layer"] = build_layer(D=D, S=S)
        _PROGS["ln"] = build_reduce_ln(ROWS=B * S // 8, D=D)
    RPC = B * S // 8
    for l in range(DEPTH):
        packs = [pack_core(inp, l, g) for g in range(NG)]
        in_maps = [dict(packs[c % NG], x=xcur[c // NG]) for c in range(8)]
        res = run_bass_kernel_spmd(_PROGS["layer"], in_maps, core_ids=list(range(8)))
        yp = [r["ypart"] for r in res.results]
        del in_maps, packs
        xf = xcur.reshape(B * S, D)
        in_maps = []
        for c in range(8):
            b = (c * RPC) // S
            r0 = (c * RPC) % S
            m = {"xin": np.ascontiguousarray(xf[c * RPC:(c + 1) * RPC]), "lng": ln_g[l][None, :], "lnb": ln_b[l][None, :]}
            for j in range(NG):
                m[f"p{j}"] = np.ascontiguousarray(yp[b * NG + j][r0:r0 + RPC])
            in_maps.append(m)
        res = run_bass_kernel_spmd(_PROGS["ln"], in_maps, core_ids=list(range(8)))
        xcur = np.concatenate([r["out"] for r in res.results], axis=0).reshape(B, S, D)
    return xcur.astype(np.float32)
```

```python
from contextlib import ExitStack
import os
GSTOP = int(os.environ.get('GSTOP', '99'))
MERGE = os.environ.get('MERGE', '123')
import numpy as np
import concourse.bass as bass
import concourse.mybir as mybir
from concourse.bass_utils import run_bass_kernel_spmd

F32 = mybir.dt.float32
BF16 = mybir.dt.bfloat16
ALU = mybir.AluOpType
AF = mybir.ActivationFunctionType

SIG_ROT = 30000


class Prog:
    ENG = ("pe", "act", "dve", "pool", "sp")

    def __init__(self, nc, stack):
        self._stack = stack
        self.nc = nc
        self.ops = []
        self.lw = {}
        self.rs = {}
        self.dma_cnt = {}

    cap = None

    def capture(self, f, *a):
        assert self.cap is None
        self.cap = []
        f(*a)
        L, self.cap = self.cap, None
        return L

    def replay(self, L):
        for a in L:
            if a is not None:
                self.op(*a)

    def mark(self):
        if self.cap is not None:
            self.cap.append(None)

    def merge(self, A, B):
        A = [a for a in A if a is not None]
        B = [b for b in B if b is not None]
        ia = ib = 0
        while ia < len(A) or ib < len(B):
            if ib >= len(B) or (ia < len(A) and ia * len(B) <= ib * len(A)):
                self.op(*A[ia])
                ia += 1
            else:
                self.op(*B[ib])
                ib += 1

    def op(self, eng, fn, r=(), w=(), dma=None, inc=16):
        if self.cap is not None:
            self.cap.append((eng, fn, list(r), list(w), dma, inc))
            return
        r = [self.canon(k) for k in r]
        w = [self.canon(k) for k in w]
        i = len(self.ops)
        deps = set()
        raw = set()
        for k in r:
            j = self.lw.get(k)
            if j is not None:
                deps.add(j)
                raw.add(j)
        for k in w:
            j = self.lw.get(k)
            if j is not None:
                deps.add(j)
            for j in self.rs.get(k, ()):
                deps.add(j)
        keep = set()
        for j in deps:
            pj = self.ops[j]
            if pj["dma"] is None and dma is None and pj["eng"] == eng:
                if eng == "pe":
                    continue
            keep.add(j)
        o = dict(eng=eng, fn=fn, deps=keep, dma=dma, sig=False, sidx=None, inc=inc)
        if dma is not None:
            c = self.dma_cnt.get(dma, 0) + 1
            self.dma_cnt[dma] = c
            o["sidx"] = c
            o["sig"] = True
        self.ops.append(o)
        for j in keep:
            self.ops[j]["sig"] = True
        for k in r:
            self.rs.setdefault(k, []).append(i)
        for k in w:
            self.lw[k] = i
            self.rs[k] = []
        return i

    ALIAS = {"g_rn0": "g_rn", "s_rstd0": "s_rstd", "g_rstd0": "g_rstd", "m_rstd0": "m_rstd", "m_rden0": "m_rden",
             "m_W0": "m_W", "m_d1b": "m_d1", "m_a2": "m_a", "s_dt_b": "s_dt_a", "g_sp_b": "g_sp_a", "sUh": "Uh", "gUh": "Uh",
             "m_wi": "m_a", "m_emm": "m_l1", "m_nG": "m_t", "g_ktm": ("s_xtm", 0), "g_vtm": ("s_xtm", 1),
             "g_Rv": "s_xdt", "g_Rk": "s_xdt"}

    def canon(self, k):
        if isinstance(k, tuple):
            if k[0] == "cvp":
                return ("cv",) + k[1:]
            if k[0] in ("sU", "gU"):
                return ("U",) + k[1:]
            if k[0] in ("g_sz", "m_g"):
                return ("s_sz",) + k[1:]
            if k[0] == "m_zt":
                return ("s_sz", k[1], 2 + k[2])
            return k
        return self.ALIAS.get(k, k)

    def emit(self, final_wait_eng="sp"):
        nc = self.nc
        cnt = {e: 0 for e in self.ENG}
        for o in self.ops:
            if o["dma"] is None and o["sig"]:
                cnt[o["eng"]] += 1
                o["sidx"] = cnt[o["eng"]]
        sems = {}
        stack = self._stack
        for e in self.ENG:
            n = (cnt[e] + SIG_ROT - 1) // SIG_ROT
            for q in range(max(n, 1)):
                sems[("E", e, q)] = stack.enter_context(nc.semaphore(f"s_{e}_{q}"))
        for k in self.dma_cnt:
            nm = "d_" + "_".join(str(x) for x in (k if isinstance(k, tuple) else (k,)))
            sems[("D", k)] = stack.enter_context(nc.semaphore(nm))
        self.nsem = len(sems)

        def chan(o):
            if o["dma"] is not None:
                return ("D", o["dma"]), o["inc"] * o["sidx"]
            q, rr = divmod(o["sidx"] - 1, SIG_ROT)
            return ("E", o["eng"], q), rr + 1

        per = {e: [] for e in self.ENG}
        for o in self.ops:
            per[o["eng"]].append(o)
        final = {}
        for o in self.ops:
            if o["dma"] is not None:
                ch, v = chan(o)
                final[ch] = max(final.get(ch, 0), v)
        block = stack.enter_context(nc.Block())
        deco = {"pe": block.tensor, "act": block.scalar, "dve": block.vector,
                "pool": block.gpsimd, "sp": block.sync}
        ops = self.ops

        def make(e):
            def body(engobj):
                seen = {}
                for o in per[e]:
                    need = {}
                    for j in o["deps"]:
                        ch, v = chan(ops[j])
                        if v > need.get(ch, 0):
                            need[ch] = v
                    for ch, v in need.items():
                        if seen.get(ch, 0) >= v:
                            continue
                        engobj.wait_ge(sems[ch], v)
                        seen[ch] = v
                    ins = o["fn"](engobj)
                    if o["sig"]:
                        ch, v = chan(o)
                        ins.then_inc(sems[ch], o["inc"] if o["dma"] is not None else 1)
                if e == final_wait_eng:
                    for ch, v in final.items():
                        if seen.get(ch, 0) < v:
                            engobj.wait_ge(sems[ch], v)
            return body

        for e in self.ENG:
            if per[e] or e == final_wait_eng:
                deco[e](make(e))


D_MODEL = 4096
SEQ = 4096
BATCH = 2
DEPTH = 2
NG = 4
WT = 256
NT = 28
NCOL = NT * WT
MIXW = 2048
RMS_EPS = 1e-6
LN_EPS = 1e-5
ALPHA = (2 * DEPTH) ** 0.25

PR_SDTB, PR_SALOG, PR_SD, PR_GDTB, PR_GALOG, PR_MIB, PR_MFB = 0, 16, 32, 48, 52, 56, 57
PR_MLNW, PR_GNW, PR_SNW = 64, 64 + 512, 64 + 512 + 128
PR_N = 64 + 512 + 128 + 1024
C_ID, C_TRI, C_U, C_ONES, C_BDS, C_BD, C_OFFT, C_NEG = range(8)
NCONST = 8


def make_consts():
    i = np.arange(128)
    k, l = i[:, None], i[None, :]
    c = np.zeros((NCONST, 128, 128), np.float32)
    c[C_ID] = np.eye(128)
    c[C_TRI] = (k <= l)
    c[C_U] = (k > l)
    c[C_ONES] = 1.0
    same = (k // 64) == (l // 64)
    c[C_BDS] = (k > l) & same
    c[C_BD] = same
    c[C_OFFT] = (k < 64) & (l >= 64)
    c[C_NEG] = np.where(k > l, -30000.0, 0.0)
    return np.ascontiguousarray(c.transpose(1, 0, 2).reshape(128, NCONST * 128))


def build_layer(D=D_MODEL, S=SEQ, TB=256, debug=False, parts=("ssm", "gdn", "ml")):
    KC = D // 128
    NB = S // TB
    NTT = TB // 128
    nc = bass.Bass("TRN2", target_bir_lowering=False)
    x_d = nc.dram_tensor("x", [S, D], F32, kind="ExternalInput").ap()
    wcat_d = nc.dram_tensor("wcat", [D, NCOL], F32, kind="ExternalInput").ap()
    wout_d = nc.dram_tensor("wout", [MIXW, D], F32, kind="ExternalInput").ap()
    convp_d = nc.dram_tensor("convp", [128, 24 * 5], F32, kind="ExternalInput").ap()
    prow_d = nc.dram_tensor("prow", [1, PR_N], F32, kind="ExternalInput").ap()
    const_d = nc.dram_tensor("consts", [128, NCONST * 128], F32, kind="ExternalInput").ap()
    ypart_d = nc.dram_tensor("ypart", [S, D], F32, kind="ExternalOutput").ap()
    wscr_d = nc.dram_tensor("wscr", [NT, 128, KC, WT], BF16, kind="Internal").ap()
    OW = 512 if KC >= 32 else WT
    NOC = D // OW
    woscr_d = nc.dram_tensor("woscr", [NOC, 128, 16, OW], BF16, kind="Internal").ap()
    if debug:
        ymix_d = nc.dram_tensor("ymix", [16, 128, S], BF16, kind="ExternalOutput").ap()

    with ExitStack() as st:
        def T(name, shape, dt=F32):
            return st.enter_context(nc.sbuf_tensor("sb_" + name, shape, dt))
        P = Prog(nc, st)
        psb = [st.enter_context(nc.psum_tensor(f"psb{i}", [128, 512], F32)) for i in range(8)]

        class PSA:
            def __init__(self, banks):
                self.banks = banks
                self.pos = 0

            def get(self, nq):
                bank = self.banks[self.pos % len(self.banks)]
                self.pos += 1
                return psb[bank][:, 0:nq * 128], [("ps", bank)]
        ps_proj = PSA([0, 1])
        ps_mix = PSA([2, 3, 4, 5, 6, 7])

        consts = T("consts", [128, NCONST, 128])
        cid = consts[:, C_ID, :]
        ctri = consts[:, C_TRI, :]
        cU = consts[:, C_U, :]
        cones = consts[:, C_ONES, :]
        idb = T("idb", [128, 128], BF16)
        prow = T("prow", [128, PR_N])
        convp = T("convp", [128, 24, 5])
        nA_s = T("nA_s", [128, 16])
        nA_g = T("nA_g", [128, 4])
        XLW = 512 if D >= 512 else D
        xld = [T(f"xld{i}", [128, XLW]) for i in range(2)]
        xT = T("xT", [128, KC, TB], BF16)
        wt = [T(f"wt{i}", [128, max(KC, 16), WT], BF16) for i in range(2)]
        CK = 2 if KC >= 2 else 1
        assert CK == 2 and WT == 256
        HS = T("HS", [128, 9216])
        bar = T("bar", [128, 1])
        cin = [HS[:, i * 512:(i + 1) * 512].rearrange("p (k c) -> p k c", k=CK) for i in range(2)]
        cout = [HS[:, 1024 + i * 256:1024 + (i + 1) * 256].bitcast(BF16).rearrange("p (k c) -> p k c", k=CK) for i in range(2)]
        HS_KEYS = [("cin", 0), ("cin", 1), ("cout", 0), ("cout", 1), "s_gu", ("s_mte", 0), ("s_mte", 1), "s_xw"]
        for h_ in range(4):
            HS_KEYS += [(k_, h_) for k_ in ("hA", "hB", "hC", "hD", "hA0a", "hA0b", "hAToff", "hattn", "huw")]
            HS_KEYS += [(k_, h_, i_) for k_ in ("hPP", "hTT") for i_ in range(2)]

        def barrier():
            P.op("pool", lambda e: e.memset(bar[:], 0.0), w=HS_KEYS)
        U = T("U", [128, 12, 3 + TB])
        hist_s = T("hist_s", [128, 12, 3])
        hist_g = T("hist_g", [128, 12, 3])
        cv = T("cv", [128, 12, TB])
        ymixT = T("ymixT", [128, 16, TB], BF16)
        stage = [T(f"stage{i}", [128, WT]) for i in range(2)]
        junk = T("junk", [128, 512])
        epsc = T("epsc", [128, 1])

        P.op("sp", lambda e: e.dma_start(out=consts[:], in_=const_d.rearrange("p (c k) -> p c k", c=NCONST)), w=["consts"], dma="consts")
        P.op("sp", lambda e: e.dma_start(out=prow[:], in_=prow_d.partition_broadcast(128)[:, 0, :]), w=["prow"], dma="prow")
        P.op("sp", lambda e: e.dma_start(out=convp[:], in_=convp_d.rearrange("p (c k) -> p c k", c=24)), w=["convp"], dma="convp")
        P.op("dve", lambda e: e.tensor_copy(out=idb[:], in_=cid), r=["consts"], w=["idb"])
        P.op("pool", lambda e: e.memset(epsc[:], RMS_EPS), w=["epsc"])
        P.op("act", lambda e: e.activation(out=nA_s[:], in_=prow[:, PR_SALOG:PR_SALOG + 16], func=AF.Exp), r=["prow"], w=["nA_s"])
        P.op("dve", lambda e: e.tensor_scalar(out=nA_s[:], in0=nA_s[:], scalar1=-1.0, scalar2=None, op0=ALU.mult), r=["nA_s"], w=["nA_s"])
        P.op("act", lambda e: e.activation(out=nA_g[:], in_=prow[:, PR_GALOG:PR_GALOG + 4], func=AF.Exp), r=["prow"], w=["nA_g"])
        P.op("dve", lambda e: e.tensor_scalar(out=nA_g[:], in0=nA_g[:], scalar1=-1.0, scalar2=None, op0=ALU.mult), r=["nA_g"], w=["nA_g"])

        wv = wcat_d.rearrange("(kc p) c -> p kc c", p=128)
        wov = wout_d.rearrange("(kc p) c -> p kc c", p=128)
        PK = 8 if KC >= 8 else KC
        NPW = KC // PK
        NPO = 16 // PK if PK <= 16 else 1
        order = [8, 9, 10, 11, 12, 13, 27, 22, 23, 24, 25, 2, 3, 4, 5, 6, 7, 20, 21, 0, 1, 14, 15, 18, 19, 16, 17]
        for t in order:
            for pi in range(NPW):
                k0 = pi * PK
                P.op("pool", lambda e, t=t, k0=k0: e.dma_start(out=wscr_d[t][:, k0:k0 + PK, :], in_=wv[:, k0:k0 + PK, t * WT:(t + 1) * WT]),
                     w=[("wscr", t, pi)], dma=("prew", t))
        for t in range(NOC):
            for pi in range(NPO):
                k0 = pi * PK
                P.op("pool", lambda e, t=t, k0=k0: e.dma_start(out=woscr_d[t][:, k0:k0 + PK, :], in_=wov[:, k0:k0 + PK, t * OW:(t + 1) * OW]),
                     w=[("woscr", t, pi)], dma=("preo", t))

        barrier()
        wslot = [0]

        def load_w(t):
            s = wslot[0] % 2
            wslot[0] += 1
            P.op("sp", lambda e: e.dma_start(out=wt[s][:, 0:KC, :], in_=wscr_d[t]), r=[("wscr", t, pi) for pi in range(NPW)], w=[("wt", s)], dma=("wt", s))
            return s

        def wo_view(s):
            if OW == WT:
                return wt[s][:, 0:16, :]
            return wt[s][:].rearrange("p k c -> p (k c)")[:, 0:16 * OW].rearrange("p (k c) -> p k c", k=16)

        def load_wo(t):
            s = wslot[0] % 2
            wslot[0] += 1
            P.op("sp", lambda e: e.dma_start(out=wo_view(s), in_=woscr_d[t]), r=[("woscr", t, pi) for pi in range(NPO)], w=[("wt", s)], dma=("wt", s))
            return s

        def proj_fm(s, j, M=128, c0=None):
            ps, keys = ps_proj.get(2)
            cc = j * 128 if c0 is None else c0

            def fn(e):
                for kc in range(KC):
                    ins = e.matmul(ps[0:M, 0:TB], lhsT=wt[s][:, kc, cc:cc + M], rhs=xT[:, kc, :], start=(kc == 0), stop=(kc == KC - 1))
                return ins
            P.op("pe", fn, r=[("wt", s), "xT"], w=keys)
            return ps, keys

        def proj_tm(s, tt):
            ps, keys = ps_proj.get(2)

            def fn(e):
                for kc in range(KC):
                    ins = e.matmul(ps[:, 0:WT], lhsT=xT[:, kc, tt * 128:(tt + 1) * 128], rhs=wt[s][:, kc, :], start=(kc == 0), stop=(kc == KC - 1))
                return ins
            P.op("pe", fn, r=[("wt", s), "xT"], w=keys)
            return ps, keys

        def load_xT(tb):
            n = 0
            for tt in range(NTT):
                r0 = tb * TB + tt * 128
                for c0 in range(0, D, XLW):
                    s = n % 2
                    n += 1
                    P.op("sp", lambda e, s=s, r0=r0, c0=c0: e.dma_start(out=xld[s][:], in_=x_d[r0:r0 + 128, c0:c0 + XLW]),
                         w=[("xld", s)], dma=("xld", s))
                    for q0 in range(0, XLW, 512):
                        nq = min(4, (XLW - q0) // 128)
                        ps, keys = ps_mix.get(nq)

                        def fn(e, s=s, q0=q0, nq=nq, ps=ps):
                            for q in range(nq):
                                ins = e.transpose(out=ps[:, q * 128:(q + 1) * 128], in_=xld[s][:, q0 + q * 128:q0 + (q + 1) * 128], identity=cid)
                            return ins
                        P.op("pe", fn, r=[("xld", s), "consts"], w=keys)
                        kc0 = (c0 + q0) // 128
                        P.op("act", lambda e, ps=ps, kc0=kc0, nq=nq, tt=tt: e.activation(
                            out=xT[:, kc0:kc0 + nq, tt * 128:(tt + 1) * 128],
                            in_=ps[:, 0:nq * 128].rearrange("p (q t) -> p q t", q=nq), func=AF.Copy),
                            r=keys, w=["xT"])

        def conv_blocks(Ubuf, hist, cbase, nblk, key):
            P.op("pool", lambda e: e.tensor_copy(out=Ubuf[:, 0:nblk, 0:3], in_=hist[:, 0:nblk, :]), r=[key + "hist"], w=[key + "Uh"])
            for b in range(nblk):
                eng = "dve"
                rk = [(key + "U", b), key + "Uh", "convp"]

                P.op(eng, lambda e, b=b: e.tensor_scalar(out=cv[:, b, :], in0=Ubuf[:, b, 3:3 + TB], scalar1=convp[:, cbase + b, 3:4], scalar2=convp[:, cbase + b, 4:5],
                                                         op0=ALU.mult, op1=ALU.add), r=rk, w=[("cvp", b)])
                for k in range(3):
                    P.op(eng, lambda e, b=b, k=k: e.scalar_tensor_tensor(out=cv[:, b, :], in0=Ubuf[:, b, k:k + TB], scalar=convp[:, cbase + b, k:k + 1], in1=cv[:, b, :],
                                                                         op0=ALU.mult, op1=ALU.add), r=rk + [("cvp", b)], w=[("cvp", b)])
                P.op("act", lambda e, b=b: e.activation(out=cv[:, b, :], in_=cv[:, b, :], func=AF.Silu), r=[("cvp", b)], w=[("cv", b)])
            P.op("pool", lambda e: e.tensor_copy(out=hist[:, 0:nblk, :], in_=Ubuf[:, 0:nblk, TB:TB + 3]),
                 r=[(key + "U", b) for b in range(nblk)] + [key + "Uh"], w=[key + "hist"])

        def softplus_(out, xin, nparts, shape_keys_r, wkey, tmpa, tmpb):
            P.op("act", lambda e: e.activation(out=tmpa, in_=xin, func=AF.Abs), r=shape_keys_r, w=[wkey + "_a"])
            P.op("act", lambda e: e.activation(out=tmpa, in_=tmpa, func=AF.Exp, scale=-1.0), r=[wkey + "_a"], w=[wkey + "_b"])
            P.op("act", lambda e: e.activation(out=tmpb, in_=tmpa, func=AF.Ln, bias=1.0), r=[wkey + "_b"], w=[wkey + "_c"])
            P.op("dve", lambda e: e.scalar_tensor_tensor(out=out, in0=xin, scalar=0.0, in1=tmpb, op0=ALU.max, op1=ALU.add),
                 r=list(shape_keys_r) + [wkey + "_c"], w=[wkey])

        def transpose_out_bf(src_bf, ncb, blk0, c, rkeys):
            ps, keys = ps_mix.get((ncb + 1) // 2)
            psv = ps.bitcast(BF16)

            def fn(e):
                for i in range(ncb):
                    ins = e.transpose(out=psv[:, i * 128:(i + 1) * 128], in_=src_bf[:, i * 128:(i + 1) * 128], identity=idb[:])
                return ins
            P.op("pe", fn, r=list(rkeys) + ["idb"], w=keys)
            P.op("act", lambda e: e.activation(out=ymixT[:, blk0:blk0 + ncb, c * 128:(c + 1) * 128],
                                                in_=psv[:, 0:ncb * 128].rearrange("p (b t) -> p b t", b=ncb), func=AF.Copy),
                 r=keys, w=[("ymixT", blk0)])

        s_small = T("s_small", [128, NTT, 32])
        big_tm = T("big_tm", [128, NTT, 1024])
        s_xtm = T("s_xtm", [128, 1024])
        s_xdt = T("s_xdt", [128, 1024])
        if "ssm" in parts:
            s_sz = big_tm
            s_dt = T("s_dt", [128, NTT, 16])
            s_a = T("s_a", [128, NTT, 16])
            s_t1 = T("s_t1", [128, NTT, 16])
            s_t2 = T("s_t2", [128, NTT, 16])
            s_xx = T("s_xx", [128, NTT, 16])
            s_state = T("s_state", [128, 2, 512])
            s_pre = T("s_pre", [128, 48])
            s_ex = T("s_ex", [128, 48])
            s_xw = HS[:, 2048:3072]
            s_btm = T("s_btm", [128, 2, 128])
            s_gu = HS[:, 0:1024].rearrange("p (h l) -> p h l", h=8)
            s_mt = HS[:, 1024:2048].rearrange("p (h l) -> p h l", h=8)
            s_cbm = T("s_cbm", [128, 128])
            s_y = T("s_y", [128, 512])
            s_ybf = T("s_ybf", [128, 512], BF16)
            s_ssq = T("s_ssq", [128, 1])
            s_rstd = T("s_rstd", [128, 1])
            P.op("pool", lambda e: e.memset(s_state[:], 0.0), w=["s_state0", "s_state1"])
            P.op("pool", lambda e: e.memset(hist_s[:], 0.0), w=["shist"])

        def ssm_proj(tb, part="all"):
            for ti, t in enumerate(range(8, 14) if part in ("all", "fm") else ()):
                s = load_w(t)
                for j in range(2):
                    ps, keys = proj_fm(s, j)
                    b = ti * 2 + j
                    P.op("act", lambda e, ps=ps, b=b: e.activation(out=U[:, b, 3:3 + TB], in_=ps[:, 0:TB], func=AF.Copy), r=keys, w=[("sU", b)])
            for ti, t in enumerate(range(22, 26) if part in ("all", "z") else ()):
                s = load_w(t)
                for tt in range(NTT):
                    ps, keys = proj_tm(s, tt)
                    P.op("act", lambda e, ps=ps, ti=ti, tt=tt: e.activation(out=s_sz[:, tt, ti * WT:(ti + 1) * WT], in_=ps[:, 0:WT], func=AF.Silu),
                         r=keys, w=[("s_sz", tt, ti)])

        def small_proj(tb):
            s = load_w(27)
            for tt in range(NTT):
                ps, keys = proj_tm(s, tt)
                P.op("dve", lambda e, ps=ps, tt=tt: e.tensor_copy(out=s_small[:, tt, :], in_=ps[:, 0:32]), r=keys, w=[("small", tt)])
            return s

        def ssm_mixer(tb):
            barrier()
            conv_blocks(U, hist_s, 12, 12, "s")
            P.mark()
            smk = [("small", tt) for tt in range(NTT)]
            P.op("dve", lambda e: e.tensor_tensor(out=s_xx[:], in0=s_small[:, :, 8:24],
                                                  in1=prow[:, PR_SDTB:PR_SDTB + 16].unsqueeze(1).broadcast_to([128, NTT, 16]), op=ALU.add),
                 r=smk + ["prow"], w=["s_xx"])
            softplus_(s_dt[:], s_xx[:], 128, ["s_xx"], "s_dt", s_t1[:], s_t2[:])
            P.op("dve", lambda e: e.tensor_tensor(out=s_a[:], in0=s_dt[:], in1=nA_s[:].unsqueeze(1).broadcast_to([128, NTT, 16]), op=ALU.mult),
                 r=["s_dt", "nA_s"], w=["s_a"])
            for c in range(NTT):
                csl = slice(c * 128, (c + 1) * 128)
                ps, keys = ps_mix.get(1)

                def fn(e, ps=ps, c=c):
                    e.matmul(ps[:, 0:16], lhsT=ctri, rhs=s_a[:, c, :], start=True, stop=True)
                    return e.matmul(ps[:, 16:32], lhsT=cones, rhs=s_a[:, c, :], start=True, stop=True)
                P.op("pe", fn, r=["consts", "s_a"], w=keys)
                P.op("act", lambda e, ps=ps: e.activation(out=s_pre[:, 0:32], in_=ps[:, 0:32], func=AF.Copy), r=keys, w=["s_pre"])
                P.op("dve", lambda e: e.tensor_tensor(out=s_pre[:, 32:48], in0=s_pre[:, 16:32], in1=s_pre[:, 0:16], op=ALU.subtract), r=["s_pre"], w=["s_pre2"])
                P.op("act", lambda e: e.activation(out=s_ex[:], in_=s_pre[:], func=AF.Exp), r=["s_pre", "s_pre2"], w=["s_ex"])
                for half in range(2):
                    ps, keys = ps_mix.get(4)

                    def fn(e, ps=ps, half=half, csl=csl):
                        for q in range(4):
                            ins = e.transpose(out=ps[:, q * 128:(q + 1) * 128], in_=cv[:, half * 4 + q, csl], identity=cid)
                        return ins
                    P.op("pe", fn, r=[("cv", half * 4 + q) for q in range(4)] + ["consts"], w=keys)
                    P.op("act", lambda e, ps=ps, half=half: e.activation(out=s_xtm[:, half * 512:(half + 1) * 512], in_=ps[:, 0:512], func=AF.Copy),
                         r=keys, w=[("s_xtm", half)])
                ps, keys = ps_mix.get(2)

                def fn(e, ps=ps, csl=csl):
                    e.transpose(out=ps[:, 0:128], in_=cv[:, 8, csl], identity=cid)
                    return e.transpose(out=ps[:, 128:256], in_=cv[:, 9, csl], identity=cid)
                P.op("pe", fn, r=[("cv", 8), ("cv", 9), "consts"], w=keys)
                P.op("dve", lambda e, ps=ps: e.tensor_copy(out=s_btm[:], in_=ps[:, 0:256].rearrange("p (g n) -> p g n", g=2)), r=keys, w=["s_btm"])
                xk = [("s_xtm", 0), ("s_xtm", 1)]
                v3 = lambda t_: t_[:, 0:1024].rearrange("p (h q) -> p h q", h=16)
                P.op("pool", lambda e, c=c: e.tensor_tensor(out=v3(s_xdt), in0=v3(s_xtm), in1=s_dt[:, c, :].unsqueeze(2).broadcast_to([128, 16, 64]), op=ALU.mult),
                     r=xk + ["s_dt"], w=["s_xdt"])
                P.op("pool", lambda e: e.tensor_tensor(out=v3(s_xw), in0=v3(s_xdt), in1=s_ex[:, 32:48].unsqueeze(2).broadcast_to([128, 16, 64]), op=ALU.mult),
                     r=["s_xdt", "s_ex"], w=["s_xw"])
                for gi in range(2):
                    ps, keys = ps_mix.get(1)
                    P.op("pe", lambda e, ps=ps, gi=gi, csl=csl: e.matmul(ps[:, 0:128], lhsT=cv[:, 8 + gi, csl], rhs=cv[:, 10 + gi, csl], start=True, stop=True),
                         r=[("cv", 8 + gi), ("cv", 10 + gi)], w=keys)
                    P.op("dve", lambda e, ps=ps: e.tensor_tensor(out=s_cbm[:], in0=ps[:, 0:128], in1=ctri, op=ALU.mult), r=keys + ["consts"], w=["s_cbm"])
                    P.op("pool", lambda e, c=c, gi=gi: e.tensor_tensor(out=s_gu[:], in0=s_a[:, c, gi * 8:(gi + 1) * 8].unsqueeze(2).broadcast_to([128, 8, 128]),
                                                                       in1=cU.unsqueeze(1).broadcast_to([128, 8, 128]), op=ALU.mult),
                         r=["s_a", "consts"], w=["s_gu"])
                    for hh in range(2):
                        ps, keys = ps_mix.get(4)

                        def fn(e, ps=ps, hh=hh):
                            for q in range(4):
                                ins = e.matmul(ps[:, q * 128:(q + 1) * 128], lhsT=s_gu[:, hh * 4 + q, :], rhs=ctri, start=True, stop=True)
                            return ins
                        P.op("pe", fn, r=["s_gu", "consts"], w=keys)
                        P.op("act", lambda e, ps=ps, hh=hh: e.activation(out=s_mt[:, hh * 4:(hh + 1) * 4, :], in_=ps[:, 0:512].rearrange("p (h l) -> p h l", h=4), func=AF.Exp),
                             r=keys, w=[("s_mte", hh)])
                    P.op("dve", lambda e: e.tensor_tensor(out=s_mt[:], in0=s_mt[:], in1=s_cbm[:].unsqueeze(1).broadcast_to([128, 8, 128]), op=ALU.mult),
                         r=[("s_mte", 0), ("s_mte", 1), "s_cbm"], w=[("s_mte", 0), ("s_mte", 1)])
                    psd, kd = ps_mix.get(4)

                    def fn(e, psd=psd, gi=gi):
                        for h in range(8):
                            hg = gi * 8 + h
                            ins = e.matmul(psd[:, h * 64:(h + 1) * 64], lhsT=s_mt[:, h, :], rhs=s_xdt[:, hg * 64:(hg + 1) * 64], start=True, stop=True)
                        return ins
                    P.op("pe", fn, r=[("s_mte", 0), ("s_mte", 1), "s_xdt"], w=kd)
                    pso, ko = ps_mix.get(4)
                    P.op("pe", lambda e, pso=pso, gi=gi, csl=csl: e.matmul(pso[:, 0:512], lhsT=cv[:, 10 + gi, csl], rhs=s_state[:, gi, :], start=True, stop=True),
                         r=[("cv", 10 + gi), f"s_state{gi}"], w=ko)
                    y3 = s_y[:].rearrange("p (h q) -> p h q", h=8)
                    P.op("dve", lambda e, pso=pso, gi=gi: e.tensor_tensor(out=y3, in0=pso[:, 0:512].rearrange("p (h q) -> p h q", h=8),
                                                                          in1=s_ex[:, gi * 8:(gi + 1) * 8].unsqueeze(2).broadcast_to([128, 8, 64]), op=ALU.mult),
                         r=ko + ["s_ex"], w=["s_y"])
                    P.op("pool", lambda e, gi=gi: e.tensor_tensor(out=junk[:].rearrange("p (h q) -> p h q", h=8), in0=s_xtm[:, gi * 512:(gi + 1) * 512].rearrange("p (h q) -> p h q", h=8),
                                                                  in1=prow[:, PR_SD + gi * 8:PR_SD + (gi + 1) * 8].unsqueeze(2).broadcast_to([128, 8, 64]), op=ALU.mult),
                         r=[("s_xtm", gi), "prow"], w=["junk"])
                    P.op("pool", lambda e, gi=gi: e.tensor_tensor(out=s_y[:], in0=s_y[:], in1=junk[:], op=ALU.add), r=["s_y", "junk"], w=["s_y"])
                    P.op("dve", lambda e, psd=psd: e.tensor_tensor(out=s_y[:], in0=s_y[:], in1=psd[:, 0:512], op=ALU.add), r=kd + ["s_y"], w=["s_y"])
                    P.op("pool", lambda e, c=c, gi=gi: e.tensor_tensor(out=s_y[:], in0=s_y[:], in1=s_sz[:, c, gi * 512:(gi + 1) * 512], op=ALU.mult),
                         r=["s_y", ("s_sz", c, 2 * gi), ("s_sz", c, 2 * gi + 1)], w=["s_y"])
                    P.op("act", lambda e: e.activation(out=junk[:], in_=s_y[:], func=AF.Square, accum_out=s_ssq[:]), r=["s_y"], w=["junk", "s_ssq"])
                    P.op("act", lambda e: e.activation(out=s_rstd[:], in_=s_ssq[:], func=AF.Sqrt, scale=1.0 / 512, bias=epsc[:]), r=["s_ssq", "epsc"], w=["s_rstd0"])
                    P.op("dve", lambda e: e.reciprocal(out=s_rstd[:], in_=s_rstd[:]), r=["s_rstd0"], w=["s_rstd"])
                    P.op("dve", lambda e, gi=gi: e.scalar_tensor_tensor(out=s_ybf[:], in0=s_y[:], scalar=s_rstd[:], in1=prow[:, PR_SNW + gi * 512:PR_SNW + (gi + 1) * 512],
                                                                        op0=ALU.mult, op1=ALU.mult), r=["s_y", "s_rstd", "prow"], w=["s_ybf"])
                    transpose_out_bf(s_ybf, 4, 8 + gi * 4, c, ["s_ybf"])
                    pss, ks = ps_mix.get(4)
                    P.op("pe", lambda e, pss=pss, gi=gi: e.matmul(pss[:, 0:512], lhsT=s_btm[:, gi, :], rhs=s_xw[:, gi * 512:(gi + 1) * 512], start=True, stop=True),
                         r=["s_btm", "s_xw"], w=ks)
                    st3 = s_state[:, gi, :].rearrange("p (h q) -> p h q", h=8)
                    P.op("pool", lambda e, st3=st3, gi=gi: e.tensor_tensor(out=st3, in0=st3, in1=s_ex[:, 16 + gi * 8:16 + (gi + 1) * 8].unsqueeze(2).broadcast_to([128, 8, 64]), op=ALU.mult),
                         r=[f"s_state{gi}", "s_ex"], w=[f"s_state{gi}"])
                    P.op("dve", lambda e, pss=pss, gi=gi: e.tensor_tensor(out=s_state[:, gi, :], in0=s_state[:, gi, :], in1=pss[:, 0:512], op=ALU.add),
                         r=ks + [f"s_state{gi}"], w=[f"s_state{gi}"])

        if "gdn" in parts:
            g_sz = big_tm[:, :, 0:512]
            g_sq = T("g_sq", [128, TB])
            g_rn = T("g_rn", [128, TB])
            g_beta = T("g_beta", [128, NTT, 4])
            g_xx = T("g_xx", [128, NTT, 4])
            g_t1s = T("g_t1s", [128, NTT, 4])
            g_t2s = T("g_t2s", [128, NTT, 4])
            g_sp = T("g_sp", [128, NTT, 4])
            g_g = T("g_g", [128, NTT, 4])
            g_pre = T("g_pre", [128, 12])
            g_ex = T("g_ex", [128, 12])
            g_bg = T("g_bg", [128, 4])
            g_ktm = s_xtm[:, 0:512].rearrange("p (h d) -> p h d", h=4)
            g_vtm = s_xtm[:, 512:1024].rearrange("p (h d) -> p h d", h=4)
            g_R = s_xdt[:].rearrange("p (h c) -> p h c", h=4)
            g_kd = T("g_kd", [128, 4, 128])
            g_H = []
            for h_ in range(4):
                b0 = h_ * 2304
                v2 = lambda o_: HS[:, b0 + o_:b0 + o_ + 256].rearrange("p (a l) -> p a l", a=2)
                v1 = lambda o_: HS[:, b0 + o_:b0 + o_ + 128]
                g_H.append(dict(A=v2(0), B=v2(256), C=v1(512), D=v1(640), A0=v2(768), AToff=v1(1024), attnT=v1(1152),
                                PP=[v2(1280), v2(1536)], TT=[v1(1792), v1(1920)], uw=v2(2048)))
            g_o = T("g_o", [128, 4, 128])
            g_o2 = T("g_o2", [128, 4, 128])
            g_ssq = T("g_ssq", [128, 4])
            g_rstd = T("g_rstd", [128, 4])
            g_obf = T("g_obf", [128, 512], BF16)
            g_S = T("g_S", [128, 4, 128])
            P.op("pool", lambda e: e.memset(g_S[:], 0.0), w=[("g_S", h) for h in range(4)])
            P.op("pool", lambda e: e.memset(hist_g[:], 0.0), w=["ghist"])
            cBDm = consts[:, C_BD, :]
            cOFFT = consts[:, C_OFFT, :]
            cNEG = consts[:, C_NEG, :]

        def gdn_proj(tb, part="all"):
            for ti, t in enumerate(range(2, 8) if part in ("all", "fm") else ()):
                s = load_w(t)
                for j in range(2):
                    ps, keys = proj_fm(s, j)
                    b = ti * 2 + j
                    P.op("act", lambda e, ps=ps, b=b: e.activation(out=U[:, b, 3:3 + TB], in_=ps[:, 0:TB], func=AF.Copy), r=keys, w=[("gU", b)])
            for ti, t in enumerate(range(20, 22) if part in ("all", "z") else ()):
                s = load_w(t)
                for tt in range(NTT):
                    ps, keys = proj_tm(s, tt)
                    P.op("act", lambda e, ps=ps, ti=ti, tt=tt: e.activation(out=g_sz[:, tt, ti * WT:(ti + 1) * WT], in_=ps[:, 0:WT], func=AF.Silu),
                         r=keys, w=[("g_sz", tt, ti)])

        def head_gen(c, csl, h):
            Hh = g_H[h]
            kA, kB, kC, kD = ("hA", h), ("hB", h), ("hC", h), ("hD", h)
            kA0a, kA0b, kAo, kat, kuw = ("hA0a", h), ("hA0b", h), ("hAToff", h), ("hattn", h), ("huw", h)
            GU, E_, t1, Af, A0, AToff, attnT, PP, TT, uw = (Hh[k_] for k_ in ("A", "B", "C", "D", "A0", "AToff", "attnT", "PP", "TT", "uw"))
            X1 = GU.rearrange("p a l -> p (a l)")
            X2 = E_.rearrange("p a l -> p (a l)")
            vn, tmp_ = t1, Af
            gcol = g_g[:, c, h:h + 1]
            P.op("dve", lambda e: e.tensor_scalar(out=GU[:, 0, :], in0=cU, scalar1=gcol, scalar2=None, op0=ALU.mult), r=["consts", "g_g"], w=[kA])
            yield
            P.op("dve", lambda e: e.tensor_scalar(out=GU[:, 1, :], in0=ctri, scalar1=gcol, scalar2=None, op0=ALU.mult), r=["consts", "g_g"], w=[kA])
            yield
            ps, keys = ps_mix.get(4)
            kT = cv[:, 4 + h, csl]
            qT_ = cv[:, h, csl]

            def fn(e, ps=ps):
                e.matmul(ps[:, 0:128], lhsT=GU[:, 0, :], rhs=ctri, start=True, stop=False)
                e.matmul(ps[:, 0:128], lhsT=cid, rhs=cNEG, start=False, stop=True)
                e.matmul(ps[:, 128:256], lhsT=GU[:, 1, :], rhs=cU, start=True, stop=True)
                e.matmul(ps[:, 256:384], lhsT=kT, rhs=kT, start=True, stop=True)
                return e.matmul(ps[:, 384:512], lhsT=kT, rhs=qT_, start=True, stop=True)
            P.op("pe", fn, r=[kA, "consts", ("cv", 4 + h), ("cv", h)], w=keys)
            yield
            P.op("act", lambda e, ps=ps: e.activation(out=E_[:], in_=ps[:, 0:256].rearrange("p (a l) -> p a l", a=2), func=AF.Exp), r=keys, w=[kB])
            yield
            P.op("dve", lambda e, ps=ps: e.tensor_tensor(out=t1[:], in0=ps[:, 256:384], in1=E_[:, 1, :], op=ALU.mult), r=keys + [kB], w=[kC])
            yield
            P.op("dve", lambda e, ps=ps: e.tensor_tensor(out=attnT[:], in0=ps[:, 384:512], in1=E_[:, 0, :], op=ALU.mult), r=keys + [kB], w=[kat])
            yield
            P.op("dve", lambda e: e.scalar_tensor_tensor(out=Af[:], in0=t1[:], scalar=g_beta[:, c, h:h + 1], in1=cU, op0=ALU.mult, op1=ALU.mult),
                 r=[kC, "g_beta", "consts"], w=[kD])
            yield
            ps, keys = ps_mix.get(1)
            P.op("pe", lambda e, ps=ps: e.transpose(out=ps[:, 0:128], in_=Af[:], identity=cid), r=[kD, "consts"], w=keys)
            yield
            P.op("pool", lambda e: e.tensor_tensor(out=A0[:, 0, :], in0=Af[:], in1=cBDm, op=ALU.mult), r=[kD, "consts"], w=[kA0a])
            yield
            P.op("dve", lambda e, ps=ps: e.tensor_tensor(out=A0[:, 1, :], in0=ps[:, 0:128], in1=cBDm, op=ALU.mult), r=keys + ["consts"], w=[kA0b])
            yield
            P.op("dve", lambda e, ps=ps: e.tensor_tensor(out=AToff[:], in0=ps[:, 0:128], in1=cOFFT, op=ALU.mult), r=keys + ["consts"], w=[kAo])
            yield
            P.op("pool", lambda e: e.tensor_tensor(out=TT[0][:], in0=cid, in1=A0[:, 1, :], op=ALU.subtract), r=[kA0b, "consts"], w=[("hTT", h, 0)])
            yield
            Pc, Pk = A0, [kA0a, kA0b]
            tcur = 0
            for k in range(1, 6):
                ps, keys = ps_mix.get(2)

                def fn(e, ps=ps, Pc=Pc, k=k):
                    ins = e.matmul(ps[:, 0:128], lhsT=Pc[:, 1, :], rhs=Pc[:, 0, :], start=True, stop=True)
                    if k < 5:
                        ins = e.matmul(ps[:, 128:256], lhsT=Pc[:, 0, :], rhs=Pc[:, 1, :], start=True, stop=True)
                    return ins
                P.op("pe", fn, r=Pk, w=keys)
                yield
                Pn = PP[k % 2]
                nk = [("hPP", h, k % 2)]
                P.op("act", lambda e, ps=ps, Pn=Pn: e.activation(out=Pn[:], in_=ps[:, 0:256].rearrange("p (a l) -> p a l", a=2), func=AF.Copy), r=keys, w=nk)
                yield
                ps, keys = ps_mix.get(1)
                P.op("pe", lambda e, ps=ps, Pn=Pn, tcur=tcur: e.matmul(ps[:, 0:128], lhsT=Pn[:, 0, :], rhs=TT[tcur][:], start=True, stop=True),
                     r=nk + [("hTT", h, tcur)], w=keys)
                yield
                P.op("dve", lambda e, ps=ps, tcur=tcur: e.tensor_tensor(out=TT[1 - tcur][:], in0=TT[tcur][:], in1=ps[:, 0:128], op=ALU.add),
                     r=keys + [("hTT", h, tcur)], w=[("hTT", h, 1 - tcur)])
                yield
                tcur = 1 - tcur
                Pc, Pk = Pn, nk
            TTf = TT[tcur]
            tk = [("hTT", h, tcur)]
            ps, keys = ps_mix.get(2)
            P.op("pe", lambda e, ps=ps: e.matmul(ps[:, 0:256], lhsT=TTf[:], rhs=g_R[:, h, :], start=True, stop=True), r=tk + ["g_Rv", "g_Rk"], w=keys)
            yield
            P.op("act", lambda e, ps=ps: e.activation(out=X1, in_=ps[:, 0:256], func=AF.Copy), r=keys, w=[kA])
            yield
            ps, keys = ps_mix.get(2)
            P.op("pe", lambda e, ps=ps: e.matmul(ps[:, 0:256], lhsT=AToff[:], rhs=X1, start=True, stop=True), r=[kAo, kA], w=keys)
            yield
            P.op("dve", lambda e, ps=ps: e.tensor_tensor(out=X2, in0=g_R[:, h, :], in1=ps[:, 0:256], op=ALU.subtract), r=keys + ["g_Rv", "g_Rk"], w=[kB])
            yield
            ps, keys = ps_mix.get(2)

            def fn(e, ps=ps):
                e.matmul(ps[:, 0:128], lhsT=TTf[:], rhs=X2[:, 0:128], start=True, stop=True)
                return e.matmul(ps[:, 128:256], lhsT=X2[:, 128:256], rhs=TTf[:], start=True, stop=True)
            P.op("pe", fn, r=tk + [kB], w=keys)
            yield
            P.op("act", lambda e, ps=ps: e.activation(out=uw[:], in_=ps[:, 0:256].rearrange("p (a l) -> p a l", a=2), func=AF.Copy), r=keys, w=[kuw])
            yield
            ps, keys = ps_mix.get(2)

            def fn(e, ps=ps):
                e.matmul(ps[:, 0:128], lhsT=uw[:, 1, :], rhs=g_S[:, h, :], start=True, stop=True)
                return e.matmul(ps[:, 128:256], lhsT=qT_, rhs=g_S[:, h, :], start=True, stop=True)
            P.op("pe", fn, r=[kuw, ("g_S", h), ("cv", h)], w=keys)
            yield
            P.op("dve", lambda e, ps=ps: e.tensor_tensor(out=vn[:], in0=uw[:, 0, :], in1=ps[:, 0:128], op=ALU.subtract), r=keys + [kuw], w=[kC])
            yield
            P.op("dve", lambda e, ps=ps: e.tensor_scalar(out=tmp_[:], in0=ps[:, 128:256], scalar1=g_ex[:, h:h + 1], scalar2=None, op0=ALU.mult), r=keys + ["g_ex"], w=[kD])
            yield
            ps, keys = ps_mix.get(2)

            def fn(e, ps=ps):
                e.matmul(ps[:, 0:128], lhsT=attnT[:], rhs=vn[:], start=True, stop=True)
                return e.matmul(ps[:, 128:256], lhsT=g_kd[:, h, :], rhs=vn[:], start=True, stop=True)
            P.op("pe", fn, r=[kat, kC, "g_kd"], w=keys)
            yield
            P.op("dve", lambda e, ps=ps: e.tensor_tensor(out=g_o[:, h, :], in0=tmp_[:], in1=ps[:, 0:128], op=ALU.add), r=keys + [kD], w=[("g_o", h)])
            yield
            P.op("dve", lambda e, ps=ps: e.scalar_tensor_tensor(out=g_S[:, h, :], in0=g_S[:, h, :], scalar=g_ex[:, 4 + h:5 + h], in1=ps[:, 128:256],
                                                                op0=ALU.mult, op1=ALU.add), r=keys + [("g_S", h), "g_ex"], w=[("g_S", h)])
            yield

        def gdn_mixer(tb):
            barrier()
            conv_blocks(U, hist_g, 0, 12, "g")
            for b in range(8):
                P.op("act", lambda e, b=b: e.activation(out=g_sq[:], in_=cv[:, b, :], func=AF.Square), r=[("cv", b)], w=["g_sq"])
                ps, keys = ps_mix.get(2)
                P.op("pe", lambda e, ps=ps: e.matmul(ps[:, 0:TB], lhsT=cones, rhs=g_sq[:], start=True, stop=True), r=["g_sq", "consts"], w=keys)
                P.op("act", lambda e, ps=ps: e.activation(out=g_rn[:], in_=ps[:, 0:TB], func=AF.Sqrt, bias=epsc[:]), r=keys + ["epsc"], w=["g_rn0"])
                P.op("dve", lambda e: e.reciprocal(out=g_rn[:], in_=g_rn[:]), r=["g_rn0"], w=["g_rn"])
                sc_ = 128 ** -0.5 if b < 4 else 1.0
                P.op("dve", lambda e, b=b, sc_=sc_: e.scalar_tensor_tensor(out=cv[:, b, :], in0=cv[:, b, :], scalar=sc_, in1=g_rn[:], op0=ALU.mult, op1=ALU.mult),
                     r=[("cv", b), "g_rn"], w=[("cv", b)])
            if GSTOP <= 1:
                return
            smk = [("small", tt) for tt in range(NTT)]
            P.op("act", lambda e: e.activation(out=g_beta[:], in_=s_small[:, :, 0:4], func=AF.Sigmoid), r=smk, w=["g_beta"])
            P.op("dve", lambda e: e.tensor_tensor(out=g_xx[:], in0=s_small[:, :, 4:8],
                                                  in1=prow[:, PR_GDTB:PR_GDTB + 4].unsqueeze(1).broadcast_to([128, NTT, 4]), op=ALU.add),
                 r=smk + ["prow"], w=["g_xx"])
            softplus_(g_sp[:], g_xx[:], 128, ["g_xx"], "g_sp", g_t1s[:], g_t2s[:])
            P.op("dve", lambda e: e.tensor_tensor(out=g_g[:], in0=g_sp[:], in1=nA_g[:].unsqueeze(1).broadcast_to([128, NTT, 4]), op=ALU.mult),
                 r=["g_sp", "nA_g"], w=["g_g"])
            if GSTOP <= 2:
                return
            for c in range(NTT):
                csl = slice(c * 128, (c + 1) * 128)
                ps, keys = ps_mix.get(1)

                def fn(e, ps=ps, c=c):
                    e.matmul(ps[:, 0:4], lhsT=ctri, rhs=g_g[:, c, :], start=True, stop=True)
                    return e.matmul(ps[:, 4:8], lhsT=cones, rhs=g_g[:, c, :], start=True, stop=True)
                P.op("pe", fn, r=["consts", "g_g"], w=keys)
                P.op("act", lambda e, ps=ps: e.activation(out=g_pre[:, 0:8], in_=ps[:, 0:8], func=AF.Copy), r=keys, w=["g_pre"])
                P.op("dve", lambda e: e.tensor_tensor(out=g_pre[:, 8:12], in0=g_pre[:, 4:8], in1=g_pre[:, 0:4], op=ALU.subtract), r=["g_pre"], w=["g_pre2"])
                P.op("act", lambda e: e.activation(out=g_ex[:], in_=g_pre[:], func=AF.Exp), r=["g_pre", "g_pre2"], w=["g_ex"])
                P.op("dve", lambda e, c=c: e.tensor_tensor(out=g_bg[:], in0=g_beta[:, c, :], in1=g_ex[:, 0:4], op=ALU.mult), r=["g_beta", "g_ex"], w=["g_bg"])
                for which, dst, key in ((4, g_ktm, "g_ktm"), (8, g_vtm, "g_vtm")):
                    ps, keys = ps_mix.get(4)

                    def fn(e, ps=ps, which=which, csl=csl):
                        for q in range(4):
                            ins = e.transpose(out=ps[:, q * 128:(q + 1) * 128], in_=cv[:, which + q, csl], identity=cid)
                        return ins
                    P.op("pe", fn, r=[("cv", which + q) for q in range(4)] + ["consts"], w=keys)
                    P.op("act", lambda e, ps=ps, dst=dst: e.activation(out=dst[:], in_=ps[:, 0:512].rearrange("p (h d) -> p h d", h=4), func=AF.Copy), r=keys, w=[key])
                P.op("dve", lambda e, c=c: e.tensor_tensor(out=g_R[:, :, 0:128], in0=g_vtm[:], in1=g_beta[:, c, :].unsqueeze(2).broadcast_to([128, 4, 128]), op=ALU.mult),
                     r=["g_vtm", "g_beta"], w=["g_Rv"])
                P.op("pool", lambda e: e.tensor_tensor(out=g_R[:, :, 128:256], in0=g_ktm[:], in1=g_bg[:].unsqueeze(2).broadcast_to([128, 4, 128]), op=ALU.mult),
                     r=["g_ktm", "g_bg"], w=["g_Rk"])
                P.op("pool", lambda e: e.tensor_tensor(out=g_kd[:], in0=g_ktm[:], in1=g_ex[:, 8:12].unsqueeze(2).broadcast_to([128, 4, 128]), op=ALU.mult),
                     r=["g_ktm", "g_ex"], w=["g_kd"])
                if GSTOP <= 3:
                    continue
                gens = [head_gen(c, csl, h) for h in range(4)]
                live = list(gens)
                while live:
                    nxt = []
                    for g_ in live:
                        try:
                            next(g_)
                            nxt.append(g_)
                        except StopIteration:
                            pass
                    live = nxt
                if GSTOP <= 8:
                    continue
                ok = [("g_o", h) for h in range(4)]
                P.op("act", lambda e: e.activation(out=g_o2[:], in_=g_o[:], func=AF.Square), r=ok, w=["g_o2"])
                P.op("dve", lambda e: e.tensor_reduce(out=g_ssq[:], in_=g_o2[:], axis=mybir.AxisListType.X, op=ALU.add), r=["g_o2"], w=["g_ssq"])
                P.op("act", lambda e: e.activation(out=g_rstd[:], in_=g_ssq[:], func=AF.Sqrt, scale=1.0 / 128, bias=epsc[:]), r=["g_ssq", "epsc"], w=["g_rstd0"])
                P.op("dve", lambda e: e.reciprocal(out=g_rstd[:], in_=g_rstd[:]), r=["g_rstd0"], w=["g_rstd"])
                P.op("pool", lambda e: e.tensor_tensor(out=g_o2[:], in0=g_o[:], in1=g_rstd[:].unsqueeze(2).broadcast_to([128, 4, 128]), op=ALU.mult), r=ok + ["g_rstd", "g_o2"], w=["g_o2"])
                P.op("pool", lambda e: e.tensor_tensor(out=g_o2[:], in0=g_o2[:], in1=prow[:, PR_GNW:PR_GNW + 128].unsqueeze(1).broadcast_to([128, 4, 128]), op=ALU.mult),
                     r=["g_o2", "prow"], w=["g_o2"])
                P.op("dve", lambda e, c=c: e.tensor_tensor(out=g_obf[:], in0=g_o2[:].rearrange("p h d -> p (h d)"), in1=g_sz[:, c, :], op=ALU.mult),
                     r=["g_o2", ("g_sz", c, 0), ("g_sz", c, 1)], w=["g_obf"])
                transpose_out_bf(g_obf, 4, 4, c, ["g_obf"])

        if "ml" in parts:
            m_qT = T("m_qT", [128, 2, TB], BF16)
            m_kT = T("m_kT", [128, 2, TB], BF16)
            m_ktm = T("m_ktm", [128, NTT, 256])
            m_v = T("m_v", [128, NTT, 512], BF16)
            m_g = big_tm[:, :, 0:512]
            m_zt = big_tm[:, :, 512:1024]
            RN = ["li", "t", "a", "l1", "lf", "m", "F", "G", "H", "one"]
            m_rows = {nm: T("m_r_" + nm, [1, TB]) for nm in RN}
            m_rows["wi"] = m_rows["a"]
            m_rows["emm"] = m_rows["l1"]
            m_rows["nG"] = m_rows["t"]
            m_carry = T("m_carry", [1, 1])
            m_C = T("m_C", [128, 2, 512])
            m_Cb = T("m_Cb", [128, 2, 512], BF16)
            m_n = T("m_n", [128, 2])
            m_nb = T("m_nb", [128, 2], BF16)
            m_W = T("m_W", [128, 128])
            m_Wm = T("m_Wm", [128, 128])
            m_sc = T("m_sc", [128, 128], BF16)
            m_cols = T("m_cols", [128, 4])
            m_tmp = T("m_tmp", [128, 512])
            m_h = T("m_h", [128, 512])
            m_den = T("m_den", [128, 2])
            m_d1 = T("m_d1", [128, 1])
            m_rden = T("m_rden", [128, 1])
            m_ssq = T("m_ssq", [128, 1])
            m_rstd = T("m_rstd", [128, 1])
            m_comb = T("m_comb", [128, 1])
            m_g2 = T("m_g2", [128, 512])
            m_hbf = T("m_hbf", [128, 512], BF16)
            m_kws = T("m_kws", [128, 256], BF16)
            m_onesb = T("m_onesb", [128, 1], BF16)
            P.op("pool", lambda e: e.memset(m_C[:], 0.0), w=["m_C"])
            P.op("pool", lambda e: e.memset(m_Cb[:], 0.0), w=["m_Cb"])
            P.op("pool", lambda e: e.memset(m_n[:], 0.0), w=["m_n"])
            P.op("pool", lambda e: e.memset(m_nb[:], 0.0), w=["m_nb"])
            P.op("pool", lambda e: e.memset(m_carry[:], 0.0), w=["m_carry"])
            P.op("pool", lambda e: e.memset(m_rows["one"][:], 1.0), w=["m_one"])
            P.op("pool", lambda e: e.memset(m_onesb[:], 1.0), w=["m_onesb"])

        def ml_proj(tb, part="all"):
            if part == "o":
                for ti, t in ((2, 16), (3, 17)):
                    s = load_w(t)
                    for tt in range(NTT):
                        ps, keys = proj_tm(s, tt)
                        cs = slice((ti % 2) * WT, (ti % 2 + 1) * WT)
                        P.op("act", lambda e, ps=ps, tt=tt, cs=cs: e.activation(out=m_g[:, tt, cs], in_=ps[:, 0:WT], func=AF.Sigmoid), r=keys, w=[("m_g", tt, ti % 2)])
                return
            s = load_w(0)
            for j in range(2):
                ps, keys = proj_fm(s, j)
                P.op("act", lambda e, ps=ps, j=j: e.activation(out=m_qT[:, j, :], in_=ps[:, 0:TB], func=AF.Identity, scale=256 ** -0.5), r=keys, w=[("m_qT", j)])
            s = load_w(1)
            for j in range(2):
                ps, keys = proj_fm(s, j)
                P.op("act", lambda e, ps=ps, j=j: e.activation(out=m_kT[:, j, :], in_=ps[:, 0:TB], func=AF.Copy), r=keys, w=[("m_kT", j)])
            for ti, t in enumerate(range(14, 20)):
                if part == "a" and ti in (2, 3):
                    continue
                s = load_w(t)
                for tt in range(NTT):
                    ps, keys = proj_tm(s, tt)
                    cs = slice((ti % 2) * WT, (ti % 2 + 1) * WT)
                    if ti < 2:
                        P.op("act", lambda e, ps=ps, tt=tt, cs=cs: e.activation(out=m_v[:, tt, cs], in_=ps[:, 0:WT], func=AF.Copy), r=keys, w=[("m_v", tt, ti)])
                    elif ti < 4:
                        P.op("act", lambda e, ps=ps, tt=tt, cs=cs: e.activation(out=m_g[:, tt, cs], in_=ps[:, 0:WT], func=AF.Sigmoid), r=keys, w=[("m_g", tt, ti % 2)])
                    else:
                        P.op("act", lambda e, ps=ps, tt=tt, cs=cs: e.activation(out=m_zt[:, tt, cs], in_=ps[:, 0:WT], func=AF.Silu), r=keys, w=[("m_zt", tt, ti % 2)])
            for tt in range(NTT):
                ps, keys = ps_proj.get(1)
                psv = ps.bitcast(BF16)

                def fn(e, psv=psv, tt=tt):
                    e.transpose(out=psv[:, 0:128], in_=m_kT[:, 0, tt * 128:(tt + 1) * 128], identity=idb[:])
                    return e.transpose(out=psv[:, 128:256], in_=m_kT[:, 1, tt * 128:(tt + 1) * 128], identity=idb[:])
                P.op("pe", fn, r=[("m_kT", 0), ("m_kT", 1), "idb"], w=keys)
                P.op("act", lambda e, psv=psv, tt=tt: e.activation(out=m_ktm[:, tt, :], in_=psv[:, 0:256], func=AF.Copy), r=keys, w=[("m_ktm", tt)])
            smk_ = [("small", tt) for tt in range(NTT)]
            for col, nm, boff in ((24, "li", PR_MIB), (25, "t", PR_MFB)):
                ps, keys = ps_proj.get(2)

                def fn(e, ps=ps, col=col):
                    for tt in range(NTT):
                        ins = e.transpose(out=ps[0:1, tt * 128:(tt + 1) * 128], in_=s_small[:, tt, col:col + 1], identity=cid)
                    return ins
                P.op("pe", fn, r=smk_ + ["consts"], w=keys)
                P.op("act", lambda e, ps=ps, nm=nm, boff=boff: e.activation(out=m_rows[nm][:], in_=ps[0:1, 0:TB], func=AF.Identity, bias=prow[0:1, boff:boff + 1]),
                     r=keys + ["prow"], w=["m_" + nm])

        def ml_mixer(tb):
            R_ = m_rows
            for tt in range(NTT):
                P.op("pool", lambda e, tt=tt: e.tensor_tensor(out=m_g[:, tt, :], in0=m_g[:, tt, :], in1=m_zt[:, tt, :], op=ALU.mult),
                     r=[("m_g", tt, 0), ("m_g", tt, 1), ("m_zt", tt, 0), ("m_zt", tt, 1)], w=[("m_g", tt, 0), ("m_g", tt, 1)])
            P.op("act", lambda e: e.activation(out=R_["a"][:], in_=R_["t"][:], func=AF.Abs), r=["m_t"], w=["m_a"])
            P.op("act", lambda e: e.activation(out=R_["a"][:], in_=R_["a"][:], func=AF.Exp, scale=-1.0), r=["m_a"], w=["m_a2"])
            P.op("act", lambda e: e.activation(out=R_["l1"][:], in_=R_["a"][:], func=AF.Ln, bias=1.0), r=["m_a2"], w=["m_l1"])
            P.op("dve", lambda e: e.scalar_tensor_tensor(out=R_["lf"][:], in0=R_["t"][:], scalar=0.0, in1=R_["l1"][:], op0=ALU.min, op1=ALU.subtract), r=["m_t", "m_l1"], w=["m_lf"])
            P.op("dve", lambda e: e.tensor_tensor_scan(out=R_["m"][:], data0=R_["lf"][:], data1=R_["li"][:], initial=m_carry[:], op0=ALU.add, op1=ALU.max),
                 r=["m_lf", "m_li", "m_carry"], w=["m_m"])
            P.op("dve", lambda e: e.tensor_tensor_scan(out=R_["F"][:], data0=R_["one"][:], data1=R_["lf"][:], initial=0.0, op0=ALU.mult, op1=ALU.add),
                 r=["m_lf", "m_one"], w=["m_F"])
            P.op("dve", lambda e: e.tensor_tensor(out=R_["G"][:], in0=R_["F"][:], in1=R_["m"][:], op=ALU.subtract), r=["m_F", "m_m"], w=["m_G"])
            P.op("dve", lambda e: e.tensor_tensor(out=R_["H"][:], in0=R_["li"][:], in1=R_["F"][:], op=ALU.subtract), r=["m_F", "m_li"], w=["m_H"])
            P.op("dve", lambda e: e.tensor_scalar(out=R_["nG"][:], in0=R_["G"][:], scalar1=-1.0, scalar2=None, op0=ALU.mult), r=["m_G"], w=["m_nG"])
            P.op("act", lambda e: e.activation(out=R_["emm"][:], in_=R_["m"][:], func=AF.Exp, scale=-1.0), r=["m_m"], w=["m_emm"])
            one = R_["one"]
            for c in range(NTT):
                csl = slice(c * 128, (c + 1) * 128)
                bias_ap = m_carry[:] if c == 0 else R_["nG"][0:1, c * 128 - 1:c * 128]
                P.op("act", lambda e, csl=csl, bias_ap=bias_ap: e.activation(out=R_["wi"][0:1, csl], in_=R_["G"][0:1, csl], func=AF.Exp, bias=bias_ap),
                     r=["m_G", "m_nG", "m_carry"], w=["m_wi"])
                ps, keys = ps_mix.get(1)

                def fn(e, ps=ps, csl=csl):
                    e.matmul(ps[:, 0:128], lhsT=R_["H"][0:1, csl], rhs=one[0:1, 0:128], start=True, stop=False)
                    return e.matmul(ps[:, 0:128], lhsT=one[0:1, 0:128], rhs=R_["G"][0:1, csl], start=False, stop=True)
                P.op("pe", fn, r=["m_H", "m_G", "m_one"], w=keys)
                P.op("dve", lambda e, ps=ps: e.tensor_scalar(out=m_W[:], in0=ps[:, 0:128], scalar1=0.0, scalar2=None, op0=ALU.min), r=keys, w=["m_W0"])
                P.op("act", lambda e: e.activation(out=m_W[:], in_=m_W[:], func=AF.Exp), r=["m_W0"], w=["m_W"])
                P.op("pool", lambda e: e.tensor_tensor(out=m_Wm[:], in0=m_W[:], in1=ctri, op=ALU.mult), r=["m_W", "consts"], w=["m_Wm"])
                ps, keys = ps_mix.get(1)

                def fn(e, ps=ps, csl=csl, c=c):
                    e.matmul(ps[:, 0:1], lhsT=R_["wi"][0:1, csl], rhs=one[0:1, 0:1], start=True, stop=True)
                    e.matmul(ps[:, 1:2], lhsT=R_["emm"][0:1, csl], rhs=one[0:1, 0:1], start=True, stop=True)
                    return e.matmul(ps[:, 2:3], lhsT=one[0:1, 0:128], rhs=R_["wi"][0:1, c * 128 + 127:c * 128 + 128], start=True, stop=True)
                P.op("pe", fn, r=["m_wi", "m_emm", "m_one"], w=keys)
                P.op("act", lambda e, ps=ps: e.activation(out=m_cols[:, 0:3], in_=ps[:, 0:3], func=AF.Copy), r=keys, w=["m_cols"])
                ps, keys = ps_mix.get(1)

                def fn(e, ps=ps, csl=csl):
                    e.matmul(ps[:, 0:128], lhsT=m_kT[:, 0, csl], rhs=m_qT[:, 0, csl], start=True, stop=False)
                    return e.matmul(ps[:, 0:128], lhsT=m_kT[:, 1, csl], rhs=m_qT[:, 1, csl], start=False, stop=True)
                P.op("pe", fn, r=[("m_kT", 0), ("m_kT", 1), ("m_qT", 0), ("m_qT", 1)], w=keys)
                P.op("dve", lambda e, ps=ps: e.tensor_tensor(out=m_sc[:], in0=ps[:, 0:128], in1=m_Wm[:], op=ALU.mult), r=keys + ["m_Wm"], w=["m_sc"])
                psi, ki = ps_mix.get(4)

                def fn(e, psi=psi, csl=csl):
                    e.matmul(psi[:, 0:512], lhsT=m_qT[:, 0, csl], rhs=m_Cb[:, 0, :], start=True, stop=False)
                    return e.matmul(psi[:, 0:512], lhsT=m_qT[:, 1, csl], rhs=m_Cb[:, 1, :], start=False, stop=True)
                P.op("pe", fn, r=[("m_qT", 0), ("m_qT", 1), "m_Cb"], w=ki)
                P.op("dve", lambda e, psi=psi: e.tensor_scalar(out=m_tmp[:], in0=psi[:, 0:512], scalar1=m_cols[:, 0:1], scalar2=None, op0=ALU.mult), r=ki + ["m_cols"], w=["m_tmp"])
                psa, ka = ps_mix.get(4)
                P.op("pe", lambda e, psa=psa, c=c: e.matmul(psa[:, 0:512], lhsT=m_sc[:], rhs=m_v[:, c, :], start=True, stop=True), r=["m_sc", ("m_v", c, 0), ("m_v", c, 1)], w=ka)
                P.op("dve", lambda e, psa=psa: e.tensor_tensor(out=m_h[:], in0=m_tmp[:], in1=psa[:, 0:512], op=ALU.add), r=ka + ["m_tmp"], w=["m_h"])
                ps, keys = ps_mix.get(1)

                def fn(e, ps=ps, csl=csl):
                    e.matmul(ps[:, 0:1], lhsT=m_sc[:], rhs=m_onesb[:], start=True, stop=True)
                    e.matmul(ps[:, 1:2], lhsT=m_qT[:, 0, csl], rhs=m_nb[:, 0:1], start=True, stop=False)
                    return e.matmul(ps[:, 1:2], lhsT=m_qT[:, 1, csl], rhs=m_nb[:, 1:2], start=False, stop=True)
                P.op("pe", fn, r=["m_sc", "m_onesb", ("m_qT", 0), ("m_qT", 1), "m_nb"], w=keys)
                P.op("act", lambda e, ps=ps: e.activation(out=m_den[:], in_=ps[:, 0:2], func=AF.Copy), r=keys, w=["m_den"])
                P.op("dve", lambda e: e.scalar_tensor_tensor(out=m_d1[:], in0=m_den[:, 1:2], scalar=m_cols[:, 0:1], in1=m_den[:, 0:1], op0=ALU.mult, op1=ALU.add),
                     r=["m_den", "m_cols"], w=["m_d1"])
                P.op("act", lambda e: e.activation(out=m_d1[:], in_=m_d1[:], func=AF.Abs), r=["m_d1"], w=["m_d1b"])
                P.op("dve", lambda e: e.tensor_tensor(out=m_rden[:], in0=m_d1[:], in1=m_cols[:, 1:2], op=ALU.max), r=["m_d1b", "m_cols"], w=["m_rden0"])
                P.op("dve", lambda e: e.reciprocal(out=m_rden[:], in_=m_rden[:]), r=["m_rden0"], w=["m_rden"])
                P.op("dve", lambda e: e.tensor_scalar(out=m_h[:], in0=m_h[:], scalar1=m_rden[:], scalar2=None, op0=ALU.mult), r=["m_h", "m_rden"], w=["m_h"])
                P.op("act", lambda e: e.activation(out=junk[:], in_=m_h[:], func=AF.Square, accum_out=m_ssq[:]), r=["m_h"], w=["junk", "m_ssq"])
                P.op("act", lambda e: e.activation(out=m_rstd[:], in_=m_ssq[:], func=AF.Sqrt, scale=1.0 / 512, bias=epsc[:]), r=["m_ssq", "epsc"], w=["m_rstd0"])
                P.op("dve", lambda e: e.reciprocal(out=m_rstd[:], in_=m_rstd[:]), r=["m_rstd0"], w=["m_rstd"])
                P.op("dve", lambda e: e.tensor_copy(out=m_comb[:], in_=m_rstd[:]), r=["m_rstd"], w=["m_comb"])
                P.op("pool", lambda e, c=c: e.tensor_tensor(out=m_g2[:], in0=m_g[:, c, :], in1=prow[:, PR_MLNW:PR_MLNW + 512], op=ALU.mult), r=[("m_g", c, 0), ("m_g", c, 1), "prow"], w=["m_g2"])
                P.op("dve", lambda e: e.scalar_tensor_tensor(out=m_hbf[:], in0=m_h[:], scalar=m_comb[:], in1=m_g2[:], op0=ALU.mult, op1=ALU.mult),
                     r=["m_h", "m_comb", "m_g2"], w=["m_hbf"])
                transpose_out_bf(m_hbf, 4, 0, c, ["m_hbf"])
                P.op("dve", lambda e, c=c: e.tensor_scalar(out=m_kws[:], in0=m_ktm[:, c, :], scalar1=m_W[:, 127:128], scalar2=None, op0=ALU.mult), r=[("m_ktm", c), "m_W"], w=["m_kws"])
                for kc in range(2):
                    ps, keys = ps_mix.get(4)
                    P.op("pe", lambda e, ps=ps, kc=kc, c=c: e.matmul(ps[:, 0:512], lhsT=m_kws[:, kc * 128:(kc + 1) * 128], rhs=m_v[:, c, :], start=True, stop=True),
                         r=["m_kws", ("m_v", c, 0), ("m_v", c, 1)], w=keys)
                    P.op("dve", lambda e, ps=ps, kc=kc: e.scalar_tensor_tensor(out=m_C[:, kc, :], in0=m_C[:, kc, :], scalar=m_cols[:, 2:3], in1=ps[:, 0:512], op0=ALU.mult, op1=ALU.add),
                         r=keys + ["m_C", "m_cols", "m_Cb"], w=["m_C"])
                ps, keys = ps_mix.get(1)

                def fn(e, ps=ps):
                    e.matmul(ps[:, 0:1], lhsT=m_kws[:, 0:128], rhs=m_onesb[:], start=True, stop=True)
                    return e.matmul(ps[:, 1:2], lhsT=m_kws[:, 128:256], rhs=m_onesb[:], start=True, stop=True)
                P.op("pe", fn, r=["m_kws", "m_onesb"], w=keys)
                P.op("dve", lambda e, ps=ps: e.scalar_tensor_tensor(out=m_n[:], in0=m_n[:], scalar=m_cols[:, 2:3], in1=ps[:, 0:2], op0=ALU.mult, op1=ALU.add),
                     r=keys + ["m_n", "m_cols"], w=["m_n"])
                P.op("pool", lambda e: e.tensor_copy(out=m_Cb[:], in_=m_C[:]), r=["m_C"], w=["m_Cb"])
                P.op("pool", lambda e: e.tensor_copy(out=m_nb[:], in_=m_n[:]), r=["m_n"], w=["m_nb"])
            P.op("dve", lambda e: e.tensor_copy(out=m_carry[:], in_=R_["m"][0:1, TB - 1:TB]), r=["m_m", "m_wi"], w=["m_carry"])

        if "ssm" not in parts or "gdn" not in parts or "ml" not in parts:
            P.op("pool", lambda e: e.memset(ymixT[:], 0.0), w=[("ymixT", b) for b in range(16)])
        ymk = [("ymixT", b) for b in (0, 4, 8, 12)] + [("ymixT", b) for b in range(16)]

        def outproj(tb):
            if debug:
                P.op("pool", lambda e, tb=tb: e.dma_start(out=ymix_d[:, :, tb * TB:(tb + 1) * TB].rearrange("b p t -> p b t"), in_=ymixT[:]),
                     r=ymk, w=[("ymixd", tb)], dma="ymixd")
            n = 0
            for t in range(NOC):
                s = load_wo(t)
                wv_ = wo_view(s)
                for tt in range(NTT):
                    ps, keys = ps_proj.get(OW // 128)

                    def fn(e, ps=ps, wv_=wv_, tt=tt):
                        for kc in range(16):
                            ins = e.matmul(ps[:, 0:OW], lhsT=ymixT[:, kc, tt * 128:(tt + 1) * 128], rhs=wv_[:, kc, :], start=(kc == 0), stop=(kc == 15))
                        return ins
                    P.op("pe", fn, r=ymk + [("wt", s)], w=keys)
                    r0 = tb * TB + tt * 128
                    for hf in range(OW // WT):
                        ss = n % 2
                        n += 1
                        P.op("act", lambda e, ps=ps, ss=ss, hf=hf: e.activation(out=stage[ss][:], in_=ps[:, hf * WT:(hf + 1) * WT], func=AF.Copy), r=keys, w=[("stage", ss)])
                        c0 = t * OW + hf * WT
                        P.op("pool", lambda e, ss=ss, r0=r0, c0=c0: e.dma_start(out=ypart_d[r0:r0 + 128, c0:c0 + WT], in_=stage[ss][:]),
                             r=[("stage", ss)], w=[("ypart", r0, c0)], dma=("stage", ss))

        full = all(p_ in parts for p_ in ("ssm", "gdn", "ml"))
        if not full or os.environ.get("NOMERGE"):
            for tb in range(NB):
                load_xT(tb)
                if "ssm" in parts:
                    ssm_proj(tb)
                if "ssm" in parts or "gdn" in parts:
                    small_proj(tb)
                if "ssm" in parts:
                    ssm_mixer(tb)
                if "gdn" in parts:
                    gdn_proj(tb)
                    gdn_mixer(tb)
                if "ml" in parts:
                    ml_proj(tb)
                    ml_mixer(tb)
                outproj(tb)
        else:
            def head_phase(tb):
                ssm_proj(tb, "fm")
                small_proj(tb)

            def tail_phase(tb):
                ml_mixer(tb)
                outproj(tb)
            load_xT(0)
            head_phase(0)
            ssm_proj(0, "z")
            for tb in range(NB):
                A = P.capture(ssm_mixer, tb)
                mk = A.index(None)
                B = P.capture(gdn_proj, tb, "fm")
                P.replay(A[:mk])
                if "1" in MERGE:
                    P.merge(A[mk:], B)
                else:
                    P.replay(A[mk:])
                    P.replay(B)
                gdn_proj(tb, "z")
                A = P.capture(gdn_mixer, tb)
                B = P.capture(ml_proj, tb, "a")
                if "2" in MERGE:
                    P.merge(A, B)
                else:
                    P.replay(B)
                    P.replay(A)
                ml_proj(tb, "o")
                A = P.capture(ml_mixer, tb)
                if tb + 1 < NB:
                    load_xT(tb + 1)
                    B = P.capture(head_phase, tb + 1)
                    if "3" in MERGE:
                        P.merge(A, B)
                    else:
                        P.replay(A)
                        P.replay(B)
                    outproj(tb)
                    ssm_proj(tb + 1, "z")
                else:
                    P.replay(A)
                    outproj(tb)
        P.emit()
        build_layer.last_prog = P
    return nc


ML_OFF, GDN_OFF, SSM_OFF = 0, 8200, 16424


def core_columns(g):
    r = np.arange
    ml = ML_OFF
    gd = GDN_OFF
    ss = SSM_OFF
    cols = [
        ml + g * 256 + r(256),
        ml + 1024 + g * 256 + r(256),
        gd + g * 512 + r(512),
        gd + 2048 + g * 512 + r(512),
        gd + 4096 + g * 512 + r(512),
        ss + 4096 + g * 1024 + r(1024),
        ss + 8192 + g * 256 + r(256),
        ss + 9216 + g * 256 + r(256),
        ml + 2048 + g * 512 + r(512),
        ml + 4096 + g * 512 + r(512),
        ml + 6144 + g * 512 + r(512),
        gd + 6144 + g * 512 + r(512),
        ss + g * 1024 + r(1024),
        ml + 1024 + g * 256 + r(256),
        gd + 8192 + g * 4 + r(4),
        gd + 8208 + g * 4 + r(4),
        ss + 10240 + g * 16 + r(16),
        np.array([ml + 8192 + g]),
        np.array([ml + 8196 + g]),
    ]
    return np.concatenate(cols)


def pack_core(inp, l, g):
    cols = core_columns(g)
    D = inp["w_in"].shape[1]
    wcat = np.zeros((D, NCOL), np.float32)
    wcat[:, :cols.size] = inp["w_in"][l][:, cols]
    mixrows = np.concatenate([g * 512 + np.arange(512), 2048 + g * 512 + np.arange(512), 4096 + g * 1024 + np.arange(1024)])
    wout = np.ascontiguousarray(inp["w_out"][l][mixrows, :])
    gch = np.concatenate([g * 512 + np.arange(512), 2048 + g * 512 + np.arange(512), 4096 + g * 512 + np.arange(512)])
    sch = np.concatenate([g * 1024 + np.arange(1024), 4096 + g * 256 + np.arange(256), 5120 + g * 256 + np.arange(256)])
    cp = np.zeros((24 * 128, 5), np.float32)
    cp[:1536, 0:4] = inp["gdn_conv_w"][l][:, gch].T
    cp[1536:, 0:4] = inp["ssm_conv_w"][l][:, sch].T
    cp[1536:, 4] = inp["ssm_conv_b"][l][sch]
    convp = np.ascontiguousarray(cp.reshape(24, 128, 5).transpose(1, 0, 2).reshape(128, 120))
    pr = np.zeros((1, PR_N), np.float32)
    pr[0, PR_SDTB:PR_SDTB + 16] = inp["ssm_dt_bias"][l][16 * g:16 * g + 16]
    pr[0, PR_SALOG:PR_SALOG + 16] = inp["ssm_A_log"][l][16 * g:16 * g + 16]
    pr[0, PR_SD:PR_SD + 16] = inp["ssm_D"][l][16 * g:16 * g + 16]
    pr[0, PR_GDTB:PR_GDTB + 4] = inp["gdn_dt_bias"][l][4 * g:4 * g + 4]
    pr[0, PR_GALOG:PR_GALOG + 4] = inp["gdn_A_log"][l][4 * g:4 * g + 4]
    pr[0, PR_MIB] = inp["ml_i_bias"][l][g]
    pr[0, PR_MFB] = inp["ml_f_bias"][l][g]
    pr[0, PR_MLNW:PR_MLNW + 512] = inp["ml_norm_w"][l][g * 512:(g + 1) * 512]
    pr[0, PR_GNW:PR_GNW + 128] = inp["gdn_norm_w"][l]
    pr[0, PR_SNW:PR_SNW + 1024] = inp["ssm_norm_w"][l][g * 1024:(g + 1) * 1024]
    return {"wcat": wcat, "wout": wout, "convp": convp, "prow": pr, "consts": make_consts()}


def build_reduce_ln(ROWS=1024, D=D_MODEL):
    nc = bass.Bass("TRN2", target_bir_lowering=False)
    x_d = nc.dram_tensor("xin", [ROWS, D], F32, kind="ExternalInput").ap()
    p_d = [nc.dram_tensor(f"p{j}", [ROWS, D], F32, kind="ExternalInput").ap() for j in range(4)]
    g_d = nc.dram_tensor("lng", [1, D], F32, kind="ExternalInput").ap()
    b_d = nc.dram_tensor("lnb", [1, D], F32, kind="ExternalInput").ap()
    o_d = nc.dram_tensor("out", [ROWS, D], F32, kind="ExternalOutput").ap()
    CW = 2048 if D >= 2048 else D
    NCH = D // CW
    with ExitStack() as st:
        def T(name, shape, dt=F32):
            return st.enter_context(nc.sbuf_tensor("sb_" + name, shape, dt))
        P = Prog(nc, st)
        gb = T("gb", [128, D])
        bb = T("bb", [128, D])
        acc = [T(f"acc{i}", [128, D]) for i in range(2)]
        ld = [[T(f"ld{i}_{j}", [128, CW]) for j in range(5)] for i in range(2)]
        junk = T("junk", [128, D])
        stt = T("stt", [128, 8])
        epsl = T("epsl", [128, 1])
        P.op("sp", lambda e: e.dma_start(out=gb[:], in_=g_d.partition_broadcast(128)[:, 0, :]), w=["gb"], dma="gb")
        P.op("act", lambda e: e.dma_start(out=bb[:], in_=b_d.partition_broadcast(128)[:, 0, :]), w=["bb"], dma="bb")
        P.op("pool", lambda e: e.memset(epsl[:], LN_EPS), w=["epsl"])
        n = 0
        queues = ["sp", "act", "pool", "sp", "act"]
        for ti in range(ROWS // 128):
            r0 = ti * 128
            a = acc[ti % 2]
            ak = ("acc", ti % 2)
            for ch in range(NCH):
                s = n % 2
                n += 1
                cs = slice(ch * CW, (ch + 1) * CW)
                srcs = [x_d] + p_d
                for j in range(5):
                    P.op(queues[j], lambda e, s=s, j=j, cs=cs, r0=r0, srcs=srcs: e.dma_start(out=ld[s][j][:], in_=srcs[j][r0:r0 + 128, cs]),
                         w=[("ld", s, j)], dma=("ld", s, j))
                P.op("dve", lambda e, s=s, cs=cs, a=a: e.scalar_tensor_tensor(out=a[:, cs], in0=ld[s][0][:], scalar=float(ALPHA), in1=ld[s][1][:], op0=ALU.mult, op1=ALU.add),
                     r=[("ld", s, 0), ("ld", s, 1)], w=[ak])
                P.op("pool", lambda e, s=s: e.tensor_tensor(out=ld[s][3][:], in0=ld[s][3][:], in1=ld[s][4][:], op=ALU.add), r=[("ld", s, 3), ("ld", s, 4)], w=[("ld", s, 3)])
                P.op("dve", lambda e, s=s, cs=cs, a=a: e.tensor_tensor(out=a[:, cs], in0=a[:, cs], in1=ld[s][2][:], op=ALU.add), r=[ak, ("ld", s, 2)], w=[ak])
                P.op("dve", lambda e, s=s, cs=cs, a=a: e.tensor_tensor(out=a[:, cs], in0=a[:, cs], in1=ld[s][3][:], op=ALU.add), r=[ak, ("ld", s, 3)], w=[ak])
            P.op("act", lambda e, a=a: e.activation(out=junk[:], in_=a[:], func=AF.Identity, accum_out=stt[:, 0:1]), r=[ak], w=["junk", "st0"])
            P.op("dve", lambda e: e.tensor_scalar(out=stt[:, 1:2], in0=stt[:, 0:1], scalar1=-1.0 / D, scalar2=None, op0=ALU.mult), r=["st0"], w=["st1"])
            P.op("act", lambda e, a=a: e.activation(out=a[:], in_=a[:], func=AF.Identity, bias=stt[:, 1:2]), r=[ak, "st1"], w=[ak])
            P.op("act", lambda e, a=a: e.activation(out=junk[:], in_=a[:], func=AF.Square, accum_out=stt[:, 2:3]), r=[ak], w=["junk", "st2"])
            P.op("act", lambda e: e.activation(out=stt[:, 3:4], in_=stt[:, 2:3], func=AF.Sqrt, scale=1.0 / D, bias=epsl[:]), r=["st2", "epsl"], w=["st3"])
            P.op("dve", lambda e: e.reciprocal(out=stt[:, 4:5], in_=stt[:, 3:4]), r=["st3"], w=["st4"])
            P.op("dve", lambda e, a=a: e.scalar_tensor_tensor(out=a[:], in0=a[:], scalar=stt[:, 4:5], in1=gb[:], op0=ALU.mult, op1=ALU.mult), r=[ak, "st4", "gb"], w=[ak])
            P.op("dve", lambda e, a=a: e.tensor_tensor(out=a[:], in0=a[:], in1=bb[:], op=ALU.add), r=[ak, "bb"], w=[ak])
            P.op("pool", lambda e, a=a, r0=r0: e.dma_start(out=o_d[r0:r0 + 128, :], in_=a[:]), r=[ak], w=[("out", ti)], dma=ak)
        P.emit()
    return nc


_PROGS = {}


def kernel(x, w_in, w_out, ml_i_bias, ml_f_bias, ml_norm_w, gdn_conv_w, gdn_A_log, gdn_dt_bias,
           gdn_norm_w, ssm_conv_w, ssm_conv_b, ssm_A_log, ssm_dt_bias, ssm_D, ssm_norm_w, ln_g, ln_b):
    inp = dict(w_in=np.asarray(w_in), w_out=np.asarray(w_out), ml_i_bias=np.asarray(ml_i_bias), ml_f_bias=np.asarray(ml_f_bias),
               ml_norm_w=np.asarray(ml_norm_w), gdn_conv_w=np.asarray(gdn_conv_w), gdn_A_log=np.asarray(gdn_A_log),
               gdn_dt_bias=np.asarray(gdn_dt_bias), gdn_norm_w=np.asarray(gdn_norm_w), ssm_conv_w=np.asarray(ssm_conv_w),
               ssm_conv_b=np.asarray(ssm_conv_b), ssm_A_log=np.asarray(ssm_A_log), ssm_dt_bias=np.asarray(ssm_dt_bias),
               ssm_D=np.asarray(ssm_D), ssm_norm_w=np.asarray(ssm_norm_w))
    ln_g = np.asarray(ln_g, np.float32)
    ln_b = np.asarray(ln_b, np.float32)
    xcur = np.ascontiguousarray(np.asarray(x, np.float32))
    B, S, D = xcur.shape
    if "layer" not in _PROGS:
        _PROGS["layer"] = build_layer(D=D, S=S)
        _PROGS["ln"] = build_reduce_ln(ROWS=B * S // 8, D=D)
    RPC = B * S // 8
    for l in range(DEPTH):
        packs = [pack_core(inp, l, g) for g in range(NG)]
        in_maps = [dict(packs[c % NG], x=xcur[c // NG]) for c in range(8)]
        res = run_bass_kernel_spmd(_PROGS["layer"], in_maps, core_ids=list(range(8)))
        yp = [r["ypart"] for r in res.results]
        del in_maps, packs
        xf = xcur.reshape(B * S, D)
        in_maps = []
        for c in range(8):
            b = (c * RPC) // S
            r0 = (c * RPC) % S
            m = {"xin": np.ascontiguousarray(xf[c * RPC:(c + 1) * RPC]), "lng": ln_g[l][None, :], "lnb": ln_b[l][None, :]}
            for j in range(NG):
                m[f"p{j}"] = np.ascontiguousarray(yp[b * NG + j][r0:r0 + RPC])
            in_maps.append(m)
        res = run_bass_kernel_spmd(_PROGS["ln"], in_maps, core_ids=list(range(8)))
        xcur = np.concatenate([r["out"] for r in res.results], axis=0).reshape(B, S, D)
    return xcur.astype(np.float32)
```

```python
from contextlib import ExitStack
import os
GSTOP = int(os.environ.get('GSTOP', '99'))
MERGE = os.environ.get('MERGE', '123')
import numpy as np
import concourse.bass as bass
import concourse.mybir as mybir
from concourse.bass_utils import run_bass_kernel_spmd

F32 = mybir.dt.float32
BF16 = mybir.dt.bfloat16
F32R = mybir.dt.float32r
ALU = mybir.AluOpType
AF = mybir.ActivationFunctionType

SIG_ROT = 30000


class Prog:
    ENG = ("pe", "act", "dve", "pool", "sp")

    def __init__(self, nc, stack):
        self._stack = stack
        self.nc = nc
        self.ops = []
        self.lw = {}
        self.rs = {}
        self.dma_cnt = {}

    cap = None

    def capture(self, f, *a):
        assert self.cap is None
        self.cap = []
        f(*a)
        L, self.cap = self.cap, None
        return L

    def replay(self, L):
        for a in L:
            if a is not None:
                self.op(*a)

    def mark(self):
        if self.cap is not None:
            self.cap.append(None)

    def merge(self, A, B):
        A = [a for a in A if a is not None]
        B = [b for b in B if b is not None]
        ia = ib = 0
        while ia < len(A) or ib < len(B):
            if ib >= len(B) or (ia < len(A) and ia * len(B) <= ib * len(A)):
                self.op(*A[ia])
                ia += 1
            else:
                self.op(*B[ib])
                ib += 1

    def op(self, eng, fn, r=(), w=(), dma=None, inc=16):
        if self.cap is not None:
            self.cap.append((eng, fn, list(r), list(w), dma, inc))
            return
        r = [self.canon(k) for k in r]
        w = [self.canon(k) for k in w]
        i = len(self.ops)
        deps = set()
        raw = set()
        for k in r:
            j = self.lw.get(k)
            if j is not None:
                deps.add(j)
                raw.add(j)
        for k in w:
            j = self.lw.get(k)
            if j is not None:
                deps.add(j)
            for j in self.rs.get(k, ()):
                deps.add(j)
        keep = set()
        for j in deps:
            pj = self.ops[j]
            if pj["dma"] is None and dma is None and pj["eng"] == eng:
                if eng == "pe":
                    continue
            keep.add(j)
        o = dict(eng=eng, fn=fn, deps=keep, dma=dma, sig=False, sidx=None, inc=inc)
        if dma is not None:
            c = self.dma_cnt.get(dma, 0) + 1
            self.dma_cnt[dma] = c
            o["sidx"] = c
            o["sig"] = True
        self.ops.append(o)
        for j in keep:
            self.ops[j]["sig"] = True
        for k in r:
            self.rs.setdefault(k, []).append(i)
        for k in w:
            self.lw[k] = i
            self.rs[k] = []
        return i

    ALIAS = {"g_rn0": "g_rn", "s_rstd0": "s_rstd", "g_rstd0": "g_rstd", "m_rstd0": "m_rstd", "m_rden0": "m_rden",
             "m_W0": "m_W", "m_d1b": "m_d1", "m_a2": "m_a", "s_dt_b": "s_dt_a", "g_sp_b": "g_sp_a", "sUh": "Uh", "gUh": "Uh",
             "m_wi": "m_a", "m_emm": "m_l1", "m_nG": "m_t", "g_ktm": ("s_xtm", 0), "g_vtm": ("s_xtm", 1),
             "g_Rv": "s_xdt", "g_Rk": "s_xdt"}

    def canon(self, k):
        if isinstance(k, tuple):
            if k[0] == "cvp":
                return ("cv",) + k[1:]
            if k[0] in ("sU", "gU"):
                return ("U",) + k[1:]
            if k[0] in ("g_sz", "m_g"):
                return ("s_sz",) + k[1:]
            if k[0] == "m_zt":
                return ("s_sz", k[1], 2 + k[2])
            return k
        return self.ALIAS.get(k, k)

    def emit(self, final_wait_eng="sp"):
        nc = self.nc
        cnt = {e: 0 for e in self.ENG}
        for o in self.ops:
            if o["dma"] is None and o["sig"]:
                cnt[o["eng"]] += 1
                o["sidx"] = cnt[o["eng"]]
        sems = {}
        stack = self._stack
        for e in self.ENG:
            n = (cnt[e] + SIG_ROT - 1) // SIG_ROT
            for q in range(max(n, 1)):
                sems[("E", e, q)] = stack.enter_context(nc.semaphore(f"s_{e}_{q}"))
        for k in self.dma_cnt:
            nm = "d_" + "_".join(str(x) for x in (k if isinstance(k, tuple) else (k,)))
            sems[("D", k)] = stack.enter_context(nc.semaphore(nm))
        self.nsem = len(sems)

        def chan(o):
            if o["dma"] is not None:
                return ("D", o["dma"]), o["inc"] * o["sidx"]
            q, rr = divmod(o["sidx"] - 1, SIG_ROT)
            return ("E", o["eng"], q), rr + 1

        per = {e: [] for e in self.ENG}
        for o in self.ops:
            per[o["eng"]].append(o)
        final = {}
        for o in self.ops:
            if o["dma"] is not None:
                ch, v = chan(o)
                final[ch] = max(final.get(ch, 0), v)
        block = stack.enter_context(nc.Block())
        deco = {"pe": block.tensor, "act": block.scalar, "dve": block.vector,
                "pool": block.gpsimd, "sp": block.sync}
        ops = self.ops

        def make(e):
            def body(engobj):
                seen = {}
                for o in per[e]:
                    need = {}
                    for j in o["deps"]:
                        ch, v = chan(ops[j])
                        if v > need.get(ch, 0):
                            need[ch] = v
                    for ch, v in need.items():
                        if seen.get(ch, 0) >= v:
                            continue
                        engobj.wait_ge(sems[ch], v)
                        seen[ch] = v
                    ins = o["fn"](engobj)
                    if o["sig"]:
                        ch, v = chan(o)
                        ins.then_inc(sems[ch], o["inc"] if o["dma"] is not None else 1)
                if e == final_wait_eng:
                    for ch, v in final.items():
                        if seen.get(ch, 0) < v:
                            engobj.wait_ge(sems[ch], v)
            return body

        for e in self.ENG:
            if per[e] or e == final_wait_eng:
                deco[e](make(e))


D_MODEL = 4096
SEQ = 4096
BATCH = 2
DEPTH = 2
NG = 4
WT = 256
NT = 28
NCOL = NT * WT
MIXW = 2048
RMS_EPS = 1e-6
LN_EPS = 1e-5
ALPHA = (2 * DEPTH) ** 0.25

PR_SDTB, PR_SALOG, PR_SD, PR_GDTB, PR_GALOG, PR_MIB, PR_MFB = 0, 16, 32, 48, 52, 56, 57
PR_MLNW, PR_GNW, PR_SNW = 64, 64 + 512, 64 + 512 + 128
PR_N = 64 + 512 + 128 + 1024
C_ID, C_TRI, C_U, C_ONES, C_BDS, C_BD, C_OFFT, C_NEG = range(8)
NCONST = 8


def make_consts():
    i = np.arange(128)
    k, l = i[:, None], i[None, :]
    c = np.zeros((NCONST, 128, 128), np.float32)
    c[C_ID] = np.eye(128)
    c[C_TRI] = (k <= l)
    c[C_U] = (k > l)
    c[C_ONES] = 1.0
    same = (k // 64) == (l // 64)
    c[C_BDS] = (k > l) & same
    c[C_BD] = same
    c[C_OFFT] = (k < 64) & (l >= 64)
    c[C_NEG] = np.where(k > l, -30000.0, 0.0)
    return np.ascontiguousarray(c.transpose(1, 0, 2).reshape(128, NCONST * 128))


def build_layer(D=D_MODEL, S=SEQ, TB=256, debug=False, parts=("ssm", "gdn", "ml")):
    KC = D // 128
    NB = S // TB
    NTT = TB // 128
    nc = bass.Bass("TRN2", target_bir_lowering=False)
    x_d = nc.dram_tensor("x", [S, D], F32, kind="ExternalInput").ap()
    wcat_d = nc.dram_tensor("wcat", [D, NCOL], F32, kind="ExternalInput").ap()
    wout_d = nc.dram_tensor("wout", [MIXW, D], F32, kind="ExternalInput").ap()
    convp_d = nc.dram_tensor("convp", [128, 24 * 5], F32, kind="ExternalInput").ap()
    prow_d = nc.dram_tensor("prow", [1, PR_N], F32, kind="ExternalInput").ap()
    const_d = nc.dram_tensor("consts", [128, NCONST * 128], F32, kind="ExternalInput").ap()
    ypart_d = nc.dram_tensor("ypart", [S, D], F32, kind="ExternalOutput").ap()
    wscr_d = nc.dram_tensor("wscr", [NT, 128, KC, WT], BF16, kind="Internal").ap()
    OW = 512 if KC >= 32 else WT
    NOC = D // OW
    woscr_d = nc.dram_tensor("woscr", [NOC, 128, 16, OW], BF16, kind="Internal").ap()
    if debug:
        ymix_d = nc.dram_tensor("ymix", [16, 128, S], BF16, kind="ExternalOutput").ap()

    with ExitStack() as st:
        def T(name, shape, dt=F32):
            return st.enter_context(nc.sbuf_tensor("sb_" + name, shape, dt))
        P = Prog(nc, st)
        psb = [st.enter_context(nc.psum_tensor(f"psb{i}", [128, 512], F32)) for i in range(8)]

        class PSA:
            def __init__(self, banks):
                self.banks = banks
                self.pos = 0

            def get(self, nq):
                bank = self.banks[self.pos % len(self.banks)]
                self.pos += 1
                return psb[bank][:, 0:nq * 128], [("ps", bank)]
        ps_proj = PSA([0, 1])
        ps_mix = PSA([2, 3, 4, 5, 6, 7])

        consts = T("consts", [128, NCONST, 128])
        cid = consts[:, C_ID, :]
        ctri = consts[:, C_TRI, :]
        cU = consts[:, C_U, :]
        cones = consts[:, C_ONES, :]
        idb = T("idb", [128, 128], BF16)
        prow = T("prow", [128, PR_N])
        convp = T("convp", [128, 24, 5])
        nA_s = T("nA_s", [128, 16])
        nA_g = T("nA_g", [128, 4])
        XLW = 512 if D >= 512 else D
        xld = [T(f"xld{i}", [128, XLW]) for i in range(2)]
        xT = T("xT", [128, KC, TB], BF16)
        wt = [T(f"wt{i}", [128, max(KC, 16), WT], BF16) for i in range(2)]
        CK = 2 if KC >= 2 else 1
        assert CK == 2 and WT == 256
        HS = T("HS", [128, 5120])
        HS2 = T("HS2", [128, 4096])
        bar = T("bar", [128, 1])
        HS_KEYS = [("cin", 0), ("cin", 1), ("cout", 0), ("cout", 1), "s_gu", ("s_mte", 0), ("s_mte", 1), "s_xw"]
        for h_ in range(4):
            HS_KEYS += [(k_, h_) for k_ in ("hA", "hB", "hC", "hD", "hA0a", "hA0b", "hAToff", "hattn", "huw")]
            HS_KEYS += [(k_, h_, i_) for k_ in ("hPP", "hTT") for i_ in range(2)]

        def barrier():
            P.op("pool", lambda e: e.memset(bar[:], 0.0), w=HS_KEYS)
        U = T("U", [128, 12, 3 + TB])
        hist_s = T("hist_s", [128, 12, 3])
        hist_g = T("hist_g", [128, 12, 3])
        cv = T("cv", [128, 12, TB])
        ymixT = T("ymixT", [128, 16, TB], BF16)
        stage = [T(f"stage{i}", [128, WT]) for i in range(2)]
        junk = T("junk", [128, 512])
        epsc = T("epsc", [128, 1])

        P.op("sp", lambda e: e.dma_start(out=consts[:], in_=const_d.rearrange("p (c k) -> p c k", c=NCONST)), w=["consts"], dma="consts")
        P.op("sp", lambda e: e.dma_start(out=prow[:], in_=prow_d.partition_broadcast(128)[:, 0, :]), w=["prow"], dma="prow")
        P.op("sp", lambda e: e.dma_start(out=convp[:], in_=convp_d.rearrange("p (c k) -> p c k", c=24)), w=["convp"], dma="convp")
        P.op("dve", lambda e: e.tensor_copy(out=idb[:], in_=cid), r=["consts"], w=["idb"])
        ctri_r = T("ctri_r", [128, 128], F32R)
        P.op("dve", lambda e: e.tensor_copy(out=ctri_r[:], in_=ctri), r=["consts"], w=["ctri_r"])
        P.op("pool", lambda e: e.memset(epsc[:], RMS_EPS), w=["epsc"])
        P.op("act", lambda e: e.activation(out=nA_s[:], in_=prow[:, PR_SALOG:PR_SALOG + 16], func=AF.Exp), r=["prow"], w=["nA_s"])
        P.op("dve", lambda e: e.tensor_scalar(out=nA_s[:], in0=nA_s[:], scalar1=-1.0, scalar2=None, op0=ALU.mult), r=["nA_s"], w=["nA_s"])
        P.op("act", lambda e: e.activation(out=nA_g[:], in_=prow[:, PR_GALOG:PR_GALOG + 4], func=AF.Exp), r=["prow"], w=["nA_g"])
        P.op("dve", lambda e: e.tensor_scalar(out=nA_g[:], in0=nA_g[:], scalar1=-1.0, scalar2=None, op0=ALU.mult), r=["nA_g"], w=["nA_g"])

        wv = wcat_d.rearrange("(kc p) c -> p kc c", p=128)
        wov = wout_d.rearrange("(kc p) c -> p kc c", p=128)
        PK = 8 if KC >= 8 else KC
        NPW = KC // PK
        NPO = 16 // PK if PK <= 16 else 1
        order = [8, 9, 10, 11, 12, 13, 27, 22, 23, 24, 25, 2, 3, 4, 5, 6, 7, 20, 21, 0, 1, 14, 15, 18, 19, 16, 17]
        for t in order:
            for pi in range(NPW):
                k0 = pi * PK
                P.op("pool", lambda e, t=t, k0=k0: e.dma_start(out=wscr_d[t][:, k0:k0 + PK, :], in_=wv[:, k0:k0 + PK, t * WT:(t + 1) * WT]),
                     w=[("wscr", t, pi)], dma=("prew", t))
        for t in range(NOC):
            for pi in range(NPO):
                k0 = pi * PK
                P.op("pool", lambda e, t=t, k0=k0: e.dma_start(out=woscr_d[t][:, k0:k0 + PK, :], in_=wov[:, k0:k0 + PK, t * OW:(t + 1) * OW]),
                     w=[("woscr", t, pi)], dma=("preo", t))

        barrier()
        wslot = [0]

        def load_w(t):
            s = wslot[0] % 2
            wslot[0] += 1
            P.op("sp", lambda e: e.dma_start(out=wt[s][:, 0:KC, :], in_=wscr_d[t]), r=[("wscr", t, pi) for pi in range(NPW)], w=[("wt", s)], dma=("wt", s))
            return s

        def wo_view(s):
            if OW == WT:
                return wt[s][:, 0:16, :]
            return wt[s][:].rearrange("p k c -> p (k c)")[:, 0:16 * OW].rearrange("p (k c) -> p k c", k=16)

        def load_wo(t):
            s = wslot[0] % 2
            wslot[0] += 1
            P.op("sp", lambda e: e.dma_start(out=wo_view(s), in_=woscr_d[t]), r=[("woscr", t, pi) for pi in range(NPO)], w=[("wt", s)], dma=("wt", s))
            return s

        def proj_fm(s, j, M=128, c0=None):
            ps, keys = ps_proj.get(2)
            cc = j * 128 if c0 is None else c0

            def fn(e):
                for kc in range(KC):
                    ins = e.matmul(ps[0:M, 0:TB], lhsT=wt[s][:, kc, cc:cc + M], rhs=xT[:, kc, :], start=(kc == 0), stop=(kc == KC - 1))
                return ins
            P.op("pe", fn, r=[("wt", s), "xT"], w=keys)
            return ps, keys

        def proj_tm(s, tt):
            ps, keys = ps_proj.get(2)

            def fn(e):
                for kc in range(KC):
                    ins = e.matmul(ps[:, 0:WT], lhsT=xT[:, kc, tt * 128:(tt + 1) * 128], rhs=wt[s][:, kc, :], start=(kc == 0), stop=(kc == KC - 1))
                return ins
            P.op("pe", fn, r=[("wt", s), "xT"], w=keys)
            return ps, keys

        def load_xT(tb):
            n = 0
            for tt in range(NTT):
                r0 = tb * TB + tt * 128
                for c0 in range(0, D, XLW):
                    s = n % 2
                    n += 1
                    P.op("sp", lambda e, s=s, r0=r0, c0=c0: e.dma_start(out=xld[s][:], in_=x_d[r0:r0 + 128, c0:c0 + XLW]),
                         w=[("xld", s)], dma=("xld", s))
                    for q0 in range(0, XLW, 512):
                        nq = min(4, (XLW - q0) // 128)
                        ps, keys = ps_mix.get(nq)

                        def fn(e, s=s, q0=q0, nq=nq, ps=ps):
                            for q in range(nq):
                                ins = e.transpose(out=ps[:, q * 128:(q + 1) * 128], in_=xld[s][:, q0 + q * 128:q0 + (q + 1) * 128], identity=cid)
                            return ins
                        P.op("pe", fn, r=[("xld", s), "consts"], w=keys)
                        kc0 = (c0 + q0) // 128
                        P.op("act", lambda e, ps=ps, kc0=kc0, nq=nq, tt=tt: e.activation(
                            out=xT[:, kc0:kc0 + nq, tt * 128:(tt + 1) * 128],
                            in_=ps[:, 0:nq * 128].rearrange("p (q t) -> p q t", q=nq), func=AF.Copy),
                            r=keys, w=["xT"])

        def conv_blocks(Ubuf, hist, cbase, nblk, key):
            P.op("pool", lambda e: e.tensor_copy(out=Ubuf[:, 0:nblk, 0:3], in_=hist[:, 0:nblk, :]), r=[key + "hist"], w=[key + "Uh"])
            for b in range(nblk):
                eng = "dve"
                rk = [(key + "U", b), key + "Uh", "convp"]

                P.op(eng, lambda e, b=b: e.tensor_scalar(out=cv[:, b, :], in0=Ubuf[:, b, 3:3 + TB], scalar1=convp[:, cbase + b, 3:4], scalar2=convp[:, cbase + b, 4:5],
                                                         op0=ALU.mult, op1=ALU.add), r=rk, w=[("cvp", b)])
                for k in range(3):
                    P.op(eng, lambda e, b=b, k=k: e.scalar_tensor_tensor(out=cv[:, b, :], in0=Ubuf[:, b, k:k + TB], scalar=convp[:, cbase + b, k:k + 1], in1=cv[:, b, :],
                                                                         op0=ALU.mult, op1=ALU.add), r=rk + [("cvp", b)], w=[("cvp", b)])
                P.op("act", lambda e, b=b: e.activation(out=cv[:, b, :], in_=cv[:, b, :], func=AF.Silu), r=[("cvp", b)], w=[("cv", b)])
            P.op("pool", lambda e: e.tensor_copy(out=hist[:, 0:nblk, :], in_=Ubuf[:, 0:nblk, TB:TB + 3]),
                 r=[(key + "U", b) for b in range(nblk)] + [key + "Uh"], w=[key + "hist"])

        def softplus_(out, xin, nparts, shape_keys_r, wkey, tmpa, tmpb):
            P.op("act", lambda e: e.activation(out=tmpa, in_=xin, func=AF.Abs), r=shape_keys_r, w=[wkey + "_a"])
            P.op("act", lambda e: e.activation(out=tmpa, in_=tmpa, func=AF.Exp, scale=-1.0), r=[wkey + "_a"], w=[wkey + "_b"])
            P.op("act", lambda e: e.activation(out=tmpb, in_=tmpa, func=AF.Ln, bias=1.0), r=[wkey + "_b"], w=[wkey + "_c"])
            P.op("dve", lambda e: e.scalar_tensor_tensor(out=out, in0=xin, scalar=0.0, in1=tmpb, op0=ALU.max, op1=ALU.add),
                 r=list(shape_keys_r) + [wkey + "_c"], w=[wkey])

        def transpose_out_bf(src_bf, ncb, blk0, c, rkeys):
            ps, keys = ps_mix.get((ncb + 1) // 2)
            psv = ps.bitcast(BF16)

            def fn(e):
                for i in range(ncb):
                    ins = e.transpose(out=psv[:, i * 128:(i + 1) * 128], in_=src_bf[:, i * 128:(i + 1) * 128], identity=idb[:])
                return ins
            P.op("pe", fn, r=list(rkeys) + ["idb"], w=keys)
            P.op("act", lambda e: e.activation(out=ymixT[:, blk0:blk0 + ncb, c * 128:(c + 1) * 128],
                                                in_=psv[:, 0:ncb * 128].rearrange("p (b t) -> p b t", b=ncb), func=AF.Copy),
                 r=keys, w=[("ymixT", blk0)])

        s_small = T("s_small", [128, NTT, 32])
        big_tm = T("big_tm", [128, NTT, 1024])
        s_xtm = T("s_xtm", [128, 1024])
        s_xdt = T("s_xdt", [128, 1024])
        if "ssm" in parts:
            s_sz = big_tm
            s_dt = T("s_dt", [128, NTT, 16])
            s_a = T("s_a", [128, NTT, 16])
            s_t1 = T("s_t1", [128, NTT, 16])
            s_t2 = T("s_t2", [128, NTT, 16])
            s_xx = T("s_xx", [128, NTT, 16])
            s_state = T("s_state", [128, 2, 512])
            s_pre = T("s_pre", [128, 48])
            s_ex = T("s_ex", [128, 48])
            s_xw = HS[:, 2048:3072]
            s_btm = T("s_btm", [128, 2, 128])
            s_gu = HS[:, 0:1024].rearrange("p (h l) -> p h l", h=8)
            s_mt = HS[:, 1024:2048].rearrange("p (h l) -> p h l", h=8)
            s_cbm = T("s_cbm", [128, 128])
            s_y = T("s_y", [128, 512])
            s_ybf = T("s_ybf", [128, 512], BF16)
            s_ssq = T("s_ssq", [128, 1])
            s_rstd = T("s_rstd", [128, 1])
            P.op("pool", lambda e: e.memset(s_state[:], 0.0), w=["s_state0", "s_state1"])
            P.op("pool", lambda e: e.memset(hist_s[:], 0.0), w=["shist"])

        def ssm_proj(tb, part="all"):
            for ti, t in enumerate(range(8, 14) if part in ("all", "fm") else ()):
                s = load_w(t)
                for j in range(2):
                    ps, keys = proj_fm(s, j)
                    b = ti * 2 + j
                    P.op("act", lambda e, ps=ps, b=b: e.activation(out=U[:, b, 3:3 + TB], in_=ps[:, 0:TB], func=AF.Copy), r=keys, w=[("sU", b)])
            for ti, t in enumerate(range(22, 26) if part in ("all", "z") else ()):
                s = load_w(t)
                for tt in range(NTT):
                    ps, keys = proj_tm(s, tt)
                    P.op("act", lambda e, ps=ps, ti=ti, tt=tt: e.activation(out=s_sz[:, tt, ti * WT:(ti + 1) * WT], in_=ps[:, 0:WT], func=AF.Silu),
                         r=keys, w=[("s_sz", tt, ti)])

        def small_proj(tb):
            s = load_w(27)
            for tt in range(NTT):
                ps, keys = proj_tm(s, tt)
                P.op("dve", lambda e, ps=ps, tt=tt: e.tensor_copy(out=s_small[:, tt, :], in_=ps[:, 0:32]), r=keys, w=[("small", tt)])
            return s

        def ssm_mixer(tb):
            barrier()
            conv_blocks(U, hist_s, 12, 12, "s")
            P.mark()
            smk = [("small", tt) for tt in range(NTT)]
            P.op("dve", lambda e: e.tensor_tensor(out=s_xx[:], in0=s_small[:, :, 8:24],
                                                  in1=prow[:, PR_SDTB:PR_SDTB + 16].unsqueeze(1).broadcast_to([128, NTT, 16]), op=ALU.add),
                 r=smk + ["prow"], w=["s_xx"])
            softplus_(s_dt[:], s_xx[:], 128, ["s_xx"], "s_dt", s_t1[:], s_t2[:])
            P.op("dve", lambda e: e.tensor_tensor(out=s_a[:], in0=s_dt[:], in1=nA_s[:].unsqueeze(1).broadcast_to([128, NTT, 16]), op=ALU.mult),
                 r=["s_dt", "nA_s"], w=["s_a"])
            for c in range(NTT):
                csl = slice(c * 128, (c + 1) * 128)
                ps, keys = ps_mix.get(1)

                def fn(e, ps=ps, c=c):
                    e.matmul(ps[:, 0:16], lhsT=ctri, rhs=s_a[:, c, :], start=True, stop=True)
                    return e.matmul(ps[:, 16:32], lhsT=cones, rhs=s_a[:, c, :], start=True, stop=True)
                P.op("pe", fn, r=["consts", "s_a"], w=keys)
                P.op("act", lambda e, ps=ps: e.activation(out=s_pre[:, 0:32], in_=ps[:, 0:32], func=AF.Copy), r=keys, w=["s_pre"])
                P.op("dve", lambda e: e.tensor_tensor(out=s_pre[:, 32:48], in0=s_pre[:, 16:32], in1=s_pre[:, 0:16], op=ALU.subtract), r=["s_pre"], w=["s_pre2"])
                P.op("act", lambda e: e.activation(out=s_ex[:], in_=s_pre[:], func=AF.Exp), r=["s_pre", "s_pre2"], w=["s_ex"])
                for half in range(2):
                    ps, keys = ps_mix.get(4)

                    def fn(e, ps=ps, half=half, csl=csl):
                        for q in range(4):
                            ins = e.transpose(out=ps[:, q * 128:(q + 1) * 128], in_=cv[:, half * 4 + q, csl], identity=cid)
                        return ins
                    P.op("pe", fn, r=[("cv", half * 4 + q) for q in range(4)] + ["consts"], w=keys)
                    P.op("act", lambda e, ps=ps, half=half: e.activation(out=s_xtm[:, half * 512:(half + 1) * 512], in_=ps[:, 0:512], func=AF.Copy),
                         r=keys, w=[("s_xtm", half)])
                ps, keys = ps_mix.get(2)

                def fn(e, ps=ps, csl=csl):
                    e.transpose(out=ps[:, 0:128], in_=cv[:, 8, csl], identity=cid)
                    return e.transpose(out=ps[:, 128:256], in_=cv[:, 9, csl], identity=cid)
                P.op("pe", fn, r=[("cv", 8), ("cv", 9), "consts"], w=keys)
                P.op("dve", lambda e, ps=ps: e.tensor_copy(out=s_btm[:].bitcast(F32R), in_=ps[:, 0:256].rearrange("p (g n) -> p g n", g=2)), r=keys, w=["s_btm"])
                xk = [("s_xtm", 0), ("s_xtm", 1)]
                v3 = lambda t_: t_[:, 0:1024].rearrange("p (h q) -> p h q", h=16)
                v3r = lambda t_: t_[:, 0:1024].bitcast(F32R).rearrange("p (h q) -> p h q", h=16)
                P.op("pool", lambda e, c=c: e.tensor_tensor(out=v3r(s_xdt), in0=v3(s_xtm), in1=s_dt[:, c, :].unsqueeze(2).broadcast_to([128, 16, 64]), op=ALU.mult),
                     r=xk + ["s_dt"], w=["s_xdt"])
                P.op("pool", lambda e: e.tensor_tensor(out=v3r(s_xw), in0=v3(s_xdt), in1=s_ex[:, 32:48].unsqueeze(2).broadcast_to([128, 16, 64]), op=ALU.mult),
                     r=["s_xdt", "s_ex"], w=["s_xw"])
                for gi in range(2):
                    ps, keys = ps_mix.get(1)
                    P.op("pe", lambda e, ps=ps, gi=gi, csl=csl: e.matmul(ps[:, 0:128], lhsT=cv[:, 8 + gi, csl], rhs=cv[:, 10 + gi, csl], start=True, stop=True),
                         r=[("cv", 8 + gi), ("cv", 10 + gi)], w=keys)
                    P.op("dve", lambda e, ps=ps: e.tensor_tensor(out=s_cbm[:], in0=ps[:, 0:128], in1=ctri, op=ALU.mult), r=keys + ["consts"], w=["s_cbm"])
                    P.op("pool", lambda e, c=c, gi=gi: e.tensor_tensor(out=s_gu[:].bitcast(F32R), in0=s_a[:, c, gi * 8:(gi + 1) * 8].unsqueeze(2).broadcast_to([128, 8, 128]),
                                                                       in1=cU.unsqueeze(1).broadcast_to([128, 8, 128]), op=ALU.mult),
                         r=["s_a", "consts"], w=["s_gu"])
                    for hh in range(2):
                        ps, keys = ps_mix.get(4)

                        def fn(e, ps=ps, hh=hh):
                            for q in range(4):
                                ins = e.matmul(ps[:, q * 128:(q + 1) * 128], lhsT=s_gu[:, hh * 4 + q, :].bitcast(F32R), rhs=ctri_r[:], start=True, stop=True)
                            return ins
                        P.op("pe", fn, r=["s_gu", "ctri_r"], w=keys)
                        P.op("act", lambda e, ps=ps, hh=hh: e.activation(out=s_mt[:, hh * 4:(hh + 1) * 4, :].bitcast(F32R), in_=ps[:, 0:512].rearrange("p (h l) -> p h l", h=4), func=AF.Exp),
                             r=keys, w=[("s_mte", hh)])
                    P.op("dve", lambda e: e.tensor_tensor(out=s_mt[:].bitcast(F32R), in0=s_mt[:], in1=s_cbm[:].unsqueeze(1).broadcast_to([128, 8, 128]), op=ALU.mult),
                         r=[("s_mte", 0), ("s_mte", 1), "s_cbm"], w=[("s_mte", 0), ("s_mte", 1)])
                    psd, kd = ps_mix.get(4)

                    def fn(e, psd=psd, gi=gi):
                        for h in range(8):
                            hg = gi * 8 + h
                            ins = e.matmul(psd[:, h * 64:(h + 1) * 64], lhsT=s_mt[:, h, :].bitcast(F32R), rhs=s_xdt[:, hg * 64:(hg + 1) * 64].bitcast(F32R), start=True, stop=True)
                        return ins
                    P.op("pe", fn, r=[("s_mte", 0), ("s_mte", 1), "s_xdt"], w=kd)
                    pso, ko = ps_mix.get(4)
                    P.op("pe", lambda e, pso=pso, gi=gi, csl=csl: e.matmul(pso[:, 0:512], lhsT=cv[:, 10 + gi, csl], rhs=s_state[:, gi, :], start=True, stop=True),
                         r=[("cv", 10 + gi), f"s_state{gi}"], w=ko)
                    y3 = s_y[:].rearrange("p (h q) -> p h q", h=8)
                    P.op("dve", lambda e, pso=pso, gi=gi: e.tensor_tensor(out=y3, in0=pso[:, 0:512].rearrange("p (h q) -> p h q", h=8),
                                                                          in1=s_ex[:, gi * 8:(gi + 1) * 8].unsqueeze(2).broadcast_to([128, 8, 64]), op=ALU.mult),
                         r=ko + ["s_ex"], w=["s_y"])
                    P.op("pool", lambda e, gi=gi: e.tensor_tensor(out=junk[:].rearrange("p (h q) -> p h q", h=8), in0=s_xtm[:, gi * 512:(gi + 1) * 512].rearrange("p (h q) -> p h q", h=8),
                                                                  in1=prow[:, PR_SD + gi * 8:PR_SD + (gi + 1) * 8].unsqueeze(2).broadcast_to([128, 8, 64]), op=ALU.mult),
                         r=[("s_xtm", gi), "prow"], w=["junk"])
                    P.op("pool", lambda e, gi=gi: e.tensor_tensor(out=s_y[:], in0=s_y[:], in1=junk[:], op=ALU.add), r=["s_y", "junk"], w=["s_y"])
                    P.op("dve", lambda e, psd=psd: e.tensor_tensor(out=s_y[:], in0=s_y[:], in1=psd[:, 0:512], op=ALU.add), r=kd + ["s_y"], w=["s_y"])
                    P.op("pool", lambda e, c=c, gi=gi: e.tensor_tensor(out=s_y[:], in0=s_y[:], in1=s_sz[:, c, gi * 512:(gi + 1) * 512], op=ALU.mult),
                         r=["s_y", ("s_sz", c, 2 * gi), ("s_sz", c, 2 * gi + 1)], w=["s_y"])
                    P.op("act", lambda e: e.activation(out=junk[:], in_=s_y[:], func=AF.Square, accum_out=s_ssq[:]), r=["s_y"], w=["junk", "s_ssq"])
                    P.op("act", lambda e: e.activation(out=s_rstd[:], in_=s_ssq[:], func=AF.Sqrt, scale=1.0 / 512, bias=epsc[:]), r=["s_ssq", "epsc"], w=["s_rstd0"])
                    P.op("dve", lambda e: e.reciprocal(out=s_rstd[:], in_=s_rstd[:]), r=["s_rstd0"], w=["s_rstd"])
                    P.op("dve", lambda e, gi=gi: e.scalar_tensor_tensor(out=s_ybf[:], in0=s_y[:], scalar=s_rstd[:], in1=prow[:, PR_SNW + gi * 512:PR_SNW + (gi + 1) * 512],
                                                                        op0=ALU.mult, op1=ALU.mult), r=["s_y", "s_rstd", "prow"], w=["s_ybf"])
                    transpose_out_bf(s_ybf, 4, 8 + gi * 4, c, ["s_ybf"])
                    pss, ks = ps_mix.get(4)
                    P.op("pe", lambda e, pss=pss, gi=gi: e.matmul(pss[:, 0:512], lhsT=s_btm[:, gi, :].bitcast(F32R), rhs=s_xw[:, gi * 512:(gi + 1) * 512].bitcast(F32R), start=True, stop=True),
                         r=["s_btm", "s_xw"], w=ks)
                    st3 = s_state[:, gi, :].rearrange("p (h q) -> p h q", h=8)
                    P.op("pool", lambda e, st3=st3, gi=gi: e.tensor_tensor(out=st3, in0=st3, in1=s_ex[:, 16 + gi * 8:16 + (gi + 1) * 8].unsqueeze(2).broadcast_to([128, 8, 64]), op=ALU.mult),
                         r=[f"s_state{gi}", "s_ex"], w=[f"s_state{gi}"])
                    P.op("dve", lambda e, pss=pss, gi=gi: e.tensor_tensor(out=s_state[:, gi, :], in0=s_state[:, gi, :], in1=pss[:, 0:512], op=ALU.add),
                         r=ks + [f"s_state{gi}"], w=[f"s_state{gi}"])

        if "gdn" in parts:
            g_sz = big_tm[:, :, 0:512]
            g_sq = T("g_sq", [128, TB])
            g_rn = T("g_rn", [128, TB])
            g_beta = T("g_beta", [128, NTT, 4])
            g_xx = T("g_xx", [128, NTT, 4])
            g_t1s = T("g_t1s", [128, NTT, 4])
            g_t2s = T("g_t2s", [128, NTT, 4])
            g_sp = T("g_sp", [128, NTT, 4])
            g_g = T("g_g", [128, NTT, 4])
            g_pre = T("g_pre", [128, 12])
            g_ex = T("g_ex", [128, 12])
            g_bg = T("g_bg", [128, 4])
            g_ktm = s_xtm[:, 0:512].rearrange("p (h d) -> p h d", h=4)
            g_vtm = s_xtm[:, 512:1024].rearrange("p (h d) -> p h d", h=4)
            g_R = s_xdt[:].rearrange("p (h c) -> p h c", h=4)
            g_kd = T("g_kd", [128, 4, 128])
            g_H = []
            for h_ in range(4):
                b1 = h_ * 1280
                b2 = h_ * 1024
                v2 = lambda o_: HS[:, o_:o_ + 256].rearrange("p (a l) -> p a l", a=2)
                v1 = lambda o_: HS[:, o_:o_ + 128]
                w2 = lambda o_: HS2[:, o_:o_ + 256].rearrange("p (a l) -> p a l", a=2)
                w1 = lambda o_: HS2[:, o_:o_ + 128]
                g_H.append(dict(A=v2(b1), B=v2(b1 + 256), C=v1(b1 + 512), D=v1(b1 + 640), AToff=v1(b1 + 768), attnT=v1(b1 + 896), uw=v2(b1 + 1024),
                                A0=w2(b2), PP=[w2(b2 + 256), w2(b2 + 512)], TT=[w1(b2 + 768), w1(b2 + 896)]))
            g_o = T("g_o", [128, 4, 128])
            g_o2 = T("g_o2", [128, 4, 128])
            g_ssq = T("g_ssq", [128, 4])
            g_rstd = T("g_rstd", [128, 4])
            g_obf = T("g_obf", [128, 512], BF16)
            g_S = T("g_S", [128, 4, 128])
            P.op("pool", lambda e: e.memset(g_S[:], 0.0), w=[("g_S", h) for h in range(4)])
            P.op("pool", lambda e: e.memset(hist_g[:], 0.0), w=["ghist"])
            cBDm = consts[:, C_BD, :]
            cOFFT = consts[:, C_OFFT, :]
            cNEG = consts[:, C_NEG, :]

        def gdn_proj(tb, part="all"):
            for ti, t in enumerate(range(2, 8) if part in ("all", "fm") else ()):
                s = load_w(t)
                for j in range(2):
                    ps, keys = proj_fm(s, j)
                    b = ti * 2 + j
                    P.op("act", lambda e, ps=ps, b=b: e.activation(out=U[:, b, 3:3 + TB], in_=ps[:, 0:TB], func=AF.Copy), r=keys, w=[("gU", b)])
            for ti, t in enumerate(range(20, 22) if part in ("all", "z") else ()):
                s = load_w(t)
                for tt in range(NTT):
                    ps, keys = proj_tm(s, tt)
                    P.op("act", lambda e, ps=ps, ti=ti, tt=tt: e.activation(out=g_sz[:, tt, ti * WT:(ti + 1) * WT], in_=ps[:, 0:WT], func=AF.Silu),
                         r=keys, w=[("g_sz", tt, ti)])

        def head_gen(c, csl, h):
            Hh = g_H[h]
            kA, kB, kC, kD = ("hA", h), ("hB", h), ("hC", h), ("hD", h)
            kA0a, kA0b, kAo, kat, kuw = ("hA0a", h), ("hA0b", h), ("hAToff", h), ("hattn", h), ("huw", h)
            GU, E_, t1, Af, A0, AToff, attnT, PP, TT, uw = (Hh[k_] for k_ in ("A", "B", "C", "D", "A0", "AToff", "attnT", "PP", "TT", "uw"))
            X1 = GU.rearrange("p a l -> p (a l)")
            X2 = E_.rearrange("p a l -> p (a l)")
            vn, tmp_ = t1, Af
            gcol = g_g[:, c, h:h + 1]
            P.op("dve", lambda e: e.tensor_scalar(out=GU[:, 0, :].bitcast(F32R), in0=cU, scalar1=gcol, scalar2=None, op0=ALU.mult), r=["consts", "g_g"], w=[kA])
            yield
            P.op("dve", lambda e: e.tensor_scalar(out=GU[:, 1, :].bitcast(F32R), in0=ctri, scalar1=gcol, scalar2=None, op0=ALU.mult), r=["consts", "g_g"], w=[kA])
            yield
            ps, keys = ps_mix.get(4)
            kT = cv[:, 4 + h, csl]
            qT_ = cv[:, h, csl]

            def fn(e, ps=ps):
                e.matmul(ps[:, 0:128], lhsT=GU[:, 0, :], rhs=ctri, start=True, stop=False)
                e.matmul(ps[:, 0:128], lhsT=cid, rhs=cNEG, start=False, stop=True)
                e.matmul(ps[:, 128:256], lhsT=GU[:, 1, :], rhs=cU, start=True, stop=True)
                e.matmul(ps[:, 256:384], lhsT=kT, rhs=kT, start=True, stop=True)
                return e.matmul(ps[:, 384:512], lhsT=kT, rhs=qT_, start=True, stop=True)
            P.op("pe", fn, r=[kA, "consts", ("cv", 4 + h), ("cv", h)], w=keys)
            yield
            P.op("act", lambda e, ps=ps: e.activation(out=E_[:].bitcast(F32R), in_=ps[:, 0:256].rearrange("p (a l) -> p a l", a=2), func=AF.Exp), r=keys, w=[kB])
            yield
            P.op("dve", lambda e, ps=ps: e.tensor_tensor(out=t1[:].bitcast(F32R), in0=ps[:, 256:384], in1=E_[:, 1, :], op=ALU.mult), r=keys + [kB], w=[kC])
            yield
            P.op("dve", lambda e, ps=ps: e.tensor_tensor(out=attnT[:].bitcast(F32R), in0=ps[:, 384:512], in1=E_[:, 0, :], op=ALU.mult), r=keys + [kB], w=[kat])
            yield
            P.op("dve", lambda e: e.scalar_tensor_tensor(out=Af[:].bitcast(F32R), in0=t1[:], scalar=g_beta[:, c, h:h + 1], in1=cU, op0=ALU.mult, op1=ALU.mult),
                 r=[kC, "g_beta", "consts"], w=[kD])
            yield
            ps, keys = ps_mix.get(1)
            P.op("pe", lambda e, ps=ps: e.transpose(out=ps[:, 0:128], in_=Af[:], identity=cid), r=[kD, "consts"], w=keys)
            yield
            P.op("pool", lambda e: e.tensor_tensor(out=A0[:, 0, :], in0=Af[:], in1=cBDm, op=ALU.mult), r=[kD, "consts"], w=[kA0a])
            yield
            P.op("dve", lambda e, ps=ps: e.tensor_tensor(out=A0[:, 1, :], in0=ps[:, 0:128], in1=cBDm, op=ALU.mult), r=keys + ["consts"], w=[kA0b])
            yield
            P.op("dve", lambda e, ps=ps: e.tensor_tensor(out=AToff[:].bitcast(F32R), in0=ps[:, 0:128], in1=cOFFT, op=ALU.mult), r=keys + ["consts"], w=[kAo])
            yield
            P.op("pool", lambda e: e.tensor_tensor(out=TT[0][:], in0=cid, in1=A0[:, 1, :], op=ALU.subtract), r=[kA0b, "consts"], w=[("hTT", h, 0)])
            yield
            Pc, Pk = A0, [kA0a, kA0b]
            tcur = 0
            for k in range(1, 6):
                ps, keys = ps_mix.get(2)

                def fn(e, ps=ps, Pc=Pc, k=k):
                    ins = e.matmul(ps[:, 0:128], lhsT=Pc[:, 1, :], rhs=Pc[:, 0, :], start=True, stop=True)
                    if k < 5:
                        ins = e.matmul(ps[:, 128:256], lhsT=Pc[:, 0, :], rhs=Pc[:, 1, :], start=True, stop=True)
                    return ins
                P.op("pe", fn, r=Pk, w=keys)
                yield
                Pn = PP[k % 2]
                nk = [("hPP", h, k % 2)]
                P.op("act", lambda e, ps=ps, Pn=Pn: e.activation(out=Pn[:], in_=ps[:, 0:256].rearrange("p (a l) -> p a l", a=2), func=AF.Copy), r=keys, w=nk)
                yield
                ps, keys = ps_mix.get(1)
                P.op("pe", lambda e, ps=ps, Pn=Pn, tcur=tcur: e.matmul(ps[:, 0:128], lhsT=Pn[:, 0, :], rhs=TT[tcur][:], start=True, stop=True),
                     r=nk + [("hTT", h, tcur)], w=keys)
                yield
                P.op("dve", lambda e, ps=ps, tcur=tcur: e.tensor_tensor(out=TT[1 - tcur][:], in0=TT[tcur][:], in1=ps[:, 0:128], op=ALU.add),
                     r=keys + [("hTT", h, tcur)], w=[("hTT", h, 1 - tcur)])
                yield
                tcur = 1 - tcur
                Pc, Pk = Pn, nk
            TTf = TT[tcur]
            tk = [("hTT", h, tcur)]
            ps, keys = ps_mix.get(2)
            P.op("pe", lambda e, ps=ps: e.matmul(ps[:, 0:256], lhsT=TTf[:], rhs=g_R[:, h, :], start=True, stop=True), r=tk + ["g_Rv", "g_Rk"], w=keys)
            yield
            P.op("act", lambda e, ps=ps: e.activation(out=X1.bitcast(F32R), in_=ps[:, 0:256], func=AF.Copy), r=keys, w=[kA])
            yield
            ps, keys = ps_mix.get(2)
            P.op("pe", lambda e, ps=ps: e.matmul(ps[:, 0:256], lhsT=AToff[:].bitcast(F32R), rhs=X1.bitcast(F32R), start=True, stop=True), r=[kAo, kA], w=keys)
            yield
            P.op("dve", lambda e, ps=ps: e.tensor_tensor(out=X2.bitcast(F32R), in0=g_R[:, h, :], in1=ps[:, 0:256], op=ALU.subtract), r=keys + ["g_Rv", "g_Rk"], w=[kB])
            yield
            ps, keys = ps_mix.get(2)

            def fn(e, ps=ps):
                e.matmul(ps[:, 0:128], lhsT=TTf[:], rhs=X2[:, 0:128], start=True, stop=True)
                return e.matmul(ps[:, 128:256], lhsT=X2[:, 128:256], rhs=TTf[:], start=True, stop=True)
            P.op("pe", fn, r=tk + [kB], w=keys)
            yield
            P.op("act", lambda e, ps=ps: e.activation(out=uw[:].bitcast(F32R), in_=ps[:, 0:256].rearrange("p (a l) -> p a l", a=2), func=AF.Copy), r=keys, w=[kuw])
            yield
            ps, keys = ps_mix.get(2)

            def fn(e, ps=ps):
                e.matmul(ps[:, 0:128], lhsT=uw[:, 1, :], rhs=g_S[:, h, :], start=True, stop=True)
                return e.matmul(ps[:, 128:256], lhsT=qT_, rhs=g_S[:, h, :], start=True, stop=True)
            P.op("pe", fn, r=[kuw, ("g_S", h), ("cv", h)], w=keys)
            yield
            P.op("dve", lambda e, ps=ps: e.tensor_tensor(out=vn[:].bitcast(F32R), in0=uw[:, 0, :], in1=ps[:, 0:128], op=ALU.subtract), r=keys + [kuw], w=[kC])
            yield
            P.op("dve", lambda e, ps=ps: e.tensor_scalar(out=tmp_[:].bitcast(F32R), in0=ps[:, 128:256], scalar1=g_ex[:, h:h + 1], scalar2=None, op0=ALU.mult), r=keys + ["g_ex"], w=[kD])
            yield
            ps, keys = ps_mix.get(2)

            def fn(e, ps=ps):
                e.matmul(ps[:, 0:128], lhsT=attnT[:].bitcast(F32R), rhs=vn[:].bitcast(F32R), start=True, stop=True)
                return e.matmul(ps[:, 128:256], lhsT=g_kd[:, h, :].bitcast(F32R), rhs=vn[:].bitcast(F32R), start=True, stop=True)
            P.op("pe", fn, r=[kat, kC, "g_kd"], w=keys)
            yield
            P.op("dve", lambda e, ps=ps: e.tensor_tensor(out=g_o[:, h, :], in0=tmp_[:], in1=ps[:, 0:128], op=ALU.add), r=keys + [kD], w=[("g_o", h)])
            yield
            P.op("dve", lambda e, ps=ps: e.scalar_tensor_tensor(out=g_S[:, h, :], in0=g_S[:, h, :], scalar=g_ex[:, 4 + h:5 + h], in1=ps[:, 128:256],
                                                                op0=ALU.mult, op1=ALU.add), r=keys + [("g_S", h), "g_ex"], w=[("g_S", h)])
            yield

        def gdn_mixer(tb):
            barrier()
            conv_blocks(U, hist_g, 0, 12, "g")
            for b in range(8):
                P.op("act", lambda e, b=b: e.activation(out=g_sq[:], in_=cv[:, b, :], func=AF.Square), r=[("cv", b)], w=["g_sq"])
                ps, keys = ps_mix.get(2)
                P.op("pe", lambda e, ps=ps: e.matmul(ps[:, 0:TB], lhsT=cones, rhs=g_sq[:], start=True, stop=True), r=["g_sq", "consts"], w=keys)
                P.op("act", lambda e, ps=ps: e.activation(out=g_rn[:], in_=ps[:, 0:TB], func=AF.Sqrt, bias=epsc[:]), r=keys + ["epsc"], w=["g_rn0"])
                P.op("dve", lambda e: e.reciprocal(out=g_rn[:], in_=g_rn[:]), r=["g_rn0"], w=["g_rn"])
                sc_ = 128 ** -0.5 if b < 4 else 1.0
                P.op("dve", lambda e, b=b, sc_=sc_: e.scalar_tensor_tensor(out=cv[:, b, :], in0=cv[:, b, :], scalar=sc_, in1=g_rn[:], op0=ALU.mult, op1=ALU.mult),
                     r=[("cv", b), "g_rn"], w=[("cv", b)])
            if GSTOP <= 1:
                return
            smk = [("small", tt) for tt in range(NTT)]
            P.op("act", lambda e: e.activation(out=g_beta[:], in_=s_small[:, :, 0:4], func=AF.Sigmoid), r=smk, w=["g_beta"])
            P.op("dve", lambda e: e.tensor_tensor(out=g_xx[:], in0=s_small[:, :, 4:8],
                                                  in1=prow[:, PR_GDTB:PR_GDTB + 4].unsqueeze(1).broadcast_to([128, NTT, 4]), op=ALU.add),
                 r=smk + ["prow"], w=["g_xx"])
            softplus_(g_sp[:], g_xx[:], 128, ["g_xx"], "g_sp", g_t1s[:], g_t2s[:])
            P.op("dve", lambda e: e.tensor_tensor(out=g_g[:], in0=g_sp[:], in1=nA_g[:].unsqueeze(1).broadcast_to([128, NTT, 4]), op=ALU.mult),
                 r=["g_sp", "nA_g"], w=["g_g"])
            if GSTOP <= 2:
                return
            for c in range(NTT):
                csl = slice(c * 128, (c + 1) * 128)
                ps, keys = ps_mix.get(1)

                def fn(e, ps=ps, c=c):
                    e.matmul(ps[:, 0:4], lhsT=ctri, rhs=g_g[:, c, :], start=True, stop=True)
                    return e.matmul(ps[:, 4:8], lhsT=cones, rhs=g_g[:, c, :], start=True, stop=True)
                P.op("pe", fn, r=["consts", "g_g"], w=keys)
                P.op("act", lambda e, ps=ps: e.activation(out=g_pre[:, 0:8], in_=ps[:, 0:8], func=AF.Copy), r=keys, w=["g_pre"])
                P.op("dve", lambda e: e.tensor_tensor(out=g_pre[:, 8:12], in0=g_pre[:, 4:8], in1=g_pre[:, 0:4], op=ALU.subtract), r=["g_pre"], w=["g_pre2"])
                P.op("act", lambda e: e.activation(out=g_ex[:], in_=g_pre[:], func=AF.Exp), r=["g_pre", "g_pre2"], w=["g_ex"])
                P.op("dve", lambda e, c=c: e.tensor_tensor(out=g_bg[:], in0=g_beta[:, c, :], in1=g_ex[:, 0:4], op=ALU.mult), r=["g_beta", "g_ex"], w=["g_bg"])
                for which, dst, key in ((4, g_ktm, "g_ktm"), (8, g_vtm, "g_vtm")):
                    ps, keys = ps_mix.get(4)

                    def fn(e, ps=ps, which=which, csl=csl):
                        for q in range(4):
                            ins = e.transpose(out=ps[:, q * 128:(q + 1) * 128], in_=cv[:, which + q, csl], identity=cid)
                        return ins
                    P.op("pe", fn, r=[("cv", which + q) for q in range(4)] + ["consts"], w=keys)
                    P.op("act", lambda e, ps=ps, dst=dst: e.activation(out=dst[:], in_=ps[:, 0:512].rearrange("p (h d) -> p h d", h=4), func=AF.Copy), r=keys, w=[key])
                P.op("dve", lambda e, c=c: e.tensor_tensor(out=g_R[:, :, 0:128].bitcast(F32R), in0=g_vtm[:], in1=g_beta[:, c, :].unsqueeze(2).broadcast_to([128, 4, 128]), op=ALU.mult),
                     r=["g_vtm", "g_beta"], w=["g_Rv"])
                P.op("pool", lambda e: e.tensor_tensor(out=g_R[:, :, 128:256].bitcast(F32R), in0=g_ktm[:], in1=g_bg[:].unsqueeze(2).broadcast_to([128, 4, 128]), op=ALU.mult),
                     r=["g_ktm", "g_bg"], w=["g_Rk"])
                P.op("pool", lambda e: e.tensor_tensor(out=g_kd[:].bitcast(F32R), in0=g_ktm[:], in1=g_ex[:, 8:12].unsqueeze(2).broadcast_to([128, 4, 128]), op=ALU.mult),
                     r=["g_ktm", "g_ex"], w=["g_kd"])
                if GSTOP <= 3:
                    continue
                gens = [head_gen(c, csl, h) for h in range(4)]
                live = list(gens)
                while live:
                    nxt = []
                    for g_ in live:
                        try:
                            next(g_)
                            nxt.append(g_)
                        except StopIteration:
                            pass
                    live = nxt
                if GSTOP <= 8:
                    continue
                ok = [("g_o", h) for h in range(4)]
                P.op("act", lambda e: e.activation(out=g_o2[:], in_=g_o[:], func=AF.Square), r=ok, w=["g_o2"])
                P.op("dve", lambda e: e.tensor_reduce(out=g_ssq[:], in_=g_o2[:], axis=mybir.AxisListType.X, op=ALU.add), r=["g_o2"], w=["g_ssq"])
                P.op("act", lambda e: e.activation(out=g_rstd[:], in_=g_ssq[:], func=AF.Sqrt, scale=1.0 / 128, bias=epsc[:]), r=["g_ssq", "epsc"], w=["g_rstd0"])
                P.op("dve", lambda e: e.reciprocal(out=g_rstd[:], in_=g_rstd[:]), r=["g_rstd0"], w=["g_rstd"])
                P.op("pool", lambda e: e.tensor_tensor(out=g_o2[:], in0=g_o[:], in1=g_rstd[:].unsqueeze(2).broadcast_to([128, 4, 128]), op=ALU.mult), r=ok + ["g_rstd", "g_o2"], w=["g_o2"])
                P.op("pool", lambda e: e.tensor_tensor(out=g_o2[:], in0=g_o2[:], in1=prow[:, PR_GNW:PR_GNW + 128].unsqueeze(1).broadcast_to([128, 4, 128]), op=ALU.mult),
                     r=["g_o2", "prow"], w=["g_o2"])
                P.op("dve", lambda e, c=c: e.tensor_tensor(out=g_obf[:], in0=g_o2[:].rearrange("p h d -> p (h d)"), in1=g_sz[:, c, :], op=ALU.mult),
                     r=["g_o2", ("g_sz", c, 0), ("g_sz", c, 1)], w=["g_obf"])
                transpose_out_bf(g_obf, 4, 4, c, ["g_obf"])

        if "ml" in parts:
            m_qT = T("m_qT", [128, 2, TB], BF16)
            m_kT = T("m_kT", [128, 2, TB], BF16)
            m_ktm = T("m_ktm", [128, NTT, 256])
            m_v = T("m_v", [128, NTT, 512], BF16)
            m_g = big_tm[:, :, 0:512]
            m_zt = big_tm[:, :, 512:1024]
            RN = ["li", "t", "a", "l1", "lf", "m", "F", "G", "H", "one"]
            m_rows = {nm: T("m_r_" + nm, [1, TB]) for nm in RN}
            m_rows["wi"] = m_rows["a"]
            m_rows["emm"] = m_rows["l1"]
            m_rows["nG"] = m_rows["t"]
            m_carry = T("m_carry", [1, 1])
            m_C = T("m_C", [128, 2, 512])
            m_Cb = T("m_Cb", [128, 2, 512], BF16)
            m_n = T("m_n", [128, 2])
            m_nb = T("m_nb", [128, 2], BF16)
            m_W = T("m_W", [128, 128])
            m_Wm = T("m_Wm", [128, 128])
            m_sc = T("m_sc", [128, 128], BF16)
            m_cols = T("m_cols", [128, 4])
            m_tmp = T("m_tmp", [128, 512])
            m_h = T("m_h", [128, 512])
            m_den = T("m_den", [128, 2])
            m_d1 = T("m_d1", [128, 1])
            m_rden = T("m_rden", [128, 1])
            m_ssq = T("m_ssq", [128, 1])
            m_rstd = T("m_rstd", [128, 1])
            m_comb = T("m_comb", [128, 1])
            m_g2 = T("m_g2", [128, 512])
            m_hbf = T("m_hbf", [128, 512], BF16)
            m_kws = T("m_kws", [128, 256], BF16)
            m_onesb = T("m_onesb", [128, 1], BF16)
            P.op("pool", lambda e: e.memset(m_C[:], 0.0), w=["m_C"])
            P.op("pool", lambda e: e.memset(m_Cb[:], 0.0), w=["m_Cb"])
            P.op("pool", lambda e: e.memset(m_n[:], 0.0), w=["m_n"])
            P.op("pool", lambda e: e.memset(m_nb[:], 0.0), w=["m_nb"])
            P.op("pool", lambda e: e.memset(m_carry[:], 0.0), w=["m_carry"])
            P.op("pool", lambda e: e.memset(m_rows["one"][:], 1.0), w=["m_one"])
            P.op("pool", lambda e: e.memset(m_onesb[:], 1.0), w=["m_onesb"])

        def ml_proj(tb, part="all"):
            if part == "o":
                for ti, t in ((2, 16), (3, 17)):
                    s = load_w(t)
                    for tt in range(NTT):
                        ps, keys = proj_tm(s, tt)
                        cs = slice((ti % 2) * WT, (ti % 2 + 1) * WT)
                        P.op("act", lambda e, ps=ps, tt=tt, cs=cs: e.activation(out=m_g[:, tt, cs], in_=ps[:, 0:WT], func=AF.Sigmoid), r=keys, w=[("m_g", tt, ti % 2)])
                return
            s = load_w(0)
            for j in range(2):
                ps, keys = proj_fm(s, j)
                P.op("act", lambda e, ps=ps, j=j: e.activation(out=m_qT[:, j, :], in_=ps[:, 0:TB], func=AF.Identity, scale=256 ** -0.5), r=keys, w=[("m_qT", j)])
            s = load_w(1)
            for j in range(2):
                ps, keys = proj_fm(s, j)
                P.op("act", lambda e, ps=ps, j=j: e.activation(out=m_kT[:, j, :], in_=ps[:, 0:TB], func=AF.Copy), r=keys, w=[("m_kT", j)])
            for ti, t in enumerate(range(14, 20)):
                if part == "a" and ti in (2, 3):
                    continue
                s = load_w(t)
                for tt in range(NTT):
                    ps, keys = proj_tm(s, tt)
                    cs = slice((ti % 2) * WT, (ti % 2 + 1) * WT)
                    if ti < 2:
                        P.op("act", lambda e, ps=ps, tt=tt, cs=cs: e.activation(out=m_v[:, tt, cs], in_=ps[:, 0:WT], func=AF.Copy), r=keys, w=[("m_v", tt, ti)])
                    elif ti < 4:
                        P.op("act", lambda e, ps=ps, tt=tt, cs=cs: e.activation(out=m_g[:, tt, cs], in_=ps[:, 0:WT], func=AF.Sigmoid), r=keys, w=[("m_g", tt, ti % 2)])
                    else:
                        P.op("act", lambda e, ps=ps, tt=tt, cs=cs: e.activation(out=m_zt[:, tt, cs], in_=ps[:, 0:WT], func=AF.Silu), r=keys, w=[("m_zt", tt, ti % 2)])
            for tt in range(NTT):
                ps, keys = ps_proj.get(1)
                psv = ps.bitcast(BF16)

                def fn(e, psv=psv, tt=tt):
                    e.transpose(out=psv[:, 0:128], in_=m_kT[:, 0, tt * 128:(tt + 1) * 128], identity=idb[:])
                    return e.transpose(out=psv[:, 128:256], in_=m_kT[:, 1, tt * 128:(tt + 1) * 128], identity=idb[:])
                P.op("pe", fn, r=[("m_kT", 0), ("m_kT", 1), "idb"], w=keys)
                P.op("act", lambda e, psv=psv, tt=tt: e.activation(out=m_ktm[:, tt, :], in_=psv[:, 0:256], func=AF.Copy), r=keys, w=[("m_ktm", tt)])
            smk_ = [("small", tt) for tt in range(NTT)]
            for col, nm, boff in ((24, "li", PR_MIB), (25, "t", PR_MFB)):
                ps, keys = ps_proj.get(2)

                def fn(e, ps=ps, col=col):
                    for tt in range(NTT):
                        ins = e.transpose(out=ps[0:1, tt * 128:(tt + 1) * 128], in_=s_small[:, tt, col:col + 1], identity=cid)
                    return ins
                P.op("pe", fn, r=smk_ + ["consts"], w=keys)
                P.op("act", lambda e, ps=ps, nm=nm, boff=boff: e.activation(out=m_rows[nm][:], in_=ps[0:1, 0:TB], func=AF.Identity, bias=prow[0:1, boff:boff + 1]),
                     r=keys + ["prow"], w=["m_" + nm])

        def ml_mixer(tb):
            R_ = m_rows
            for tt in range(NTT):
                P.op("pool", lambda e, tt=tt: e.tensor_tensor(out=m_g[:, tt, :], in0=m_g[:, tt, :], in1=m_zt[:, tt, :], op=ALU.mult),
                     r=[("m_g", tt, 0), ("m_g", tt, 1), ("m_zt", tt, 0), ("m_zt", tt, 1)], w=[("m_g", tt, 0), ("m_g", tt, 1)])
            P.op("act", lambda e: e.activation(out=R_["a"][:], in_=R_["t"][:], func=AF.Abs), r=["m_t"], w=["m_a"])
            P.op("act", lambda e: e.activation(out=R_["a"][:], in_=R_["a"][:], func=AF.Exp, scale=-1.0), r=["m_a"], w=["m_a2"])
            P.op("act", lambda e: e.activation(out=R_["l1"][:], in_=R_["a"][:], func=AF.Ln, bias=1.0), r=["m_a2"], w=["m_l1"])
            P.op("dve", lambda e: e.scalar_tensor_tensor(out=R_["lf"][:], in0=R_["t"][:], scalar=0.0, in1=R_["l1"][:], op0=ALU.min, op1=ALU.subtract), r=["m_t", "m_l1"], w=["m_lf"])
            P.op("dve", lambda e: e.tensor_tensor_scan(out=R_["m"][:], data0=R_["lf"][:], data1=R_["li"][:], initial=m_carry[:], op0=ALU.add, op1=ALU.max),
                 r=["m_lf", "m_li", "m_carry"], w=["m_m"])
            P.op("dve", lambda e: e.tensor_tensor_scan(out=R_["F"][:], data0=R_["one"][:], data1=R_["lf"][:], initial=0.0, op0=ALU.mult, op1=ALU.add),
                 r=["m_lf", "m_one"], w=["m_F"])
            P.op("dve", lambda e: e.tensor_tensor(out=R_["G"][:], in0=R_["F"][:], in1=R_["m"][:], op=ALU.subtract), r=["m_F", "m_m"], w=["m_G"])
            P.op("dve", lambda e: e.tensor_tensor(out=R_["H"][:], in0=R_["li"][:], in1=R_["F"][:], op=ALU.subtract), r=["m_F", "m_li"], w=["m_H"])
            P.op("dve", lambda e: e.tensor_scalar(out=R_["nG"][:], in0=R_["G"][:], scalar1=-1.0, scalar2=None, op0=ALU.mult), r=["m_G"], w=["m_nG"])
            P.op("act", lambda e: e.activation(out=R_["emm"][:], in_=R_["m"][:], func=AF.Exp, scale=-1.0), r=["m_m"], w=["m_emm"])
            one = R_["one"]
            for c in range(NTT):
                csl = slice(c * 128, (c + 1) * 128)
                bias_ap = m_carry[:] if c == 0 else R_["nG"][0:1, c * 128 - 1:c * 128]
                P.op("act", lambda e, csl=csl, bias_ap=bias_ap: e.activation(out=R_["wi"][0:1, csl], in_=R_["G"][0:1, csl], func=AF.Exp, bias=bias_ap),
                     r=["m_G", "m_nG", "m_carry"], w=["m_wi"])
                ps, keys = ps_mix.get(1)

                def fn(e, ps=ps, csl=csl):
                    e.matmul(ps[:, 0:128], lhsT=R_["H"][0:1, csl], rhs=one[0:1, 0:128], start=True, stop=False)
                    return e.matmul(ps[:, 0:128], lhsT=one[0:1, 0:128], rhs=R_["G"][0:1, csl], start=False, stop=True)
                P.op("pe", fn, r=["m_H", "m_G", "m_one"], w=keys)
                P.op("dve", lambda e, ps=ps: e.tensor_scalar(out=m_W[:], in0=ps[:, 0:128], scalar1=0.0, scalar2=None, op0=ALU.min), r=keys, w=["m_W0"])
                P.op("act", lambda e: e.activation(out=m_W[:], in_=m_W[:], func=AF.Exp), r=["m_W0"], w=["m_W"])
                P.op("pool", lambda e: e.tensor_tensor(out=m_Wm[:], in0=m_W[:], in1=ctri, op=ALU.mult), r=["m_W", "consts"], w=["m_Wm"])
                ps, keys = ps_mix.get(1)

                def fn(e, ps=ps, csl=csl, c=c):
                    e.matmul(ps[:, 0:1], lhsT=R_["wi"][0:1, csl], rhs=one[0:1, 0:1], start=True, stop=True)
                    e.matmul(ps[:, 1:2], lhsT=R_["emm"][0:1, csl], rhs=one[0:1, 0:1], start=True, stop=True)
                    return e.matmul(ps[:, 2:3], lhsT=one[0:1, 0:128], rhs=R_["wi"][0:1, c * 128 + 127:c * 128 + 128], start=True, stop=True)
                P.op("pe", fn, r=["m_wi", "m_emm", "m_one"], w=keys)
                P.op("act", lambda e, ps=ps: e.activation(out=m_cols[:, 0:3], in_=ps[:, 0:3], func=AF.Copy), r=keys, w=["m_cols"])
                ps, keys = ps_mix.get(1)

                def fn(e, ps=ps, csl=csl):
                    e.matmul(ps[:, 0:128], lhsT=m_kT[:, 0, csl], rhs=m_qT[:, 0, csl], start=True, stop=False)
                    return e.matmul(ps[:, 0:128], lhsT=m_kT[:, 1, csl], rhs=m_qT[:, 1, csl], start=False, stop=True)
                P.op("pe", fn, r=[("m_kT", 0), ("m_kT", 1), ("m_qT", 0), ("m_qT", 1)], w=keys)
                P.op("dve", lambda e, ps=ps: e.tensor_tensor(out=m_sc[:], in0=ps[:, 0:128], in1=m_Wm[:], op=ALU.mult), r=keys + ["m_Wm"], w=["m_sc"])
                psi, ki = ps_mix.get(4)

                def fn(e, psi=psi, csl=csl):
                    e.matmul(psi[:, 0:512], lhsT=m_qT[:, 0, csl], rhs=m_Cb[:, 0, :], start=True, stop=False)
                    return e.matmul(psi[:, 0:512], lhsT=m_qT[:, 1, csl], rhs=m_Cb[:, 1, :], start=False, stop=True)
                P.op("pe", fn, r=[("m_qT", 0), ("m_qT", 1), "m_Cb"], w=ki)
                P.op("dve", lambda e, psi=psi: e.tensor_scalar(out=m_tmp[:], in0=psi[:, 0:512], scalar1=m_cols[:, 0:1], scalar2=None, op0=ALU.mult), r=ki + ["m_cols"], w=["m_tmp"])
                psa, ka = ps_mix.get(4)
                P.op("pe", lambda e, psa=psa, c=c: e.matmul(psa[:, 0:512], lhsT=m_sc[:], rhs=m_v[:, c, :], start=True, stop=True), r=["m_sc", ("m_v", c, 0), ("m_v", c, 1)], w=ka)
                P.op("dve", lambda e, psa=psa: e.tensor_tensor(out=m_h[:], in0=m_tmp[:], in1=psa[:, 0:512], op=ALU.add), r=ka + ["m_tmp"], w=["m_h"])
                ps, keys = ps_mix.get(1)

                def fn(e, ps=ps, csl=csl):
                    e.matmul(ps[:, 0:1], lhsT=m_sc[:], rhs=m_onesb[:], start=True, stop=True)
                    e.matmul(ps[:, 1:2], lhsT=m_qT[:, 0, csl], rhs=m_nb[:, 0:1], start=True, stop=False)
                    return e.matmul(ps[:, 1:2], lhsT=m_qT[:, 1, csl], rhs=m_nb[:, 1:2], start=False, stop=True)
                P.op("pe", fn, r=["m_sc", "m_onesb", ("m_qT", 0), ("m_qT", 1), "m_nb"], w=keys)
                P.op("act", lambda e, ps=ps: e.activation(out=m_den[:], in_=ps[:, 0:2], func=AF.Copy), r=keys, w=["m_den"])
                P.op("dve", lambda e: e.scalar_tensor_tensor(out=m_d1[:], in0=m_den[:, 1:2], scalar=m_cols[:, 0:1], in1=m_den[:, 0:1], op0=ALU.mult, op1=ALU.add),
                     r=["m_den", "m_cols"], w=["m_d1"])
                P.op("act", lambda e: e.activation(out=m_d1[:], in_=m_d1[:], func=AF.Abs), r=["m_d1"], w=["m_d1b"])
                P.op("dve", lambda e: e.tensor_tensor(out=m_rden[:], in0=m_d1[:], in1=m_cols[:, 1:2], op=ALU.max), r=["m_d1b", "m_cols"], w=["m_rden0"])
                P.op("dve", lambda e: e.reciprocal(out=m_rden[:], in_=m_rden[:]), r=["m_rden0"], w=["m_rden"])
                P.op("dve", lambda e: e.tensor_scalar(out=m_h[:], in0=m_h[:], scalar1=m_rden[:], scalar2=None, op0=ALU.mult), r=["m_h", "m_rden"], w=["m_h"])
                P.op("act", lambda e: e.activation(out=junk[:], in_=m_h[:], func=AF.Square, accum_out=m_ssq[:]), r=["m_h"], w=["junk", "m_ssq"])
                P.op("act", lambda e: e.activation(out=m_rstd[:], in_=m_ssq[:], func=AF.Sqrt, scale=1.0 / 512, bias=epsc[:]), r=["m_ssq", "epsc"], w=["m_rstd0"])
                P.op("dve", lambda e: e.reciprocal(out=m_rstd[:], in_=m_rstd[:]), r=["m_rstd0"], w=["m_rstd"])
                P.op("dve", lambda e: e.tensor_copy(out=m_comb[:], in_=m_rstd[:]), r=["m_rstd"], w=["m_comb"])
                P.op("pool", lambda e, c=c: e.tensor_tensor(out=m_g2[:], in0=m_g[:, c, :], in1=prow[:, PR_MLNW:PR_MLNW + 512], op=ALU.mult), r=[("m_g", c, 0), ("m_g", c, 1), "prow"], w=["m_g2"])
                P.op("dve", lambda e: e.scalar_tensor_tensor(out=m_hbf[:], in0=m_h[:], scalar=m_comb[:], in1=m_g2[:], op0=ALU.mult, op1=ALU.mult),
                     r=["m_h", "m_comb", "m_g2"], w=["m_hbf"])
                transpose_out_bf(m_hbf, 4, 0, c, ["m_hbf"])
                P.op("dve", lambda e, c=c: e.tensor_scalar(out=m_kws[:], in0=m_ktm[:, c, :], scalar1=m_W[:, 127:128], scalar2=None, op0=ALU.mult), r=[("m_ktm", c), "m_W"], w=["m_kws"])
                for kc in range(2):
                    ps, keys = ps_mix.get(4)
                    P.op("pe", lambda e, ps=ps, kc=kc, c=c: e.matmul(ps[:, 0:512], lhsT=m_kws[:, kc * 128:(kc + 1) * 128], rhs=m_v[:, c, :], start=True, stop=True),
                         r=["m_kws", ("m_v", c, 0), ("m_v", c, 1)], w=keys)
                    P.op("dve", lambda e, ps=ps, kc=kc: e.scalar_tensor_tensor(out=m_C[:, kc, :], in0=m_C[:, kc, :], scalar=m_cols[:, 2:3], in1=ps[:, 0:512], op0=ALU.mult, op1=ALU.add),
                         r=keys + ["m_C", "m_cols", "m_Cb"], w=["m_C"])
                ps, keys = ps_mix.get(1)

                def fn(e, ps=ps):
                    e.matmul(ps[:, 0:1], lhsT=m_kws[:, 0:128], rhs=m_onesb[:], start=True, stop=True)
                    return e.matmul(ps[:, 1:2], lhsT=m_kws[:, 128:256], rhs=m_onesb[:], start=True, stop=True)
                P.op("pe", fn, r=["m_kws", "m_onesb"], w=keys)
                P.op("dve", lambda e, ps=ps: e.scalar_tensor_tensor(out=m_n[:], in0=m_n[:], scalar=m_cols[:, 2:3], in1=ps[:, 0:2], op0=ALU.mult, op1=ALU.add),
                     r=keys + ["m_n", "m_cols"], w=["m_n"])
                P.op("pool", lambda e: e.tensor_copy(out=m_Cb[:], in_=m_C[:]), r=["m_C"], w=["m_Cb"])
                P.op("pool", lambda e: e.tensor_copy(out=m_nb[:], in_=m_n[:]), r=["m_n"], w=["m_nb"])
            P.op("dve", lambda e: e.tensor_copy(out=m_carry[:], in_=R_["m"][0:1, TB - 1:TB]), r=["m_m", "m_wi"], w=["m_carry"])

        if "ssm" not in parts or "gdn" not in parts or "ml" not in parts:
            P.op("pool", lambda e: e.memset(ymixT[:], 0.0), w=[("ymixT", b) for b in range(16)])
        ymk = [("ymixT", b) for b in (0, 4, 8, 12)] + [("ymixT", b) for b in range(16)]

        def outproj(tb):
            if debug:
                P.op("pool", lambda e, tb=tb: e.dma_start(out=ymix_d[:, :, tb * TB:(tb + 1) * TB].rearrange("b p t -> p b t"), in_=ymixT[:]),
                     r=ymk, w=[("ymixd", tb)], dma="ymixd")
            n = 0
            for t in range(NOC):
                s = load_wo(t)
                wv_ = wo_view(s)
                for tt in range(NTT):
                    ps, keys = ps_proj.get(OW // 128)

                    def fn(e, ps=ps, wv_=wv_, tt=tt):
                        for kc in range(16):
                            ins = e.matmul(ps[:, 0:OW], lhsT=ymixT[:, kc, tt * 128:(tt + 1) * 128], rhs=wv_[:, kc, :], start=(kc == 0), stop=(kc == 15))
                        return ins
                    P.op("pe", fn, r=ymk + [("wt", s)], w=keys)
                    r0 = tb * TB + tt * 128
                    for hf in range(OW // WT):
                        ss = n % 2
                        n += 1
                        P.op("act", lambda e, ps=ps, ss=ss, hf=hf: e.activation(out=stage[ss][:], in_=ps[:, hf * WT:(hf + 1) * WT], func=AF.Copy), r=keys, w=[("stage", ss)])
                        c0 = t * OW + hf * WT
                        P.op("pool", lambda e, ss=ss, r0=r0, c0=c0: e.dma_start(out=ypart_d[r0:r0 + 128, c0:c0 + WT], in_=stage[ss][:]),
                             r=[("stage", ss)], w=[("ypart", r0, c0)], dma=("stage", ss))

        full = all(p_ in parts for p_ in ("ssm", "gdn", "ml"))
        if not full or os.environ.get("NOMERGE"):
            for tb in range(NB):
                load_xT(tb)
                if "ssm" in parts:
                    ssm_proj(tb)
                if "ssm" in parts or "gdn" in parts:
                    small_proj(tb)
                if "ssm" in parts:
                    ssm_mixer(tb)
                if "gdn" in parts:
                    gdn_proj(tb)
                    gdn_mixer(tb)
                if "ml" in parts:
                    ml_proj(tb)
                    ml_mixer(tb)
                outproj(tb)
        else:
            def head_phase(tb):
                ssm_proj(tb, "fm")
                small_proj(tb)

            def tail_phase(tb):
                ml_mixer(tb)
                outproj(tb)
            load_xT(0)
            head_phase(0)
            ssm_proj(0, "z")
            for tb in range(NB):
                A = P.capture(ssm_mixer, tb)
                mk = A.index(None)
                B = P.capture(gdn_proj, tb, "fm")
                P.replay(A[:mk])
                if "1" in MERGE:
                    P.merge(A[mk:], B)
                else:
                    P.replay(A[mk:])
                    P.replay(B)
                gdn_proj(tb, "z")
                A = P.capture(gdn_mixer, tb)
                B = P.capture(ml_proj, tb, "a")
                if "2" in MERGE:
                    P.merge(A, B)
                else:
                    P.replay(B)
                    P.replay(A)
                ml_proj(tb, "o")
                A = P.capture(ml_mixer, tb)
                if tb + 1 < NB:
                    load_xT(tb + 1)
                    B = P.capture(head_phase, tb + 1)
                    if "3" in MERGE:
                        P.merge(A, B)
                    else:
                        P.replay(A)
                        P.replay(B)
                    outproj(tb)
                    ssm_proj(tb + 1, "z")
                else:
                    P.replay(A)
                    outproj(tb)
        P.emit()
        build_layer.last_prog = P
    return nc


ML_OFF, GDN_OFF, SSM_OFF = 0, 8200, 16424


def core_columns(g):
    r = np.arange
    ml = ML_OFF
    gd = GDN_OFF
    ss = SSM_OFF
    cols = [
        ml + g * 256 + r(256),
        ml + 1024 + g * 256 + r(256),
        gd + g * 512 + r(512),
        gd + 2048 + g * 512 + r(512),
        gd + 4096 + g * 512 + r(512),
        ss + 4096 + g * 1024 + r(1024),
        ss + 8192 + g * 256 + r(256),
        ss + 9216 + g * 256 + r(256),
        ml + 2048 + g * 512 + r(512),
        ml + 4096 + g * 512 + r(512),
        ml + 6144 + g * 512 + r(512),
        gd + 6144 + g * 512 + r(512),
        ss + g * 1024 + r(1024),
        ml + 1024 + g * 256 + r(256),
        gd + 8192 + g * 4 + r(4),
        gd + 8208 + g * 4 + r(4),
        ss + 10240 + g * 16 + r(16),
        np.array([ml + 8192 + g]),
        np.array([ml + 8196 + g]),
    ]
    return np.concatenate(cols)


def pack_core(inp, l, g):
    cols = core_columns(g)
    D = inp["w_in"].shape[1]
    wcat = np.zeros((D, NCOL), np.float32)
    wcat[:, :cols.size] = inp["w_in"][l][:, cols]
    mixrows = np.concatenate([g * 512 + np.arange(512), 2048 + g * 512 + np.arange(512), 4096 + g * 1024 + np.arange(1024)])
    wout = np.ascontiguousarray(inp["w_out"][l][mixrows, :])
    gch = np.concatenate([g * 512 + np.arange(512), 2048 + g * 512 + np.arange(512), 4096 + g * 512 + np.arange(512)])
    sch = np.concatenate([g * 1024 + np.arange(1024), 4096 + g * 256 + np.arange(256), 5120 + g * 256 + np.arange(256)])
    cp = np.zeros((24 * 128, 5), np.float32)
    cp[:1536, 0:4] = inp["gdn_conv_w"][l][:, gch].T
    cp[1536:, 0:4] = inp["ssm_conv_w"][l][:, sch].T
    cp[1536:, 4] = inp["ssm_conv_b"][l][sch]
    convp = np.ascontiguousarray(cp.reshape(24, 128, 5).transpose(1, 0, 2).reshape(128, 120))
    pr = np.zeros((1, PR_N), np.float32)
    pr[0, PR_SDTB:PR_SDTB + 16] = inp["ssm_dt_bias"][l][16 * g:16 * g + 16]
    pr[0, PR_SALOG:PR_SALOG + 16] = inp["ssm_A_log"][l][16 * g:16 * g + 16]
    pr[0, PR_SD:PR_SD + 16] = inp["ssm_D"][l][16 * g:16 * g + 16]
    pr[0, PR_GDTB:PR_GDTB + 4] = inp["gdn_dt_bias"][l][4 * g:4 * g + 4]
    pr[0, PR_GALOG:PR_GALOG + 4] = inp["gdn_A_log"][l][4 * g:4 * g + 4]
    pr[0, PR_MIB] = inp["ml_i_bias"][l][g]
    pr[0, PR_MFB] = inp["ml_f_bias"][l][g]
    pr[0, PR_MLNW:PR_MLNW + 512] = inp["ml_norm_w"][l][g * 512:(g + 1) * 512]
    pr[0, PR_GNW:PR_GNW + 128] = inp["gdn_norm_w"][l]
    pr[0, PR_SNW:PR_SNW + 1024] = inp["ssm_norm_w"][l][g * 1024:(g + 1) * 1024]
    return {"wcat": wcat, "wout": wout, "convp": convp, "prow": pr, "consts": make_consts()}


def build_reduce_ln(ROWS=1024, D=D_MODEL):
    nc = bass.Bass("TRN2", target_bir_lowering=False)
    x_d = nc.dram_tensor("xin", [ROWS, D], F32, kind="ExternalInput").ap()
    p_d = [nc.dram_tensor(f"p{j}", [ROWS, D], F32, kind="ExternalInput").ap() for j in range(4)]
    g_d = nc.dram_tensor("lng", [1, D], F32, kind="ExternalInput").ap()
    b_d = nc.dram_tensor("lnb", [1, D], F32, kind="ExternalInput").ap()
    o_d = nc.dram_tensor("out", [ROWS, D], F32, kind="ExternalOutput").ap()
    CW = 2048 if D >= 2048 else D
    NCH = D // CW
    with ExitStack() as st:
        def T(name, shape, dt=F32):
            return st.enter_context(nc.sbuf_tensor("sb_" + name, shape, dt))
        P = Prog(nc, st)
        gb = T("gb", [128, D])
        bb = T("bb", [128, D])
        acc = [T(f"acc{i}", [128, D]) for i in range(2)]
        ld = [[T(f"ld{i}_{j}", [128, CW]) for j in range(5)] for i in range(2)]
        junk = T("junk", [128, D])
        stt = T("stt", [128, 8])
        epsl = T("epsl", [128, 1])
        P.op("sp", lambda e: e.dma_start(out=gb[:], in_=g_d.partition_broadcast(128)[:, 0, :]), w=["gb"], dma="gb")
        P.op("act", lambda e: e.dma_start(out=bb[:], in_=b_d.partition_broadcast(128)[:, 0, :]), w=["bb"], dma="bb")
        P.op("pool", lambda e: e.memset(epsl[:], LN_EPS), w=["epsl"])
        n = 0
        queues = ["sp", "act", "pool", "sp", "act"]
        for ti in range(ROWS // 128):
            r0 = ti * 128
            a = acc[ti % 2]
            ak = ("acc", ti % 2)
            for ch in range(NCH):
                s = n % 2
                n += 1
                cs = slice(ch * CW, (ch + 1) * CW)
                srcs = [x_d] + p_d
                for j in range(5):
                    P.op(queues[j], lambda e, s=s, j=j, cs=cs, r0=r0, srcs=srcs: e.dma_start(out=ld[s][j][:], in_=srcs[j][r0:r0 + 128, cs]),
                         w=[("ld", s, j)], dma=("ld", s, j))
                P.op("dve", lambda e, s=s, cs=cs, a=a: e.scalar_tensor_tensor(out=a[:, cs], in0=ld[s][0][:], scalar=float(ALPHA), in1=ld[s][1][:], op0=ALU.mult, op1=ALU.add),
                     r=[("ld", s, 0), ("ld", s, 1)], w=[ak])
                P.op("pool", lambda e, s=s: e.tensor_tensor(out=ld[s][3][:], in0=ld[s][3][:], in1=ld[s][4][:], op=ALU.add), r=[("ld", s, 3), ("ld", s, 4)], w=[("ld", s, 3)])
                P.op("dve", lambda e, s=s, cs=cs, a=a: e.tensor_tensor(out=a[:, cs], in0=a[:, cs], in1=ld[s][2][:], op=ALU.add), r=[ak, ("ld", s, 2)], w=[ak])
                P.op("dve", lambda e, s=s, cs=cs, a=a: e.tensor_tensor(out=a[:, cs], in0=a[:, cs], in1=ld[s][3][:], op=ALU.add), r=[ak, ("ld", s, 3)], w=[ak])
            P.op("act", lambda e, a=a: e.activation(out=junk[:], in_=a[:], func=AF.Identity, accum_out=stt[:, 0:1]), r=[ak], w=["junk", "st0"])
            P.op("dve", lambda e: e.tensor_scalar(out=stt[:, 1:2], in0=stt[:, 0:1], scalar1=-1.0 / D, scalar2=None, op0=ALU.mult), r=["st0"], w=["st1"])
            P.op("act", lambda e, a=a: e.activation(out=a[:], in_=a[:], func=AF.Identity, bias=stt[:, 1:2]), r=[ak, "st1"], w=[ak])
            P.op("act", lambda e, a=a: e.activation(out=junk[:], in_=a[:], func=AF.Square, accum_out=stt[:, 2:3]), r=[ak], w=["junk", "st2"])
            P.op("act", lambda e: e.activation(out=stt[:, 3:4], in_=stt[:, 2:3], func=AF.Sqrt, scale=1.0 / D, bias=epsl[:]), r=["st2", "epsl"], w=["st3"])
            P.op("dve", lambda e: e.reciprocal(out=stt[:, 4:5], in_=stt[:, 3:4]), r=["st3"], w=["st4"])
            P.op("dve", lambda e, a=a: e.scalar_tensor_tensor(out=a[:], in0=a[:], scalar=stt[:, 4:5], in1=gb[:], op0=ALU.mult, op1=ALU.mult), r=[ak, "st4", "gb"], w=[ak])
            P.op("dve", lambda e, a=a: e.tensor_tensor(out=a[:], in0=a[:], in1=bb[:], op=ALU.add), r=[ak, "bb"], w=[ak])
            P.op("pool", lambda e, a=a, r0=r0: e.dma_start(out=o_d[r0:r0 + 128, :], in_=a[:]), r=[ak], w=[("out", ti)], dma=ak)
        P.emit()
    return nc


_PROGS = {}


def kernel(x, w_in, w_out, ml_i_bias, ml_f_bias, ml_norm_w, gdn_conv_w, gdn_A_log, gdn_dt_bias,
           gdn_norm_w, ssm_conv_w, ssm_conv_b, ssm_A_log, ssm_dt_bias, ssm_D, ssm_norm_w, ln_g, ln_b):
    inp = dict(w_in=np.asarray(w_in), w_out=np.asarray(w_out), ml_i_bias=np.asarray(ml_i_bias), ml_f_bias=np.asarray(ml_f_bias),
               ml_norm_w=np.asarray(ml_norm_w), gdn_conv_w=np.asarray(gdn_conv_w), gdn_A_log=np.asarray(gdn_A_log),
               gdn_dt_bias=np.asarray(gdn_dt_bias), gdn_norm_w=np.asarray(gdn_norm_w), ssm_conv_w=np.asarray(ssm_conv_w),
               ssm_conv_b=np.asarray(ssm_conv_b), ssm_A_log=np.asarray(ssm_A_log), ssm_dt_bias=np.asarray(ssm_dt_bias),
               ssm_D=np.asarray(ssm_D), ssm_norm_w=np.asarray(ssm_norm_w))
    ln_g = np.asarray(ln_g, np.float32)
    ln_b = np.asarray(ln_b, np.float32)
    xcur = np.ascontiguousarray(np.asarray(x, np.float32))
    B, S, D = xcur.shape
    if "layer" not in _PROGS:
        _PROGS["layer"] = build_layer(D=D, S=S)
        _PROGS["ln"] = build_reduce_ln(ROWS=B * S // 8, D=D)
    RPC = B * S // 8
    for l in range(DEPTH):
        packs = [pack_core(inp, l, g) for g in range(NG)]
        in_maps = [dict(packs[c % NG], x=xcur[c // NG]) for c in range(8)]
        res = run_bass_kernel_spmd(_PROGS["layer"], in_maps, core_ids=list(range(8)))
        yp = [r["ypart"] for r in res.results]
        del in_maps, packs
        xf = xcur.reshape(B * S, D)
        in_maps = []
        for c in range(8):
            b = (c * RPC) // S
            r0 = (c * RPC) % S
            m = {"xin": np.ascontiguousarray(xf[c * RPC:(c + 1) * RPC]), "lng": ln_g[l][None, :], "lnb": ln_b[l][None, :]}
            for j in range(NG):
                m[f"p{j}"] = np.ascontiguousarray(yp[b * NG + j][r0:r0 + RPC])
            in_maps.append(m)
        res = run_bass_kernel_spmd(_PROGS["ln"], in_maps, core_ids=list(range(8)))
        xcur = np.concatenate([r["out"] for r in res.results], axis=0).reshape(B, S, D)
    return xcur.astype(np.float32)
```
